# Optimizing a Trainium2 kernel written in Bass

```python
import math
import jax, jax.numpy as jnp
from jax import lax
import numpy as np

D_MODEL = 2048
BATCH = 4
SEQ = 2048
DEPTH = 2
DEC_BATCH = 128
DEC_SEQ = 1
PAST_LEN = 16384
PAGE_SIZE = 128

GDN_HEADS = 8
GDN_DK = 128
GDN_DV = 128
GLA_HEADS = 4
GLA_DK = 128
GLA_DV = 256
GLA_RANK = 16
GLA_TAU = 16.0
SSD_HEADS = 32
SSD_HEADDIM = 64
SSD_GROUPS = 4
SSD_STATE = 128
SSD_INNER = SSD_HEADS * SSD_HEADDIM
SSD_NORM_GROUPS = SSD_GROUPS
CONV_WIDTH = 4
CHUNK = 64
D_FF = 5504
N_BRANCH = 3
N_MOD = 9
EPS = 1e-6

GDN_QK = GDN_HEADS * GDN_DK
GDN_V = GDN_HEADS * GDN_DV
GDN_CONV_DIM = 2 * GDN_QK + GDN_V
GLA_QK = GLA_HEADS * GLA_DK
GLA_V = GLA_HEADS * GLA_DV
SSD_BC = SSD_GROUPS * SSD_STATE
SSD_CONV_DIM = SSD_INNER + 2 * SSD_BC
IN_SPLITS = (GDN_CONV_DIM, GDN_V, GDN_HEADS, GDN_HEADS,
             GLA_QK, GLA_QK, GLA_V, GLA_RANK, GLA_V,
             SSD_INNER, SSD_CONV_DIM, SSD_HEADS,
             N_BRANCH * D_MODEL)
IN_TOTAL = sum(IN_SPLITS)

kernel_name = "hybrid_gdn_gla_ssd_macaron_adaln_step"


def split_cols(t, sizes):
    offs = [int(o) for o in np.cumsum(sizes)[:-1]]
    return jnp.split(t, offs, axis=-1)


def rms_norm(x, w):
    xf = x.astype(jnp.float32)
    y = xf * lax.rsqrt(jnp.mean(xf * xf, axis=-1, keepdims=True) + EPS)
    return (y * w.astype(jnp.float32)).astype(x.dtype)


def l2norm(x):
    return x * lax.rsqrt(jnp.sum(x * x, axis=-1, keepdims=True) + EPS)


def swiglu(h, wg, wu, wd):
    return (jax.nn.silu(h @ wg) * (h @ wu)) @ wd


def causal_conv(x, buf, w):
    L = x.shape[1]
    xp = jnp.concatenate([buf.astype(x.dtype), x], axis=1)
    y = xp[:, 0:L] * w[0]
    for j in range(1, CONV_WIDTH):
        y = y + xp[:, j:j + L] * w[j]
    return y, xp[:, xp.shape[1] - (CONV_WIDTH - 1):]


def to_chunks(t, chunk):
    b, l = t.shape[:2]
    return jnp.moveaxis(t.reshape((b, l // chunk, chunk) + t.shape[2:]), 1, 0)


def from_chunks(t):
    n, b, c = t.shape[:3]
    return jnp.moveaxis(t, 0, 1).reshape((b, n * c) + t.shape[3:])


def gated_delta_rule(q, k, v, beta, g, S0, chunk):
    tri = jnp.tril(jnp.ones((chunk, chunk), bool))
    strict = jnp.tril(jnp.ones((chunk, chunk), bool), -1)
    eye = jnp.eye(chunk, dtype=jnp.float32)
    dv = v.shape[-1]

    def step(S, inp):
        qc, kc, vc, bc, gc = inp
        gcum = jnp.cumsum(gc, axis=1).transpose(0, 2, 1)
        decay = jnp.exp(jnp.where(tri, gcum[..., :, None] - gcum[..., None, :], -jnp.inf))
        bt = bc.transpose(0, 2, 1)
        kk = jnp.einsum('bthd,bshd->bhts', kc, kc)
        a_mat = eye + jnp.where(strict, bt[..., :, None] * decay * kk, 0.0)
        rhs = jnp.concatenate([jnp.einsum('bshv,bhs->bhsv', vc, bt),
                               jnp.einsum('bshd,bhs->bhsd', kc, bt * jnp.exp(gcum))], axis=-1)
        sol = lax.linalg.triangular_solve(a_mat, rhs, left_side=True, lower=True, unit_diagonal=True)
        w = sol[..., :dv] - jnp.einsum('bhsd,bhdv->bhsv', sol[..., dv:], S)
        qk = jnp.einsum('bthd,bshd->bhts', qc, kc) * decay
        o = (jnp.einsum('bthd,bhdv->bhtv', qc, S) * jnp.exp(gcum)[..., None]
             + jnp.einsum('bhts,bhsv->bhtv', qk, w))
        g_last = gcum[..., -1:]
        S = S * jnp.exp(g_last)[..., None] + jnp.einsum('bshd,bhs,bhsv->bhdv', kc, jnp.exp(g_last - gcum), w)
        return S, o.transpose(0, 2, 1, 3)

    S, o = lax.scan(step, S0, tuple(to_chunks(t, chunk) for t in (q, k, v, beta, g)))
    return from_chunks(o), S


def gla_recurrence(q, k, v, log_a, S0, chunk):
    tri = jnp.tril(jnp.ones((chunk, chunk), bool))[None, :, :, None, None]

    def step(S, inp):
        qc, kc, vc, ac = inp
        b = jnp.cumsum(ac, axis=1)
        decay = jnp.exp(jnp.where(tri, b[:, :, None] - b[:, None, :], -jnp.inf))
        att = jnp.sum(qc[:, :, None] * kc[:, None, :] * decay, axis=-1)
        o = (jnp.einsum('btsh,bshv->bthv', att, vc)
             + jnp.einsum('bthd,bhdv->bthv', qc * jnp.exp(b), S))
        b_last = b[:, -1]
        S = S * jnp.exp(b_last)[..., None] + jnp.einsum('bshd,bshv->bhdv', kc * jnp.exp(b_last[:, None] - b), vc)
        return S, o

    S, o = lax.scan(step, S0, tuple(to_chunks(t, chunk) for t in (q, k, v, log_a)))
    return from_chunks(o), S


def ssd_recurrence(x, dt, A, Bm, Cm, h0, chunk):
    tri = jnp.tril(jnp.ones((chunk, chunk), bool))
    rep = x.shape[2] // Bm.shape[2]

    def step(h, inp):
        xc, dtc, bc, cc = inp
        acum = jnp.cumsum(dtc * A, axis=1).transpose(0, 2, 1)
        decay = jnp.exp(jnp.where(tri, acum[..., :, None] - acum[..., None, :], -jnp.inf))
        cb = jnp.repeat(jnp.einsum('btgn,bsgn->bgts', cc, bc), rep, axis=1)
        xdt = xc * dtc[..., None]
        ch = jnp.repeat(cc, rep, axis=2)
        bh = jnp.repeat(bc, rep, axis=2)
        y = (jnp.einsum('bhts,bshp->bthp', cb * decay, xdt)
             + jnp.einsum('bthn,bhpn->bthp', ch, h) * jnp.exp(acum).transpose(0, 2, 1)[..., None])
        a_last = acum[..., -1:]
        h = h * jnp.exp(a_last)[..., None] + jnp.einsum('bshn,bhs,bshp->bhpn', bh, jnp.exp(a_last - acum), xdt)
        return h, y

    h, y = lax.scan(step, h0, tuple(to_chunks(t, chunk) for t in (x, dt, Bm, Cm)))
    return from_chunks(y), h


def token_mixer(xn, p, conv_a_buf, s_a, s_b, conv_c_buf, s_c):
    f32 = jnp.float32
    bsz, L, _ = xn.shape
    (qkv_a, z_a, beta_a, dec_a, q_b, k_b, v_b, lr_b, r_b,
     z_c, xbc_c, dt_c, gates) = split_cols(xn @ p['w_in'], IN_SPLITS)

    qkv_a, conv_a_new = causal_conv(qkv_a, conv_a_buf, p['gdn_conv_w'])
    q_a, k_a, v_a = split_cols(jax.nn.silu(qkv_a.astype(f32)), (GDN_QK, GDN_QK, GDN_V))
    q_a = l2norm(q_a.reshape(bsz, L, GDN_HEADS, GDN_DK)) * (GDN_DK ** -0.5)
    k_a = l2norm(k_a.reshape(bsz, L, GDN_HEADS, GDN_DK))
    v_a = v_a.reshape(bsz, L, GDN_HEADS, GDN_DV)
    beta_a = jax.nn.sigmoid(beta_a.astype(f32))
    g_a = -jnp.exp(p['gdn_a_log'].astype(f32)) * jax.nn.softplus(dec_a.astype(f32) + p['gdn_dt_bias'].astype(f32))

    q_b = q_b.astype(f32).reshape(bsz, L, GLA_HEADS, GLA_DK) * (GLA_DK ** -0.5)
    k_b = k_b.astype(f32).reshape(bsz, L, GLA_HEADS, GLA_DK)
    v_b = v_b.astype(f32).reshape(bsz, L, GLA_HEADS, GLA_DV)
    la_b = (jax.nn.log_sigmoid((lr_b @ p['gla_w_gate'] + p['gla_b_gate']).astype(f32)) / GLA_TAU
            ).reshape(bsz, L, GLA_HEADS, GLA_DK)

    xbc_c, conv_c_new = causal_conv(xbc_c, conv_c_buf, p['ssd_conv_w'])
    xbc_c = jax.nn.silu((xbc_c + p['ssd_conv_b']).astype(f32))
    x_c, b_c, c_c = split_cols(xbc_c, (SSD_INNER, SSD_BC, SSD_BC))
    x_c = x_c.reshape(bsz, L, SSD_HEADS, SSD_HEADDIM)
    b_c = b_c.reshape(bsz, L, SSD_GROUPS, SSD_STATE)
    c_c = c_c.reshape(bsz, L, SSD_GROUPS, SSD_STATE)
    dt = jax.nn.softplus(dt_c.astype(f32) + p['ssd_dt_bias'].astype(f32))
    A = -jnp.exp(p['ssd_a_log'].astype(f32))

    chunk = min(CHUNK, L)
    Lp = -(-L // chunk) * chunk

    def pad(t):
        return jnp.pad(t, [(0, 0), (0, Lp - L)] + [(0, 0)] * (t.ndim - 2))

    o_a, s_a_new = gated_delta_rule(pad(q_a), pad(k_a), pad(v_a), pad(beta_a), pad(g_a), s_a.astype(f32), chunk)
    o_b, s_b_new = gla_recurrence(pad(q_b), pad(k_b), pad(v_b), pad(la_b), s_b.astype(f32), chunk)
    y_c, s_c_new = ssd_recurrence(pad(x_c), pad(dt), A, pad(b_c), pad(c_c), s_c.astype(f32), chunk)

    o_a = (rms_norm(o_a[:, :L], p['gdn_norm_w'])
           * jax.nn.silu(z_a.astype(f32).reshape(bsz, L, GDN_HEADS, GDN_DV))).reshape(bsz, L, GDN_V)
    o_b = (rms_norm(o_b[:, :L], p['gla_norm_w'])
           * jax.nn.silu(r_b.astype(f32).reshape(bsz, L, GLA_HEADS, GLA_DV))).reshape(bsz, L, GLA_V)
    y_c = (y_c[:, :L] + p['ssd_d'].astype(f32)[:, None] * x_c).reshape(bsz, L, SSD_INNER)
    y_c = y_c * jax.nn.silu(z_c.astype(f32))
    y_c = rms_norm(y_c.reshape(bsz, L, SSD_NORM_GROUPS, SSD_INNER // SSD_NORM_GROUPS),
                   p['ssd_norm_w'].reshape(SSD_NORM_GROUPS, SSD_INNER // SSD_NORM_GROUPS)).reshape(bsz, L, SSD_INNER)

    g_a_br, g_b_br, g_c_br = jnp.split(jax.nn.sigmoid(gates.astype(f32)), N_BRANCH, axis=-1)
    merged = (g_a_br * (o_a @ p['w_branch_gdn']) + g_b_br * (o_b @ p['w_branch_gla'])
              + g_c_br * (y_c @ p['w_branch_ssd']))
    out = (merged @ p['w_out']).astype(xn.dtype)
    new_state = (conv_a_new.astype(conv_a_buf.dtype), s_a_new.astype(s_a.dtype), s_b_new.astype(s_b.dtype),
                 conv_c_new.astype(conv_c_buf.dtype), s_c_new.astype(s_c.dtype))
    return out, new_state


def decoder_layer(x, c, p, st):
    mod = jax.nn.silu(c) @ p['w_ada'] + p['b_ada']
    sh1, sc1, gt1, sh2, sc2, gt2, sh3, sc3, gt3 = jnp.split(mod[:, None, :], N_MOD, axis=-1)
    h = rms_norm(x, p['norm1']) * (1.0 + sc1) + sh1
    x = x + 0.5 * gt1 * swiglu(h, p['ffn1_wg'], p['ffn1_wu'], p['ffn1_wd'])
    h = rms_norm(x, p['norm2']) * (1.0 + sc2) + sh2
    m, st = token_mixer(h, p, *st)
    x = x + gt2 * m
    h = rms_norm(x, p['norm3']) * (1.0 + sc3) + sh3
    x = x + 0.5 * gt3 * swiglu(h, p['ffn2_wg'], p['ffn2_wu'], p['ffn2_wd'])
    return x, st


def run_trunk(x, c, states, layer_params, final_norm):
    per_layer = []
    for l in range(DEPTH):
        x, st = decoder_layer(x, c, layer_params[l], tuple(s[l] for s in states))
        per_layer.append(st)
    new_states = tuple(jnp.stack([st[i] for st in per_layer]) for i in range(len(states)))
    return rms_norm(x, final_norm), new_states


def setup_inputs(seed: int = 0) -> dict:
    key = jax.random.key(seed)
    ks = jax.random.split(key, 48)
    f32 = jnp.float32
    L, D = DEPTH, D_MODEL

    def nrm(k, shape, scale):
        return scale * jax.random.normal(k, shape, f32)

    def gain(k, shape):
        return 1.0 + nrm(k, shape, 0.02)

    def dt_bias(k, shape):
        dt = jnp.exp(jax.random.uniform(k, shape, f32, math.log(1e-3), math.log(1e-1)))
        return dt + jnp.log(-jnp.expm1(-dt))

    def a_log(k, shape):
        return jnp.log(jax.random.uniform(k, shape, f32, 1.0, 16.0))

    return {
        "x_prompt": nrm(ks[0], (BATCH, SEQ, D), 1.0),
        "x_sample": nrm(ks[1], (DEC_BATCH, DEC_SEQ, D), 1.0),
        "state_gdn_conv": nrm(ks[2], (L, DEC_BATCH, CONV_WIDTH - 1, GDN_CONV_DIM), 1.0),
        "state_gdn": nrm(ks[3], (L, DEC_BATCH, GDN_HEADS, GDN_DK, GDN_DV), 0.1),
        "state_gla": nrm(ks[4], (L, DEC_BATCH, GLA_HEADS, GLA_DK, GLA_DV), 0.1),
        "state_ssd_conv": nrm(ks[5], (L, DEC_BATCH, CONV_WIDTH - 1, SSD_CONV_DIM), 1.0),
        "state_ssd": nrm(ks[6], (L, DEC_BATCH, SSD_HEADS, SSD_HEADDIM, SSD_STATE), 0.1),
        "c_prompt": nrm(ks[7], (BATCH, D), 1.0),
        "c_sample": nrm(ks[8], (DEC_BATCH, D), 1.0),
        "w_ada": nrm(ks[9], (L, D, N_MOD * D), 0.5 * D ** -0.5),
        "b_ada": nrm(ks[10], (L, N_MOD * D), 0.02),
        "norm1": gain(ks[11], (L, D)),
        "norm2": gain(ks[12], (L, D)),
        "norm3": gain(ks[13], (L, D)),
        "ffn1_wg": nrm(ks[14], (L, D, D_FF), D ** -0.5),
        "ffn1_wu": nrm(ks[15], (L, D, D_FF), D ** -0.5),
        "ffn1_wd": nrm(ks[16], (L, D_FF, D), D_FF ** -0.5),
        "ffn2_wg": nrm(ks[17], (L, D, D_FF), D ** -0.5),
        "ffn2_wu": nrm(ks[18], (L, D, D_FF), D ** -0.5),
        "ffn2_wd": nrm(ks[19], (L, D_FF, D), D_FF ** -0.5),
        "w_in": nrm(ks[20], (L, D, IN_TOTAL), D ** -0.5),
        "gdn_conv_w": nrm(ks[21], (L, CONV_WIDTH, GDN_CONV_DIM), CONV_WIDTH ** -0.5),
        "gdn_a_log": a_log(ks[22], (L, GDN_HEADS)),
        "gdn_dt_bias": dt_bias(ks[23], (L, GDN_HEADS)),
        "gdn_norm_w": gain(ks[24], (L, GDN_DV)),
        "gla_w_gate": nrm(ks[25], (L, GLA_RANK, GLA_QK), GLA_RANK ** -0.5),
        "gla_b_gate": nrm(ks[26], (L, GLA_QK), 0.02),
        "gla_norm_w": gain(ks[27], (L, GLA_DV)),
        "ssd_conv_w": nrm(ks[28], (L, CONV_WIDTH, SSD_CONV_DIM), CONV_WIDTH ** -0.5),
        "ssd_conv_b": nrm(ks[29], (L, SSD_CONV_DIM), 0.02),
        "ssd_a_log": a_log(ks[30], (L, SSD_HEADS)),
        "ssd_dt_bias": dt_bias(ks[31], (L, SSD_HEADS)),
        "ssd_d": gain(ks[32], (L, SSD_HEADS)),
        "ssd_norm_w": gain(ks[33], (L, SSD_INNER)),
        "w_branch_gdn": nrm(ks[34], (L, GDN_V, D), GDN_V ** -0.5),
        "w_branch_gla": nrm(ks[35], (L, GLA_V, D), GLA_V ** -0.5),
        "w_branch_ssd": nrm(ks[36], (L, SSD_INNER, D), SSD_INNER ** -0.5),
        "w_out": nrm(ks[37], (L, D, D), D ** -0.5),
        "final_norm": gain(ks[38], (D,)),
    }


def reference(x_prompt, x_sample, state_gdn_conv, state_gdn, state_gla, state_ssd_conv, state_ssd,
              c_prompt, c_sample, w_ada, b_ada, norm1, norm2, norm3,
              ffn1_wg, ffn1_wu, ffn1_wd, ffn2_wg, ffn2_wu, ffn2_wd,
              w_in, gdn_conv_w, gdn_a_log, gdn_dt_bias, gdn_norm_w,
              gla_w_gate, gla_b_gate, gla_norm_w,
              ssd_conv_w, ssd_conv_b, ssd_a_log, ssd_dt_bias, ssd_d, ssd_norm_w,
              w_branch_gdn, w_branch_gla, w_branch_ssd, w_out, final_norm):
    layer_params = [dict(w_ada=w_ada[l], b_ada=b_ada[l], norm1=norm1[l], norm2=norm2[l], norm3=norm3[l],
                         ffn1_wg=ffn1_wg[l], ffn1_wu=ffn1_wu[l], ffn1_wd=ffn1_wd[l],
                         ffn2_wg=ffn2_wg[l], ffn2_wu=ffn2_wu[l], ffn2_wd=ffn2_wd[l],
                         w_in=w_in[l], gdn_conv_w=gdn_conv_w[l], gdn_a_log=gdn_a_log[l],
                         gdn_dt_bias=gdn_dt_bias[l], gdn_norm_w=gdn_norm_w[l],
                         gla_w_gate=gla_w_gate[l], gla_b_gate=gla_b_gate[l], gla_norm_w=gla_norm_w[l],
                         ssd_conv_w=ssd_conv_w[l], ssd_conv_b=ssd_conv_b[l], ssd_a_log=ssd_a_log[l],
                         ssd_dt_bias=ssd_dt_bias[l], ssd_d=ssd_d[l], ssd_norm_w=ssd_norm_w[l],
                         w_branch_gdn=w_branch_gdn[l], w_branch_gla=w_branch_gla[l],
                         w_branch_ssd=w_branch_ssd[l], w_out=w_out[l])
                    for l in range(DEPTH)]
    sample_states = (state_gdn_conv, state_gdn, state_gla, state_ssd_conv, state_ssd)
    nb = x_prompt.shape[0]
    prompt_states = tuple(jnp.zeros((s.shape[0], nb) + s.shape[2:], x_prompt.dtype) for s in sample_states)
    y_prompt, (p_gdn_conv, p_gdn, p_gla, p_ssd_conv, p_ssd) = run_trunk(
        x_prompt, c_prompt, prompt_states, layer_params, final_norm)
    y_sample, (s_gdn_conv, s_gdn, s_gla, s_ssd_conv, s_ssd) = run_trunk(
        x_sample, c_sample, sample_states, layer_params, final_norm)
    return (y_prompt, y_sample, p_gdn_conv, p_gdn, p_gla, p_ssd_conv, p_ssd,
            s_gdn_conv, s_gdn, s_gla, s_ssd_conv, s_ssd)
```

```python
import numpy as np
from contextlib import ExitStack
import concourse.bass as bass
import concourse.mybir as mybir
from concourse.bass_utils import run_bass_kernel_spmd

F32, BF16 = mybir.dt.float32, mybir.dt.bfloat16
AF = mybir.ActivationFunctionType
ALU = mybir.AluOpType
AX = mybir.AxisListType

FULL = dict(D=2048, FF=5504, SEQ=2048, HA=8, HB=4, HC=32, G=4)
NT = 512
NS = 16
NV = 17
EPS = 1e-6
RS = 3


class _Rec:
    def __init__(self):
        self.call = None

    def __getattr__(self, name):
        def f(*a, **k):
            self.call = (name, a, k)
            return self
        return f


def _bind(fn):
    r = _Rec()
    fn(r)
    name, a, k = r.call
    return lambda eng: getattr(eng, name)(*a, **k)


class Prog:
    ENG = ("pe", "act", "dve", "pool", "sp")

    def __init__(self):
        self.ops = {e: [] for e in self.ENG}
        self.cnt = {e: 0 for e in ("pe", "act", "dve")}
        self.lastw = {}
        self.readers = {}
        self.dcnt = {}
        self.known = {e: {} for e in self.ENG}

    def _deps(self, eng, reads, writes):
        need = {}

        def add(t, raw):
            s, v, e = t
            if e == eng and eng == "pe":
                return
            if need.get(s, 0) < v:
                need[s] = v
        for k in reads:
            if k in self.lastw:
                add(self.lastw[k], True)
        for k in writes:
            if k in self.lastw:
                add(self.lastw[k], True)
            for r in self.readers.get(k, ()):
                add(r, False)
        kn = self.known[eng]
        out = []
        for s, v in need.items():
            if kn.get(s, 0) < v:
                kn[s] = v
                out.append((s, v))
        return out

    def _commit(self, tok, reads, writes):
        for k in reads:
            self.readers.setdefault(k, []).append(tok)
        for k in writes:
            self.lastw[k] = tok
            self.readers[k] = []

    def op(self, eng, fn, reads=(), writes=()):
        writes = list(writes) + [k for k in reads if k.startswith("ps")]
        reads = [k for k in reads if not k.startswith("ps")]
        waits = self._deps(eng, reads, writes)
        self.cnt[eng] += 1
        s = "c_" + eng
        self.ops[eng].append((waits, _bind(fn), s, 1))
        self._commit((s, self.cnt[eng], eng), reads, writes)

    def dma(self, q, fn, semkey, reads=(), writes=()):
        waits = self._deps(q, reads, writes)
        n = self.dcnt.get(semkey, 0)
        s = "d_" + semkey
        if n > 0 and self.known[q].get(s, 0) < 16 * n:
            self.known[q][s] = 16 * n
            waits.append((s, 16 * n))
        self.dcnt[semkey] = n + 1
        self.ops[q].append((waits, _bind(fn), s, 16))
        self._commit((s, 16 * (n + 1), "dma"), reads, writes)

    def emit(self, nc, es):
        names = ["c_pe", "c_act", "c_dve"] + ["d_" + k for k in self.dcnt]
        sems = {n: es.enter_context(nc.semaphore(n)) for n in names}
        finals = [(sems["d_" + k], 16 * n) for k, n in self.dcnt.items()]
        block = es.enter_context(nc.Block())
        decs = {"pe": block.tensor, "act": block.scalar, "dve": block.vector, "pool": block.gpsimd, "sp": block.sync}
        for e in self.ENG:
            ops = self.ops[e]

            def body(eng, ops=ops, last=(e == "sp")):
                for waits, fn, s, inc in ops:
                    for ws, wv in waits:
                        eng.wait_ge(sems[ws], wv)
                    fn(eng).then_inc(sems[s], inc)
                if last:
                    for sm, v in finals:
                        eng.wait_ge(sm, v)
            decs[e](body)


def unit_plan(cfg):
    HA, HB, HC, G, KC = cfg["HA"], cfg["HB"], cfg["HC"], cfg["G"], cfg["D"] // 128
    names = ["beta", "dec"]
    for h in range(HA):
        names += [f"aq{h}", f"ak{h}", f"av{h}", f"az{h}"]
    names += ["lr"]
    for h in range(HB):
        names += [f"bq{h}", f"bk{h}", f"bv{h}_0", f"bv{h}_1", f"br{h}_0", f"br{h}_1"]
    names += ["dt"]
    for g in range(G):
        names += [f"cB{g}", f"cC{g}"]
    for u in range(HC // 2):
        names += [f"cx{u}", f"cz{u}"]
    for br in range(3):
        for dc in range(KC):
            names += [f"gate{br}_{dc}"]
    return {n: i for i, n in enumerate(names)}


def build(cfg):
    D, FF, SEQ, HA, HB, HC, G = (cfg[k] for k in ("D", "FF", "SEQ", "HA", "HB", "HC", "G"))
    KC = D // 128
    FC = FF // 128
    FH = (FC + 2) // 3
    NU_C = HC // 2
    HPG = HC // G
    UPN = NU_C // G
    NCA = 3 * HA
    NCC = NU_C + 2 * G
    NPT = SEQ // NT
    UP = unit_plan(cfg)
    NUW = len(UP)
    assert KC <= 16 and NU_C <= 16 and HC <= 32

    nc = bass.Bass("TRN2", target_bir_lowering=False)
    P = Prog()
    es = ExitStack()

    def din(name, shape):
        return nc.dram_tensor(name, list(shape), F32, kind="ExternalInput").ap()

    def dout(name, shape):
        return nc.dram_tensor(name, list(shape), F32, kind="ExternalOutput").ap()

    def sb(name, shape, dt=F32):
        return es.enter_context(nc.sbuf_tensor(name, list(shape), dt))

    xp_d = din("xp", [128, KC, SEQ])
    xs_d = din("xs", [128, KC, NS])
    cT_d = din("cT", [128, KC, NV])
    consts_d = din("consts", [128, 512])
    wada_d = din("wada", [2, 9 * KC, 128, KC, 128])
    bada_d = din("bada", [2, 128, 9 * KC])
    norms_d = din("norms", [128, 7, KC])
    ffn_d = {}
    for l in range(2):
        for w in (1, 2):
            ffn_d[(l, w)] = (din(f"wg{l}{w}", [FC, 128, KC, 128]), din(f"wu{l}{w}", [FC, 128, KC, 128]),
                             din(f"wd{l}{w}", [KC, 128, FC, 128]))
    win_d = [din(f"win{l}", [NUW, 128, KC, 128]) for l in range(2)]
    wba_d = [din(f"wba{l}", [KC, 128, HA, 128]) for l in range(2)]
    wbb_d = [din(f"wbb{l}", [KC, 128, 2 * HB, 128]) for l in range(2)]
    wbc_d = [din(f"wbc{l}", [KC, 128, NU_C, 128]) for l in range(2)]
    wo_d = [din(f"wo{l}", [KC, 128, KC, 128]) for l in range(2)]
    NPV = 4 + HB + 2 * NU_C + 32
    pv_d = din("pv", [2, 128, NPV])
    cwa_d = din("cwa", [2, 128, NCA, 4])
    cwc_d = din("cwc", [2, 128, NCC, 5])
    nrm_d = din("nrm", [2, 128, 384])
    wgate_d = din("wgate", [2, 16, HB * 128])
    s_gdn_d = din("s_gdn", [2, NS, HA, 128, 128])
    s_gla_d = din("s_gla", [2, NS, HB, 128, 256])
    s_ssd_d = din("s_ssd", [2, NS, NU_C, 128, 128])
    s_cva_d = din("s_cva", [2, 128, NCA, 3, NS])
    s_cvc_d = din("s_cvc", [2, 128, NCC, 3, NS])
    yp_d = dout("yp", [128, KC, SEQ])
    ys_d = dout("ys", [128, KC, NS])
    o_gdn_d = dout("o_gdn", [2, HA, 128, 128])
    o_gla_d = dout("o_gla", [2, HB, 128, 256])
    o_ssd_d = dout("o_ssd", [2, NU_C, 128, 128])
    o_cva_d = dout("o_cva", [2, 128, NCA, 3])
    o_cvc_d = dout("o_cvc", [2, 128, NCC, 3])
    so_gdn_d = dout("so_gdn", [2, NS, HA, 128, 128])
    so_gla_d = dout("so_gla", [2, NS, HB, 128, 256])
    so_ssd_d = dout("so_ssd", [2, NS, NU_C, 128, 128])
    so_cva_d = dout("so_cva", [2, 128, NCA, 3, NS])
    so_cvc_d = dout("so_cvc", [2, 128, NCC, 3, NS])

    x = sb("x", [128, KC, NT])
    h = sb("h", [128, KC, NT], BF16)
    scr = sb("scr", [128, 32, NT], BF16)
    ring = sb("ring", [128, RS, 16, 128], BF16)
    wdr = sb("wdr", [128, 2, FH, 128], BF16)
    mod = [sb(f"mod{l}", [128, 9 * KC, NV]) for l in range(2)]
    cst = sb("cst", [128, 512])
    ident_b = sb("ident_b", [128, 128], BF16)
    ones_b = sb("ones_b", [128, 128], BF16)
    U_b = sb("U_b", [128, 128], BF16)
    norms = sb("norms_sb", [128, 7, KC])
    scT = sb("scT", [128, KC, NV], BF16)
    tmp = sb("tmp", [128, 2, NT])
    rstd = sb("rstd", [128, NT])
    pv = sb("pv_sb", [128, 2, NPV])
    cwa = sb("cwa_sb", [128, 2, NCA, 4])
    cwc = sb("cwc_sb", [128, 2, NCC, 5])
    nrm1 = sb("nrm_sb", [128, 384])
    wgate = sb("wgate_b", [16, 2, HB * 128], BF16)
    nSa, nSb, nSc = 2 * HA * 128, 2 * HB * 256, 2 * NU_C * 128
    nA, nC = NCA * 3 * NS, NCC * 3 * NS
    PSM = sb("PSM", [128, max(nSa + nSb + nSc, nA + nC + NCA * NS + NCC * NS + 512)])
    Sa = PSM[:, 0:nSa].rearrange("p (l h d) -> p l h d", l=2, h=HA)
    Sb_ = PSM[:, nSa:nSa + nSb].rearrange("p (l h d) -> p l h d", l=2, h=HB)
    Sc = PSM[:, nSa + nSb:nSa + nSb + nSc].rearrange("p (l h d) -> p l h d", l=2, h=NU_C)
    shA = PSM[:, 0:nA].rearrange("p (u j b) -> p u j b", u=NCA, j=3)
    shC = PSM[:, nA:nA + nC].rearrange("p (u j b) -> p u j b", u=NCC, j=3)
    o2 = nA + nC
    xsA = PSM[:, o2:o2 + NCA * NS].rearrange("p (u b) -> p u b", u=NCA)
    xsC = PSM[:, o2 + NCA * NS:o2 + NCA * NS + NCC * NS].rearrange("p (u b) -> p u b", u=NCC)
    o3 = o2 + NCA * NS + NCC * NS
    SS = PSM[:, o3:o3 + 512].rearrange("p (s d) -> p s d", s=2)
    PKEYS = [f"Sa{l}_{i}" for l in range(2) for i in range(HA)] + [f"Sb{l}_{i}" for l in range(2) for i in range(HB)] + \
            [f"Sc{l}_{i}" for l in range(2) for i in range(NU_C)]
    hista = sb("hista", [128, 2, NCA, 3])
    histc = sb("histc", [128, 2, NCC, 3])
    W = [sb(f"W{i}", [128, NT + 4]) for i in range(9)]
    Vb = [sb(f"V{i}", [128, NT], BF16) for i in range(6)]
    A_ = [sb(f"a{i}", [128, 256 if i == 4 else 128]) for i in range(13)]
    B_ = [sb(f"b{i}", [128, 256 if i in (0, 1, 10, 11, 12) else 128], BF16) for i in range(13)]
    tok = sb("tokv", [128, 16, 64])
    cs1 = sb("cs1", [128, 8])
    psb = [es.enter_context(nc.psum_tensor(f"ps{i}", [128, NT], F32)) for i in range(8)]
    ident = cst[:, 0:128]
    Umat = cst[:, 128:256]
    Lsmat = cst[:, 256:384]
    ones_f = cst[:, 384:512]

    def pk(b, q0=0, q1=4):
        return [f"ps{b}"]

    state = {"ring": 0, "wd": 0, "tmp": 0}

    def ring_load(src_ap, nk=KC):
        s = state["ring"] % RS
        state["ring"] += 1
        P.dma("pool", lambda e, s=s: e.dma_start(out=ring[:, s, 0:nk, :], in_=src_ap), f"ring{s}", writes=[f"ring{s}"])
        return s

    def nexttmp():
        i = state["tmp"] % 2
        state["tmp"] += 1
        return i

    def V(fn, r, w):
        P.op("dve", fn, r, w)

    def Ac(fn, r, w):
        P.op("act", fn, r, w)

    def T(fn, r, w):
        P.op("pe", fn, r, w)

    def ld(dst, src, key):
        P.dma("sp", lambda e: e.dma_start(out=dst, in_=src), key, writes=[key])
    ld(cst[:], consts_d, "cst")
    ld(norms[:], norms_d, "norms")
    bada = W[1][:, 0:2 * 9 * KC].rearrange("p (l u) -> p l u", l=2)
    cT = W[0][:, 0:KC * NV].rearrange("p (k v) -> p k v", k=KC)
    P.dma("sp", lambda e: e.dma_start(out=bada, in_=bada_d.rearrange("l p u -> p l u")), "bada", writes=["W1"])
    P.dma("sp", lambda e: e.dma_start(out=cT, in_=cT_d), "cT", writes=["W0"])
    ld(pv[:], pv_d.rearrange("l p u -> p l u"), "pv")
    ld(cwa[:], cwa_d.rearrange("l p u j -> p l u j"), "cwa")
    ld(cwc[:], cwc_d.rearrange("l p u j -> p l u j"), "cwc")
    P.dma("pool", lambda e: e.dma_start(out=wgate[:], in_=wgate_d.rearrange("l p u -> p l u")), "wgate", writes=["wgate"])
    V(lambda e: e.tensor_copy(out=ident_b[:], in_=cst[:, 0:128]), ["cst"], ["ident_b"])
    V(lambda e: e.tensor_copy(out=ones_b[:], in_=cst[:, 384:512]), ["cst"], ["ones_b"])
    V(lambda e: e.tensor_copy(out=U_b[:], in_=cst[:, 128:256]), ["cst"], ["U_b"])
    Ac(lambda e: e.activation(out=scT[:], in_=cT, func=AF.Silu), ["W0"], ["scT"])
    for l in range(2):
        for c in (0, 2):
            Ac(lambda e, l=l, c=c: e.activation(out=pv[:, l, c:c + 1], in_=pv[:, l, c:c + 1], func=AF.Exp), ["pv"], ["pv"])
            V(lambda e, l=l, c=c: e.tensor_scalar(out=pv[:, l, c:c + 1], in0=pv[:, l, c:c + 1], scalar1=-1.0, scalar2=None, op0=ALU.mult), ["pv"], ["pv"])
        V(lambda e, l=l: e.tensor_scalar(out=pv[:, l, 4:4 + HB], in0=pv[:, l, 4:4 + HB], scalar1=-1.0, scalar2=None, op0=ALU.mult), ["pv"], ["pv"])
    PV_BG, PV_D, PV_NW, PV_OH = 4, 4 + HB, 4 + HB + NU_C, 4 + HB + 2 * NU_C

    for l in range(2):
        for u in range(9 * KC):
            s = ring_load(wada_d[l, u])
            pb = u % 2
            for kc in range(KC):
                T(lambda e, s=s, kc=kc, pb=pb: e.matmul(psb[pb][:, 0:NV], lhsT=ring[:, s, kc, :], rhs=scT[:, kc, :],
                                                         start=(kc == 0), stop=(kc == KC - 1)),
                  [f"ring{s}", "scT"], pk(pb))
            V(lambda e, l=l, u=u, pb=pb: e.tensor_scalar(out=mod[l][:, u, :], in0=psb[pb][:, 0:NV], scalar1=bada[:, l, u:u + 1],
                                                          scalar2=None, op0=ALU.add),
              pk(pb) + ["W1"], [f"mod{l}"])
        for m in range(3):
            sc = mod[l][:, (3 * m + 1) * KC:(3 * m + 2) * KC, :]
            gt = mod[l][:, (3 * m + 2) * KC:(3 * m + 3) * KC, :]
            nb = norms[:, 3 * l + m, :].unsqueeze(2).to_broadcast([128, KC, NV])
            V(lambda e, sc=sc, nb=nb: e.scalar_tensor_tensor(out=sc, in0=sc, scalar=1.0, in1=nb, op0=ALU.add, op1=ALU.mult),
              [f"mod{l}", "norms"], [f"mod{l}"])
            if m != 1:
                V(lambda e, gt=gt: e.tensor_scalar(out=gt, in0=gt, scalar1=0.5, scalar2=None, op0=ALU.mult), [f"mod{l}"], [f"mod{l}"])

    def rms_stats(nt):
        for kc in range(KC):
            Ac(lambda e, kc=kc: e.activation(out=scr[:, kc, 0:nt], in_=x[:, kc, 0:nt], func=AF.Square), [f"x{kc}"], [f"scr{kc}"])
        for kc in range(KC):
            T(lambda e, kc=kc: e.matmul(psb[7][:, 0:nt], lhsT=ones_b[:], rhs=scr[:, kc, 0:nt], start=(kc == 0), stop=(kc == KC - 1)),
              [f"scr{kc}", "ones_b"], pk(7))
        Ac(lambda e: e.activation(out=rstd[:, 0:nt], in_=psb[7][:, 0:nt], func=AF.Ln, scale=1.0 / D, bias=EPS), pk(7), ["rstd"])
        Ac(lambda e: e.activation(out=rstd[:, 0:nt], in_=rstd[:, 0:nt], func=AF.Exp, scale=-0.5), ["rstd"], ["rstd"])

    def norm_mod(nt, samp, A, B, lname):
        rms_stats(nt)
        for kc in range(KC):
            t = nexttmp()
            o = h[:, kc, 0:nt]
            if not samp:
                V(lambda e, kc=kc, t=t: e.scalar_tensor_tensor(out=tmp[:, t, 0:nt], in0=x[:, kc, 0:nt], scalar=A[:, kc, 0:1],
                                                                in1=rstd[:, 0:nt], op0=ALU.mult, op1=ALU.mult),
                  [f"x{kc}", "rstd", lname], [f"tmp{t}"])
                Ac(lambda e, kc=kc, t=t, o=o: e.activation(out=o, in_=tmp[:, t, 0:nt], func=AF.Identity, bias=B[:, kc, 0:1]),
                   [f"tmp{t}", lname], [f"h{kc}"])
            else:
                V(lambda e, kc=kc, t=t: e.tensor_tensor(out=tmp[:, t, 0:nt], in0=x[:, kc, 0:nt], in1=rstd[:, 0:nt], op=ALU.mult),
                  [f"x{kc}", "rstd"], [f"tmp{t}"])
                V(lambda e, kc=kc, t=t: e.tensor_tensor(out=tmp[:, t, 0:nt], in0=tmp[:, t, 0:nt], in1=A[:, kc, 1:NV], op=ALU.mult),
                  [f"tmp{t}", lname], [f"tmp{t}"])
                V(lambda e, kc=kc, t=t, o=o: e.tensor_tensor(out=o, in0=tmp[:, t, 0:nt], in1=B[:, kc, 1:NV], op=ALU.add),
                  [f"tmp{t}", lname], [f"h{kc}"])

    def resid_add(nt, samp, dc, pbank, G, lname):
        if not samp:
            V(lambda e: e.scalar_tensor_tensor(out=x[:, dc, 0:nt], in0=psb[pbank][:, 0:nt], scalar=G[:, dc, 0:1],
                                               in1=x[:, dc, 0:nt], op0=ALU.mult, op1=ALU.add),
              pk(pbank) + [f"x{dc}", lname], [f"x{dc}"])
        else:
            t = nexttmp()
            V(lambda e: e.tensor_tensor(out=tmp[:, t, 0:nt], in0=psb[pbank][:, 0:nt], in1=G[:, dc, 1:NV], op=ALU.mult),
              pk(pbank) + [lname], [f"tmp{t}"])
            V(lambda e: e.tensor_tensor(out=x[:, dc, 0:nt], in0=x[:, dc, 0:nt], in1=tmp[:, t, 0:nt], op=ALU.add),
              [f"tmp{t}", f"x{dc}"], [f"x{dc}"])

    def ffn(l, w, nt, samp):
        m = 0 if w == 1 else 2
        lname = f"mod{l}"
        sh = mod[l][:, (3 * m) * KC:(3 * m + 1) * KC, :]
        A = mod[l][:, (3 * m + 1) * KC:(3 * m + 2) * KC, :]
        G = mod[l][:, (3 * m + 2) * KC:(3 * m + 3) * KC, :]
        norm_mod(nt, samp, A, sh, lname)
        wg_d, wu_d, wd_d = ffn_d[(l, w)]
        for (j0, j1) in ((0, FH), (FH, min(2 * FH, FC)), (min(2 * FH, FC), FC)):
            if j1 <= j0:
                continue
            for j in range(j0, j1):
                b = j % 2
                sg = ring_load(wg_d[j])
                su = ring_load(wu_d[j])
                for kc in range(KC):
                    T(lambda e, sg=sg, kc=kc, b=b: e.matmul(psb[b][:, 0:nt], lhsT=ring[:, sg, kc, :], rhs=h[:, kc, 0:nt],
                                                             start=(kc == 0), stop=(kc == KC - 1)),
                      [f"ring{sg}", f"h{kc}"], pk(b))
                for kc in range(KC):
                    T(lambda e, su=su, kc=kc, b=b: e.matmul(psb[2 + b][:, 0:nt], lhsT=ring[:, su, kc, :], rhs=h[:, kc, 0:nt],
                                                             start=(kc == 0), stop=(kc == KC - 1)),
                      [f"ring{su}", f"h{kc}"], pk(2 + b))
                t = nexttmp()
                Ac(lambda e, b=b, t=t: e.activation(out=tmp[:, t, 0:nt], in_=psb[b][:, 0:nt], func=AF.Silu), pk(b), [f"tmp{t}"])
                V(lambda e, b=b, t=t, jj=j - j0: e.tensor_tensor(out=scr[:, jj, 0:nt], in0=tmp[:, t, 0:nt], in1=psb[2 + b][:, 0:nt], op=ALU.mult),
                  [f"tmp{t}"] + pk(2 + b), [f"scr{j - j0}"])
            nj = j1 - j0
            for dc in range(KC):
                ws = state["wd"] % 2
                state["wd"] += 1
                b = 4 + dc % 2
                P.dma("pool", lambda e, ws=ws, dc=dc, j0=j0, j1=j1, nj=nj: e.dma_start(out=wdr[:, ws, 0:nj, :], in_=wd_d[dc, :, j0:j1, :]),
                      f"wdr{ws}", writes=[f"wdr{ws}"])
                for jj in range(nj):
                    T(lambda e, ws=ws, jj=jj, b=b, nj=nj: e.matmul(psb[b][:, 0:nt], lhsT=wdr[:, ws, jj, :], rhs=scr[:, jj, 0:nt],
                                                                    start=(jj == 0), stop=(jj == nj - 1)),
                      [f"wdr{ws}", f"scr{jj}"], pk(b))
                resid_add(nt, samp, dc, b, G, lname)

    hk = [f"h{kc}" for kc in range(KC)]

    def proj(l, uname, pbank, nt):
        s = ring_load(win_d[l][UP[uname]])
        for kc in range(KC):
            T(lambda e, s=s, kc=kc: e.matmul(psb[pbank][:, 0:nt], lhsT=ring[:, s, kc, :], rhs=h[:, kc, 0:nt], start=(kc == 0), stop=(kc == KC - 1)),
              [f"ring{s}", f"h{kc}"], pk(pbank))

    def conv_unit(l, pbank, nt, samp, cw_ap, cwkey, bias_ap, hist_ap, histkey, shist, xstash, shkey, xskey, rawW, outW):
        raw, out = W[rawW], W[outW]
        rk, ok = f"W{rawW}", f"W{outW}"
        if not samp:
            V(lambda e: e.tensor_copy(out=raw[:, 0:3], in_=hist_ap), [histkey], [rk])
            Ac(lambda e: e.activation(out=raw[:, 3:3 + nt], in_=psb[pbank][:, 0:nt], func=AF.Copy), pk(pbank), [rk])
            V(lambda e: e.tensor_scalar(out=out[:, 0:nt], in0=raw[:, 0:nt], scalar1=cw_ap[:, 0:1], scalar2=None, op0=ALU.mult), [rk, cwkey], [ok])
            for j in range(1, 4):
                V(lambda e, j=j: e.scalar_tensor_tensor(out=out[:, 0:nt], in0=raw[:, j:j + nt], scalar=cw_ap[:, j:j + 1], in1=out[:, 0:nt],
                                                        op0=ALU.mult, op1=ALU.add), [rk, ok, cwkey], [ok])
            V(lambda e: e.tensor_copy(out=hist_ap, in_=raw[:, nt:nt + 3]), [rk], [histkey])
        else:
            Ac(lambda e: e.activation(out=xstash, in_=psb[pbank][:, 0:nt], func=AF.Copy), pk(pbank), [xskey])
            V(lambda e: e.tensor_scalar(out=out[:, 0:nt], in0=xstash, scalar1=cw_ap[:, 3:4], scalar2=None, op0=ALU.mult), [xskey, cwkey], [ok])
            for j in range(3):
                V(lambda e, j=j: e.scalar_tensor_tensor(out=out[:, 0:nt], in0=shist[:, j, :], scalar=cw_ap[:, j:j + 1], in1=out[:, 0:nt],
                                                        op0=ALU.mult, op1=ALU.add), [shkey, ok, cwkey], [ok])
        if bias_ap is not None:
            Ac(lambda e: e.activation(out=out[:, 0:nt], in_=out[:, 0:nt], func=AF.Silu, bias=bias_ap), [ok, cwkey], [ok])
        else:
            Ac(lambda e: e.activation(out=out[:, 0:nt], in_=out[:, 0:nt], func=AF.Silu), [ok], [ok])

    def rinv_of(srcW, nt, dstW):
        V(lambda e: e.tensor_tensor(out=Vb[5][:, 0:nt], in0=W[srcW][:, 0:nt], in1=W[srcW][:, 0:nt], op=ALU.mult), [f"W{srcW}"], ["V5"])
        T(lambda e: e.matmul(psb[6][:, 0:nt], lhsT=ones_b[:], rhs=Vb[5][:, 0:nt], start=True, stop=True), ["V5", "ones_b"], pk(6))
        Ac(lambda e: e.activation(out=W[dstW][:, 0:nt], in_=psb[6][:, 0:nt], func=AF.Ln, bias=EPS), pk(6), [f"W{dstW}"])
        Ac(lambda e: e.activation(out=W[dstW][:, 0:nt], in_=W[dstW][:, 0:nt], func=AF.Exp, scale=-0.5), [f"W{dstW}"], [f"W{dstW}"])

    def decay_setup(l, nt, CL, nch, uname, col_alog, col_dtb, nheads, dtW, with_dt, kstride):
        proj(l, uname, 5, nt)
        Ac(lambda e: e.activation(out=W[6][:, 0:nt], in_=psb[5][:, 0:nt], func=AF.Exp, bias=pv[:, l, col_dtb:col_dtb + 1]), pk(5) + ["pv"], ["W6"])
        Ac(lambda e: e.activation(out=W[6][:, 0:nt], in_=W[6][:, 0:nt], func=AF.Ln, bias=1.0), ["W6"], ["W6"])
        if with_dt:
            V(lambda e: e.tensor_copy(out=W[dtW][:, 0:nt], in_=W[6][:, 0:nt]), ["W6"], [f"W{dtW}"])
        V(lambda e: e.tensor_scalar(out=W[6][:, 0:nt], in0=W[6][:, 0:nt], scalar1=pv[:, l, col_alog:col_alog + 1], scalar2=None, op0=ALU.mult),
          ["W6", "pv"], ["W6"])
        if CL == 1:
            V(lambda e: e.tensor_copy(out=W[7][:, 0:nt], in_=W[6][:, 0:nt]), ["W6"], ["W7"])
        else:
            for c in range(nch):
                sl = slice(c * CL, (c + 1) * CL)
                V(lambda e, sl=sl: e.tensor_tensor_scan(out=W[7][:, sl], data0=W[6][:, sl], data1=W[6][:, sl], initial=0.0, op0=ALU.add, op1=ALU.bypass),
                  ["W6"], ["W7"])
        for c in range(nch):
            sl = slice(c * CL, (c + 1) * CL)
            T(lambda e, sl=sl: e.transpose(out=psb[6][0:CL, 0:128], in_=W[7][:, sl], identity=ident), ["W7", "cst"], pk(6, 0, 1))
            T(lambda e, sl=sl: e.transpose(out=psb[6][0:CL, 128:256], in_=W[dtW][:, sl], identity=ident), [f"W{dtW}", "cst"], pk(6, 1, 2))
            V(lambda e, c=c: e.tensor_copy(out=tok[0:CL, c, 0:nheads], in_=psb[6][0:CL, 0:nheads]), pk(6, 0, 1), ["tok"])
            V(lambda e, c=c: e.tensor_copy(out=tok[0:CL, c, kstride:kstride + nheads], in_=psb[6][0:CL, 128:128 + nheads]), pk(6, 1, 2), ["tok"])

    def decay_mats(l, CL, c, hd, need_dl):
        sl = slice(c * CL, (c + 1) * CL)
        V(lambda e: e.tensor_scalar(out=A_[3][:, 0:CL], in0=W[7][:, sl], scalar1=pv[:, l, PV_OH + hd:PV_OH + hd + 1], scalar2=None, op0=ALU.mult),
          ["W7", "pv"], ["a3"])
        T(lambda e: e.matmul(psb[4][0:CL, 0:CL], lhsT=ones_f[:, 0:CL], rhs=A_[3][:, 0:CL], start=True, stop=True), ["a3", "cst"], pk(4, 0, 1))
        V(lambda e: e.tensor_scalar(out=A_[0][0:CL, 0:CL], in0=psb[4][0:CL, 0:CL], scalar1=tok[0:CL, c, hd:hd + 1], scalar2=None, op0=ALU.subtract),
          pk(4, 0, 1) + ["tok"], ["a0"])
        V(lambda e: e.tensor_scalar(out=A_[1][0:CL, 0:CL], in0=A_[0][0:CL, 0:CL], scalar1=0.0, scalar2=None, op0=ALU.min), ["a0"], ["a1"])
        Ac(lambda e: e.activation(out=A_[1][0:CL, 0:CL], in_=A_[1][0:CL, 0:CL], func=AF.Exp), ["a1"], ["a1"])
        V(lambda e: e.tensor_tensor(out=A_[1][0:CL, 0:CL], in0=A_[1][0:CL, 0:CL], in1=Umat[0:CL, 0:CL], op=ALU.mult), ["a1", "cst"], ["a1"])
        if need_dl and CL > 1:
            V(lambda e: e.tensor_scalar(out=A_[2][0:CL, 0:CL], in0=A_[0][0:CL, 0:CL], scalar1=0.0, scalar2=None, op0=ALU.max), ["a0"], ["a2"])
            Ac(lambda e: e.activation(out=A_[2][0:CL, 0:CL], in_=A_[2][0:CL, 0:CL], func=AF.Exp, scale=-1.0), ["a2"], ["a2"])
            V(lambda e: e.tensor_tensor(out=A_[2][0:CL, 0:CL], in0=A_[2][0:CL, 0:CL], in1=Lsmat[0:CL, 0:CL], op=ALU.mult), ["a2", "cst"], ["a2"])
        Ac(lambda e: e.activation(out=cs1[0:CL, 0:1], in_=A_[0][0:CL, CL - 1:CL], func=AF.Exp), ["a0"], ["cs1"])

    def out_norm_T(l, CL, c, pbank, dvw, nw_ap, gateW, dst_units, dstkeys):
        sl = slice(c * CL, (c + 1) * CL)
        Ac(lambda e: e.activation(out=A_[4][0:CL, 0:dvw], in_=psb[pbank][0:CL, 0:dvw], func=AF.Square), pk(pbank, 0, 2), ["a4"])
        V(lambda e: e.tensor_reduce(out=cs1[0:CL, 4:5], in_=A_[4][0:CL, 0:dvw], axis=AX.X, op=ALU.add), ["a4"], ["cs1b"])
        Ac(lambda e: e.activation(out=cs1[0:CL, 4:5], in_=cs1[0:CL, 4:5], func=AF.Ln, scale=1.0 / dvw, bias=EPS), ["cs1b"], ["cs1b"])
        Ac(lambda e: e.activation(out=cs1[0:CL, 4:5], in_=cs1[0:CL, 4:5], func=AF.Exp, scale=-0.5), ["cs1b"], ["cs1b"])
        V(lambda e: e.scalar_tensor_tensor(out=A_[4][0:CL, 0:dvw], in0=psb[pbank][0:CL, 0:dvw], scalar=cs1[0:CL, 4:5], in1=nw_ap[0:CL, 0:dvw],
                                           op0=ALU.mult, op1=ALU.mult), pk(pbank, 0, 2) + ["cs1b", "nrm"], ["a4"])
        for i, (du, dk_) in enumerate(zip(dst_units, dstkeys)):
            T(lambda e, i=i: e.transpose(out=psb[6][:, 256 + i * 128:256 + i * 128 + CL], in_=A_[4][0:CL, i * 128:(i + 1) * 128], identity=ident[0:CL, 0:CL]),
              ["a4", "cst"], pk(6, 2 + i, 3 + i))
            V(lambda e, i=i, du=du: e.tensor_tensor(out=scr[:, du, sl], in0=psb[6][:, 256 + i * 128:256 + i * 128 + CL], in1=W[gateW[i]][:, sl], op=ALU.mult),
              pk(6, 2 + i, 3 + i) + [f"W{gateW[i]}"], [dk_])

    def mixer(l, nt, samp):
        CL = 1 if samp else 128
        nch = nt // CL
        lname = f"mod{l}"
        sh = mod[l][:, 3 * KC:4 * KC, :]
        A = mod[l][:, 4 * KC:5 * KC, :]
        Gt = mod[l][:, 5 * KC:6 * KC, :]
        norm_mod(nt, samp, A, sh, lname)
        P.dma("sp", lambda e: e.dma_start(out=nrm1[:], in_=nrm_d[l]), "nrm", writes=["nrm"])
        if samp:
            P.dma("sp", lambda e: e.dma_start(out=shA[:], in_=s_cva_d[l]), "shA", writes=["shA"] + PKEYS)
            P.dma("sp", lambda e: e.dma_start(out=shC[:], in_=s_cvc_d[l]), "shC", writes=["shC"] + PKEYS)
        OB = 16

        def load_state(src_ap, width, buf):
            P.dma("sp", lambda e: e.dma_start(out=SS[:, buf, 0:width], in_=src_ap), f"SS{buf}", writes=[f"SS{buf}"])

        def store_state(dst_ap, src_sb, key, skey):
            P.dma("sp", lambda e: e.dma_start(out=dst_ap, in_=src_sb), skey, reads=[key], writes=["odram"])

        CL_all, nch_all = CL, nch
        CL = 1 if samp else 64
        nch = nt // CL
        proj(l, "beta", 4, nt)
        Ac(lambda e: e.activation(out=W[8][:, 0:nt], in_=psb[4][:, 0:nt], func=AF.Sigmoid), pk(4), ["W8"])
        decay_setup(l, nt, CL, nch, "dec", 0, 1, HA, 8, False, 8)
        for c in range(nch):
            V(lambda e, c=c: e.tensor_scalar(out=tok[0:CL, c, 16:16 + HA], in0=tok[0:CL, c, 8:8 + HA], scalar1=-1.0, scalar2=None, op0=ALU.mult), ["tok"], ["tok"])
            Ac(lambda e, c=c: e.activation(out=tok[0:CL, c, 24:24 + HA], in_=tok[0:CL, c, 0:HA], func=AF.Exp), ["tok"], ["tok"])
            V(lambda e, c=c: e.tensor_tensor(out=tok[0:CL, c, 24:24 + HA], in0=tok[0:CL, c, 24:24 + HA], in1=tok[0:CL, c, 8:8 + HA], op=ALU.mult), ["tok"], ["tok"])

        for hd in range(HA):
            for i, (nm, ow) in enumerate((("aq", 1), ("ak", 2), ("av", 3))):
                proj(l, f"{nm}{hd}", i % 2, nt)
                u = i * HA + hd
                conv_unit(l, i % 2, nt, samp, cwa[:, l, u, :], "cwa", None, hista[:, l, u, :], f"hista{l}", shA[:, u, :, :], xsA[:, u, :], "shA", "xsA", 0, ow)
            proj(l, f"az{hd}", 3, nt)
            Ac(lambda e: e.activation(out=W[4][:, 0:nt], in_=psb[3][:, 0:nt], func=AF.Silu), pk(3), ["W4"])
            rinv_of(1, nt, 5)
            V(lambda e: e.scalar_tensor_tensor(out=Vb[0][:, 0:nt], in0=W[1][:, 0:nt], scalar=128 ** -0.5, in1=W[5][:, 0:nt], op0=ALU.mult, op1=ALU.mult),
              ["W1", "W5"], ["V0"])
            rinv_of(2, nt, 5)
            V(lambda e: e.tensor_tensor(out=W[2][:, 0:nt], in0=W[2][:, 0:nt], in1=W[5][:, 0:nt], op=ALU.mult), ["W2", "W5"], ["W2"])
            V(lambda e: e.tensor_copy(out=Vb[1][:, 0:nt], in_=W[2][:, 0:nt]), ["W2"], ["V1"])
            for c in range(nch):
                sl = slice(c * CL, (c + 1) * CL)
                tk = "tok"
                if samp:
                    buf = c % 2
                    load_state(s_gdn_d[l, c, hd], 128, buf)
                    S = SS[:, buf, 0:128]
                    Sk = f"SS{buf}"
                else:
                    S = Sa[:, l, hd, :]
                    Sk = f"Sa{l}_{hd}"
                    if c == 0 and state.get("ptile", 0) == 0:
                        V(lambda e, S=S: e.memset(S, 0.0), [], [Sk])
                V(lambda e, S=S: e.tensor_copy(out=B_[0][:, 0:128], in_=S), [Sk], ["b0"])
                decay_mats(l, CL, c, hd, True)
                T(lambda e, sl=sl: e.transpose(out=psb[5][0:CL, 0:128], in_=W[3][:, sl], identity=ident), ["W3", "cst"], pk(5, 0, 1))
                Ac(lambda e: e.activation(out=A_[12][0:CL, 0:128], in_=psb[5][0:CL, 0:128], func=AF.Copy), pk(5, 0, 1), ["a12"])
                T(lambda e, sl=sl: e.transpose(out=psb[5][0:CL, 128:256], in_=W[2][:, sl], identity=ident), ["W2", "cst"], pk(5, 1, 2))
                Ac(lambda e: e.activation(out=A_[11][0:CL, 0:128], in_=psb[5][0:CL, 128:256], func=AF.Copy), pk(5, 1, 2), ["a11"])
                V(lambda e: e.tensor_scalar(out=B_[3][0:CL, 0:128], in0=psb[5][0:CL, 128:256], scalar1=cs1[0:CL, 0:1], scalar2=None, op0=ALU.mult),
                  pk(5, 1, 2) + ["cs1"], ["b3"])
                if CL > 1:
                    q = slice(0, CL)
                    T(lambda e, sl=sl: e.matmul(psb[4][q, 128:128 + CL], lhsT=Vb[1][:, sl], rhs=Vb[1][:, sl], start=True, stop=True), ["V1"], pk(4, 1, 2))
                    V(lambda e, c=c: e.scalar_tensor_tensor(out=A_[5][q, q], in0=psb[4][q, 128:128 + CL], scalar=tok[q, c, 16 + hd:17 + hd], in1=A_[2][q, q],
                                                            op0=ALU.mult, op1=ALU.mult), pk(4, 1, 2) + [tk, "a2"], ["a5"])
                    T(lambda e: e.transpose(out=psb[4][q, 256:256 + CL], in_=A_[5][q, q], identity=ident[q, q]), ["a5", "cst"], pk(4, 2, 3))
                    Ac(lambda e: e.activation(out=A_[6][q, q], in_=psb[4][q, 256:256 + CL], func=AF.Copy), pk(4, 2, 3), ["a6"])
                    V(lambda e: e.tensor_tensor(out=A_[9][q, q], in0=psb[4][q, 256:256 + CL], in1=ident[q, q], op=ALU.add), pk(4, 2, 3) + ["cst"], ["a9"])
                    Nc, Mc, Nn, Mn = 5, 6, 7, 8
                    nlev = CL.bit_length() - 2
                    for lev in range(1, nlev + 1):
                        T(lambda e, Mc=Mc, Nc=Nc: e.matmul(psb[4][q, 128:128 + CL], lhsT=A_[Mc][q, q], rhs=A_[Nc][q, q], start=True, stop=True),
                          [f"a{Mc}", f"a{Nc}"], pk(4, 1, 2))
                        if lev < nlev:
                            T(lambda e, Mc=Mc, Nc=Nc: e.matmul(psb[4][q, 256:256 + CL], lhsT=A_[Nc][q, q], rhs=A_[Mc][q, q], start=True, stop=True),
                              [f"a{Mc}", f"a{Nc}"], pk(4, 2, 3))
                        Ac(lambda e, Nn=Nn: e.activation(out=A_[Nn][q, q], in_=psb[4][q, 128:128 + CL], func=AF.Copy), pk(4, 1, 2), [f"a{Nn}"])
                        if lev < nlev:
                            V(lambda e, Mn=Mn: e.tensor_copy(out=A_[Mn][q, q], in_=psb[4][q, 256:256 + CL]), pk(4, 2, 3), [f"a{Mn}"])
                        T(lambda e, Nn=Nn: e.matmul(psb[4][q, 384:384 + CL], lhsT=A_[Nn][q, q], rhs=A_[9][q, q], start=True, stop=True),
                          [f"a{Nn}", "a9"], pk(4, 3, 4))
                        V(lambda e: e.tensor_tensor(out=A_[9][q, q], in0=A_[9][q, q], in1=psb[4][q, 384:384 + CL], op=ALU.add), pk(4, 3, 4) + ["a9"], ["a9"])
                        Nc, Mc, Nn, Mn = Nn, Mn, Nc, Mc
                    Rap = A_[9]
                    Rk = "a9"
                else:
                    Rap = cst
                    Rk = "cst"
                V(lambda e, c=c, Rap=Rap: e.tensor_scalar(out=A_[5][0:CL, 0:CL], in0=Rap[0:CL, 0:CL], scalar1=tok[0:CL, c, 8 + hd:9 + hd], scalar2=None, op0=ALU.mult),
                  [Rk, tk], ["a5"])
                V(lambda e, c=c, Rap=Rap: e.tensor_scalar(out=A_[6][0:CL, 0:CL], in0=Rap[0:CL, 0:CL], scalar1=tok[0:CL, c, 24 + hd:25 + hd], scalar2=None, op0=ALU.mult),
                  [Rk, tk], ["a6"])
                T(lambda e: e.matmul(psb[5][:, 256:256 + CL], lhsT=A_[11][0:CL, 0:128], rhs=A_[6][0:CL, 0:CL], start=True, stop=True), ["a11", "a6"], pk(5, 2, 3))
                V(lambda e: e.tensor_scalar(out=A_[7][:, 0:CL], in0=psb[5][:, 256:256 + CL], scalar1=-1.0, scalar2=None, op0=ALU.mult), pk(5, 2, 3), ["a7"])
                T(lambda e: e.matmul(psb[5][0:CL, 384:512], lhsT=A_[5][0:CL, 0:CL], rhs=A_[12][0:CL, 0:128], start=True, stop=False), ["a5", "a12"], pk(5, 3, 4))
                T(lambda e, S=S: e.matmul(psb[5][0:CL, 384:512], lhsT=A_[7][:, 0:CL], rhs=S, start=False, stop=True), ["a7", Sk], pk(5, 3, 4))
                V(lambda e: e.tensor_copy(out=B_[7][0:CL, 0:128], in_=psb[5][0:CL, 384:512]), pk(5, 3, 4), ["b7"])
                T(lambda e, sl=sl: e.matmul(psb[6][0:CL, 0:CL], lhsT=Vb[1][:, sl], rhs=Vb[0][:, sl], start=True, stop=True), ["V1", "V0"], pk(6, 0, 1))
                V(lambda e: e.tensor_tensor(out=B_[8][0:CL, 0:CL], in0=psb[6][0:CL, 0:CL], in1=A_[1][0:CL, 0:CL], op=ALU.mult), pk(6, 0, 1) + ["a1"], ["b8"])
                T(lambda e: e.matmul(psb[6][:, 128:128 + CL], lhsT=ones_f, rhs=A_[3][:, 0:CL], start=True, stop=True), ["a3", "cst"], pk(6, 1, 2))
                Ac(lambda e: e.activation(out=A_[10][:, 0:CL], in_=psb[6][:, 128:128 + CL], func=AF.Exp), pk(6, 1, 2), ["a10"])
                V(lambda e, sl=sl: e.tensor_tensor(out=B_[9][:, 0:CL], in0=Vb[0][:, sl], in1=A_[10][:, 0:CL], op=ALU.mult), ["V0", "a10"], ["b9"])
                T(lambda e: e.matmul(psb[7][0:CL, 0:128], lhsT=B_[9][:, 0:CL], rhs=B_[0][:, 0:128], start=True, stop=False), ["b9", "b0"], pk(7, 0, 1))
                T(lambda e: e.matmul(psb[7][0:CL, 0:128], lhsT=B_[8][0:CL, 0:CL], rhs=B_[7][0:CL, 0:128], start=False, stop=True), ["b8", "b7"], pk(7, 0, 1))
                T(lambda e: e.matmul(psb[7][:, 256:384], lhsT=B_[3][0:CL, 0:128], rhs=B_[7][0:CL, 0:128], start=True, stop=True), ["b3", "b7"], pk(7, 2, 3))
                V(lambda e, S=S: e.scalar_tensor_tensor(out=S, in0=S, scalar=A_[10][:, CL - 1:CL], in1=psb[7][:, 256:384], op0=ALU.mult, op1=ALU.add),
                  [Sk, "a10"] + pk(7, 2, 3), [Sk])
                if samp:
                    store_state(so_gdn_d[l, c, hd], S, Sk, f"sst{buf}")
                out_norm_T(l, CL, c, 7, 128, nrm1[:, 0:128], [4], [OB + hd], [f"scr{OB + hd}"])
            if not samp and state.get("ptile", 0) == NPT - 1:
                store_state(o_gdn_d[l, hd], Sa[:, l, hd, :], f"Sa{l}_{hd}", "ost")
        if samp:
            P.dma("sp", lambda e: e.dma_start(out=so_cva_d[l][:, :, 0:2, :], in_=shA[:, :, 1:3, :]), "shst", reads=["shA"], writes=["odram"])
            P.dma("sp", lambda e: e.dma_start(out=so_cva_d[l][:, :, 2, :], in_=xsA[:]), "shst", reads=["xsA"], writes=["odram"])
        elif state.get("ptile", 0) == NPT - 1:
            P.dma("sp", lambda e: e.dma_start(out=o_cva_d[l], in_=hista[:, l, :, :]), "ost", reads=[f"hista{l}"], writes=["odram"])
        merge_branch(l, nt, 0, wba_d[l], HA, OB, first=True)
        CL, nch = CL_all, nch_all

        proj(l, "lr", 4, nt)
        Ac(lambda e: e.activation(out=Vb[2][0:16, 0:nt], in_=psb[4][0:16, 0:nt], func=AF.Copy), pk(4), ["V2"])
        for hd in range(HB):
            T(lambda e, hd=hd: e.matmul(psb[4][:, 0:nt], lhsT=wgate[:, l, hd * 128:(hd + 1) * 128], rhs=Vb[2][0:16, 0:nt], start=True, stop=True),
              ["wgate", "V2"], pk(4))
            Ac(lambda e, hd=hd: e.activation(out=W[5][:, 0:nt], in_=psb[4][:, 0:nt], func=AF.Exp, scale=-1.0, bias=pv[:, l, PV_BG + hd:PV_BG + hd + 1]),
               pk(4) + ["pv"], ["W5"])
            Ac(lambda e: e.activation(out=W[5][:, 0:nt], in_=W[5][:, 0:nt], func=AF.Ln, bias=1.0), ["W5"], ["W5"])
            if CL > 1:
                for c in range(nch):
                    sl = slice(c * CL, (c + 1) * CL)
                    V(lambda e, sl=sl: e.tensor_tensor_scan(out=W[6][:, sl], data0=W[5][:, sl], data1=W[5][:, sl], initial=0.0, op0=ALU.add, op1=ALU.bypass),
                      ["W5"], ["W6"])
            else:
                V(lambda e: e.tensor_copy(out=W[6][:, 0:nt], in_=W[5][:, 0:nt]), ["W5"], ["W6"])
            Ac(lambda e: e.activation(out=W[7][:, 0:nt], in_=W[6][:, 0:nt], func=AF.Exp, scale=-1.0 / 16), ["W6"], ["W7"])
            Ac(lambda e: e.activation(out=W[8][:, 0:nt], in_=W[6][:, 0:nt], func=AF.Exp, scale=1.0 / 16), ["W6"], ["W8"])
            proj(l, f"bq{hd}", 0, nt)
            V(lambda e: e.scalar_tensor_tensor(out=Vb[0][:, 0:nt], in0=psb[0][:, 0:nt], scalar=128 ** -0.5, in1=W[7][:, 0:nt], op0=ALU.mult, op1=ALU.mult),
              pk(0) + ["W7"], ["V0"])
            proj(l, f"bk{hd}", 1, nt)
            Ac(lambda e: e.activation(out=W[1][:, 0:nt], in_=psb[1][:, 0:nt], func=AF.Copy), pk(1), ["W1"])
            V(lambda e: e.tensor_tensor(out=Vb[1][:, 0:nt], in0=W[1][:, 0:nt], in1=W[8][:, 0:nt], op=ALU.mult), ["W1", "W8"], ["V1"])
            for c in range(nch):
                sl = slice(c * CL, (c + 1) * CL)
                V(lambda e, c=c: e.tensor_scalar(out=cs1[:, 2:3], in0=W[6][:, (c + 1) * CL - 1:(c + 1) * CL], scalar1=-1.0 / 16, scalar2=None, op0=ALU.mult),
                  ["W6"], ["cs1c"])
                Ac(lambda e, sl=sl: e.activation(out=W[2][:, sl], in_=W[6][:, sl], func=AF.Exp, scale=1.0 / 16, bias=cs1[:, 2:3]), ["W6", "cs1c"], ["W2"])
            V(lambda e: e.tensor_tensor(out=W[2][:, 0:nt], in0=W[2][:, 0:nt], in1=W[1][:, 0:nt], op=ALU.mult), ["W2", "W1"], ["W2"])
            proj(l, f"bv{hd}_0", 2, nt)
            Ac(lambda e: e.activation(out=W[3][:, 0:nt], in_=psb[2][:, 0:nt], func=AF.Copy), pk(2), ["W3"])
            proj(l, f"bv{hd}_1", 3, nt)
            Ac(lambda e: e.activation(out=W[4][:, 0:nt], in_=psb[3][:, 0:nt], func=AF.Copy), pk(3), ["W4"])
            proj(l, f"br{hd}_0", 0, nt)
            Ac(lambda e: e.activation(out=W[1][:, 0:nt], in_=psb[0][:, 0:nt], func=AF.Silu), pk(0), ["W1"])
            proj(l, f"br{hd}_1", 1, nt)
            Ac(lambda e: e.activation(out=W[5][:, 0:nt], in_=psb[1][:, 0:nt], func=AF.Silu), pk(1), ["W5"])
            for c in range(nch):
                sl = slice(c * CL, (c + 1) * CL)
                if samp:
                    buf = c % 2
                    load_state(s_gla_d[l, c, hd], 256, buf)
                    S = SS[:, buf, 0:256]
                    Sk = f"SS{buf}"
                else:
                    S = Sb_[:, l, hd, :]
                    Sk = f"Sb{l}_{hd}"
                    if c == 0 and state.get("ptile", 0) == 0:
                        V(lambda e, S=S: e.memset(S, 0.0), [], [Sk])
                V(lambda e, S=S: e.tensor_copy(out=B_[0][:, 0:256], in_=S), [Sk], ["b0"])
                T(lambda e, sl=sl: e.transpose(out=psb[5][0:CL, 0:128], in_=W[3][:, sl], identity=ident), ["W3", "cst"], pk(5, 0, 1))
                T(lambda e, sl=sl: e.transpose(out=psb[5][0:CL, 128:256], in_=W[4][:, sl], identity=ident), ["W4", "cst"], pk(5, 1, 2))
                Ac(lambda e: e.activation(out=B_[1][0:CL, 0:256], in_=psb[5][0:CL, 0:256], func=AF.Copy), pk(5, 0, 2), ["b1"])
                T(lambda e, sl=sl: e.transpose(out=psb[5][0:CL, 256:384], in_=W[2][:, sl], identity=ident), ["W2", "cst"], pk(5, 2, 3))
                Ac(lambda e: e.activation(out=B_[3][0:CL, 0:128], in_=psb[5][0:CL, 256:384], func=AF.Copy), pk(5, 2, 3), ["b3"])
                T(lambda e, sl=sl: e.matmul(psb[6][0:CL, 0:CL], lhsT=Vb[1][:, sl], rhs=Vb[0][:, sl], start=True, stop=True), ["V1", "V0"], pk(6, 0, 1))
                V(lambda e: e.tensor_tensor(out=B_[8][0:CL, 0:CL], in0=psb[6][0:CL, 0:CL], in1=Umat[0:CL, 0:CL], op=ALU.mult), pk(6, 0, 1) + ["cst"], ["b8"])
                T(lambda e, sl=sl: e.matmul(psb[7][0:CL, 0:256], lhsT=Vb[0][:, sl], rhs=B_[0][:, 0:256], start=True, stop=False), ["V0", "b0"], pk(7, 0, 2))
                T(lambda e: e.matmul(psb[7][0:CL, 0:256], lhsT=B_[8][0:CL, 0:CL], rhs=B_[1][0:CL, 0:256], start=False, stop=True), ["b8", "b1"], pk(7, 0, 2))
                T(lambda e: e.matmul(psb[7][:, 256:512], lhsT=B_[3][0:CL, 0:128], rhs=B_[1][0:CL, 0:256], start=True, stop=True), ["b3", "b1"], pk(7, 2, 4))
                V(lambda e, S=S, c=c: e.scalar_tensor_tensor(out=S, in0=S, scalar=W[7][:, (c + 1) * CL - 1:(c + 1) * CL], in1=psb[7][:, 256:512],
                                                             op0=ALU.mult, op1=ALU.add), [Sk, "W7"] + pk(7, 2, 4), [Sk])
                if samp:
                    store_state(so_gla_d[l, c, hd], S, Sk, f"sst{buf}")
                out_norm_T(l, CL, c, 7, 256, nrm1[:, 128:384], [1, 5], [OB + 2 * hd, OB + 2 * hd + 1], [f"scr{OB + 2 * hd}", f"scr{OB + 2 * hd + 1}"])
            if not samp and state.get("ptile", 0) == NPT - 1:
                store_state(o_gla_d[l, hd], Sb_[:, l, hd, :], f"Sb{l}_{hd}", "ost")
        merge_branch(l, nt, 1, wbb_d[l], 2 * HB, OB, first=False)

        decay_setup(l, nt, CL, nch, "dt", 2, 3, HC, 8, True, 32)

        V(lambda e: e.memset(B_[10][:, 0:256], 0.0), [], ["b10"])
        V(lambda e: e.memset(B_[11][:, 0:256], 0.0), [], ["b11"])
        V(lambda e: e.memset(B_[12][:, 0:256], 0.0), [], ["b12"])
        for g in range(G):
            for (nm, cu, dstV, keepW) in ((f"cB{g}", NU_C + g, 3, None), (f"cC{g}", NU_C + G + g, 4, 5)):
                proj(l, nm, 0, nt)
                conv_unit(l, 0, nt, samp, cwc[:, l, cu, 0:4], "cwc", cwc[:, l, cu, 4:5], histc[:, l, cu, :], f"histc{l}", shC[:, cu, :, :], xsC[:, cu, :],
                          "shC", "xsC", 0, 1)
                V(lambda e, dstV=dstV: e.tensor_copy(out=Vb[dstV][:, 0:nt], in_=W[1][:, 0:nt]), ["W1"], [f"V{dstV}"])
                if keepW is not None:
                    V(lambda e: e.tensor_copy(out=W[5][:, 0:nt], in_=W[1][:, 0:nt]), ["W1"], ["W5"])
                else:
                    V(lambda e: e.tensor_copy(out=W[6][:, 0:nt], in_=W[1][:, 0:nt]), ["W1"], ["W6"])
            for uu in range(UPN):
                u = g * UPN + uu
                proj(l, f"cx{u}", 1, nt)
                conv_unit(l, 1, nt, samp, cwc[:, l, u, 0:4], "cwc", cwc[:, l, u, 4:5], histc[:, l, u, :], f"histc{l}", shC[:, u, :, :], xsC[:, u, :],
                          "shC", "xsC", 0, 2)
                proj(l, f"cz{u}", 2, nt)
                Ac(lambda e: e.activation(out=W[3][:, 0:nt], in_=psb[2][:, 0:nt], func=AF.Silu), pk(2), ["W3"])
                for c in range(nch):
                    sl = slice(c * CL, (c + 1) * CL)
                    tk = "tok"
                    if samp:
                        buf = c % 2
                        load_state(s_ssd_d[l, c, u], 128, buf)
                        S = SS[:, buf, 0:128]
                        Sk = f"SS{buf}"
                    else:
                        S = Sc[:, l, u, :]
                        Sk = f"Sc{l}_{u}"
                        if c == 0 and state.get("ptile", 0) == 0:
                            V(lambda e, S=S: e.memset(S, 0.0), [], [Sk])
                    T(lambda e, sl=sl: e.matmul(psb[5][0:CL, 0:CL], lhsT=Vb[3][:, sl], rhs=Vb[4][:, sl], start=True, stop=True), ["V3", "V4"], pk(5, 0, 1))
                    V(lambda e: e.tensor_copy(out=A_[11][0:CL, 0:CL], in_=psb[5][0:CL, 0:CL]), pk(5, 0, 1), ["a11"])
                    T(lambda e, sl=sl: e.transpose(out=psb[5][0:CL, 128:256], in_=W[6][:, sl], identity=ident), ["W6", "cst"], pk(5, 1, 2))
                    Ac(lambda e: e.activation(out=B_[2][0:CL, 0:128], in_=psb[5][0:CL, 128:256], func=AF.Copy), pk(5, 1, 2), ["b2"])
                    T(lambda e, sl=sl: e.transpose(out=psb[5][0:CL, 256:384], in_=W[2][:, sl], identity=ident), ["W2", "cst"], pk(5, 2, 3))
                    V(lambda e: e.tensor_copy(out=A_[12][0:CL, 0:128], in_=psb[5][0:CL, 256:384]), pk(5, 2, 3), ["a12"])
                    for hh in range(2):
                        hd = 2 * u + hh
                        hs = slice(hh * 64, hh * 64 + 64)
                        po = hh * 128
                        decay_mats(l, CL, c, hd, False)
                        V(lambda e: e.tensor_tensor(out=B_[8][0:CL, 0:CL], in0=A_[11][0:CL, 0:CL], in1=A_[1][0:CL, 0:CL], op=ALU.mult), ["a11", "a1"], ["b8"])
                        T(lambda e: e.matmul(psb[6][:, 128:128 + CL], lhsT=ones_f, rhs=A_[3][:, 0:CL], start=True, stop=True), ["a3", "cst"], pk(6, 1, 2))
                        Ac(lambda e: e.activation(out=A_[10][:, 0:CL], in_=psb[6][:, 128:128 + CL], func=AF.Exp), pk(6, 1, 2), ["a10"])
                        V(lambda e, sl=sl: e.tensor_tensor(out=B_[9][:, 0:CL], in0=W[5][:, sl], in1=A_[10][:, 0:CL], op=ALU.mult), ["W5", "a10"], ["b9"])
                        V(lambda e, hs=hs, po=po, c=c, hd=hd: e.tensor_scalar(out=B_[10][0:CL, po + hh_off(hs):po + hh_off(hs) + 64], in0=A_[12][0:CL, hs],
                                                                              scalar1=tok[0:CL, c, 32 + hd:33 + hd], scalar2=None, op0=ALU.mult),
                          ["a12", tk], ["b10"])
                        V(lambda e, hs=hs, po=po: e.tensor_scalar(out=B_[12][0:CL, po + hh_off(hs):po + hh_off(hs) + 64],
                                                                  in0=B_[10][0:CL, po + hh_off(hs):po + hh_off(hs) + 64],
                                                                  scalar1=cs1[0:CL, 0:1], scalar2=None, op0=ALU.mult), ["b10", "cs1"], ["b12"])
                        V(lambda e, hs=hs, po=po, S=S: e.tensor_copy(out=B_[11][:, po + hh_off(hs):po + hh_off(hs) + 64], in_=S[:, hs]), [Sk], ["b11"])
                        T(lambda e, po=po, hh=hh: e.matmul(psb[7][:, 0:CL], lhsT=B_[10][0:CL, po:po + 128], rhs=B_[8][0:CL, 0:CL], start=(hh == 0), stop=False),
                          ["b10", "b8"], pk(7, 0, 1))
                        T(lambda e, po=po, hh=hh: e.matmul(psb[7][:, 0:CL], lhsT=B_[11][:, po:po + 128], rhs=B_[9][:, 0:CL], start=False, stop=(hh == 1)),
                          ["b11", "b9"], pk(7, 0, 1))
                        T(lambda e, po=po, hh=hh: e.matmul(psb[2][:, 0:128], lhsT=B_[2][0:CL, 0:128], rhs=B_[12][0:CL, po:po + 128], start=(hh == 0), stop=(hh == 1)),
                          ["b2", "b12"], pk(2))
                        V(lambda e, hh=hh: e.tensor_copy(out=cs1[:, 5 + hh:6 + hh], in_=A_[10][:, CL - 1:CL]), ["a10"], ["cs1d"])
                    for hh in range(2):
                        hs = slice(hh * 64, hh * 64 + 64)
                        V(lambda e, hs=hs, hh=hh, S=S: e.scalar_tensor_tensor(out=S[:, hs], in0=S[:, hs], scalar=cs1[:, 5 + hh:6 + hh], in1=psb[2][:, hh * 64:hh * 64 + 64],
                                                                              op0=ALU.mult, op1=ALU.add), [Sk, "cs1d"] + pk(2), [Sk])
                    if samp:
                        store_state(so_ssd_d[l, c, u], S, Sk, f"sst{buf}")
                    V(lambda e, sl=sl, u=u: e.scalar_tensor_tensor(out=W[4][:, sl], in0=W[2][:, sl], scalar=pv[:, l, PV_D + u:PV_D + u + 1], in1=psb[7][:, 0:CL],
                                                                    op0=ALU.mult, op1=ALU.add), ["W2", "pv"] + pk(7, 0, 1), ["W4"])
                    V(lambda e, sl=sl: e.tensor_tensor(out=W[4][:, sl], in0=W[4][:, sl], in1=W[3][:, sl], op=ALU.mult), ["W4", "W3"], ["W4"])
                if not samp and state.get("ptile", 0) == NPT - 1:
                    store_state(o_ssd_d[l, u], Sc[:, l, u, :], f"Sc{l}_{u}", "ost")
                V(lambda e, u=u: e.tensor_copy(out=scr[:, OB + u, 0:nt], in_=W[4][:, 0:nt]), ["W4"], [f"scr{OB + u}"])
                V(lambda e, uu=uu: e.tensor_tensor(out=Vb[5][:, 0:nt], in0=W[4][:, 0:nt], in1=W[4][:, 0:nt], op=ALU.mult), ["W4"], ["V5"])
                T(lambda e, uu=uu: e.matmul(psb[3][:, 0:nt], lhsT=ones_b[:], rhs=Vb[5][:, 0:nt], start=(uu == 0), stop=(uu == UPN - 1)), ["V5", "ones_b"], pk(3))
            Ac(lambda e: e.activation(out=W[1][:, 0:nt], in_=psb[3][:, 0:nt], func=AF.Ln, scale=1.0 / (UPN * 128), bias=EPS), pk(3), ["W1"])
            Ac(lambda e: e.activation(out=W[1][:, 0:nt], in_=W[1][:, 0:nt], func=AF.Exp, scale=-0.5), ["W1"], ["W1"])
            for uu in range(UPN):
                u = g * UPN + uu
                V(lambda e, uu=uu, u=u: e.scalar_tensor_tensor(out=scr[:, OB + u, 0:nt], in0=scr[:, OB + u, 0:nt], scalar=pv[:, l, PV_NW + u:PV_NW + u + 1], in1=W[1][:, 0:nt],
                                                                op0=ALU.mult, op1=ALU.mult), [f"scr{OB + u}", "pv", "W1"], [f"scr{OB + u}"])
        if samp:
            P.dma("sp", lambda e: e.dma_start(out=so_cvc_d[l][:, :, 0:2, :], in_=shC[:, :, 1:3, :]), "shst", reads=["shC"], writes=["odram"])
            P.dma("sp", lambda e: e.dma_start(out=so_cvc_d[l][:, :, 2, :], in_=xsC[:]), "shst", reads=["xsC"], writes=["odram"])
        elif state.get("ptile", 0) == NPT - 1:
            P.dma("sp", lambda e: e.dma_start(out=o_cvc_d[l], in_=histc[:, l, :, :]), "ost", reads=[f"histc{l}"], writes=["odram"])
        merge_branch(l, nt, 2, wbc_d[l], NU_C, OB, first=False)

        for dc in range(KC):
            s = ring_load(wo_d[l][dc])
            b = dc % 2
            for kc in range(KC):
                T(lambda e, s=s, kc=kc, b=b: e.matmul(psb[b][:, 0:nt], lhsT=ring[:, s, kc, :], rhs=scr[:, kc, 0:nt], start=(kc == 0), stop=(kc == KC - 1)),
                  [f"ring{s}", f"scr{kc}"], pk(b))
            resid_add(nt, samp, dc, b, Gt, lname)

    def hh_off(hs):
        return hs.start

    def merge_branch(l, nt, br, w_d, nk, OB, first):
        for dc in range(KC):
            b = dc % 2
            proj(l, f"gate{br}_{dc}", 2 + b, nt)
            t = nexttmp()
            Ac(lambda e, b=b, t=t: e.activation(out=tmp[:, t, 0:nt], in_=psb[2 + b][:, 0:nt], func=AF.Sigmoid), pk(2 + b), [f"tmp{t}"])
            s = ring_load(w_d[dc], nk)
            for kc in range(nk):
                T(lambda e, s=s, kc=kc, b=b: e.matmul(psb[b][:, 0:nt], lhsT=ring[:, s, kc, :], rhs=scr[:, OB + kc, 0:nt], start=(kc == 0), stop=(kc == nk - 1)),
                  [f"ring{s}", f"scr{OB + kc}"], pk(b))
            if first:
                V(lambda e, b=b, t=t, dc=dc: e.tensor_tensor(out=scr[:, dc, 0:nt], in0=tmp[:, t, 0:nt], in1=psb[b][:, 0:nt], op=ALU.mult),
                  [f"tmp{t}"] + pk(b), [f"scr{dc}"])
            else:
                V(lambda e, b=b, t=t: e.tensor_tensor(out=tmp[:, t, 0:nt], in0=tmp[:, t, 0:nt], in1=psb[b][:, 0:nt], op=ALU.mult),
                  [f"tmp{t}"] + pk(b), [f"tmp{t}"])
                V(lambda e, t=t, dc=dc: e.tensor_tensor(out=scr[:, dc, 0:nt], in0=scr[:, dc, 0:nt], in1=tmp[:, t, 0:nt], op=ALU.add),
                  [f"tmp{t}", f"scr{dc}"], [f"scr{dc}"])


    tiles = [("p", i) for i in range(NPT)] + [("s", 0)]
    xkeys = [f"x{kc}" for kc in range(KC)]
    V(lambda e: e.memset(hista[:], 0.0), [], ["hista0", "hista1"])
    V(lambda e: e.memset(histc[:], 0.0), [], ["histc0", "histc1"])
    for (kind, ti) in tiles:
        samp = kind == "s"
        nt = NS if samp else NT
        state["ptile"] = ti
        src = xs_d if samp else xp_d[:, :, ti * NT:(ti + 1) * NT]
        P.dma("sp", lambda e, src=src, nt=nt: e.dma_start(out=x[:, :, 0:nt], in_=src), "xload", writes=xkeys)
        for l in range(2):
            ffn(l, 1, nt, samp)
            mixer(l, nt, samp)
            ffn(l, 2, nt, samp)
        dst = ys_d if samp else yp_d[:, :, ti * NT:(ti + 1) * NT]
        rms_stats(nt)
        fn_w = norms[:, 6, :]
        for kc in range(KC):
            V(lambda e, kc=kc, nt=nt: e.scalar_tensor_tensor(out=x[:, kc, 0:nt], in0=x[:, kc, 0:nt], scalar=fn_w[:, kc:kc + 1],
                                                            in1=rstd[:, 0:nt], op0=ALU.mult, op1=ALU.mult),
              [f"x{kc}", "rstd", "norms"], [f"x{kc}"])
        P.dma("sp", lambda e, dst=dst, nt=nt: e.dma_start(out=dst, in_=x[:, :, 0:nt]), "ystore", reads=xkeys, writes=["ydram"])

    P.emit(nc, es)
    es.close()
    return nc


def _units(w):
    K, N = w.shape
    return np.ascontiguousarray(w.reshape(K // 128, 128, N // 128, 128).transpose(2, 1, 0, 3))


def _fm(v):
    n, Fd = v.shape
    return np.ascontiguousarray(v.reshape(n, Fd // 128, 128).transpose(2, 1, 0))


def _consts():
    c = np.zeros((128, 512), np.float32)
    c[:, 0:128] = np.eye(128)
    c[:, 128:256] = np.triu(np.ones((128, 128)))
    c[:, 256:384] = np.tril(np.ones((128, 128)), -1)
    c[:, 384:512] = 1.0
    return c


def _padcols(w, n=128):
    K, c = w.shape
    out = np.zeros((K, n), np.float32)
    out[:, :c] = w
    return out


_NC = {}


def kernel(_cfg=None, **inp):
    cfg = dict(FULL if _cfg is None else _cfg)
    D, FF, SEQ, HA, HB, HC, G = (cfg[k] for k in ("D", "FF", "SEQ", "HA", "HB", "HC", "G"))
    KC = D // 128
    NU_C = HC // 2
    NCA, NCC = 3 * HA, NU_C + 2 * G
    f = lambda a: np.asarray(a, np.float32)
    shared = {"consts": _consts()}
    shared["wada"] = np.stack([_units(f(inp["w_ada"][l])) for l in range(2)])
    shared["bada"] = np.stack([np.ascontiguousarray(f(inp["b_ada"][l]).reshape(9 * KC, 128).T) for l in range(2)])
    nl = [f(inp[k][l]) for l in range(2) for k in ("norm1", "norm2", "norm3")] + [f(inp["final_norm"])]
    shared["norms"] = np.ascontiguousarray(np.stack(nl).reshape(7, KC, 128).transpose(2, 0, 1))
    for l in range(2):
        for w in (1, 2):
            shared[f"wg{l}{w}"] = _units(f(inp[f"ffn{w}_wg"][l]))
            shared[f"wu{l}{w}"] = _units(f(inp[f"ffn{w}_wu"][l]))
            shared[f"wd{l}{w}"] = _units(f(inp[f"ffn{w}_wd"][l]))
    QK_A, V_A = HA * 128, HA * 128
    splits = [2 * QK_A + V_A, V_A, HA, HA, HB * 128, HB * 128, HB * 256, 16, HB * 256, HC * 64, HC * 64 + 2 * G * 128, HC, 3 * D]
    offs = np.concatenate([[0], np.cumsum(splits)])
    UP = unit_plan(cfg)
    NPV = 4 + HB + 2 * NU_C + 32
    pvs, cwas, cwcs, nrms, wgates = [], [], [], [], []
    for l in range(2):
        w = f(inp["w_in"][l])
        grp = [w[:, offs[i]:offs[i + 1]] for i in range(13)]
        qkv_a, z_a, beta_a, dec_a, q_b, k_b, v_b, lr_b, r_b, z_c, xbc_c, dt_c, gates = grp
        cols = {}
        cols["beta"] = _padcols(beta_a)
        cols["dec"] = _padcols(dec_a)
        for h in range(HA):
            cols[f"aq{h}"] = qkv_a[:, h * 128:(h + 1) * 128]
            cols[f"ak{h}"] = qkv_a[:, QK_A + h * 128:QK_A + (h + 1) * 128]
            cols[f"av{h}"] = qkv_a[:, 2 * QK_A + h * 128:2 * QK_A + (h + 1) * 128]
            cols[f"az{h}"] = z_a[:, h * 128:(h + 1) * 128]
        cols["lr"] = _padcols(lr_b)
        for h in range(HB):
            cols[f"bq{h}"] = q_b[:, h * 128:(h + 1) * 128]
            cols[f"bk{h}"] = k_b[:, h * 128:(h + 1) * 128]
            for i in range(2):
                cols[f"bv{h}_{i}"] = v_b[:, h * 256 + i * 128:h * 256 + (i + 1) * 128]
                cols[f"br{h}_{i}"] = r_b[:, h * 256 + i * 128:h * 256 + (i + 1) * 128]
        cols["dt"] = _padcols(dt_c)
        inner = HC * 64
        for g in range(G):
            cols[f"cB{g}"] = xbc_c[:, inner + g * 128:inner + (g + 1) * 128]
            cols[f"cC{g}"] = xbc_c[:, inner + G * 128 + g * 128:inner + G * 128 + (g + 1) * 128]
        for u in range(NU_C):
            cols[f"cx{u}"] = xbc_c[:, u * 128:(u + 1) * 128]
            cols[f"cz{u}"] = z_c[:, u * 128:(u + 1) * 128]
        for br in range(3):
            for dc in range(KC):
                cols[f"gate{br}_{dc}"] = gates[:, br * D + dc * 128:br * D + (dc + 1) * 128]
        arr = np.zeros((len(UP), 128, KC, 128), np.float32)
        for n, i in UP.items():
            arr[i] = cols[n].reshape(KC, 128, 128).transpose(1, 0, 2)
        shared[f"win{l}"] = arr
        shared[f"wba{l}"] = _units(f(inp["w_branch_gdn"][l]))
        shared[f"wbb{l}"] = _units(f(inp["w_branch_gla"][l]))
        shared[f"wbc{l}"] = _units(f(inp["w_branch_ssd"][l]))
        shared[f"wo{l}"] = _units(f(inp["w_out"][l]))
        pvl = np.zeros((128, NPV), np.float32)
        pvl[:HA, 0] = f(inp["gdn_a_log"][l])
        pvl[:HA, 1] = f(inp["gdn_dt_bias"][l])
        pvl[:HC, 2] = f(inp["ssd_a_log"][l])
        pvl[:HC, 3] = f(inp["ssd_dt_bias"][l])
        pvl[:, 4:4 + HB] = f(inp["gla_b_gate"][l]).reshape(HB, 128).T
        pvl[:, 4 + HB:4 + HB + NU_C] = np.repeat(f(inp["ssd_d"][l]).reshape(NU_C, 2), 64, axis=1).T
        pvl[:, 4 + HB + NU_C:4 + HB + 2 * NU_C] = f(inp["ssd_norm_w"][l]).reshape(NU_C, 128).T
        pvl[:32, 4 + HB + 2 * NU_C:] = np.eye(32)
        pvs.append(pvl)
        cwas.append(np.ascontiguousarray(f(inp["gdn_conv_w"][l]).reshape(4, NCA, 128).transpose(2, 1, 0)))
        cw = f(inp["ssd_conv_w"][l])
        cb = f(inp["ssd_conv_b"][l])
        cwc = np.concatenate([cw.reshape(4, NCC, 128), cb.reshape(1, NCC, 128)], 0)
        cwcs.append(np.ascontiguousarray(cwc.transpose(2, 1, 0)))
        nrms.append(np.concatenate([np.tile(f(inp["gdn_norm_w"][l])[None], (128, 1)), np.tile(f(inp["gla_norm_w"][l])[None], (128, 1))], 1))
        wgates.append(f(inp["gla_w_gate"][l]))
    shared["pv"] = np.stack(pvs)
    shared["cwa"] = np.stack(cwas)
    shared["cwc"] = np.stack(cwcs)
    shared["nrm"] = np.ascontiguousarray(np.stack(nrms))
    shared["wgate"] = np.stack(wgates)

    xp = f(inp["x_prompt"])
    xs = f(inp["x_sample"])[:, 0, :]
    cp = f(inp["c_prompt"])
    cs = f(inp["c_sample"])
    sg, sl_, ss, sca, scc = (f(inp[k]) for k in ("state_gdn", "state_gla", "state_ssd", "state_gdn_conv", "state_ssd_conv"))
    in_maps = []
    for c in range(8):
        m = dict(shared)
        b0 = 16 * c
        m["xp"] = _fm(xp[c % 4])
        m["xs"] = _fm(xs[b0:b0 + 16])
        m["cT"] = _fm(np.concatenate([cp[c % 4][None], cs[b0:b0 + 16]], 0))
        m["s_gdn"] = np.ascontiguousarray(sg[:, b0:b0 + 16])
        m["s_gla"] = np.ascontiguousarray(sl_[:, b0:b0 + 16])
        m["s_ssd"] = np.ascontiguousarray(ss[:, b0:b0 + 16].reshape(2, 16, NU_C, 2, 64, 128).transpose(0, 1, 2, 5, 3, 4).reshape(2, 16, NU_C, 128, 128))
        m["s_cva"] = np.ascontiguousarray(sca[:, b0:b0 + 16].reshape(2, 16, 3, NCA, 128).transpose(0, 4, 3, 2, 1))
        m["s_cvc"] = np.ascontiguousarray(scc[:, b0:b0 + 16].reshape(2, 16, 3, NCC, 128).transpose(0, 4, 3, 2, 1))
        in_maps.append(m)
    key = tuple(sorted(cfg.items()))
    if key not in _NC:
        _NC[key] = build(cfg)
    res = run_bass_kernel_spmd(_NC[key], in_maps, core_ids=list(range(8)))
    R = res.results
    y_prompt = np.stack([R[c]["yp"].transpose(2, 1, 0).reshape(SEQ, D) for c in range(4)])
    y_sample = np.concatenate([R[c]["ys"].transpose(2, 1, 0).reshape(NS, D) for c in range(8)])[:, None, :]

    def conv_p(name, ncu):
        return np.stack([R[c][name].transpose(0, 3, 2, 1).reshape(2, 3, ncu * 128) for c in range(4)], 1)

    def conv_s(name, ncu):
        return np.concatenate([R[c][name].transpose(0, 4, 3, 2, 1).reshape(2, 16, 3, ncu * 128) for c in range(8)], 1)

    def ssd_back(a):
        sh = a.shape[:-3]
        return a.reshape(*sh, NU_C, 128, 2, 64).transpose(*range(len(sh)), len(sh), len(sh) + 2, len(sh) + 3, len(sh) + 1).reshape(*sh, HC, 64, 128)
    p_gdn_conv = conv_p("o_cva", NCA)
    p_gdn = np.stack([R[c]["o_gdn"] for c in range(4)], 1)
    p_gla = np.stack([R[c]["o_gla"] for c in range(4)], 1)
    p_ssd_conv = conv_p("o_cvc", NCC)
    p_ssd = np.stack([ssd_back(R[c]["o_ssd"]) for c in range(4)], 1)
    s_gdn_conv = conv_s("so_cva", NCA)
    s_gdn = np.concatenate([R[c]["so_gdn"] for c in range(8)], 1)
    s_gla = np.concatenate([R[c]["so_gla"] for c in range(8)], 1)
    s_ssd_conv = conv_s("so_cvc", NCC)
    s_ssd = np.concatenate([ssd_back(R[c]["so_ssd"]) for c in range(8)], 1)
    outs = (y_prompt, y_sample, p_gdn_conv, p_gdn, p_gla, p_ssd_conv, p_ssd, s_gdn_conv, s_gdn, s_gla, s_ssd_conv, s_ssd)
    return tuple(np.ascontiguousarray(o, dtype=np.float32) for o in outs)
```

```python
import numpy as np
from contextlib import ExitStack
import concourse.bass as bass
import concourse.mybir as mybir
from concourse.bass_utils import run_bass_kernel_spmd

F32, BF16 = mybir.dt.float32, mybir.dt.bfloat16
AF = mybir.ActivationFunctionType
ALU = mybir.AluOpType
AX = mybir.AxisListType

FULL = dict(D=2048, FF=5504, SEQ=2048, HA=8, HB=4, HC=32, G=4)
NT = 512
NS = 16
NV = 17
EPS = 1e-6
RS = 3


class _Rec:
    def __init__(self):
        self.call = None

    def __getattr__(self, name):
        def f(*a, **k):
            self.call = (name, a, k)
            return self
        return f


def _bind(fn):
    r = _Rec()
    fn(r)
    name, a, k = r.call
    return lambda eng: getattr(eng, name)(*a, **k)


class Prog:
    ENG = ("pe", "act", "dve", "pool", "sp")

    def __init__(self):
        self.ops = {e: [] for e in self.ENG}
        self.cnt = {e: 0 for e in ("pe", "act", "dve")}
        self.lastw = {}
        self.readers = {}
        self.dcnt = {}
        self.known = {e: {} for e in self.ENG}

    def _deps(self, eng, reads, writes):
        need = {}

        def add(t, raw):
            s, v, e = t
            if e == eng and eng == "pe":
                return
            if need.get(s, 0) < v:
                need[s] = v
        for k in reads:
            if k in self.lastw:
                add(self.lastw[k], True)
        for k in writes:
            if k in self.lastw:
                add(self.lastw[k], True)
            for r in self.readers.get(k, ()):
                add(r, False)
        kn = self.known[eng]
        out = []
        for s, v in need.items():
            if kn.get(s, 0) < v:
                kn[s] = v
                out.append((s, v))
        return out

    def _commit(self, tok, reads, writes):
        for k in reads:
            self.readers.setdefault(k, []).append(tok)
        for k in writes:
            self.lastw[k] = tok
            self.readers[k] = []

    def op(self, eng, fn, reads=(), writes=()):
        writes = list(writes) + [k for k in reads if k.startswith("ps")]
        reads = [k for k in reads if not k.startswith("ps")]
        waits = self._deps(eng, reads, writes)
        self.cnt[eng] += 1
        s = "c_" + eng
        self.ops[eng].append((waits, _bind(fn), s, 1))
        self._commit((s, self.cnt[eng], eng), reads, writes)

    def dma(self, q, fn, semkey, reads=(), writes=()):
        waits = self._deps(q, reads, writes)
        n = self.dcnt.get(semkey, 0)
        s = "d_" + semkey
        if n > 0 and self.known[q].get(s, 0) < 16 * n:
            self.known[q][s] = 16 * n
            waits.append((s, 16 * n))
        self.dcnt[semkey] = n + 1
        self.ops[q].append((waits, _bind(fn), s, 16))
        self._commit((s, 16 * (n + 1), "dma"), reads, writes)

    def emit(self, nc, es):
        names = ["c_pe", "c_act", "c_dve"] + ["d_" + k for k in self.dcnt]
        sems = {n: es.enter_context(nc.semaphore(n)) for n in names}
        finals = [(sems["d_" + k], 16 * n) for k, n in self.dcnt.items()]
        block = es.enter_context(nc.Block())
        decs = {"pe": block.tensor, "act": block.scalar, "dve": block.vector, "pool": block.gpsimd, "sp": block.sync}
        for e in self.ENG:
            ops = self.ops[e]

            def body(eng, ops=ops, last=(e == "sp")):
                for waits, fn, s, inc in ops:
                    for ws, wv in waits:
                        eng.wait_ge(sems[ws], wv)
                    fn(eng).then_inc(sems[s], inc)
                if last:
                    for sm, v in finals:
                        eng.wait_ge(sm, v)
            decs[e](body)


def unit_plan(cfg):
    HA, HB, HC, G, KC = cfg["HA"], cfg["HB"], cfg["HC"], cfg["G"], cfg["D"] // 128
    names = ["beta", "dec"]
    for h in range(HA):
        names += [f"aq{h}", f"ak{h}", f"av{h}", f"az{h}"]
    names += ["lr"]
    for h in range(HB):
        names += [f"bq{h}", f"bk{h}", f"bv{h}_0", f"bv{h}_1", f"br{h}_0", f"br{h}_1"]
    names += ["dt"]
    for g in range(G):
        names += [f"cB{g}", f"cC{g}"]
    for u in range(HC // 2):
        names += [f"cx{u}", f"cz{u}"]
    for br in range(3):
        for dc in range(KC):
            names += [f"gate{br}_{dc}"]
    return {n: i for i, n in enumerate(names)}


def build(cfg):
    D, FF, SEQ, HA, HB, HC, G = (cfg[k] for k in ("D", "FF", "SEQ", "HA", "HB", "HC", "G"))
    KC = D // 128
    FC = FF // 128
    FH = (FC + 2) // 3
    NU_C = HC // 2
    HPG = HC // G
    UPN = NU_C // G
    NCA = 3 * HA
    NCC = NU_C + 2 * G
    NPT = SEQ // NT
    UP = unit_plan(cfg)
    NUW = len(UP)
    assert KC <= 16 and NU_C <= 16 and HC <= 32

    nc = bass.Bass("TRN2", target_bir_lowering=False)
    P = Prog()
    es = ExitStack()

    def din(name, shape):
        return nc.dram_tensor(name, list(shape), F32, kind="ExternalInput").ap()

    def dout(name, shape):
        return nc.dram_tensor(name, list(shape), F32, kind="ExternalOutput").ap()

    def sb(name, shape, dt=F32):
        return es.enter_context(nc.sbuf_tensor(name, list(shape), dt))

    xp_d = din("xp", [128, KC, SEQ])
    xs_d = din("xs", [128, KC, NS])
    cT_d = din("cT", [128, KC, NV])
    consts_d = din("consts", [128, 512])
    wada_d = din("wada", [2, 9 * KC, 128, KC, 128])
    bada_d = din("bada", [2, 128, 9 * KC])
    norms_d = din("norms", [128, 7, KC])
    ffn_d = {}
    for l in range(2):
        for w in (1, 2):
            ffn_d[(l, w)] = (din(f"wg{l}{w}", [FC, 128, KC, 128]), din(f"wu{l}{w}", [FC, 128, KC, 128]),
                             din(f"wd{l}{w}", [KC, 128, FC, 128]))
    win_d = [din(f"win{l}", [NUW, 128, KC, 128]) for l in range(2)]
    wba_d = [din(f"wba{l}", [KC, 128, HA, 128]) for l in range(2)]
    wbb_d = [din(f"wbb{l}", [KC, 128, 2 * HB, 128]) for l in range(2)]
    wbc_d = [din(f"wbc{l}", [KC, 128, NU_C, 128]) for l in range(2)]
    wo_d = [din(f"wo{l}", [KC, 128, KC, 128]) for l in range(2)]
    NPV = 4 + HB + 2 * NU_C + 32
    pv_d = din("pv", [2, 128, NPV])
    cwa_d = din("cwa", [2, 128, NCA, 4])
    cwc_d = din("cwc", [2, 128, NCC, 5])
    nrm_d = din("nrm", [2, 128, 384])
    wgate_d = din("wgate", [2, 16, HB * 128])
    s_gdn_d = din("s_gdn", [2, NS, HA, 128, 128])
    s_gla_d = din("s_gla", [2, NS, HB, 128, 256])
    s_ssd_d = din("s_ssd", [2, NS, NU_C, 128, 128])
    s_cva_d = din("s_cva", [2, 128, NCA, 3, NS])
    s_cvc_d = din("s_cvc", [2, 128, NCC, 3, NS])
    yp_d = dout("yp", [128, KC, SEQ])
    ys_d = dout("ys", [128, KC, NS])
    o_gdn_d = dout("o_gdn", [2, HA, 128, 128])
    o_gla_d = dout("o_gla", [2, HB, 128, 256])
    o_ssd_d = dout("o_ssd", [2, NU_C, 128, 128])
    o_cva_d = dout("o_cva", [2, 128, NCA, 3])
    o_cvc_d = dout("o_cvc", [2, 128, NCC, 3])
    so_gdn_d = dout("so_gdn", [2, NS, HA, 128, 128])
    so_gla_d = dout("so_gla", [2, NS, HB, 128, 256])
    so_ssd_d = dout("so_ssd", [2, NS, NU_C, 128, 128])
    so_cva_d = dout("so_cva", [2, 128, NCA, 3, NS])
    so_cvc_d = dout("so_cvc", [2, 128, NCC, 3, NS])

    x = sb("x", [128, KC, NT])
    h = sb("h", [128, KC, NT], BF16)
    scr = sb("scr", [128, 32, NT], BF16)
    ring = sb("ring", [128, RS, 16, 128], BF16)
    wdr = sb("wdr", [128, 2, FH, 128], BF16)
    mod = [sb(f"mod{l}", [128, 9 * KC, NV]) for l in range(2)]
    cst = sb("cst", [128, 512])
    ident_b = sb("ident_b", [128, 128], BF16)
    ones_b = sb("ones_b", [128, 128], BF16)
    U_b = sb("U_b", [128, 128], BF16)
    norms = sb("norms_sb", [128, 7, KC])
    scT = sb("scT", [128, KC, NV], BF16)
    tmp = sb("tmp", [128, 2, NT])
    rstd = sb("rstd", [128, NT])
    pv = sb("pv_sb", [128, 2, NPV])
    cwa = sb("cwa_sb", [128, 2, NCA, 4])
    cwc = sb("cwc_sb", [128, 2, NCC, 5])
    nrm1 = sb("nrm_sb", [128, 384])
    wgate = sb("wgate_b", [16, 2, HB * 128], BF16)
    nSa, nSb, nSc = 2 * HA * 128, 2 * HB * 256, 2 * NU_C * 128
    nA, nC = NCA * 3 * NS, NCC * 3 * NS
    PSM = sb("PSM", [128, max(nSa + nSb + nSc, nA + nC + NCA * NS + NCC * NS + 512)])
    Sa = PSM[:, 0:nSa].rearrange("p (l h d) -> p l h d", l=2, h=HA)
    Sb_ = PSM[:, nSa:nSa + nSb].rearrange("p (l h d) -> p l h d", l=2, h=HB)
    Sc = PSM[:, nSa + nSb:nSa + nSb + nSc].rearrange("p (l h d) -> p l h d", l=2, h=NU_C)
    shA = PSM[:, 0:nA].rearrange("p (u j b) -> p u j b", u=NCA, j=3)
    shC = PSM[:, nA:nA + nC].rearrange("p (u j b) -> p u j b", u=NCC, j=3)
    o2 = nA + nC
    xsA = PSM[:, o2:o2 + NCA * NS].rearrange("p (u b) -> p u b", u=NCA)
    xsC = PSM[:, o2 + NCA * NS:o2 + NCA * NS + NCC * NS].rearrange("p (u b) -> p u b", u=NCC)
    o3 = o2 + NCA * NS + NCC * NS
    SS = PSM[:, o3:o3 + 512].rearrange("p (s d) -> p s d", s=2)
    PKEYS = [f"Sa{l}_{i}" for l in range(2) for i in range(HA)] + [f"Sb{l}_{i}" for l in range(2) for i in range(HB)] + \
            [f"Sc{l}_{i}" for l in range(2) for i in range(NU_C)]
    hista = sb("hista", [128, 2, NCA, 3])
    histc = sb("histc", [128, 2, NCC, 3])
    W = [sb(f"W{i}", [128, NT + 4]) for i in range(9)]
    Vb = [sb(f"V{i}", [128, NT], BF16) for i in range(6)]
    A_ = [sb(f"a{i}", [128, 256 if i == 4 else 128]) for i in range(13)]
    B_ = [sb(f"b{i}", [128, 256 if i in (0, 1, 10, 11, 12) else 128], BF16) for i in range(13)]
    tok = sb("tokv", [128, 16, 64])
    cs1 = sb("cs1", [128, 8])
    psb = [es.enter_context(nc.psum_tensor(f"ps{i}", [128, NT], F32)) for i in range(8)]
    ident = cst[:, 0:128]
    Umat = cst[:, 128:256]
    Lsmat = cst[:, 256:384]
    ones_f = cst[:, 384:512]

    def pk(b, q0=0, q1=4):
        return [f"ps{b}"]

    state = {"ring": 0, "wd": 0, "tmp": 0}

    def ring_load(src_ap, nk=KC):
        s = state["ring"] % RS
        state["ring"] += 1
        P.dma("pool", lambda e, s=s: e.dma_start(out=ring[:, s, 0:nk, :], in_=src_ap), f"ring{s}", writes=[f"ring{s}"])
        return s

    def nexttmp():
        i = state["tmp"] % 2
        state["tmp"] += 1
        return i

    def V(fn, r, w):
        P.op("dve", fn, r, w)

    def Ac(fn, r, w):
        P.op("act", fn, r, w)

    def T(fn, r, w):
        P.op("pe", fn, r, w)

    def ld(dst, src, key):
        P.dma("sp", lambda e: e.dma_start(out=dst, in_=src), key, writes=[key])
    ld(cst[:], consts_d, "cst")
    ld(norms[:], norms_d, "norms")
    bada = W[1][:, 0:2 * 9 * KC].rearrange("p (l u) -> p l u", l=2)
    cT = W[0][:, 0:KC * NV].rearrange("p (k v) -> p k v", k=KC)
    P.dma("sp", lambda e: e.dma_start(out=bada, in_=bada_d.rearrange("l p u -> p l u")), "bada", writes=["W1"])
    P.dma("sp", lambda e: e.dma_start(out=cT, in_=cT_d), "cT", writes=["W0"])
    ld(pv[:], pv_d.rearrange("l p u -> p l u"), "pv")
    ld(cwa[:], cwa_d.rearrange("l p u j -> p l u j"), "cwa")
    ld(cwc[:], cwc_d.rearrange("l p u j -> p l u j"), "cwc")
    P.dma("pool", lambda e: e.dma_start(out=wgate[:], in_=wgate_d.rearrange("l p u -> p l u")), "wgate", writes=["wgate"])
    V(lambda e: e.tensor_copy(out=ident_b[:], in_=cst[:, 0:128]), ["cst"], ["ident_b"])
    V(lambda e: e.tensor_copy(out=ones_b[:], in_=cst[:, 384:512]), ["cst"], ["ones_b"])
    V(lambda e: e.tensor_copy(out=U_b[:], in_=cst[:, 128:256]), ["cst"], ["U_b"])
    Ac(lambda e: e.activation(out=scT[:], in_=cT, func=AF.Silu), ["W0"], ["scT"])
    for l in range(2):
        for c in (0, 2):
            Ac(lambda e, l=l, c=c: e.activation(out=pv[:, l, c:c + 1], in_=pv[:, l, c:c + 1], func=AF.Exp), ["pv"], ["pv"])
            V(lambda e, l=l, c=c: e.tensor_scalar(out=pv[:, l, c:c + 1], in0=pv[:, l, c:c + 1], scalar1=-1.0, scalar2=None, op0=ALU.mult), ["pv"], ["pv"])
        V(lambda e, l=l: e.tensor_scalar(out=pv[:, l, 4:4 + HB], in0=pv[:, l, 4:4 + HB], scalar1=-1.0, scalar2=None, op0=ALU.mult), ["pv"], ["pv"])
    PV_BG, PV_D, PV_NW, PV_OH = 4, 4 + HB, 4 + HB + NU_C, 4 + HB + 2 * NU_C

    for l in range(2):
        for u in range(9 * KC):
            s = ring_load(wada_d[l, u])
            pb = u % 2
            for kc in range(KC):
                T(lambda e, s=s, kc=kc, pb=pb: e.matmul(psb[pb][:, 0:NV], lhsT=ring[:, s, kc, :], rhs=scT[:, kc, :],
                                                         start=(kc == 0), stop=(kc == KC - 1)),
                  [f"ring{s}", "scT"], pk(pb))
            V(lambda e, l=l, u=u, pb=pb: e.tensor_scalar(out=mod[l][:, u, :], in0=psb[pb][:, 0:NV], scalar1=bada[:, l, u:u + 1],
                                                          scalar2=None, op0=ALU.add),
              pk(pb) + ["W1"], [f"mod{l}"])
        for m in range(3):
            sc = mod[l][:, (3 * m + 1) * KC:(3 * m + 2) * KC, :]
            gt = mod[l][:, (3 * m + 2) * KC:(3 * m + 3) * KC, :]
            nb = norms[:, 3 * l + m, :].unsqueeze(2).to_broadcast([128, KC, NV])
            V(lambda e, sc=sc, nb=nb: e.scalar_tensor_tensor(out=sc, in0=sc, scalar=1.0, in1=nb, op0=ALU.add, op1=ALU.mult),
              [f"mod{l}", "norms"], [f"mod{l}"])
            if m != 1:
                V(lambda e, gt=gt: e.tensor_scalar(out=gt, in0=gt, scalar1=0.5, scalar2=None, op0=ALU.mult), [f"mod{l}"], [f"mod{l}"])

    def rms_stats(nt):
        for kc in range(KC):
            Ac(lambda e, kc=kc: e.activation(out=scr[:, kc, 0:nt], in_=x[:, kc, 0:nt], func=AF.Square), [f"x{kc}"], [f"scr{kc}"])
        for kc in range(KC):
            T(lambda e, kc=kc: e.matmul(psb[7][:, 0:nt], lhsT=ones_b[:], rhs=scr[:, kc, 0:nt], start=(kc == 0), stop=(kc == KC - 1)),
              [f"scr{kc}", "ones_b"], pk(7))
        Ac(lambda e: e.activation(out=rstd[:, 0:nt], in_=psb[7][:, 0:nt], func=AF.Ln, scale=1.0 / D, bias=EPS), pk(7), ["rstd"])
        Ac(lambda e: e.activation(out=rstd[:, 0:nt], in_=rstd[:, 0:nt], func=AF.Exp, scale=-0.5), ["rstd"], ["rstd"])

    def norm_mod(nt, samp, A, B, lname):
        rms_stats(nt)
        for kc in range(KC):
            t = nexttmp()
            o = h[:, kc, 0:nt]
            if not samp:
                V(lambda e, kc=kc, t=t: e.scalar_tensor_tensor(out=tmp[:, t, 0:nt], in0=x[:, kc, 0:nt], scalar=A[:, kc, 0:1],
                                                                in1=rstd[:, 0:nt], op0=ALU.mult, op1=ALU.mult),
                  [f"x{kc}", "rstd", lname], [f"tmp{t}"])
                Ac(lambda e, kc=kc, t=t, o=o: e.activation(out=o, in_=tmp[:, t, 0:nt], func=AF.Identity, bias=B[:, kc, 0:1]),
                   [f"tmp{t}", lname], [f"h{kc}"])
            else:
                V(lambda e, kc=kc, t=t: e.tensor_tensor(out=tmp[:, t, 0:nt], in0=x[:, kc, 0:nt], in1=rstd[:, 0:nt], op=ALU.mult),
                  [f"x{kc}", "rstd"], [f"tmp{t}"])
                V(lambda e, kc=kc, t=t: e.tensor_tensor(out=tmp[:, t, 0:nt], in0=tmp[:, t, 0:nt], in1=A[:, kc, 1:NV], op=ALU.mult),
                  [f"tmp{t}", lname], [f"tmp{t}"])
                V(lambda e, kc=kc, t=t, o=o: e.tensor_tensor(out=o, in0=tmp[:, t, 0:nt], in1=B[:, kc, 1:NV], op=ALU.add),
                  [f"tmp{t}", lname], [f"h{kc}"])

    def resid_add(nt, samp, dc, pbank, G, lname):
        if not samp:
            V(lambda e: e.scalar_tensor_tensor(out=x[:, dc, 0:nt], in0=psb[pbank][:, 0:nt], scalar=G[:, dc, 0:1],
                                               in1=x[:, dc, 0:nt], op0=ALU.mult, op1=ALU.add),
              pk(pbank) + [f"x{dc}", lname], [f"x{dc}"])
        else:
            t = nexttmp()
            V(lambda e: e.tensor_tensor(out=tmp[:, t, 0:nt], in0=psb[pbank][:, 0:nt], in1=G[:, dc, 1:NV], op=ALU.mult),
              pk(pbank) + [lname], [f"tmp{t}"])
            V(lambda e: e.tensor_tensor(out=x[:, dc, 0:nt], in0=x[:, dc, 0:nt], in1=tmp[:, t, 0:nt], op=ALU.add),
              [f"tmp{t}", f"x{dc}"], [f"x{dc}"])

    def ffn(l, w, nt, samp):
        m = 0 if w == 1 else 2
        lname = f"mod{l}"
        sh = mod[l][:, (3 * m) * KC:(3 * m + 1) * KC, :]
        A = mod[l][:, (3 * m + 1) * KC:(3 * m + 2) * KC, :]
        G = mod[l][:, (3 * m + 2) * KC:(3 * m + 3) * KC, :]
        norm_mod(nt, samp, A, sh, lname)
        wg_d, wu_d, wd_d = ffn_d[(l, w)]
        for (j0, j1) in ((0, FH), (FH, min(2 * FH, FC)), (min(2 * FH, FC), FC)):
            if j1 <= j0:
                continue
            for j in range(j0, j1):
                b = j % 2
                sg = ring_load(wg_d[j])
                su = ring_load(wu_d[j])
                for kc in range(KC):
                    T(lambda e, sg=sg, kc=kc, b=b: e.matmul(psb[b][:, 0:nt], lhsT=ring[:, sg, kc, :], rhs=h[:, kc, 0:nt],
                                                             start=(kc == 0), stop=(kc == KC - 1)),
                      [f"ring{sg}", f"h{kc}"], pk(b))
                for kc in range(KC):
                    T(lambda e, su=su, kc=kc, b=b: e.matmul(psb[2 + b][:, 0:nt], lhsT=ring[:, su, kc, :], rhs=h[:, kc, 0:nt],
                                                             start=(kc == 0), stop=(kc == KC - 1)),
                      [f"ring{su}", f"h{kc}"], pk(2 + b))
                t = nexttmp()
                Ac(lambda e, b=b, t=t: e.activation(out=tmp[:, t, 0:nt], in_=psb[b][:, 0:nt], func=AF.Silu), pk(b), [f"tmp{t}"])
                V(lambda e, b=b, t=t, jj=j - j0: e.tensor_tensor(out=scr[:, jj, 0:nt], in0=tmp[:, t, 0:nt], in1=psb[2 + b][:, 0:nt], op=ALU.mult),
                  [f"tmp{t}"] + pk(2 + b), [f"scr{j - j0}"])
            nj = j1 - j0
            for dc in range(KC):
                ws = state["wd"] % 2
                state["wd"] += 1
                b = 4 + dc % 2
                P.dma("pool", lambda e, ws=ws, dc=dc, j0=j0, j1=j1, nj=nj: e.dma_start(out=wdr[:, ws, 0:nj, :], in_=wd_d[dc, :, j0:j1, :]),
                      f"wdr{ws}", writes=[f"wdr{ws}"])
                for jj in range(nj):
                    T(lambda e, ws=ws, jj=jj, b=b, nj=nj: e.matmul(psb[b][:, 0:nt], lhsT=wdr[:, ws, jj, :], rhs=scr[:, jj, 0:nt],
                                                                    start=(jj == 0), stop=(jj == nj - 1)),
                      [f"wdr{ws}", f"scr{jj}"], pk(b))
                resid_add(nt, samp, dc, b, G, lname)

    hk = [f"h{kc}" for kc in range(KC)]

    def proj(l, uname, pbank, nt):
        s = ring_load(win_d[l][UP[uname]])
        for kc in range(KC):
            T(lambda e, s=s, kc=kc: e.matmul(psb[pbank][:, 0:nt], lhsT=ring[:, s, kc, :], rhs=h[:, kc, 0:nt], start=(kc == 0), stop=(kc == KC - 1)),
              [f"ring{s}", f"h{kc}"], pk(pbank))

    def conv_unit(l, pbank, nt, samp, cw_ap, cwkey, bias_ap, hist_ap, histkey, shist, xstash, shkey, xskey, rawW, outW):
        raw, out = W[rawW], W[outW]
        rk, ok = f"W{rawW}", f"W{outW}"
        if not samp:
            V(lambda e: e.tensor_copy(out=raw[:, 0:3], in_=hist_ap), [histkey], [rk])
            Ac(lambda e: e.activation(out=raw[:, 3:3 + nt], in_=psb[pbank][:, 0:nt], func=AF.Copy), pk(pbank), [rk])
            V(lambda e: e.tensor_scalar(out=out[:, 0:nt], in0=raw[:, 0:nt], scalar1=cw_ap[:, 0:1], scalar2=None, op0=ALU.mult), [rk, cwkey], [ok])
            for j in range(1, 4):
                V(lambda e, j=j: e.scalar_tensor_tensor(out=out[:, 0:nt], in0=raw[:, j:j + nt], scalar=cw_ap[:, j:j + 1], in1=out[:, 0:nt],
                                                        op0=ALU.mult, op1=ALU.add), [rk, ok, cwkey], [ok])
            V(lambda e: e.tensor_copy(out=hist_ap, in_=raw[:, nt:nt + 3]), [rk], [histkey])
        else:
            Ac(lambda e: e.activation(out=xstash, in_=psb[pbank][:, 0:nt], func=AF.Copy), pk(pbank), [xskey])
            V(lambda e: e.tensor_scalar(out=out[:, 0:nt], in0=xstash, scalar1=cw_ap[:, 3:4], scalar2=None, op0=ALU.mult), [xskey, cwkey], [ok])
            for j in range(3):
                V(lambda e, j=j: e.scalar_tensor_tensor(out=out[:, 0:nt], in0=shist[:, j, :], scalar=cw_ap[:, j:j + 1], in1=out[:, 0:nt],
                                                        op0=ALU.mult, op1=ALU.add), [shkey, ok, cwkey], [ok])
        if bias_ap is not None:
            Ac(lambda e: e.activation(out=out[:, 0:nt], in_=out[:, 0:nt], func=AF.Silu, bias=bias_ap), [ok, cwkey], [ok])
        else:
            Ac(lambda e: e.activation(out=out[:, 0:nt], in_=out[:, 0:nt], func=AF.Silu), [ok], [ok])

    def rinv_of(srcW, nt, dstW):
        V(lambda e: e.tensor_tensor(out=Vb[5][:, 0:nt], in0=W[srcW][:, 0:nt], in1=W[srcW][:, 0:nt], op=ALU.mult), [f"W{srcW}"], ["V5"])
        T(lambda e: e.matmul(psb[6][:, 0:nt], lhsT=ones_b[:], rhs=Vb[5][:, 0:nt], start=True, stop=True), ["V5", "ones_b"], pk(6))
        Ac(lambda e: e.activation(out=W[dstW][:, 0:nt], in_=psb[6][:, 0:nt], func=AF.Ln, bias=EPS), pk(6), [f"W{dstW}"])
        Ac(lambda e: e.activation(out=W[dstW][:, 0:nt], in_=W[dstW][:, 0:nt], func=AF.Exp, scale=-0.5), [f"W{dstW}"], [f"W{dstW}"])

    def decay_setup(l, nt, CL, nch, uname, col_alog, col_dtb, nheads, dtW, with_dt, kstride):
        proj(l, uname, 5, nt)
        Ac(lambda e: e.activation(out=W[6][:, 0:nt], in_=psb[5][:, 0:nt], func=AF.Exp, bias=pv[:, l, col_dtb:col_dtb + 1]), pk(5) + ["pv"], ["W6"])
        Ac(lambda e: e.activation(out=W[6][:, 0:nt], in_=W[6][:, 0:nt], func=AF.Ln, bias=1.0), ["W6"], ["W6"])
        if with_dt:
            V(lambda e: e.tensor_copy(out=W[dtW][:, 0:nt], in_=W[6][:, 0:nt]), ["W6"], [f"W{dtW}"])
        V(lambda e: e.tensor_scalar(out=W[6][:, 0:nt], in0=W[6][:, 0:nt], scalar1=pv[:, l, col_alog:col_alog + 1], scalar2=None, op0=ALU.mult),
          ["W6", "pv"], ["W6"])
        if CL == 1:
            V(lambda e: e.tensor_copy(out=W[7][:, 0:nt], in_=W[6][:, 0:nt]), ["W6"], ["W7"])
        else:
            for c in range(nch):
                sl = slice(c * CL, (c + 1) * CL)
                V(lambda e, sl=sl: e.tensor_tensor_scan(out=W[7][:, sl], data0=W[6][:, sl], data1=W[6][:, sl], initial=0.0, op0=ALU.add, op1=ALU.bypass),
                  ["W6"], ["W7"])
        for c in range(nch):
            sl = slice(c * CL, (c + 1) * CL)
            T(lambda e, sl=sl: e.transpose(out=psb[6][0:CL, 0:128], in_=W[7][:, sl], identity=ident), ["W7", "cst"], pk(6, 0, 1))
            T(lambda e, sl=sl: e.transpose(out=psb[6][0:CL, 128:256], in_=W[dtW][:, sl], identity=ident), [f"W{dtW}", "cst"], pk(6, 1, 2))
            V(lambda e, c=c: e.tensor_copy(out=tok[0:CL, c, 0:nheads], in_=psb[6][0:CL, 0:nheads]), pk(6, 0, 1), ["tok"])
            V(lambda e, c=c: e.tensor_copy(out=tok[0:CL, c, kstride:kstride + nheads], in_=psb[6][0:CL, 128:128 + nheads]), pk(6, 1, 2), ["tok"])

    def decay_mats(l, CL, c, hd, need_dl):
        sl = slice(c * CL, (c + 1) * CL)
        V(lambda e: e.tensor_scalar(out=A_[3][:, 0:CL], in0=W[7][:, sl], scalar1=pv[:, l, PV_OH + hd:PV_OH + hd + 1], scalar2=None, op0=ALU.mult),
          ["W7", "pv"], ["a3"])
        if CL == 1:
            return
        T(lambda e: e.matmul(psb[4][0:CL, 0:CL], lhsT=ones_f[:, 0:CL], rhs=A_[3][:, 0:CL], start=True, stop=True), ["a3", "cst"], pk(4, 0, 1))
        V(lambda e: e.tensor_scalar(out=A_[0][0:CL, 0:CL], in0=psb[4][0:CL, 0:CL], scalar1=tok[0:CL, c, hd:hd + 1], scalar2=None, op0=ALU.subtract),
          pk(4, 0, 1) + ["tok"], ["a0"])
        V(lambda e: e.tensor_scalar(out=A_[1][0:CL, 0:CL], in0=A_[0][0:CL, 0:CL], scalar1=0.0, scalar2=None, op0=ALU.min), ["a0"], ["a1"])
        Ac(lambda e: e.activation(out=A_[1][0:CL, 0:CL], in_=A_[1][0:CL, 0:CL], func=AF.Exp), ["a1"], ["a1"])
        V(lambda e: e.tensor_tensor(out=A_[1][0:CL, 0:CL], in0=A_[1][0:CL, 0:CL], in1=Umat[0:CL, 0:CL], op=ALU.mult), ["a1", "cst"], ["a1"])
        if need_dl and CL > 1:
            V(lambda e: e.tensor_scalar(out=A_[2][0:CL, 0:CL], in0=A_[0][0:CL, 0:CL], scalar1=0.0, scalar2=None, op0=ALU.max), ["a0"], ["a2"])
            Ac(lambda e: e.activation(out=A_[2][0:CL, 0:CL], in_=A_[2][0:CL, 0:CL], func=AF.Exp, scale=-1.0), ["a2"], ["a2"])
            V(lambda e: e.tensor_tensor(out=A_[2][0:CL, 0:CL], in0=A_[2][0:CL, 0:CL], in1=Lsmat[0:CL, 0:CL], op=ALU.mult), ["a2", "cst"], ["a2"])
        Ac(lambda e: e.activation(out=cs1[0:CL, 0:1], in_=A_[0][0:CL, CL - 1:CL], func=AF.Exp), ["a0"], ["cs1"])

    def out_norm_T(l, CL, c, pbank, dvw, nw_ap, gateW, dst_units, dstkeys):
        sl = slice(c * CL, (c + 1) * CL)
        Ac(lambda e: e.activation(out=A_[4][0:CL, 0:dvw], in_=psb[pbank][0:CL, 0:dvw], func=AF.Square), pk(pbank, 0, 2), ["a4"])
        V(lambda e: e.tensor_reduce(out=cs1[0:CL, 4:5], in_=A_[4][0:CL, 0:dvw], axis=AX.X, op=ALU.add), ["a4"], ["cs1b"])
        Ac(lambda e: e.activation(out=cs1[0:CL, 4:5], in_=cs1[0:CL, 4:5], func=AF.Ln, scale=1.0 / dvw, bias=EPS), ["cs1b"], ["cs1b"])
        Ac(lambda e: e.activation(out=cs1[0:CL, 4:5], in_=cs1[0:CL, 4:5], func=AF.Exp, scale=-0.5), ["cs1b"], ["cs1b"])
        V(lambda e: e.scalar_tensor_tensor(out=A_[4][0:CL, 0:dvw], in0=psb[pbank][0:CL, 0:dvw], scalar=cs1[0:CL, 4:5], in1=nw_ap[0:CL, 0:dvw],
                                           op0=ALU.mult, op1=ALU.mult), pk(pbank, 0, 2) + ["cs1b", "nrm"], ["a4"])
        for i, (du, dk_) in enumerate(zip(dst_units, dstkeys)):
            T(lambda e, i=i: e.transpose(out=psb[6][:, 256 + i * 128:256 + i * 128 + CL], in_=A_[4][0:CL, i * 128:(i + 1) * 128], identity=ident[0:CL, 0:CL]),
              ["a4", "cst"], pk(6, 2 + i, 3 + i))
            V(lambda e, i=i, du=du: e.tensor_tensor(out=scr[:, du, sl], in0=psb[6][:, 256 + i * 128:256 + i * 128 + CL], in1=W[gateW[i]][:, sl], op=ALU.mult),
              pk(6, 2 + i, 3 + i) + [f"W{gateW[i]}"], [dk_])

    def mixer(l, nt, samp):
        CL = 1 if samp else 128
        nch = nt // CL
        lname = f"mod{l}"
        sh = mod[l][:, 3 * KC:4 * KC, :]
        A = mod[l][:, 4 * KC:5 * KC, :]
        Gt = mod[l][:, 5 * KC:6 * KC, :]
        norm_mod(nt, samp, A, sh, lname)
        P.dma("sp", lambda e: e.dma_start(out=nrm1[:], in_=nrm_d[l]), "nrm", writes=["nrm"])
        if samp:
            P.dma("sp", lambda e: e.dma_start(out=shA[:], in_=s_cva_d[l]), "shA", writes=["shA"] + PKEYS)
            P.dma("sp", lambda e: e.dma_start(out=shC[:], in_=s_cvc_d[l]), "shC", writes=["shC"] + PKEYS)
        OB = 16

        def load_state(src_ap, width, buf):
            P.dma("sp", lambda e: e.dma_start(out=SS[:, buf, 0:width], in_=src_ap), f"SS{buf}", writes=[f"SS{buf}"])

        def store_state(dst_ap, src_sb, key, skey):
            P.dma("sp", lambda e: e.dma_start(out=dst_ap, in_=src_sb), skey, reads=[key], writes=["odram"])

        CL_all, nch_all = CL, nch
        CL = 1 if samp else 64
        nch = nt // CL
        proj(l, "beta", 4, nt)
        Ac(lambda e: e.activation(out=W[8][:, 0:nt], in_=psb[4][:, 0:nt], func=AF.Sigmoid), pk(4), ["W8"])
        decay_setup(l, nt, CL, nch, "dec", 0, 1, HA, 8, False, 8)
        for c in range(nch):
            V(lambda e, c=c: e.tensor_scalar(out=tok[0:CL, c, 16:16 + HA], in0=tok[0:CL, c, 8:8 + HA], scalar1=-1.0, scalar2=None, op0=ALU.mult), ["tok"], ["tok"])
            Ac(lambda e, c=c: e.activation(out=tok[0:CL, c, 24:24 + HA], in_=tok[0:CL, c, 0:HA], func=AF.Exp), ["tok"], ["tok"])
            V(lambda e, c=c: e.tensor_tensor(out=tok[0:CL, c, 24:24 + HA], in0=tok[0:CL, c, 24:24 + HA], in1=tok[0:CL, c, 8:8 + HA], op=ALU.mult), ["tok"], ["tok"])

        for hd in range(HA):
            for i, (nm, ow) in enumerate((("aq", 1), ("ak", 2), ("av", 3))):
                proj(l, f"{nm}{hd}", i % 2, nt)
                u = i * HA + hd
                conv_unit(l, i % 2, nt, samp, cwa[:, l, u, :], "cwa", None, hista[:, l, u, :], f"hista{l}", shA[:, u, :, :], xsA[:, u, :], "shA", "xsA", 0, ow)
            proj(l, f"az{hd}", 3, nt)
            Ac(lambda e: e.activation(out=W[4][:, 0:nt], in_=psb[3][:, 0:nt], func=AF.Silu), pk(3), ["W4"])
            rinv_of(1, nt, 5)
            V(lambda e: e.scalar_tensor_tensor(out=Vb[0][:, 0:nt], in0=W[1][:, 0:nt], scalar=128 ** -0.5, in1=W[5][:, 0:nt], op0=ALU.mult, op1=ALU.mult),
              ["W1", "W5"], ["V0"])
            rinv_of(2, nt, 5)
            V(lambda e: e.tensor_tensor(out=W[2][:, 0:nt], in0=W[2][:, 0:nt], in1=W[5][:, 0:nt], op=ALU.mult), ["W2", "W5"], ["W2"])
            V(lambda e: e.tensor_copy(out=Vb[1][:, 0:nt], in_=W[2][:, 0:nt]), ["W2"], ["V1"])
            for c in range(nch):
                sl = slice(c * CL, (c + 1) * CL)
                tk = "tok"
                if samp:
                    buf = c % 2
                    load_state(s_gdn_d[l, c, hd], 128, buf)
                    S = SS[:, buf, 0:128]
                    Sk = f"SS{buf}"
                else:
                    S = Sa[:, l, hd, :]
                    Sk = f"Sa{l}_{hd}"
                    if c == 0 and state.get("ptile", 0) == 0:
                        V(lambda e, S=S: e.memset(S, 0.0), [], [Sk])
                V(lambda e, S=S: e.tensor_copy(out=B_[0][:, 0:128], in_=S), [Sk], ["b0"])
                decay_mats(l, CL, c, hd, True)
                T(lambda e, sl=sl: e.transpose(out=psb[5][0:CL, 0:128], in_=W[3][:, sl], identity=ident), ["W3", "cst"], pk(5, 0, 1))
                Ac(lambda e: e.activation(out=A_[12][0:CL, 0:128], in_=psb[5][0:CL, 0:128], func=AF.Copy), pk(5, 0, 1), ["a12"])
                T(lambda e, sl=sl: e.transpose(out=psb[5][0:CL, 128:256], in_=W[2][:, sl], identity=ident), ["W2", "cst"], pk(5, 1, 2))
                Ac(lambda e: e.activation(out=A_[11][0:CL, 0:128], in_=psb[5][0:CL, 128:256], func=AF.Copy), pk(5, 1, 2), ["a11"])
                if CL == 1:
                    V(lambda e: e.tensor_copy(out=B_[3][0:CL, 0:128], in_=psb[5][0:CL, 128:256]), pk(5, 1, 2), ["b3"])
                else:
                    V(lambda e: e.tensor_scalar(out=B_[3][0:CL, 0:128], in0=psb[5][0:CL, 128:256], scalar1=cs1[0:CL, 0:1], scalar2=None, op0=ALU.mult),
                      pk(5, 1, 2) + ["cs1"], ["b3"])
                if CL > 1:
                    q = slice(0, CL)
                    T(lambda e, sl=sl: e.matmul(psb[4][q, 128:128 + CL], lhsT=Vb[1][:, sl], rhs=Vb[1][:, sl], start=True, stop=True), ["V1"], pk(4, 1, 2))
                    V(lambda e, c=c: e.scalar_tensor_tensor(out=A_[5][q, q], in0=psb[4][q, 128:128 + CL], scalar=tok[q, c, 16 + hd:17 + hd], in1=A_[2][q, q],
                                                            op0=ALU.mult, op1=ALU.mult), pk(4, 1, 2) + [tk, "a2"], ["a5"])
                    T(lambda e: e.transpose(out=psb[4][q, 256:256 + CL], in_=A_[5][q, q], identity=ident[q, q]), ["a5", "cst"], pk(4, 2, 3))
                    Ac(lambda e: e.activation(out=A_[6][q, q], in_=psb[4][q, 256:256 + CL], func=AF.Copy), pk(4, 2, 3), ["a6"])
                    V(lambda e: e.tensor_tensor(out=A_[9][q, q], in0=psb[4][q, 256:256 + CL], in1=ident[q, q], op=ALU.add), pk(4, 2, 3) + ["cst"], ["a9"])
                    Nc, Mc, Nn, Mn = 5, 6, 7, 8
                    nlev = CL.bit_length() - 2
                    for lev in range(1, nlev + 1):
                        T(lambda e, Mc=Mc, Nc=Nc: e.matmul(psb[4][q, 128:128 + CL], lhsT=A_[Mc][q, q], rhs=A_[Nc][q, q], start=True, stop=True),
                          [f"a{Mc}", f"a{Nc}"], pk(4, 1, 2))
                        if lev < nlev:
                            T(lambda e, Mc=Mc, Nc=Nc: e.matmul(psb[4][q, 256:256 + CL], lhsT=A_[Nc][q, q], rhs=A_[Mc][q, q], start=True, stop=True),
                              [f"a{Mc}", f"a{Nc}"], pk(4, 2, 3))
                        Ac(lambda e, Nn=Nn: e.activation(out=A_[Nn][q, q], in_=psb[4][q, 128:128 + CL], func=AF.Copy), pk(4, 1, 2), [f"a{Nn}"])
                        if lev < nlev:
                            V(lambda e, Mn=Mn: e.tensor_copy(out=A_[Mn][q, q], in_=psb[4][q, 256:256 + CL]), pk(4, 2, 3), [f"a{Mn}"])
                        T(lambda e, Nn=Nn: e.matmul(psb[4][q, 384:384 + CL], lhsT=A_[Nn][q, q], rhs=A_[9][q, q], start=True, stop=True),
                          [f"a{Nn}", "a9"], pk(4, 3, 4))
                        V(lambda e: e.tensor_tensor(out=A_[9][q, q], in0=A_[9][q, q], in1=psb[4][q, 384:384 + CL], op=ALU.add), pk(4, 3, 4) + ["a9"], ["a9"])
                        Nc, Mc, Nn, Mn = Nn, Mn, Nc, Mc
                    Rap = A_[9]
                    Rk = "a9"
                else:
                    Rap = cst
                    Rk = "cst"
                V(lambda e, c=c, Rap=Rap: e.tensor_scalar(out=A_[5][0:CL, 0:CL], in0=Rap[0:CL, 0:CL], scalar1=tok[0:CL, c, 8 + hd:9 + hd], scalar2=None, op0=ALU.mult),
                  [Rk, tk], ["a5"])
                V(lambda e, c=c, Rap=Rap: e.tensor_scalar(out=A_[6][0:CL, 0:CL], in0=Rap[0:CL, 0:CL], scalar1=tok[0:CL, c, 24 + hd:25 + hd], scalar2=None, op0=ALU.mult),
                  [Rk, tk], ["a6"])
                T(lambda e: e.matmul(psb[5][:, 256:256 + CL], lhsT=A_[11][0:CL, 0:128], rhs=A_[6][0:CL, 0:CL], start=True, stop=True), ["a11", "a6"], pk(5, 2, 3))
                V(lambda e: e.tensor_scalar(out=A_[7][:, 0:CL], in0=psb[5][:, 256:256 + CL], scalar1=-1.0, scalar2=None, op0=ALU.mult), pk(5, 2, 3), ["a7"])
                T(lambda e: e.matmul(psb[5][0:CL, 384:512], lhsT=A_[5][0:CL, 0:CL], rhs=A_[12][0:CL, 0:128], start=True, stop=False), ["a5", "a12"], pk(5, 3, 4))
                T(lambda e, S=S: e.matmul(psb[5][0:CL, 384:512], lhsT=A_[7][:, 0:CL], rhs=S, start=False, stop=True), ["a7", Sk], pk(5, 3, 4))
                V(lambda e: e.tensor_copy(out=B_[7][0:CL, 0:128], in_=psb[5][0:CL, 384:512]), pk(5, 3, 4), ["b7"])
                T(lambda e, sl=sl: e.matmul(psb[6][0:CL, 0:CL], lhsT=Vb[1][:, sl], rhs=Vb[0][:, sl], start=True, stop=True), ["V1", "V0"], pk(6, 0, 1))
                if CL == 1:
                    V(lambda e: e.tensor_copy(out=B_[8][0:CL, 0:CL], in_=psb[6][0:CL, 0:CL]), pk(6, 0, 1), ["b8"])
                else:
                    V(lambda e: e.tensor_tensor(out=B_[8][0:CL, 0:CL], in0=psb[6][0:CL, 0:CL], in1=A_[1][0:CL, 0:CL], op=ALU.mult), pk(6, 0, 1) + ["a1"], ["b8"])
                T(lambda e: e.matmul(psb[6][:, 128:128 + CL], lhsT=ones_f, rhs=A_[3][:, 0:CL], start=True, stop=True), ["a3", "cst"], pk(6, 1, 2))
                Ac(lambda e: e.activation(out=A_[10][:, 0:CL], in_=psb[6][:, 128:128 + CL], func=AF.Exp), pk(6, 1, 2), ["a10"])
                V(lambda e, sl=sl: e.tensor_tensor(out=B_[9][:, 0:CL], in0=Vb[0][:, sl], in1=A_[10][:, 0:CL], op=ALU.mult), ["V0", "a10"], ["b9"])
                T(lambda e: e.matmul(psb[7][0:CL, 0:128], lhsT=B_[9][:, 0:CL], rhs=B_[0][:, 0:128], start=True, stop=False), ["b9", "b0"], pk(7, 0, 1))
                T(lambda e: e.matmul(psb[7][0:CL, 0:128], lhsT=B_[8][0:CL, 0:CL], rhs=B_[7][0:CL, 0:128], start=False, stop=True), ["b8", "b7"], pk(7, 0, 1))
                T(lambda e: e.matmul(psb[7][:, 256:384], lhsT=B_[3][0:CL, 0:128], rhs=B_[7][0:CL, 0:128], start=True, stop=True), ["b3", "b7"], pk(7, 2, 3))
                V(lambda e, S=S: e.scalar_tensor_tensor(out=S, in0=S, scalar=A_[10][:, CL - 1:CL], in1=psb[7][:, 256:384], op0=ALU.mult, op1=ALU.add),
                  [Sk, "a10"] + pk(7, 2, 3), [Sk])
                if samp:
                    store_state(so_gdn_d[l, c, hd], S, Sk, f"sst{buf}")
                out_norm_T(l, CL, c, 7, 128, nrm1[:, 0:128], [4], [OB + hd], [f"scr{OB + hd}"])
            if not samp and state.get("ptile", 0) == NPT - 1:
                store_state(o_gdn_d[l, hd], Sa[:, l, hd, :], f"Sa{l}_{hd}", "ost")
        if samp:
            P.dma("sp", lambda e: e.dma_start(out=so_cva_d[l][:, :, 0:2, :], in_=shA[:, :, 1:3, :]), "shst", reads=["shA"], writes=["odram"])
            P.dma("sp", lambda e: e.dma_start(out=so_cva_d[l][:, :, 2, :], in_=xsA[:]), "shst", reads=["xsA"], writes=["odram"])
        elif state.get("ptile", 0) == NPT - 1:
            P.dma("sp", lambda e: e.dma_start(out=o_cva_d[l], in_=hista[:, l, :, :]), "ost", reads=[f"hista{l}"], writes=["odram"])
        merge_branch(l, nt, 0, wba_d[l], HA, OB, first=True)
        CL, nch = CL_all, nch_all

        proj(l, "lr", 4, nt)
        Ac(lambda e: e.activation(out=Vb[2][0:16, 0:nt], in_=psb[4][0:16, 0:nt], func=AF.Copy), pk(4), ["V2"])
        for hd in range(HB):
            T(lambda e, hd=hd: e.matmul(psb[4][:, 0:nt], lhsT=wgate[:, l, hd * 128:(hd + 1) * 128], rhs=Vb[2][0:16, 0:nt], start=True, stop=True),
              ["wgate", "V2"], pk(4))
            Ac(lambda e, hd=hd: e.activation(out=W[5][:, 0:nt], in_=psb[4][:, 0:nt], func=AF.Exp, scale=-1.0, bias=pv[:, l, PV_BG + hd:PV_BG + hd + 1]),
               pk(4) + ["pv"], ["W5"])
            Ac(lambda e: e.activation(out=W[5][:, 0:nt], in_=W[5][:, 0:nt], func=AF.Ln, bias=1.0), ["W5"], ["W5"])
            if CL > 1:
                for c in range(nch):
                    sl = slice(c * CL, (c + 1) * CL)
                    V(lambda e, sl=sl: e.tensor_tensor_scan(out=W[6][:, sl], data0=W[5][:, sl], data1=W[5][:, sl], initial=0.0, op0=ALU.add, op1=ALU.bypass),
                      ["W5"], ["W6"])
            else:
                V(lambda e: e.tensor_copy(out=W[6][:, 0:nt], in_=W[5][:, 0:nt]), ["W5"], ["W6"])
            Ac(lambda e: e.activation(out=W[7][:, 0:nt], in_=W[6][:, 0:nt], func=AF.Exp, scale=-1.0 / 16), ["W6"], ["W7"])
            Ac(lambda e: e.activation(out=W[8][:, 0:nt], in_=W[6][:, 0:nt], func=AF.Exp, scale=1.0 / 16), ["W6"], ["W8"])
            proj(l, f"bq{hd}", 0, nt)
            V(lambda e: e.scalar_tensor_tensor(out=Vb[0][:, 0:nt], in0=psb[0][:, 0:nt], scalar=128 ** -0.5, in1=W[7][:, 0:nt], op0=ALU.mult, op1=ALU.mult),
              pk(0) + ["W7"], ["V0"])
            proj(l, f"bk{hd}", 1, nt)
            Ac(lambda e: e.activation(out=W[1][:, 0:nt], in_=psb[1][:, 0:nt], func=AF.Copy), pk(1), ["W1"])
            V(lambda e: e.tensor_tensor(out=Vb[1][:, 0:nt], in0=W[1][:, 0:nt], in1=W[8][:, 0:nt], op=ALU.mult), ["W1", "W8"], ["V1"])
            for c in range(nch):
                sl = slice(c * CL, (c + 1) * CL)
                V(lambda e, c=c: e.tensor_scalar(out=cs1[:, 2:3], in0=W[6][:, (c + 1) * CL - 1:(c + 1) * CL], scalar1=-1.0 / 16, scalar2=None, op0=ALU.mult),
                  ["W6"], ["cs1c"])
                Ac(lambda e, sl=sl: e.activation(out=W[2][:, sl], in_=W[6][:, sl], func=AF.Exp, scale=1.0 / 16, bias=cs1[:, 2:3]), ["W6", "cs1c"], ["W2"])
            V(lambda e: e.tensor_tensor(out=W[2][:, 0:nt], in0=W[2][:, 0:nt], in1=W[1][:, 0:nt], op=ALU.mult), ["W2", "W1"], ["W2"])
            proj(l, f"bv{hd}_0", 2, nt)
            Ac(lambda e: e.activation(out=W[3][:, 0:nt], in_=psb[2][:, 0:nt], func=AF.Copy), pk(2), ["W3"])
            proj(l, f"bv{hd}_1", 3, nt)
            Ac(lambda e: e.activation(out=W[4][:, 0:nt], in_=psb[3][:, 0:nt], func=AF.Copy), pk(3), ["W4"])
            proj(l, f"br{hd}_0", 0, nt)
            Ac(lambda e: e.activation(out=W[1][:, 0:nt], in_=psb[0][:, 0:nt], func=AF.Silu), pk(0), ["W1"])
            proj(l, f"br{hd}_1", 1, nt)
            Ac(lambda e: e.activation(out=W[5][:, 0:nt], in_=psb[1][:, 0:nt], func=AF.Silu), pk(1), ["W5"])
            for c in range(nch):
                sl = slice(c * CL, (c + 1) * CL)
                if samp:
                    buf = c % 2
                    load_state(s_gla_d[l, c, hd], 256, buf)
                    S = SS[:, buf, 0:256]
                    Sk = f"SS{buf}"
                else:
                    S = Sb_[:, l, hd, :]
                    Sk = f"Sb{l}_{hd}"
                    if c == 0 and state.get("ptile", 0) == 0:
                        V(lambda e, S=S: e.memset(S, 0.0), [], [Sk])
                V(lambda e, S=S: e.tensor_copy(out=B_[0][:, 0:256], in_=S), [Sk], ["b0"])
                T(lambda e, sl=sl: e.transpose(out=psb[5][0:CL, 0:128], in_=W[3][:, sl], identity=ident), ["W3", "cst"], pk(5, 0, 1))
                T(lambda e, sl=sl: e.transpose(out=psb[5][0:CL, 128:256], in_=W[4][:, sl], identity=ident), ["W4", "cst"], pk(5, 1, 2))
                Ac(lambda e: e.activation(out=B_[1][0:CL, 0:256], in_=psb[5][0:CL, 0:256], func=AF.Copy), pk(5, 0, 2), ["b1"])
                T(lambda e, sl=sl: e.transpose(out=psb[5][0:CL, 256:384], in_=W[2][:, sl], identity=ident), ["W2", "cst"], pk(5, 2, 3))
                Ac(lambda e: e.activation(out=B_[3][0:CL, 0:128], in_=psb[5][0:CL, 256:384], func=AF.Copy), pk(5, 2, 3), ["b3"])
                T(lambda e, sl=sl: e.matmul(psb[6][0:CL, 0:CL], lhsT=Vb[1][:, sl], rhs=Vb[0][:, sl], start=True, stop=True), ["V1", "V0"], pk(6, 0, 1))
                V(lambda e: e.tensor_tensor(out=B_[8][0:CL, 0:CL], in0=psb[6][0:CL, 0:CL], in1=Umat[0:CL, 0:CL], op=ALU.mult), pk(6, 0, 1) + ["cst"], ["b8"])
                T(lambda e, sl=sl: e.matmul(psb[7][0:CL, 0:256], lhsT=Vb[0][:, sl], rhs=B_[0][:, 0:256], start=True, stop=False), ["V0", "b0"], pk(7, 0, 2))
                T(lambda e: e.matmul(psb[7][0:CL, 0:256], lhsT=B_[8][0:CL, 0:CL], rhs=B_[1][0:CL, 0:256], start=False, stop=True), ["b8", "b1"], pk(7, 0, 2))
                T(lambda e: e.matmul(psb[7][:, 256:512], lhsT=B_[3][0:CL, 0:128], rhs=B_[1][0:CL, 0:256], start=True, stop=True), ["b3", "b1"], pk(7, 2, 4))
                V(lambda e, S=S, c=c: e.scalar_tensor_tensor(out=S, in0=S, scalar=W[7][:, (c + 1) * CL - 1:(c + 1) * CL], in1=psb[7][:, 256:512],
                                                             op0=ALU.mult, op1=ALU.add), [Sk, "W7"] + pk(7, 2, 4), [Sk])
                if samp:
                    store_state(so_gla_d[l, c, hd], S, Sk, f"sst{buf}")
                out_norm_T(l, CL, c, 7, 256, nrm1[:, 128:384], [1, 5], [OB + 2 * hd, OB + 2 * hd + 1], [f"scr{OB + 2 * hd}", f"scr{OB + 2 * hd + 1}"])
            if not samp and state.get("ptile", 0) == NPT - 1:
                store_state(o_gla_d[l, hd], Sb_[:, l, hd, :], f"Sb{l}_{hd}", "ost")
        merge_branch(l, nt, 1, wbb_d[l], 2 * HB, OB, first=False)

        decay_setup(l, nt, CL, nch, "dt", 2, 3, HC, 8, True, 32)

        V(lambda e: e.memset(B_[10][:, 0:256], 0.0), [], ["b10"])
        V(lambda e: e.memset(B_[11][:, 0:256], 0.0), [], ["b11"])
        V(lambda e: e.memset(B_[12][:, 0:256], 0.0), [], ["b12"])
        for g in range(G):
            for (nm, cu, dstV, keepW) in ((f"cB{g}", NU_C + g, 3, None), (f"cC{g}", NU_C + G + g, 4, 5)):
                proj(l, nm, 0, nt)
                conv_unit(l, 0, nt, samp, cwc[:, l, cu, 0:4], "cwc", cwc[:, l, cu, 4:5], histc[:, l, cu, :], f"histc{l}", shC[:, cu, :, :], xsC[:, cu, :],
                          "shC", "xsC", 0, 1)
                V(lambda e, dstV=dstV: e.tensor_copy(out=Vb[dstV][:, 0:nt], in_=W[1][:, 0:nt]), ["W1"], [f"V{dstV}"])
                if keepW is not None:
                    V(lambda e: e.tensor_copy(out=W[5][:, 0:nt], in_=W[1][:, 0:nt]), ["W1"], ["W5"])
                else:
                    V(lambda e: e.tensor_copy(out=W[6][:, 0:nt], in_=W[1][:, 0:nt]), ["W1"], ["W6"])
            for uu in range(UPN):
                u = g * UPN + uu
                proj(l, f"cx{u}", 1, nt)
                conv_unit(l, 1, nt, samp, cwc[:, l, u, 0:4], "cwc", cwc[:, l, u, 4:5], histc[:, l, u, :], f"histc{l}", shC[:, u, :, :], xsC[:, u, :],
                          "shC", "xsC", 0, 2)
                proj(l, f"cz{u}", 2, nt)
                Ac(lambda e: e.activation(out=W[3][:, 0:nt], in_=psb[2][:, 0:nt], func=AF.Silu), pk(2), ["W3"])
                for c in range(nch):
                    sl = slice(c * CL, (c + 1) * CL)
                    tk = "tok"
                    if samp:
                        buf = c % 2
                        load_state(s_ssd_d[l, c, u], 128, buf)
                        S = SS[:, buf, 0:128]
                        Sk = f"SS{buf}"
                    else:
                        S = Sc[:, l, u, :]
                        Sk = f"Sc{l}_{u}"
                        if c == 0 and state.get("ptile", 0) == 0:
                            V(lambda e, S=S: e.memset(S, 0.0), [], [Sk])
                    T(lambda e, sl=sl: e.matmul(psb[5][0:CL, 0:CL], lhsT=Vb[3][:, sl], rhs=Vb[4][:, sl], start=True, stop=True), ["V3", "V4"], pk(5, 0, 1))
                    V(lambda e: e.tensor_copy(out=A_[11][0:CL, 0:CL], in_=psb[5][0:CL, 0:CL]), pk(5, 0, 1), ["a11"])
                    T(lambda e, sl=sl: e.transpose(out=psb[5][0:CL, 128:256], in_=W[6][:, sl], identity=ident), ["W6", "cst"], pk(5, 1, 2))
                    Ac(lambda e: e.activation(out=B_[2][0:CL, 0:128], in_=psb[5][0:CL, 128:256], func=AF.Copy), pk(5, 1, 2), ["b2"])
                    T(lambda e, sl=sl: e.transpose(out=psb[5][0:CL, 256:384], in_=W[2][:, sl], identity=ident), ["W2", "cst"], pk(5, 2, 3))
                    V(lambda e: e.tensor_copy(out=A_[12][0:CL, 0:128], in_=psb[5][0:CL, 256:384]), pk(5, 2, 3), ["a12"])
                    for hh in range(2):
                        hd = 2 * u + hh
                        hs = slice(hh * 64, hh * 64 + 64)
                        po = hh * 128
                        decay_mats(l, CL, c, hd, False)
                        if CL == 1:
                            V(lambda e: e.tensor_copy(out=B_[8][0:CL, 0:CL], in_=A_[11][0:CL, 0:CL]), ["a11"], ["b8"])
                        else:
                            V(lambda e: e.tensor_tensor(out=B_[8][0:CL, 0:CL], in0=A_[11][0:CL, 0:CL], in1=A_[1][0:CL, 0:CL], op=ALU.mult), ["a11", "a1"], ["b8"])
                        T(lambda e: e.matmul(psb[6][:, 128:128 + CL], lhsT=ones_f, rhs=A_[3][:, 0:CL], start=True, stop=True), ["a3", "cst"], pk(6, 1, 2))
                        Ac(lambda e: e.activation(out=A_[10][:, 0:CL], in_=psb[6][:, 128:128 + CL], func=AF.Exp), pk(6, 1, 2), ["a10"])
                        V(lambda e, sl=sl: e.tensor_tensor(out=B_[9][:, 0:CL], in0=W[5][:, sl], in1=A_[10][:, 0:CL], op=ALU.mult), ["W5", "a10"], ["b9"])
                        V(lambda e, hs=hs, po=po, c=c, hd=hd: e.tensor_scalar(out=B_[10][0:CL, po + hh_off(hs):po + hh_off(hs) + 64], in0=A_[12][0:CL, hs],
                                                                              scalar1=tok[0:CL, c, 32 + hd:33 + hd], scalar2=None, op0=ALU.mult),
                          ["a12", tk], ["b10"])
                        if CL == 1:
                            V(lambda e, hs=hs, po=po: e.tensor_copy(out=B_[12][0:CL, po + hh_off(hs):po + hh_off(hs) + 64],
                                                                    in_=B_[10][0:CL, po + hh_off(hs):po + hh_off(hs) + 64]), ["b10"], ["b12"])
                        else:
                            V(lambda e, hs=hs, po=po: e.tensor_scalar(out=B_[12][0:CL, po + hh_off(hs):po + hh_off(hs) + 64],
                                                                      in0=B_[10][0:CL, po + hh_off(hs):po + hh_off(hs) + 64],
                                                                      scalar1=cs1[0:CL, 0:1], scalar2=None, op0=ALU.mult), ["b10", "cs1"], ["b12"])
                        V(lambda e, hs=hs, po=po, S=S: e.tensor_copy(out=B_[11][:, po + hh_off(hs):po + hh_off(hs) + 64], in_=S[:, hs]), [Sk], ["b11"])
                        T(lambda e, po=po, hh=hh: e.matmul(psb[7][:, 0:CL], lhsT=B_[10][0:CL, po:po + 128], rhs=B_[8][0:CL, 0:CL], start=(hh == 0), stop=False),
                          ["b10", "b8"], pk(7, 0, 1))
                        T(lambda e, po=po, hh=hh: e.matmul(psb[7][:, 0:CL], lhsT=B_[11][:, po:po + 128], rhs=B_[9][:, 0:CL], start=False, stop=(hh == 1)),
                          ["b11", "b9"], pk(7, 0, 1))
                        T(lambda e, po=po, hh=hh: e.matmul(psb[2][:, 0:128], lhsT=B_[2][0:CL, 0:128], rhs=B_[12][0:CL, po:po + 128], start=(hh == 0), stop=(hh == 1)),
                          ["b2", "b12"], pk(2))
                        V(lambda e, hh=hh: e.tensor_copy(out=cs1[:, 5 + hh:6 + hh], in_=A_[10][:, CL - 1:CL]), ["a10"], ["cs1d"])
                    for hh in range(2):
                        hs = slice(hh * 64, hh * 64 + 64)
                        V(lambda e, hs=hs, hh=hh, S=S: e.scalar_tensor_tensor(out=S[:, hs], in0=S[:, hs], scalar=cs1[:, 5 + hh:6 + hh], in1=psb[2][:, hh * 64:hh * 64 + 64],
                                                                              op0=ALU.mult, op1=ALU.add), [Sk, "cs1d"] + pk(2), [Sk])
                    if samp:
                        store_state(so_ssd_d[l, c, u], S, Sk, f"sst{buf}")
                    V(lambda e, sl=sl, u=u: e.scalar_tensor_tensor(out=W[4][:, sl], in0=W[2][:, sl], scalar=pv[:, l, PV_D + u:PV_D + u + 1], in1=psb[7][:, 0:CL],
                                                                    op0=ALU.mult, op1=ALU.add), ["W2", "pv"] + pk(7, 0, 1), ["W4"])
                    V(lambda e, sl=sl: e.tensor_tensor(out=W[4][:, sl], in0=W[4][:, sl], in1=W[3][:, sl], op=ALU.mult), ["W4", "W3"], ["W4"])
                if not samp and state.get("ptile", 0) == NPT - 1:
                    store_state(o_ssd_d[l, u], Sc[:, l, u, :], f"Sc{l}_{u}", "ost")
                V(lambda e, u=u: e.tensor_copy(out=scr[:, OB + u, 0:nt], in_=W[4][:, 0:nt]), ["W4"], [f"scr{OB + u}"])
                V(lambda e, uu=uu: e.tensor_tensor(out=Vb[5][:, 0:nt], in0=W[4][:, 0:nt], in1=W[4][:, 0:nt], op=ALU.mult), ["W4"], ["V5"])
                T(lambda e, uu=uu: e.matmul(psb[3][:, 0:nt], lhsT=ones_b[:], rhs=Vb[5][:, 0:nt], start=(uu == 0), stop=(uu == UPN - 1)), ["V5", "ones_b"], pk(3))
            Ac(lambda e: e.activation(out=W[1][:, 0:nt], in_=psb[3][:, 0:nt], func=AF.Ln, scale=1.0 / (UPN * 128), bias=EPS), pk(3), ["W1"])
            Ac(lambda e: e.activation(out=W[1][:, 0:nt], in_=W[1][:, 0:nt], func=AF.Exp, scale=-0.5), ["W1"], ["W1"])
            for uu in range(UPN):
                u = g * UPN + uu
                V(lambda e, uu=uu, u=u: e.scalar_tensor_tensor(out=scr[:, OB + u, 0:nt], in0=scr[:, OB + u, 0:nt], scalar=pv[:, l, PV_NW + u:PV_NW + u + 1], in1=W[1][:, 0:nt],
                                                                op0=ALU.mult, op1=ALU.mult), [f"scr{OB + u}", "pv", "W1"], [f"scr{OB + u}"])
        if samp:
            P.dma("sp", lambda e: e.dma_start(out=so_cvc_d[l][:, :, 0:2, :], in_=shC[:, :, 1:3, :]), "shst", reads=["shC"], writes=["odram"])
            P.dma("sp", lambda e: e.dma_start(out=so_cvc_d[l][:, :, 2, :], in_=xsC[:]), "shst", reads=["xsC"], writes=["odram"])
        elif state.get("ptile", 0) == NPT - 1:
            P.dma("sp", lambda e: e.dma_start(out=o_cvc_d[l], in_=histc[:, l, :, :]), "ost", reads=[f"histc{l}"], writes=["odram"])
        merge_branch(l, nt, 2, wbc_d[l], NU_C, OB, first=False)

        for dc in range(KC):
            s = ring_load(wo_d[l][dc])
            b = dc % 2
            for kc in range(KC):
                T(lambda e, s=s, kc=kc, b=b: e.matmul(psb[b][:, 0:nt], lhsT=ring[:, s, kc, :], rhs=scr[:, kc, 0:nt], start=(kc == 0), stop=(kc == KC - 1)),
                  [f"ring{s}", f"scr{kc}"], pk(b))
            resid_add(nt, samp, dc, b, Gt, lname)

    def hh_off(hs):
        return hs.start

    def merge_branch(l, nt, br, w_d, nk, OB, first):
        for dc in range(KC):
            b = dc % 2
            proj(l, f"gate{br}_{dc}", 2 + b, nt)
            t = nexttmp()
            Ac(lambda e, b=b, t=t: e.activation(out=tmp[:, t, 0:nt], in_=psb[2 + b][:, 0:nt], func=AF.Sigmoid), pk(2 + b), [f"tmp{t}"])
            s = ring_load(w_d[dc], nk)
            for kc in range(nk):
                T(lambda e, s=s, kc=kc, b=b: e.matmul(psb[b][:, 0:nt], lhsT=ring[:, s, kc, :], rhs=scr[:, OB + kc, 0:nt], start=(kc == 0), stop=(kc == nk - 1)),
                  [f"ring{s}", f"scr{OB + kc}"], pk(b))
            if first:
                V(lambda e, b=b, t=t, dc=dc: e.tensor_tensor(out=scr[:, dc, 0:nt], in0=tmp[:, t, 0:nt], in1=psb[b][:, 0:nt], op=ALU.mult),
                  [f"tmp{t}"] + pk(b), [f"scr{dc}"])
            else:
                V(lambda e, b=b, t=t: e.tensor_tensor(out=tmp[:, t, 0:nt], in0=tmp[:, t, 0:nt], in1=psb[b][:, 0:nt], op=ALU.mult),
                  [f"tmp{t}"] + pk(b), [f"tmp{t}"])
                V(lambda e, t=t, dc=dc: e.tensor_tensor(out=scr[:, dc, 0:nt], in0=scr[:, dc, 0:nt], in1=tmp[:, t, 0:nt], op=ALU.add),
                  [f"tmp{t}", f"scr{dc}"], [f"scr{dc}"])


    tiles = [("p", i) for i in range(NPT)] + [("s", 0)]
    xkeys = [f"x{kc}" for kc in range(KC)]
    V(lambda e: e.memset(hista[:], 0.0), [], ["hista0", "hista1"])
    V(lambda e: e.memset(histc[:], 0.0), [], ["histc0", "histc1"])
    for (kind, ti) in tiles:
        samp = kind == "s"
        nt = NS if samp else NT
        state["ptile"] = ti
        src = xs_d if samp else xp_d[:, :, ti * NT:(ti + 1) * NT]
        P.dma("sp", lambda e, src=src, nt=nt: e.dma_start(out=x[:, :, 0:nt], in_=src), "xload", writes=xkeys)
        for l in range(2):
            ffn(l, 1, nt, samp)
            mixer(l, nt, samp)
            ffn(l, 2, nt, samp)
        dst = ys_d if samp else yp_d[:, :, ti * NT:(ti + 1) * NT]
        rms_stats(nt)
        fn_w = norms[:, 6, :]
        for kc in range(KC):
            V(lambda e, kc=kc, nt=nt: e.scalar_tensor_tensor(out=x[:, kc, 0:nt], in0=x[:, kc, 0:nt], scalar=fn_w[:, kc:kc + 1],
                                                            in1=rstd[:, 0:nt], op0=ALU.mult, op1=ALU.mult),
              [f"x{kc}", "rstd", "norms"], [f"x{kc}"])
        P.dma("sp", lambda e, dst=dst, nt=nt: e.dma_start(out=dst, in_=x[:, :, 0:nt]), "ystore", reads=xkeys, writes=["ydram"])

    P.emit(nc, es)
    es.close()
    return nc


def _units(w):
    K, N = w.shape
    return np.ascontiguousarray(w.reshape(K // 128, 128, N // 128, 128).transpose(2, 1, 0, 3))


def _fm(v):
    n, Fd = v.shape
    return np.ascontiguousarray(v.reshape(n, Fd // 128, 128).transpose(2, 1, 0))


def _consts():
    c = np.zeros((128, 512), np.float32)
    c[:, 0:128] = np.eye(128)
    c[:, 128:256] = np.triu(np.ones((128, 128)))
    c[:, 256:384] = np.tril(np.ones((128, 128)), -1)
    c[:, 384:512] = 1.0
    return c


def _padcols(w, n=128):
    K, c = w.shape
    out = np.zeros((K, n), np.float32)
    out[:, :c] = w
    return out


_NC = {}


def kernel(_cfg=None, **inp):
    cfg = dict(FULL if _cfg is None else _cfg)
    D, FF, SEQ, HA, HB, HC, G = (cfg[k] for k in ("D", "FF", "SEQ", "HA", "HB", "HC", "G"))
    KC = D // 128
    NU_C = HC // 2
    NCA, NCC = 3 * HA, NU_C + 2 * G
    f = lambda a: np.asarray(a, np.float32)
    shared = {"consts": _consts()}
    shared["wada"] = np.stack([_units(f(inp["w_ada"][l])) for l in range(2)])
    shared["bada"] = np.stack([np.ascontiguousarray(f(inp["b_ada"][l]).reshape(9 * KC, 128).T) for l in range(2)])
    nl = [f(inp[k][l]) for l in range(2) for k in ("norm1", "norm2", "norm3")] + [f(inp["final_norm"])]
    shared["norms"] = np.ascontiguousarray(np.stack(nl).reshape(7, KC, 128).transpose(2, 0, 1))
    for l in range(2):
        for w in (1, 2):
            shared[f"wg{l}{w}"] = _units(f(inp[f"ffn{w}_wg"][l]))
            shared[f"wu{l}{w}"] = _units(f(inp[f"ffn{w}_wu"][l]))
            shared[f"wd{l}{w}"] = _units(f(inp[f"ffn{w}_wd"][l]))
    QK_A, V_A = HA * 128, HA * 128
    splits = [2 * QK_A + V_A, V_A, HA, HA, HB * 128, HB * 128, HB * 256, 16, HB * 256, HC * 64, HC * 64 + 2 * G * 128, HC, 3 * D]
    offs = np.concatenate([[0], np.cumsum(splits)])
    UP = unit_plan(cfg)
    NPV = 4 + HB + 2 * NU_C + 32
    pvs, cwas, cwcs, nrms, wgates = [], [], [], [], []
    for l in range(2):
        w = f(inp["w_in"][l])
        grp = [w[:, offs[i]:offs[i + 1]] for i in range(13)]
        qkv_a, z_a, beta_a, dec_a, q_b, k_b, v_b, lr_b, r_b, z_c, xbc_c, dt_c, gates = grp
        cols = {}
        cols["beta"] = _padcols(beta_a)
        cols["dec"] = _padcols(dec_a)
        for h in range(HA):
            cols[f"aq{h}"] = qkv_a[:, h * 128:(h + 1) * 128]
            cols[f"ak{h}"] = qkv_a[:, QK_A + h * 128:QK_A + (h + 1) * 128]
            cols[f"av{h}"] = qkv_a[:, 2 * QK_A + h * 128:2 * QK_A + (h + 1) * 128]
            cols[f"az{h}"] = z_a[:, h * 128:(h + 1) * 128]
        cols["lr"] = _padcols(lr_b)
        for h in range(HB):
            cols[f"bq{h}"] = q_b[:, h * 128:(h + 1) * 128]
            cols[f"bk{h}"] = k_b[:, h * 128:(h + 1) * 128]
            for i in range(2):
                cols[f"bv{h}_{i}"] = v_b[:, h * 256 + i * 128:h * 256 + (i + 1) * 128]
                cols[f"br{h}_{i}"] = r_b[:, h * 256 + i * 128:h * 256 + (i + 1) * 128]
        cols["dt"] = _padcols(dt_c)
        inner = HC * 64
        for g in range(G):
            cols[f"cB{g}"] = xbc_c[:, inner + g * 128:inner + (g + 1) * 128]
            cols[f"cC{g}"] = xbc_c[:, inner + G * 128 + g * 128:inner + G * 128 + (g + 1) * 128]
        for u in range(NU_C):
            cols[f"cx{u}"] = xbc_c[:, u * 128:(u + 1) * 128]
            cols[f"cz{u}"] = z_c[:, u * 128:(u + 1) * 128]
        for br in range(3):
            for dc in range(KC):
                cols[f"gate{br}_{dc}"] = gates[:, br * D + dc * 128:br * D + (dc + 1) * 128]
        arr = np.zeros((len(UP), 128, KC, 128), np.float32)
        for n, i in UP.items():
            arr[i] = cols[n].reshape(KC, 128, 128).transpose(1, 0, 2)
        shared[f"win{l}"] = arr
        shared[f"wba{l}"] = _units(f(inp["w_branch_gdn"][l]))
        shared[f"wbb{l}"] = _units(f(inp["w_branch_gla"][l]))
        shared[f"wbc{l}"] = _units(f(inp["w_branch_ssd"][l]))
        shared[f"wo{l}"] = _units(f(inp["w_out"][l]))
        pvl = np.zeros((128, NPV), np.float32)
        pvl[:HA, 0] = f(inp["gdn_a_log"][l])
        pvl[:HA, 1] = f(inp["gdn_dt_bias"][l])
        pvl[:HC, 2] = f(inp["ssd_a_log"][l])
        pvl[:HC, 3] = f(inp["ssd_dt_bias"][l])
        pvl[:, 4:4 + HB] = f(inp["gla_b_gate"][l]).reshape(HB, 128).T
        pvl[:, 4 + HB:4 + HB + NU_C] = np.repeat(f(inp["ssd_d"][l]).reshape(NU_C, 2), 64, axis=1).T
        pvl[:, 4 + HB + NU_C:4 + HB + 2 * NU_C] = f(inp["ssd_norm_w"][l]).reshape(NU_C, 128).T
        pvl[:32, 4 + HB + 2 * NU_C:] = np.eye(32)
        pvs.append(pvl)
        cwas.append(np.ascontiguousarray(f(inp["gdn_conv_w"][l]).reshape(4, NCA, 128).transpose(2, 1, 0)))
        cw = f(inp["ssd_conv_w"][l])
        cb = f(inp["ssd_conv_b"][l])
        cwc = np.concatenate([cw.reshape(4, NCC, 128), cb.reshape(1, NCC, 128)], 0)
        cwcs.append(np.ascontiguousarray(cwc.transpose(2, 1, 0)))
        nrms.append(np.concatenate([np.tile(f(inp["gdn_norm_w"][l])[None], (128, 1)), np.tile(f(inp["gla_norm_w"][l])[None], (128, 1))], 1))
        wgates.append(f(inp["gla_w_gate"][l]))
    shared["pv"] = np.stack(pvs)
    shared["cwa"] = np.stack(cwas)
    shared["cwc"] = np.stack(cwcs)
    shared["nrm"] = np.ascontiguousarray(np.stack(nrms))
    shared["wgate"] = np.stack(wgates)

    xp = f(inp["x_prompt"])
    xs = f(inp["x_sample"])[:, 0, :]
    cp = f(inp["c_prompt"])
    cs = f(inp["c_sample"])
    sg, sl_, ss, sca, scc = (f(inp[k]) for k in ("state_gdn", "state_gla", "state_ssd", "state_gdn_conv", "state_ssd_conv"))
    in_maps = []
    for c in range(8):
        m = dict(shared)
        b0 = 16 * c
        m["xp"] = _fm(xp[c % 4])
        m["xs"] = _fm(xs[b0:b0 + 16])
        m["cT"] = _fm(np.concatenate([cp[c % 4][None], cs[b0:b0 + 16]], 0))
        m["s_gdn"] = np.ascontiguousarray(sg[:, b0:b0 + 16])
        m["s_gla"] = np.ascontiguousarray(sl_[:, b0:b0 + 16])
        m["s_ssd"] = np.ascontiguousarray(ss[:, b0:b0 + 16].reshape(2, 16, NU_C, 2, 64, 128).transpose(0, 1, 2, 5, 3, 4).reshape(2, 16, NU_C, 128, 128))
        m["s_cva"] = np.ascontiguousarray(sca[:, b0:b0 + 16].reshape(2, 16, 3, NCA, 128).transpose(0, 4, 3, 2, 1))
        m["s_cvc"] = np.ascontiguousarray(scc[:, b0:b0 + 16].reshape(2, 16, 3, NCC, 128).transpose(0, 4, 3, 2, 1))
        in_maps.append(m)
    key = tuple(sorted(cfg.items()))
    if key not in _NC:
        _NC[key] = build(cfg)
    res = run_bass_kernel_spmd(_NC[key], in_maps, core_ids=list(range(8)))
    R = res.results
    y_prompt = np.stack([R[c]["yp"].transpose(2, 1, 0).reshape(SEQ, D) for c in range(4)])
    y_sample = np.concatenate([R[c]["ys"].transpose(2, 1, 0).reshape(NS, D) for c in range(8)])[:, None, :]

    def conv_p(name, ncu):
        return np.stack([R[c][name].transpose(0, 3, 2, 1).reshape(2, 3, ncu * 128) for c in range(4)], 1)

    def conv_s(name, ncu):
        return np.concatenate([R[c][name].transpose(0, 4, 3, 2, 1).reshape(2, 16, 3, ncu * 128) for c in range(8)], 1)

    def ssd_back(a):
        sh = a.shape[:-3]
        return a.reshape(*sh, NU_C, 128, 2, 64).transpose(*range(len(sh)), len(sh), len(sh) + 2, len(sh) + 3, len(sh) + 1).reshape(*sh, HC, 64, 128)
    p_gdn_conv = conv_p("o_cva", NCA)
    p_gdn = np.stack([R[c]["o_gdn"] for c in range(4)], 1)
    p_gla = np.stack([R[c]["o_gla"] for c in range(4)], 1)
    p_ssd_conv = conv_p("o_cvc", NCC)
    p_ssd = np.stack([ssd_back(R[c]["o_ssd"]) for c in range(4)], 1)
    s_gdn_conv = conv_s("so_cva", NCA)
    s_gdn = np.concatenate([R[c]["so_gdn"] for c in range(8)], 1)
    s_gla = np.concatenate([R[c]["so_gla"] for c in range(8)], 1)
    s_ssd_conv = conv_s("so_cvc", NCC)
    s_ssd = np.concatenate([ssd_back(R[c]["so_ssd"]) for c in range(8)], 1)
    outs = (y_prompt, y_sample, p_gdn_conv, p_gdn, p_gla, p_ssd_conv, p_ssd, s_gdn_conv, s_gdn, s_gla, s_ssd_conv, s_ssd)
    return tuple(np.ascontiguousarray(o, dtype=np.float32) for o in outs)
```

```python
import numpy as np
from contextlib import ExitStack
import concourse.bass as bass
import concourse.mybir as mybir
from concourse.bass_utils import run_bass_kernel_spmd

F32, BF16 = mybir.dt.float32, mybir.dt.bfloat16
AF = mybir.ActivationFunctionType
ALU = mybir.AluOpType
AX = mybir.AxisListType

FULL = dict(D=2048, FF=5504, SEQ=2048, HA=8, HB=4, HC=32, G=4)
NT = 512
NS = 16
NV = 17
EPS = 1e-6
RS = 3


class _Rec:
    def __init__(self):
        self.call = None

    def __getattr__(self, name):
        def f(*a, **k):
            self.call = (name, a, k)
            return self
        return f


def _bind(fn):
    r = _Rec()
    fn(r)
    name, a, k = r.call
    return lambda eng: getattr(eng, name)(*a, **k)


class Prog:
    ENG = ("pe", "act", "dve", "pool", "sp")

    def __init__(self):
        self.ops = {e: [] for e in self.ENG}
        self.cnt = {e: 0 for e in ("pe", "act", "dve")}
        self.lastw = {}
        self.readers = {}
        self.dcnt = {}
        self.known = {e: {} for e in self.ENG}

    def _deps(self, eng, reads, writes):
        need = {}

        def add(t, raw):
            s, v, e = t
            if e == eng and eng == "pe":
                return
            if need.get(s, 0) < v:
                need[s] = v
        for k in reads:
            if k in self.lastw:
                add(self.lastw[k], True)
        for k in writes:
            if k in self.lastw:
                add(self.lastw[k], True)
            for r in self.readers.get(k, ()):
                add(r, False)
        kn = self.known[eng]
        out = []
        for s, v in need.items():
            if kn.get(s, 0) < v:
                kn[s] = v
                out.append((s, v))
        return out

    def _commit(self, tok, reads, writes):
        for k in reads:
            self.readers.setdefault(k, []).append(tok)
        for k in writes:
            self.lastw[k] = tok
            self.readers[k] = []

    def op(self, eng, fn, reads=(), writes=()):
        writes = list(writes) + [k for k in reads if k.startswith("ps")]
        reads = [k for k in reads if not k.startswith("ps")]
        waits = self._deps(eng, reads, writes)
        self.cnt[eng] += 1
        s = "c_" + eng
        self.ops[eng].append((waits, _bind(fn), s, 1))
        self._commit((s, self.cnt[eng], eng), reads, writes)

    def dma(self, q, fn, semkey, reads=(), writes=()):
        waits = self._deps(q, reads, writes)
        n = self.dcnt.get(semkey, 0)
        s = "d_" + semkey
        if n > 0 and self.known[q].get(s, 0) < 16 * n:
            self.known[q][s] = 16 * n
            waits.append((s, 16 * n))
        self.dcnt[semkey] = n + 1
        self.ops[q].append((waits, _bind(fn), s, 16))
        self._commit((s, 16 * (n + 1), "dma"), reads, writes)

    def emit(self, nc, es):
        names = ["c_pe", "c_act", "c_dve"] + ["d_" + k for k in self.dcnt]
        sems = {n: es.enter_context(nc.semaphore(n)) for n in names}
        finals = [(sems["d_" + k], 16 * n) for k, n in self.dcnt.items()]
        block = es.enter_context(nc.Block())
        decs = {"pe": block.tensor, "act": block.scalar, "dve": block.vector, "pool": block.gpsimd, "sp": block.sync}
        for e in self.ENG:
            ops = self.ops[e]

            def body(eng, ops=ops, last=(e == "sp")):
                for waits, fn, s, inc in ops:
                    for ws, wv in waits:
                        eng.wait_ge(sems[ws], wv)
                    fn(eng).then_inc(sems[s], inc)
                if last:
                    for sm, v in finals:
                        eng.wait_ge(sm, v)
            decs[e](body)


def unit_plan(cfg):
    HA, HB, HC, G, KC = cfg["HA"], cfg["HB"], cfg["HC"], cfg["G"], cfg["D"] // 128
    names = ["beta", "dec"]
    for h in range(HA):
        names += [f"aq{h}", f"ak{h}", f"av{h}", f"az{h}"]
    names += ["lr"]
    for h in range(HB):
        names += [f"bq{h}", f"bk{h}", f"bv{h}_0", f"bv{h}_1", f"br{h}_0", f"br{h}_1"]
    names += ["dt"]
    for g in range(G):
        names += [f"cB{g}", f"cC{g}"]
    for u in range(HC // 2):
        names += [f"cx{u}", f"cz{u}"]
    for br in range(3):
        for dc in range(KC):
            names += [f"gate{br}_{dc}"]
    return {n: i for i, n in enumerate(names)}


def build(cfg):
    D, FF, SEQ, HA, HB, HC, G = (cfg[k] for k in ("D", "FF", "SEQ", "HA", "HB", "HC", "G"))
    KC = D // 128
    FC = FF // 128
    FH = (FC + 2) // 3
    NU_C = HC // 2
    HPG = HC // G
    UPN = NU_C // G
    NCA = 3 * HA
    NCC = NU_C + 2 * G
    NPT = SEQ // NT
    UP = unit_plan(cfg)
    NUW = len(UP)
    assert KC <= 16 and NU_C <= 16 and HC <= 32

    nc = bass.Bass("TRN2", target_bir_lowering=False)
    P = Prog()
    es = ExitStack()

    def din(name, shape):
        return nc.dram_tensor(name, list(shape), F32, kind="ExternalInput").ap()

    def dout(name, shape):
        return nc.dram_tensor(name, list(shape), F32, kind="ExternalOutput").ap()

    def sb(name, shape, dt=F32):
        return es.enter_context(nc.sbuf_tensor(name, list(shape), dt))

    xp_d = din("xp", [128, KC, SEQ])
    xs_d = din("xs", [128, KC, NS])
    cT_d = din("cT", [128, KC, NV])
    consts_d = din("consts", [128, 640])
    wada_d = din("wada", [2, 9 * KC, 128, KC, 128])
    bada_d = din("bada", [2, 128, 9 * KC])
    norms_d = din("norms", [128, 7, KC])
    ffn_d = {}
    for l in range(2):
        for w in (1, 2):
            ffn_d[(l, w)] = (din(f"wg{l}{w}", [FC, 128, KC, 128]), din(f"wu{l}{w}", [FC, 128, KC, 128]),
                             din(f"wd{l}{w}", [KC, 128, FC, 128]))
    win_d = [din(f"win{l}", [NUW, 128, KC, 128]) for l in range(2)]
    wba_d = [din(f"wba{l}", [KC, 128, HA, 128]) for l in range(2)]
    wbb_d = [din(f"wbb{l}", [KC, 128, 2 * HB, 128]) for l in range(2)]
    wbc_d = [din(f"wbc{l}", [KC, 128, NU_C, 128]) for l in range(2)]
    wo_d = [din(f"wo{l}", [KC, 128, KC, 128]) for l in range(2)]
    NPV = 4 + HB + 2 * NU_C + 32
    pv_d = din("pv", [2, 128, NPV])
    cwa_d = din("cwa", [2, 128, NCA, 4])
    cwc_d = din("cwc", [2, 128, NCC, 5])
    nrm_d = din("nrm", [2, 128, 384])
    wgate_d = din("wgate", [2, 16, HB * 128])
    s_gdn_d = din("s_gdn", [2, NS, HA, 128, 128])
    s_gla_d = din("s_gla", [2, NS, HB, 128, 256])
    s_ssd_d = din("s_ssd", [2, NS, NU_C, 128, 128])
    s_cva_d = din("s_cva", [2, 128, NCA, 3, NS])
    s_cvc_d = din("s_cvc", [2, 128, NCC, 3, NS])
    yp_d = dout("yp", [128, KC, SEQ])
    ys_d = dout("ys", [128, KC, NS])
    o_gdn_d = dout("o_gdn", [2, HA, 128, 128])
    o_gla_d = dout("o_gla", [2, HB, 128, 256])
    o_ssd_d = dout("o_ssd", [2, NU_C, 128, 128])
    o_cva_d = dout("o_cva", [2, 128, NCA, 3])
    o_cvc_d = dout("o_cvc", [2, 128, NCC, 3])
    so_gdn_d = dout("so_gdn", [2, NS, HA, 128, 128])
    so_gla_d = dout("so_gla", [2, NS, HB, 128, 256])
    so_ssd_d = dout("so_ssd", [2, NS, NU_C, 128, 128])
    so_cva_d = dout("so_cva", [2, 128, NCA, 3, NS])
    so_cvc_d = dout("so_cvc", [2, 128, NCC, 3, NS])

    x = sb("x", [128, KC, NT])
    h = sb("h", [128, KC, NT], BF16)
    scr = sb("scr", [128, 32, NT], BF16)
    ring = sb("ring", [128, RS, 16, 128], BF16)
    wdr = sb("wdr", [128, 2, FH, 128], BF16)
    mod = [sb(f"mod{l}", [128, 9 * KC, NV]) for l in range(2)]
    cst = sb("cst", [128, 640])
    ones_b = sb("ones_b", [128, 128], BF16)
    norms = sb("norms_sb", [128, 7, KC])
    scT = sb("scT", [128, KC, NV], BF16)
    tmp = sb("tmp", [128, 2, NT])
    rstd = sb("rstd", [128, NT])
    pv = sb("pv_sb", [128, 2, NPV])
    cwa = sb("cwa_sb", [128, 2, NCA, 4])
    cwc = sb("cwc_sb", [128, 2, NCC, 5])
    nrm1 = sb("nrm_sb", [128, 384])
    wgate = sb("wgate_b", [16, 2, HB * 128], BF16)
    nSa, nSb, nSc = 2 * HA * 128, 2 * HB * 256, 2 * NU_C * 128
    nA, nC = NCA * 3 * NS, NCC * 3 * NS
    PSM = sb("PSM", [128, max(nSa + nSb + nSc, nA + nC + NCA * NS + NCC * NS + 512)])
    Sa = PSM[:, 0:nSa].rearrange("p (l h d) -> p l h d", l=2, h=HA)
    Sb_ = PSM[:, nSa:nSa + nSb].rearrange("p (l h d) -> p l h d", l=2, h=HB)
    Sc = PSM[:, nSa + nSb:nSa + nSb + nSc].rearrange("p (l h d) -> p l h d", l=2, h=NU_C)
    shA = PSM[:, 0:nA].rearrange("p (u j b) -> p u j b", u=NCA, j=3)
    shC = PSM[:, nA:nA + nC].rearrange("p (u j b) -> p u j b", u=NCC, j=3)
    o2 = nA + nC
    xsA = PSM[:, o2:o2 + NCA * NS].rearrange("p (u b) -> p u b", u=NCA)
    xsC = PSM[:, o2 + NCA * NS:o2 + NCA * NS + NCC * NS].rearrange("p (u b) -> p u b", u=NCC)
    o3 = o2 + NCA * NS + NCC * NS
    SS = PSM[:, o3:o3 + 512].rearrange("p (s d) -> p s d", s=2)
    PKEYS = [f"Sa{l}_{i}" for l in range(2) for i in range(HA)] + [f"Sb{l}_{i}" for l in range(2) for i in range(HB)] + \
            [f"Sc{l}_{i}" for l in range(2) for i in range(NU_C)]
    hista = sb("hista", [128, 2, NCA, 3])
    histc = sb("histc", [128, 2, NCC, 3])
    W = [sb(f"W{i}", [128, NT + 4]) for i in range(9)]
    Vb = [sb(f"V{i}", [128, NT], BF16) for i in range(6)]
    A_ = [sb(f"a{i}", [128, 256 if i == 4 else 128]) for i in range(13)]
    B_ = [sb(f"b{i}", [128, 256 if i in (0, 1, 10, 11, 12) else 128], BF16) for i in range(13)]
    tok = sb("tokv", [128, 16, 64])
    cs1 = sb("cs1", [128, 8])
    psb = [es.enter_context(nc.psum_tensor(f"ps{i}", [128, NT], F32)) for i in range(8)]
    ident = cst[:, 0:128]
    Umat = cst[:, 128:256]
    Lsmat = cst[:, 256:384]
    ones_f = cst[:, 384:512]
    BDmat = cst[:, 512:640]

    def pk(b, q0=0, q1=4):
        return [f"ps{b}"]

    state = {"ring": 0, "wd": 0, "tmp": 0}

    def ring_load(src_ap, nk=KC):
        s = state["ring"] % RS
        state["ring"] += 1
        P.dma("pool", lambda e, s=s: e.dma_start(out=ring[:, s, 0:nk, :], in_=src_ap), f"ring{s}", writes=[f"ring{s}"])
        return s

    def nexttmp():
        i = state["tmp"] % 2
        state["tmp"] += 1
        return i

    def V(fn, r, w):
        P.op("dve", fn, r, w)

    def Ac(fn, r, w):
        P.op("act", fn, r, w)

    def T(fn, r, w):
        P.op("pe", fn, r, w)

    def ld(dst, src, key):
        P.dma("sp", lambda e: e.dma_start(out=dst, in_=src), key, writes=[key])
    ld(cst[:], consts_d, "cst")
    ld(norms[:], norms_d, "norms")
    bada = W[1][:, 0:2 * 9 * KC].rearrange("p (l u) -> p l u", l=2)
    cT = W[0][:, 0:KC * NV].rearrange("p (k v) -> p k v", k=KC)
    P.dma("sp", lambda e: e.dma_start(out=bada, in_=bada_d.rearrange("l p u -> p l u")), "bada", writes=["W1"])
    P.dma("sp", lambda e: e.dma_start(out=cT, in_=cT_d), "cT", writes=["W0"])
    ld(pv[:], pv_d.rearrange("l p u -> p l u"), "pv")
    ld(cwa[:], cwa_d.rearrange("l p u j -> p l u j"), "cwa")
    ld(cwc[:], cwc_d.rearrange("l p u j -> p l u j"), "cwc")
    P.dma("pool", lambda e: e.dma_start(out=wgate[:], in_=wgate_d.rearrange("l p u -> p l u")), "wgate", writes=["wgate"])
    V(lambda e: e.tensor_copy(out=ones_b[:], in_=cst[:, 384:512]), ["cst"], ["ones_b"])
    Ac(lambda e: e.activation(out=scT[:], in_=cT, func=AF.Silu), ["W0"], ["scT"])
    for l in range(2):
        for c in (0, 2):
            Ac(lambda e, l=l, c=c: e.activation(out=pv[:, l, c:c + 1], in_=pv[:, l, c:c + 1], func=AF.Exp), ["pv"], ["pv"])
            V(lambda e, l=l, c=c: e.tensor_scalar(out=pv[:, l, c:c + 1], in0=pv[:, l, c:c + 1], scalar1=-1.0, scalar2=None, op0=ALU.mult), ["pv"], ["pv"])
        V(lambda e, l=l: e.tensor_scalar(out=pv[:, l, 4:4 + HB], in0=pv[:, l, 4:4 + HB], scalar1=-1.0, scalar2=None, op0=ALU.mult), ["pv"], ["pv"])
    PV_BG, PV_D, PV_NW, PV_OH = 4, 4 + HB, 4 + HB + NU_C, 4 + HB + 2 * NU_C

    for l in range(2):
        for u in range(9 * KC):
            s = ring_load(wada_d[l, u])
            pb = u % 2
            for kc in range(KC):
                T(lambda e, s=s, kc=kc, pb=pb: e.matmul(psb[pb][:, 0:NV], lhsT=ring[:, s, kc, :], rhs=scT[:, kc, :],
                                                         start=(kc == 0), stop=(kc == KC - 1)),
                  [f"ring{s}", "scT"], pk(pb))
            V(lambda e, l=l, u=u, pb=pb: e.tensor_scalar(out=mod[l][:, u, :], in0=psb[pb][:, 0:NV], scalar1=bada[:, l, u:u + 1],
                                                          scalar2=None, op0=ALU.add),
              pk(pb) + ["W1"], [f"mod{l}"])
        for m in range(3):
            sc = mod[l][:, (3 * m + 1) * KC:(3 * m + 2) * KC, :]
            gt = mod[l][:, (3 * m + 2) * KC:(3 * m + 3) * KC, :]
            nb = norms[:, 3 * l + m, :].unsqueeze(2).to_broadcast([128, KC, NV])
            V(lambda e, sc=sc, nb=nb: e.scalar_tensor_tensor(out=sc, in0=sc, scalar=1.0, in1=nb, op0=ALU.add, op1=ALU.mult),
              [f"mod{l}", "norms"], [f"mod{l}"])
            if m != 1:
                V(lambda e, gt=gt: e.tensor_scalar(out=gt, in0=gt, scalar1=0.5, scalar2=None, op0=ALU.mult), [f"mod{l}"], [f"mod{l}"])

    def rms_stats(nt):
        for kc in range(KC):
            Ac(lambda e, kc=kc: e.activation(out=scr[:, kc, 0:nt], in_=x[:, kc, 0:nt], func=AF.Square), [f"x{kc}"], [f"scr{kc}"])
        for kc in range(KC):
            T(lambda e, kc=kc: e.matmul(psb[7][:, 0:nt], lhsT=ones_b[:], rhs=scr[:, kc, 0:nt], start=(kc == 0), stop=(kc == KC - 1)),
              [f"scr{kc}", "ones_b"], pk(7))
        Ac(lambda e: e.activation(out=rstd[:, 0:nt], in_=psb[7][:, 0:nt], func=AF.Ln, scale=1.0 / D, bias=EPS), pk(7), ["rstd"])
        Ac(lambda e: e.activation(out=rstd[:, 0:nt], in_=rstd[:, 0:nt], func=AF.Exp, scale=-0.5), ["rstd"], ["rstd"])

    def norm_mod(nt, samp, A, B, lname):
        rms_stats(nt)
        for kc in range(KC):
            t = nexttmp()
            o = h[:, kc, 0:nt]
            if not samp:
                V(lambda e, kc=kc, t=t: e.scalar_tensor_tensor(out=tmp[:, t, 0:nt], in0=x[:, kc, 0:nt], scalar=A[:, kc, 0:1],
                                                                in1=rstd[:, 0:nt], op0=ALU.mult, op1=ALU.mult),
                  [f"x{kc}", "rstd", lname], [f"tmp{t}"])
                Ac(lambda e, kc=kc, t=t, o=o: e.activation(out=o, in_=tmp[:, t, 0:nt], func=AF.Identity, bias=B[:, kc, 0:1]),
                   [f"tmp{t}", lname], [f"h{kc}"])
            else:
                V(lambda e, kc=kc, t=t: e.tensor_tensor(out=tmp[:, t, 0:nt], in0=x[:, kc, 0:nt], in1=rstd[:, 0:nt], op=ALU.mult),
                  [f"x{kc}", "rstd"], [f"tmp{t}"])
                V(lambda e, kc=kc, t=t: e.tensor_tensor(out=tmp[:, t, 0:nt], in0=tmp[:, t, 0:nt], in1=A[:, kc, 1:NV], op=ALU.mult),
                  [f"tmp{t}", lname], [f"tmp{t}"])
                V(lambda e, kc=kc, t=t, o=o: e.tensor_tensor(out=o, in0=tmp[:, t, 0:nt], in1=B[:, kc, 1:NV], op=ALU.add),
                  [f"tmp{t}", lname], [f"h{kc}"])

    def resid_add(nt, samp, dc, pbank, G, lname):
        if not samp:
            V(lambda e: e.scalar_tensor_tensor(out=x[:, dc, 0:nt], in0=psb[pbank][:, 0:nt], scalar=G[:, dc, 0:1],
                                               in1=x[:, dc, 0:nt], op0=ALU.mult, op1=ALU.add),
              pk(pbank) + [f"x{dc}", lname], [f"x{dc}"])
        else:
            t = nexttmp()
            V(lambda e: e.tensor_tensor(out=tmp[:, t, 0:nt], in0=psb[pbank][:, 0:nt], in1=G[:, dc, 1:NV], op=ALU.mult),
              pk(pbank) + [lname], [f"tmp{t}"])
            V(lambda e: e.tensor_tensor(out=x[:, dc, 0:nt], in0=x[:, dc, 0:nt], in1=tmp[:, t, 0:nt], op=ALU.add),
              [f"tmp{t}", f"x{dc}"], [f"x{dc}"])

    def ffn(l, w, nt, samp):
        m = 0 if w == 1 else 2
        lname = f"mod{l}"
        sh = mod[l][:, (3 * m) * KC:(3 * m + 1) * KC, :]
        A = mod[l][:, (3 * m + 1) * KC:(3 * m + 2) * KC, :]
        G = mod[l][:, (3 * m + 2) * KC:(3 * m + 3) * KC, :]
        norm_mod(nt, samp, A, sh, lname)
        wg_d, wu_d, wd_d = ffn_d[(l, w)]
        for (j0, j1) in ((0, FH), (FH, min(2 * FH, FC)), (min(2 * FH, FC), FC)):
            if j1 <= j0:
                continue
            for j in range(j0, j1):
                b = j % 2
                sg = ring_load(wg_d[j])
                su = ring_load(wu_d[j])
                for kc in range(KC):
                    T(lambda e, sg=sg, kc=kc, b=b: e.matmul(psb[b][:, 0:nt], lhsT=ring[:, sg, kc, :], rhs=h[:, kc, 0:nt],
                                                             start=(kc == 0), stop=(kc == KC - 1)),
                      [f"ring{sg}", f"h{kc}"], pk(b))
                for kc in range(KC):
                    T(lambda e, su=su, kc=kc, b=b: e.matmul(psb[2 + b][:, 0:nt], lhsT=ring[:, su, kc, :], rhs=h[:, kc, 0:nt],
                                                             start=(kc == 0), stop=(kc == KC - 1)),
                      [f"ring{su}", f"h{kc}"], pk(2 + b))
                t = nexttmp()
                Ac(lambda e, b=b, t=t: e.activation(out=tmp[:, t, 0:nt], in_=psb[b][:, 0:nt], func=AF.Silu), pk(b), [f"tmp{t}"])
                V(lambda e, b=b, t=t, jj=j - j0: e.tensor_tensor(out=scr[:, jj, 0:nt], in0=tmp[:, t, 0:nt], in1=psb[2 + b][:, 0:nt], op=ALU.mult),
                  [f"tmp{t}"] + pk(2 + b), [f"scr{j - j0}"])
            nj = j1 - j0
            for dc in range(KC):
                ws = state["wd"] % 2
                state["wd"] += 1
                b = 4 + dc % 2
                P.dma("pool", lambda e, ws=ws, dc=dc, j0=j0, j1=j1, nj=nj: e.dma_start(out=wdr[:, ws, 0:nj, :], in_=wd_d[dc, :, j0:j1, :]),
                      f"wdr{ws}", writes=[f"wdr{ws}"])
                for jj in range(nj):
                    T(lambda e, ws=ws, jj=jj, b=b, nj=nj: e.matmul(psb[b][:, 0:nt], lhsT=wdr[:, ws, jj, :], rhs=scr[:, jj, 0:nt],
                                                                    start=(jj == 0), stop=(jj == nj - 1)),
                      [f"wdr{ws}", f"scr{jj}"], pk(b))
                resid_add(nt, samp, dc, b, G, lname)

    hk = [f"h{kc}" for kc in range(KC)]

    def proj(l, uname, pbank, nt):
        s = ring_load(win_d[l][UP[uname]])
        for kc in range(KC):
            T(lambda e, s=s, kc=kc: e.matmul(psb[pbank][:, 0:nt], lhsT=ring[:, s, kc, :], rhs=h[:, kc, 0:nt], start=(kc == 0), stop=(kc == KC - 1)),
              [f"ring{s}", f"h{kc}"], pk(pbank))

    def conv_unit(l, pbank, nt, samp, cw_ap, cwkey, bias_ap, hist_ap, histkey, shist, xstash, shkey, xskey, rawW, outW):
        raw, out = W[rawW], W[outW]
        rk, ok = f"W{rawW}", f"W{outW}"
        if not samp:
            V(lambda e: e.tensor_copy(out=raw[:, 0:3], in_=hist_ap), [histkey], [rk])
            Ac(lambda e: e.activation(out=raw[:, 3:3 + nt], in_=psb[pbank][:, 0:nt], func=AF.Copy), pk(pbank), [rk])
            V(lambda e: e.tensor_scalar(out=out[:, 0:nt], in0=raw[:, 0:nt], scalar1=cw_ap[:, 0:1], scalar2=None, op0=ALU.mult), [rk, cwkey], [ok])
            for j in range(1, 4):
                V(lambda e, j=j: e.scalar_tensor_tensor(out=out[:, 0:nt], in0=raw[:, j:j + nt], scalar=cw_ap[:, j:j + 1], in1=out[:, 0:nt],
                                                        op0=ALU.mult, op1=ALU.add), [rk, ok, cwkey], [ok])
            V(lambda e: e.tensor_copy(out=hist_ap, in_=raw[:, nt:nt + 3]), [rk], [histkey])
        else:
            Ac(lambda e: e.activation(out=xstash, in_=psb[pbank][:, 0:nt], func=AF.Copy), pk(pbank), [xskey])
            V(lambda e: e.tensor_scalar(out=out[:, 0:nt], in0=xstash, scalar1=cw_ap[:, 3:4], scalar2=None, op0=ALU.mult), [xskey, cwkey], [ok])
            for j in range(3):
                V(lambda e, j=j: e.scalar_tensor_tensor(out=out[:, 0:nt], in0=shist[:, j, :], scalar=cw_ap[:, j:j + 1], in1=out[:, 0:nt],
                                                        op0=ALU.mult, op1=ALU.add), [shkey, ok, cwkey], [ok])
        if bias_ap is not None:
            Ac(lambda e: e.activation(out=out[:, 0:nt], in_=out[:, 0:nt], func=AF.Silu, bias=bias_ap), [ok, cwkey], [ok])
        else:
            Ac(lambda e: e.activation(out=out[:, 0:nt], in_=out[:, 0:nt], func=AF.Silu), [ok], [ok])

    def rinv_of(srcW, nt, dstW):
        V(lambda e: e.tensor_tensor(out=Vb[5][:, 0:nt], in0=W[srcW][:, 0:nt], in1=W[srcW][:, 0:nt], op=ALU.mult), [f"W{srcW}"], ["V5"])
        T(lambda e: e.matmul(psb[6][:, 0:nt], lhsT=ones_b[:], rhs=Vb[5][:, 0:nt], start=True, stop=True), ["V5", "ones_b"], pk(6))
        Ac(lambda e: e.activation(out=W[dstW][:, 0:nt], in_=psb[6][:, 0:nt], func=AF.Ln, bias=EPS), pk(6), [f"W{dstW}"])
        Ac(lambda e: e.activation(out=W[dstW][:, 0:nt], in_=W[dstW][:, 0:nt], func=AF.Exp, scale=-0.5), [f"W{dstW}"], [f"W{dstW}"])

    def decay_setup(l, nt, CL, nch, uname, col_alog, col_dtb, nheads, dtW, with_dt, kstride):
        proj(l, uname, 5, nt)
        Ac(lambda e: e.activation(out=W[6][:, 0:nt], in_=psb[5][:, 0:nt], func=AF.Exp, bias=pv[:, l, col_dtb:col_dtb + 1]), pk(5) + ["pv"], ["W6"])
        Ac(lambda e: e.activation(out=W[6][:, 0:nt], in_=W[6][:, 0:nt], func=AF.Ln, bias=1.0), ["W6"], ["W6"])
        if with_dt:
            V(lambda e: e.tensor_copy(out=W[dtW][:, 0:nt], in_=W[6][:, 0:nt]), ["W6"], [f"W{dtW}"])
        V(lambda e: e.tensor_scalar(out=W[6][:, 0:nt], in0=W[6][:, 0:nt], scalar1=pv[:, l, col_alog:col_alog + 1], scalar2=None, op0=ALU.mult),
          ["W6", "pv"], ["W6"])
        if CL == 1:
            V(lambda e: e.tensor_copy(out=W[7][:, 0:nt], in_=W[6][:, 0:nt]), ["W6"], ["W7"])
        else:
            for c in range(nch):
                sl = slice(c * CL, (c + 1) * CL)
                V(lambda e, sl=sl: e.tensor_tensor_scan(out=W[7][:, sl], data0=W[6][:, sl], data1=W[6][:, sl], initial=0.0, op0=ALU.add, op1=ALU.bypass),
                  ["W6"], ["W7"])
        for c in range(nch):
            sl = slice(c * CL, (c + 1) * CL)
            T(lambda e, sl=sl: e.transpose(out=psb[6][0:CL, 0:128], in_=W[7][:, sl], identity=ident), ["W7", "cst"], pk(6, 0, 1))
            T(lambda e, sl=sl: e.transpose(out=psb[6][0:CL, 128:256], in_=W[dtW][:, sl], identity=ident), [f"W{dtW}", "cst"], pk(6, 1, 2))
            V(lambda e, c=c: e.tensor_copy(out=tok[0:CL, c, 0:nheads], in_=psb[6][0:CL, 0:nheads]), pk(6, 0, 1), ["tok"])
            V(lambda e, c=c: e.tensor_copy(out=tok[0:CL, c, kstride:kstride + nheads], in_=psb[6][0:CL, 128:128 + nheads]), pk(6, 1, 2), ["tok"])

    def decay_mats(l, CL, c, hd, need_dl):
        sl = slice(c * CL, (c + 1) * CL)
        V(lambda e: e.tensor_scalar(out=A_[3][:, 0:CL], in0=W[7][:, sl], scalar1=pv[:, l, PV_OH + hd:PV_OH + hd + 1], scalar2=None, op0=ALU.mult),
          ["W7", "pv"], ["a3"])
        if CL == 1:
            return
        T(lambda e: e.matmul(psb[4][0:CL, 0:CL], lhsT=ones_f[:, 0:CL], rhs=A_[3][:, 0:CL], start=True, stop=True), ["a3", "cst"], pk(4, 0, 1))
        V(lambda e: e.tensor_scalar(out=A_[0][0:CL, 0:CL], in0=psb[4][0:CL, 0:CL], scalar1=tok[0:CL, c, hd:hd + 1], scalar2=None, op0=ALU.subtract),
          pk(4, 0, 1) + ["tok"], ["a0"])
        V(lambda e: e.tensor_scalar(out=A_[1][0:CL, 0:CL], in0=A_[0][0:CL, 0:CL], scalar1=0.0, scalar2=None, op0=ALU.min), ["a0"], ["a1"])
        Ac(lambda e: e.activation(out=A_[1][0:CL, 0:CL], in_=A_[1][0:CL, 0:CL], func=AF.Exp), ["a1"], ["a1"])
        V(lambda e: e.tensor_tensor(out=A_[1][0:CL, 0:CL], in0=A_[1][0:CL, 0:CL], in1=Umat[0:CL, 0:CL], op=ALU.mult), ["a1", "cst"], ["a1"])
        if need_dl and CL > 1:
            V(lambda e: e.tensor_scalar(out=A_[2][0:CL, 0:CL], in0=A_[0][0:CL, 0:CL], scalar1=0.0, scalar2=None, op0=ALU.max), ["a0"], ["a2"])
            Ac(lambda e: e.activation(out=A_[2][0:CL, 0:CL], in_=A_[2][0:CL, 0:CL], func=AF.Exp, scale=-1.0), ["a2"], ["a2"])
            V(lambda e: e.tensor_tensor(out=A_[2][0:CL, 0:CL], in0=A_[2][0:CL, 0:CL], in1=Lsmat[0:CL, 0:CL], op=ALU.mult), ["a2", "cst"], ["a2"])
        Ac(lambda e: e.activation(out=cs1[0:CL, 0:1], in_=A_[0][0:CL, CL - 1:CL], func=AF.Exp), ["a0"], ["cs1"])

    def out_norm_T(l, CL, c, pbank, dvw, nw_ap, gateW, dst_units, dstkeys):
        sl = slice(c * CL, (c + 1) * CL)
        Ac(lambda e: e.activation(out=A_[4][0:CL, 0:dvw], in_=psb[pbank][0:CL, 0:dvw], func=AF.Square), pk(pbank, 0, 2), ["a4"])
        V(lambda e: e.tensor_reduce(out=cs1[0:CL, 4:5], in_=A_[4][0:CL, 0:dvw], axis=AX.X, op=ALU.add), ["a4"], ["cs1b"])
        Ac(lambda e: e.activation(out=cs1[0:CL, 4:5], in_=cs1[0:CL, 4:5], func=AF.Ln, scale=1.0 / dvw, bias=EPS), ["cs1b"], ["cs1b"])
        Ac(lambda e: e.activation(out=cs1[0:CL, 4:5], in_=cs1[0:CL, 4:5], func=AF.Exp, scale=-0.5), ["cs1b"], ["cs1b"])
        V(lambda e: e.scalar_tensor_tensor(out=A_[4][0:CL, 0:dvw], in0=psb[pbank][0:CL, 0:dvw], scalar=cs1[0:CL, 4:5], in1=nw_ap[0:CL, 0:dvw],
                                           op0=ALU.mult, op1=ALU.mult), pk(pbank, 0, 2) + ["cs1b", "nrm"], ["a4"])
        for i, (du, dk_) in enumerate(zip(dst_units, dstkeys)):
            T(lambda e, i=i: e.transpose(out=psb[6][:, 256 + i * 128:256 + i * 128 + CL], in_=A_[4][0:CL, i * 128:(i + 1) * 128], identity=ident[0:CL, 0:CL]),
              ["a4", "cst"], pk(6, 2 + i, 3 + i))
            V(lambda e, i=i, du=du: e.tensor_tensor(out=scr[:, du, sl], in0=psb[6][:, 256 + i * 128:256 + i * 128 + CL], in1=W[gateW[i]][:, sl], op=ALU.mult),
              pk(6, 2 + i, 3 + i) + [f"W{gateW[i]}"], [dk_])

    def mixer(l, nt, samp):
        CL = 1 if samp else 128
        nch = nt // CL
        lname = f"mod{l}"
        sh = mod[l][:, 3 * KC:4 * KC, :]
        A = mod[l][:, 4 * KC:5 * KC, :]
        Gt = mod[l][:, 5 * KC:6 * KC, :]
        norm_mod(nt, samp, A, sh, lname)
        P.dma("sp", lambda e: e.dma_start(out=nrm1[:], in_=nrm_d[l]), "nrm", writes=["nrm"])
        if samp:
            P.dma("sp", lambda e: e.dma_start(out=shA[:], in_=s_cva_d[l]), "shA", writes=["shA"] + PKEYS)
            P.dma("sp", lambda e: e.dma_start(out=shC[:], in_=s_cvc_d[l]), "shC", writes=["shC"] + PKEYS)
        OB = 16

        def load_state(src_ap, width, buf):
            P.dma("sp", lambda e: e.dma_start(out=SS[:, buf, 0:width], in_=src_ap), f"SS{buf}", writes=[f"SS{buf}"])

        def store_state(dst_ap, src_sb, key, skey):
            P.dma("sp", lambda e: e.dma_start(out=dst_ap, in_=src_sb), skey, reads=[key], writes=["odram"])

        CL_all, nch_all = CL, nch
        CL = 1 if samp else 128
        nch = nt // CL
        proj(l, "beta", 4, nt)
        Ac(lambda e: e.activation(out=W[8][:, 0:nt], in_=psb[4][:, 0:nt], func=AF.Sigmoid), pk(4), ["W8"])
        decay_setup(l, nt, CL, nch, "dec", 0, 1, HA, 8, False, 8)
        for c in range(nch):
            V(lambda e, c=c: e.tensor_scalar(out=tok[0:CL, c, 16:16 + HA], in0=tok[0:CL, c, 8:8 + HA], scalar1=-1.0, scalar2=None, op0=ALU.mult), ["tok"], ["tok"])
            Ac(lambda e, c=c: e.activation(out=tok[0:CL, c, 24:24 + HA], in_=tok[0:CL, c, 0:HA], func=AF.Exp), ["tok"], ["tok"])
            V(lambda e, c=c: e.tensor_tensor(out=tok[0:CL, c, 24:24 + HA], in0=tok[0:CL, c, 24:24 + HA], in1=tok[0:CL, c, 8:8 + HA], op=ALU.mult), ["tok"], ["tok"])

        for hd in range(HA):
            for i, (nm, ow) in enumerate((("aq", 1), ("ak", 2), ("av", 3))):
                proj(l, f"{nm}{hd}", i % 2, nt)
                u = i * HA + hd
                conv_unit(l, i % 2, nt, samp, cwa[:, l, u, :], "cwa", None, hista[:, l, u, :], f"hista{l}", shA[:, u, :, :], xsA[:, u, :], "shA", "xsA", 0, ow)
            proj(l, f"az{hd}", 3, nt)
            Ac(lambda e: e.activation(out=W[4][:, 0:nt], in_=psb[3][:, 0:nt], func=AF.Silu), pk(3), ["W4"])
            rinv_of(1, nt, 5)
            V(lambda e: e.scalar_tensor_tensor(out=Vb[0][:, 0:nt], in0=W[1][:, 0:nt], scalar=128 ** -0.5, in1=W[5][:, 0:nt], op0=ALU.mult, op1=ALU.mult),
              ["W1", "W5"], ["V0"])
            rinv_of(2, nt, 5)
            V(lambda e: e.tensor_tensor(out=W[2][:, 0:nt], in0=W[2][:, 0:nt], in1=W[5][:, 0:nt], op=ALU.mult), ["W2", "W5"], ["W2"])
            V(lambda e: e.tensor_copy(out=Vb[1][:, 0:nt], in_=W[2][:, 0:nt]), ["W2"], ["V1"])
            for c in range(nch):
                sl = slice(c * CL, (c + 1) * CL)
                tk = "tok"
                if samp:
                    buf = c % 2
                    load_state(s_gdn_d[l, c, hd], 128, buf)
                    S = SS[:, buf, 0:128]
                    Sk = f"SS{buf}"
                else:
                    S = Sa[:, l, hd, :]
                    Sk = f"Sa{l}_{hd}"
                    if c == 0 and state.get("ptile", 0) == 0:
                        V(lambda e, S=S: e.memset(S, 0.0), [], [Sk])
                V(lambda e, S=S: e.tensor_copy(out=B_[0][:, 0:128], in_=S), [Sk], ["b0"])
                decay_mats(l, CL, c, hd, True)
                if CL > 1:
                    q = slice(0, CL)
                    blocked = CL > 64
                    T(lambda e, sl=sl: e.matmul(psb[4][q, 128:128 + CL], lhsT=Vb[1][:, sl], rhs=Vb[1][:, sl], start=True, stop=True), ["V1"], pk(4, 1, 2))
                    V(lambda e, c=c: e.scalar_tensor_tensor(out=A_[5][q, q], in0=psb[4][q, 128:128 + CL], scalar=tok[q, c, 16 + hd:17 + hd], in1=A_[2][q, q],
                                                            op0=ALU.mult, op1=ALU.mult), pk(4, 1, 2) + [tk, "a2"], ["a5"])
                    T(lambda e: e.transpose(out=psb[4][q, 256:256 + CL], in_=A_[5][q, q], identity=ident[q, q]), ["a5", "cst"], pk(4, 2, 3))
                    Ac(lambda e: e.activation(out=A_[6][q, q], in_=psb[4][q, 256:256 + CL], func=AF.Copy), pk(4, 2, 3), ["a6"])
                    if blocked:
                        V(lambda e: e.tensor_tensor(out=A_[7][q, q], in0=A_[5][q, q], in1=BDmat[q, q], op=ALU.mult), ["a5", "cst"], ["a7"])
                        V(lambda e: e.tensor_tensor(out=A_[8][q, q], in0=A_[6][q, q], in1=BDmat[q, q], op=ALU.mult), ["a6", "cst"], ["a8"])
                        V(lambda e: e.tensor_tensor(out=A_[5][q, q], in0=A_[5][q, q], in1=A_[7][q, q], op=ALU.subtract), ["a5", "a7"], ["a5"])
                        V(lambda e: e.tensor_tensor(out=A_[6][q, q], in0=A_[6][q, q], in1=A_[8][q, q], op=ALU.subtract), ["a6", "a8"], ["a6"])
                        V(lambda e: e.tensor_tensor(out=A_[9][q, q], in0=A_[8][q, q], in1=ident[q, q], op=ALU.add), ["a8", "cst"], ["a9"])
                        Nc, Mc, Nn, Mn = 7, 8, 11, 12
                        nlev = 5
                    else:
                        V(lambda e: e.tensor_tensor(out=A_[9][q, q], in0=psb[4][q, 256:256 + CL], in1=ident[q, q], op=ALU.add), pk(4, 2, 3) + ["cst"], ["a9"])
                        Nc, Mc, Nn, Mn = 5, 6, 7, 8
                        nlev = CL.bit_length() - 2
                    for lev in range(1, nlev + 1):
                        T(lambda e, Mc=Mc, Nc=Nc: e.matmul(psb[4][q, 128:128 + CL], lhsT=A_[Mc][q, q], rhs=A_[Nc][q, q], start=True, stop=True),
                          [f"a{Mc}", f"a{Nc}"], pk(4, 1, 2))
                        if lev < nlev:
                            T(lambda e, Mc=Mc, Nc=Nc: e.matmul(psb[4][q, 256:256 + CL], lhsT=A_[Nc][q, q], rhs=A_[Mc][q, q], start=True, stop=True),
                              [f"a{Mc}", f"a{Nc}"], pk(4, 2, 3))
                        Ac(lambda e, Nn=Nn: e.activation(out=A_[Nn][q, q], in_=psb[4][q, 128:128 + CL], func=AF.Copy), pk(4, 1, 2), [f"a{Nn}"])
                        if lev < nlev:
                            V(lambda e, Mn=Mn: e.tensor_copy(out=A_[Mn][q, q], in_=psb[4][q, 256:256 + CL]), pk(4, 2, 3), [f"a{Mn}"])
                        T(lambda e, Nn=Nn: e.matmul(psb[4][q, 384:384 + CL], lhsT=A_[Nn][q, q], rhs=A_[9][q, q], start=True, stop=True),
                          [f"a{Nn}", "a9"], pk(4, 3, 4))
                        V(lambda e: e.tensor_tensor(out=A_[9][q, q], in0=A_[9][q, q], in1=psb[4][q, 384:384 + CL], op=ALU.add), pk(4, 3, 4) + ["a9"], ["a9"])
                        Nc, Mc, Nn, Mn = Nn, Mn, Nc, Mc
                    if blocked:
                        T(lambda e: e.matmul(psb[4][q, 128:128 + CL], lhsT=A_[5][q, q], rhs=A_[9][q, q], start=True, stop=True), ["a5", "a9"], pk(4, 1, 2))
                        T(lambda e: e.transpose(out=psb[4][q, 256:256 + CL], in_=A_[9][q, q], identity=ident[q, q]), ["a9", "cst"], pk(4, 2, 3))
                        Ac(lambda e: e.activation(out=A_[7][q, q], in_=psb[4][q, 128:128 + CL], func=AF.Copy), pk(4, 1, 2), ["a7"])
                        V(lambda e: e.tensor_copy(out=A_[8][q, q], in_=psb[4][q, 256:256 + CL]), pk(4, 2, 3), ["a8"])
                        T(lambda e: e.matmul(psb[4][q, 384:384 + CL], lhsT=A_[8][q, q], rhs=A_[7][q, q], start=True, stop=True), ["a8", "a7"], pk(4, 3, 4))
                        V(lambda e: e.tensor_tensor(out=A_[9][q, q], in0=A_[9][q, q], in1=psb[4][q, 384:384 + CL], op=ALU.add), pk(4, 3, 4) + ["a9"], ["a9"])
                    Rap = A_[9]
                    Rk = "a9"
                else:
                    Rap = cst
                    Rk = "cst"
                T(lambda e, sl=sl: e.transpose(out=psb[5][0:CL, 0:128], in_=W[3][:, sl], identity=ident), ["W3", "cst"], pk(5, 0, 1))
                Ac(lambda e: e.activation(out=A_[12][0:CL, 0:128], in_=psb[5][0:CL, 0:128], func=AF.Copy), pk(5, 0, 1), ["a12"])
                T(lambda e, sl=sl: e.transpose(out=psb[5][0:CL, 128:256], in_=W[2][:, sl], identity=ident), ["W2", "cst"], pk(5, 1, 2))
                Ac(lambda e: e.activation(out=A_[11][0:CL, 0:128], in_=psb[5][0:CL, 128:256], func=AF.Copy), pk(5, 1, 2), ["a11"])
                if CL == 1:
                    V(lambda e: e.tensor_copy(out=B_[3][0:CL, 0:128], in_=psb[5][0:CL, 128:256]), pk(5, 1, 2), ["b3"])
                else:
                    V(lambda e: e.tensor_scalar(out=B_[3][0:CL, 0:128], in0=psb[5][0:CL, 128:256], scalar1=cs1[0:CL, 0:1], scalar2=None, op0=ALU.mult),
                      pk(5, 1, 2) + ["cs1"], ["b3"])
                V(lambda e, c=c, Rap=Rap: e.tensor_scalar(out=A_[5][0:CL, 0:CL], in0=Rap[0:CL, 0:CL], scalar1=tok[0:CL, c, 8 + hd:9 + hd], scalar2=None, op0=ALU.mult),
                  [Rk, tk], ["a5"])
                V(lambda e, c=c, Rap=Rap: e.tensor_scalar(out=A_[6][0:CL, 0:CL], in0=Rap[0:CL, 0:CL], scalar1=tok[0:CL, c, 24 + hd:25 + hd], scalar2=None, op0=ALU.mult),
                  [Rk, tk], ["a6"])
                T(lambda e: e.matmul(psb[5][:, 256:256 + CL], lhsT=A_[11][0:CL, 0:128], rhs=A_[6][0:CL, 0:CL], start=True, stop=True), ["a11", "a6"], pk(5, 2, 3))
                V(lambda e: e.tensor_scalar(out=A_[7][:, 0:CL], in0=psb[5][:, 256:256 + CL], scalar1=-1.0, scalar2=None, op0=ALU.mult), pk(5, 2, 3), ["a7"])
                T(lambda e: e.matmul(psb[5][0:CL, 384:512], lhsT=A_[5][0:CL, 0:CL], rhs=A_[12][0:CL, 0:128], start=True, stop=False), ["a5", "a12"], pk(5, 3, 4))
                T(lambda e, S=S: e.matmul(psb[5][0:CL, 384:512], lhsT=A_[7][:, 0:CL], rhs=S, start=False, stop=True), ["a7", Sk], pk(5, 3, 4))
                V(lambda e: e.tensor_copy(out=B_[7][0:CL, 0:128], in_=psb[5][0:CL, 384:512]), pk(5, 3, 4), ["b7"])
                T(lambda e, sl=sl: e.matmul(psb[6][0:CL, 0:CL], lhsT=Vb[1][:, sl], rhs=Vb[0][:, sl], start=True, stop=True), ["V1", "V0"], pk(6, 0, 1))
                if CL == 1:
                    V(lambda e: e.tensor_copy(out=B_[8][0:CL, 0:CL], in_=psb[6][0:CL, 0:CL]), pk(6, 0, 1), ["b8"])
                else:
                    V(lambda e: e.tensor_tensor(out=B_[8][0:CL, 0:CL], in0=psb[6][0:CL, 0:CL], in1=A_[1][0:CL, 0:CL], op=ALU.mult), pk(6, 0, 1) + ["a1"], ["b8"])
                T(lambda e: e.matmul(psb[6][:, 128:128 + CL], lhsT=ones_f, rhs=A_[3][:, 0:CL], start=True, stop=True), ["a3", "cst"], pk(6, 1, 2))
                Ac(lambda e: e.activation(out=A_[10][:, 0:CL], in_=psb[6][:, 128:128 + CL], func=AF.Exp), pk(6, 1, 2), ["a10"])
                V(lambda e, sl=sl: e.tensor_tensor(out=B_[9][:, 0:CL], in0=Vb[0][:, sl], in1=A_[10][:, 0:CL], op=ALU.mult), ["V0", "a10"], ["b9"])
                T(lambda e: e.matmul(psb[7][0:CL, 0:128], lhsT=B_[9][:, 0:CL], rhs=B_[0][:, 0:128], start=True, stop=False), ["b9", "b0"], pk(7, 0, 1))
                T(lambda e: e.matmul(psb[7][0:CL, 0:128], lhsT=B_[8][0:CL, 0:CL], rhs=B_[7][0:CL, 0:128], start=False, stop=True), ["b8", "b7"], pk(7, 0, 1))
                T(lambda e: e.matmul(psb[7][:, 256:384], lhsT=B_[3][0:CL, 0:128], rhs=B_[7][0:CL, 0:128], start=True, stop=True), ["b3", "b7"], pk(7, 2, 3))
                V(lambda e, S=S: e.scalar_tensor_tensor(out=S, in0=S, scalar=A_[10][:, CL - 1:CL], in1=psb[7][:, 256:384], op0=ALU.mult, op1=ALU.add),
                  [Sk, "a10"] + pk(7, 2, 3), [Sk])
                if samp:
                    store_state(so_gdn_d[l, c, hd], S, Sk, f"sst{buf}")
                out_norm_T(l, CL, c, 7, 128, nrm1[:, 0:128], [4], [OB + hd], [f"scr{OB + hd}"])
            if not samp and state.get("ptile", 0) == NPT - 1:
                store_state(o_gdn_d[l, hd], Sa[:, l, hd, :], f"Sa{l}_{hd}", "ost")
        if samp:
            P.dma("sp", lambda e: e.dma_start(out=so_cva_d[l][:, :, 0:2, :], in_=shA[:, :, 1:3, :]), "shst", reads=["shA"], writes=["odram"])
            P.dma("sp", lambda e: e.dma_start(out=so_cva_d[l][:, :, 2, :], in_=xsA[:]), "shst", reads=["xsA"], writes=["odram"])
        elif state.get("ptile", 0) == NPT - 1:
            P.dma("sp", lambda e: e.dma_start(out=o_cva_d[l], in_=hista[:, l, :, :]), "ost", reads=[f"hista{l}"], writes=["odram"])
        merge_branch(l, nt, 0, wba_d[l], HA, OB, first=True)
        CL, nch = CL_all, nch_all

        proj(l, "lr", 4, nt)
        Ac(lambda e: e.activation(out=Vb[2][0:16, 0:nt], in_=psb[4][0:16, 0:nt], func=AF.Copy), pk(4), ["V2"])
        for hd in range(HB):
            T(lambda e, hd=hd: e.matmul(psb[4][:, 0:nt], lhsT=wgate[:, l, hd * 128:(hd + 1) * 128], rhs=Vb[2][0:16, 0:nt], start=True, stop=True),
              ["wgate", "V2"], pk(4))
            Ac(lambda e, hd=hd: e.activation(out=W[5][:, 0:nt], in_=psb[4][:, 0:nt], func=AF.Exp, scale=-1.0, bias=pv[:, l, PV_BG + hd:PV_BG + hd + 1]),
               pk(4) + ["pv"], ["W5"])
            Ac(lambda e: e.activation(out=W[5][:, 0:nt], in_=W[5][:, 0:nt], func=AF.Ln, bias=1.0), ["W5"], ["W5"])
            if CL > 1:
                for c in range(nch):
                    sl = slice(c * CL, (c + 1) * CL)
                    V(lambda e, sl=sl: e.tensor_tensor_scan(out=W[6][:, sl], data0=W[5][:, sl], data1=W[5][:, sl], initial=0.0, op0=ALU.add, op1=ALU.bypass),
                      ["W5"], ["W6"])
            else:
                V(lambda e: e.tensor_copy(out=W[6][:, 0:nt], in_=W[5][:, 0:nt]), ["W5"], ["W6"])
            Ac(lambda e: e.activation(out=W[7][:, 0:nt], in_=W[6][:, 0:nt], func=AF.Exp, scale=-1.0 / 16), ["W6"], ["W7"])
            Ac(lambda e: e.activation(out=W[8][:, 0:nt], in_=W[6][:, 0:nt], func=AF.Exp, scale=1.0 / 16), ["W6"], ["W8"])
            proj(l, f"bq{hd}", 0, nt)
            V(lambda e: e.scalar_tensor_tensor(out=Vb[0][:, 0:nt], in0=psb[0][:, 0:nt], scalar=128 ** -0.5, in1=W[7][:, 0:nt], op0=ALU.mult, op1=ALU.mult),
              pk(0) + ["W7"], ["V0"])
            proj(l, f"bk{hd}", 1, nt)
            Ac(lambda e: e.activation(out=W[1][:, 0:nt], in_=psb[1][:, 0:nt], func=AF.Copy), pk(1), ["W1"])
            V(lambda e: e.tensor_tensor(out=Vb[1][:, 0:nt], in0=W[1][:, 0:nt], in1=W[8][:, 0:nt], op=ALU.mult), ["W1", "W8"], ["V1"])
            for c in range(nch):
                sl = slice(c * CL, (c + 1) * CL)
                V(lambda e, c=c: e.tensor_scalar(out=cs1[:, 2:3], in0=W[6][:, (c + 1) * CL - 1:(c + 1) * CL], scalar1=-1.0 / 16, scalar2=None, op0=ALU.mult),
                  ["W6"], ["cs1c"])
                Ac(lambda e, sl=sl: e.activation(out=W[2][:, sl], in_=W[6][:, sl], func=AF.Exp, scale=1.0 / 16, bias=cs1[:, 2:3]), ["W6", "cs1c"], ["W2"])
            V(lambda e: e.tensor_tensor(out=W[2][:, 0:nt], in0=W[2][:, 0:nt], in1=W[1][:, 0:nt], op=ALU.mult), ["W2", "W1"], ["W2"])
            proj(l, f"bv{hd}_0", 2, nt)
            Ac(lambda e: e.activation(out=W[3][:, 0:nt], in_=psb[2][:, 0:nt], func=AF.Copy), pk(2), ["W3"])
            proj(l, f"bv{hd}_1", 3, nt)
            Ac(lambda e: e.activation(out=W[4][:, 0:nt], in_=psb[3][:, 0:nt], func=AF.Copy), pk(3), ["W4"])
            proj(l, f"br{hd}_0", 0, nt)
            Ac(lambda e: e.activation(out=W[1][:, 0:nt], in_=psb[0][:, 0:nt], func=AF.Silu), pk(0), ["W1"])
            proj(l, f"br{hd}_1", 1, nt)
            Ac(lambda e: e.activation(out=W[5][:, 0:nt], in_=psb[1][:, 0:nt], func=AF.Silu), pk(1), ["W5"])
            for c in range(nch):
                sl = slice(c * CL, (c + 1) * CL)
                if samp:
                    buf = c % 2
                    load_state(s_gla_d[l, c, hd], 256, buf)
                    S = SS[:, buf, 0:256]
                    Sk = f"SS{buf}"
                else:
                    S = Sb_[:, l, hd, :]
                    Sk = f"Sb{l}_{hd}"
                    if c == 0 and state.get("ptile", 0) == 0:
                        V(lambda e, S=S: e.memset(S, 0.0), [], [Sk])
                V(lambda e, S=S: e.tensor_copy(out=B_[0][:, 0:256], in_=S), [Sk], ["b0"])
                T(lambda e, sl=sl: e.transpose(out=psb[5][0:CL, 0:128], in_=W[3][:, sl], identity=ident), ["W3", "cst"], pk(5, 0, 1))
                T(lambda e, sl=sl: e.transpose(out=psb[5][0:CL, 128:256], in_=W[4][:, sl], identity=ident), ["W4", "cst"], pk(5, 1, 2))
                Ac(lambda e: e.activation(out=B_[1][0:CL, 0:256], in_=psb[5][0:CL, 0:256], func=AF.Copy), pk(5, 0, 2), ["b1"])
                T(lambda e, sl=sl: e.transpose(out=psb[5][0:CL, 256:384], in_=W[2][:, sl], identity=ident), ["W2", "cst"], pk(5, 2, 3))
                Ac(lambda e: e.activation(out=B_[3][0:CL, 0:128], in_=psb[5][0:CL, 256:384], func=AF.Copy), pk(5, 2, 3), ["b3"])
                T(lambda e, sl=sl: e.matmul(psb[6][0:CL, 0:CL], lhsT=Vb[1][:, sl], rhs=Vb[0][:, sl], start=True, stop=True), ["V1", "V0"], pk(6, 0, 1))
                V(lambda e: e.tensor_tensor(out=B_[8][0:CL, 0:CL], in0=psb[6][0:CL, 0:CL], in1=Umat[0:CL, 0:CL], op=ALU.mult), pk(6, 0, 1) + ["cst"], ["b8"])
                T(lambda e, sl=sl: e.matmul(psb[7][0:CL, 0:256], lhsT=Vb[0][:, sl], rhs=B_[0][:, 0:256], start=True, stop=False), ["V0", "b0"], pk(7, 0, 2))
                T(lambda e: e.matmul(psb[7][0:CL, 0:256], lhsT=B_[8][0:CL, 0:CL], rhs=B_[1][0:CL, 0:256], start=False, stop=True), ["b8", "b1"], pk(7, 0, 2))
                T(lambda e: e.matmul(psb[7][:, 256:512], lhsT=B_[3][0:CL, 0:128], rhs=B_[1][0:CL, 0:256], start=True, stop=True), ["b3", "b1"], pk(7, 2, 4))
                V(lambda e, S=S, c=c: e.scalar_tensor_tensor(out=S, in0=S, scalar=W[7][:, (c + 1) * CL - 1:(c + 1) * CL], in1=psb[7][:, 256:512],
                                                             op0=ALU.mult, op1=ALU.add), [Sk, "W7"] + pk(7, 2, 4), [Sk])
                if samp:
                    store_state(so_gla_d[l, c, hd], S, Sk, f"sst{buf}")
                out_norm_T(l, CL, c, 7, 256, nrm1[:, 128:384], [1, 5], [OB + 2 * hd, OB + 2 * hd + 1], [f"scr{OB + 2 * hd}", f"scr{OB + 2 * hd + 1}"])
            if not samp and state.get("ptile", 0) == NPT - 1:
                store_state(o_gla_d[l, hd], Sb_[:, l, hd, :], f"Sb{l}_{hd}", "ost")
        merge_branch(l, nt, 1, wbb_d[l], 2 * HB, OB, first=False)

        decay_setup(l, nt, CL, nch, "dt", 2, 3, HC, 8, True, 32)

        V(lambda e: e.memset(B_[10][:, 0:256], 0.0), [], ["b10"])
        V(lambda e: e.memset(B_[11][:, 0:256], 0.0), [], ["b11"])
        V(lambda e: e.memset(B_[12][:, 0:256], 0.0), [], ["b12"])
        for g in range(G):
            for (nm, cu, dstV, keepW) in ((f"cB{g}", NU_C + g, 3, None), (f"cC{g}", NU_C + G + g, 4, 5)):
                proj(l, nm, 0, nt)
                conv_unit(l, 0, nt, samp, cwc[:, l, cu, 0:4], "cwc", cwc[:, l, cu, 4:5], histc[:, l, cu, :], f"histc{l}", shC[:, cu, :, :], xsC[:, cu, :],
                          "shC", "xsC", 0, 1)
                V(lambda e, dstV=dstV: e.tensor_copy(out=Vb[dstV][:, 0:nt], in_=W[1][:, 0:nt]), ["W1"], [f"V{dstV}"])
                if keepW is not None:
                    V(lambda e: e.tensor_copy(out=W[5][:, 0:nt], in_=W[1][:, 0:nt]), ["W1"], ["W5"])
                else:
                    V(lambda e: e.tensor_copy(out=W[6][:, 0:nt], in_=W[1][:, 0:nt]), ["W1"], ["W6"])
            for uu in range(UPN):
                u = g * UPN + uu
                proj(l, f"cx{u}", 1, nt)
                conv_unit(l, 1, nt, samp, cwc[:, l, u, 0:4], "cwc", cwc[:, l, u, 4:5], histc[:, l, u, :], f"histc{l}", shC[:, u, :, :], xsC[:, u, :],
                          "shC", "xsC", 0, 2)
                proj(l, f"cz{u}", 2, nt)
                Ac(lambda e: e.activation(out=W[3][:, 0:nt], in_=psb[2][:, 0:nt], func=AF.Silu), pk(2), ["W3"])
                for c in range(nch):
                    sl = slice(c * CL, (c + 1) * CL)
                    tk = "tok"
                    if samp:
                        buf = c % 2
                        load_state(s_ssd_d[l, c, u], 128, buf)
                        S = SS[:, buf, 0:128]
                        Sk = f"SS{buf}"
                    else:
                        S = Sc[:, l, u, :]
                        Sk = f"Sc{l}_{u}"
                        if c == 0 and state.get("ptile", 0) == 0:
                            V(lambda e, S=S: e.memset(S, 0.0), [], [Sk])
                    T(lambda e, sl=sl: e.matmul(psb[5][0:CL, 0:CL], lhsT=Vb[3][:, sl], rhs=Vb[4][:, sl], start=True, stop=True), ["V3", "V4"], pk(5, 0, 1))
                    V(lambda e: e.tensor_copy(out=A_[11][0:CL, 0:CL], in_=psb[5][0:CL, 0:CL]), pk(5, 0, 1), ["a11"])
                    T(lambda e, sl=sl: e.transpose(out=psb[5][0:CL, 128:256], in_=W[6][:, sl], identity=ident), ["W6", "cst"], pk(5, 1, 2))
                    Ac(lambda e: e.activation(out=B_[2][0:CL, 0:128], in_=psb[5][0:CL, 128:256], func=AF.Copy), pk(5, 1, 2), ["b2"])
                    T(lambda e, sl=sl: e.transpose(out=psb[5][0:CL, 256:384], in_=W[2][:, sl], identity=ident), ["W2", "cst"], pk(5, 2, 3))
                    V(lambda e: e.tensor_copy(out=A_[12][0:CL, 0:128], in_=psb[5][0:CL, 256:384]), pk(5, 2, 3), ["a12"])
                    for hh in range(2):
                        hd = 2 * u + hh
                        hs = slice(hh * 64, hh * 64 + 64)
                        po = hh * 128
                        decay_mats(l, CL, c, hd, False)
                        if CL == 1:
                            V(lambda e: e.tensor_copy(out=B_[8][0:CL, 0:CL], in_=A_[11][0:CL, 0:CL]), ["a11"], ["b8"])
                        else:
                            V(lambda e: e.tensor_tensor(out=B_[8][0:CL, 0:CL], in0=A_[11][0:CL, 0:CL], in1=A_[1][0:CL, 0:CL], op=ALU.mult), ["a11", "a1"], ["b8"])
                        T(lambda e: e.matmul(psb[6][:, 128:128 + CL], lhsT=ones_f, rhs=A_[3][:, 0:CL], start=True, stop=True), ["a3", "cst"], pk(6, 1, 2))
                        Ac(lambda e: e.activation(out=A_[10][:, 0:CL], in_=psb[6][:, 128:128 + CL], func=AF.Exp), pk(6, 1, 2), ["a10"])
                        V(lambda e, sl=sl: e.tensor_tensor(out=B_[9][:, 0:CL], in0=W[5][:, sl], in1=A_[10][:, 0:CL], op=ALU.mult), ["W5", "a10"], ["b9"])
                        V(lambda e, hs=hs, po=po, c=c, hd=hd: e.tensor_scalar(out=B_[10][0:CL, po + hh_off(hs):po + hh_off(hs) + 64], in0=A_[12][0:CL, hs],
                                                                              scalar1=tok[0:CL, c, 32 + hd:33 + hd], scalar2=None, op0=ALU.mult),
                          ["a12", tk], ["b10"])
                        if CL == 1:
                            V(lambda e, hs=hs, po=po: e.tensor_copy(out=B_[12][0:CL, po + hh_off(hs):po + hh_off(hs) + 64],
                                                                    in_=B_[10][0:CL, po + hh_off(hs):po + hh_off(hs) + 64]), ["b10"], ["b12"])
                        else:
                            V(lambda e, hs=hs, po=po: e.tensor_scalar(out=B_[12][0:CL, po + hh_off(hs):po + hh_off(hs) + 64],
                                                                      in0=B_[10][0:CL, po + hh_off(hs):po + hh_off(hs) + 64],
                                                                      scalar1=cs1[0:CL, 0:1], scalar2=None, op0=ALU.mult), ["b10", "cs1"], ["b12"])
                        V(lambda e, hs=hs, po=po, S=S: e.tensor_copy(out=B_[11][:, po + hh_off(hs):po + hh_off(hs) + 64], in_=S[:, hs]), [Sk], ["b11"])
                        T(lambda e, po=po, hh=hh: e.matmul(psb[7][:, 0:CL], lhsT=B_[10][0:CL, po:po + 128], rhs=B_[8][0:CL, 0:CL], start=(hh == 0), stop=False),
                          ["b10", "b8"], pk(7, 0, 1))
                        T(lambda e, po=po, hh=hh: e.matmul(psb[7][:, 0:CL], lhsT=B_[11][:, po:po + 128], rhs=B_[9][:, 0:CL], start=False, stop=(hh == 1)),
                          ["b11", "b9"], pk(7, 0, 1))
                        T(lambda e, po=po, hh=hh: e.matmul(psb[2][:, 0:128], lhsT=B_[2][0:CL, 0:128], rhs=B_[12][0:CL, po:po + 128], start=(hh == 0), stop=(hh == 1)),
                          ["b2", "b12"], pk(2))
                        V(lambda e, hh=hh: e.tensor_copy(out=cs1[:, 5 + hh:6 + hh], in_=A_[10][:, CL - 1:CL]), ["a10"], ["cs1d"])
                    for hh in range(2):
                        hs = slice(hh * 64, hh * 64 + 64)
                        V(lambda e, hs=hs, hh=hh, S=S: e.scalar_tensor_tensor(out=S[:, hs], in0=S[:, hs], scalar=cs1[:, 5 + hh:6 + hh], in1=psb[2][:, hh * 64:hh * 64 + 64],
                                                                              op0=ALU.mult, op1=ALU.add), [Sk, "cs1d"] + pk(2), [Sk])
                    if samp:
                        store_state(so_ssd_d[l, c, u], S, Sk, f"sst{buf}")
                    V(lambda e, sl=sl, u=u: e.scalar_tensor_tensor(out=W[4][:, sl], in0=W[2][:, sl], scalar=pv[:, l, PV_D + u:PV_D + u + 1], in1=psb[7][:, 0:CL],
                                                                    op0=ALU.mult, op1=ALU.add), ["W2", "pv"] + pk(7, 0, 1), ["W4"])
                    V(lambda e, sl=sl: e.tensor_tensor(out=W[4][:, sl], in0=W[4][:, sl], in1=W[3][:, sl], op=ALU.mult), ["W4", "W3"], ["W4"])
                if not samp and state.get("ptile", 0) == NPT - 1:
                    store_state(o_ssd_d[l, u], Sc[:, l, u, :], f"Sc{l}_{u}", "ost")
                V(lambda e, u=u: e.tensor_copy(out=scr[:, OB + u, 0:nt], in_=W[4][:, 0:nt]), ["W4"], [f"scr{OB + u}"])
                V(lambda e, uu=uu: e.tensor_tensor(out=Vb[5][:, 0:nt], in0=W[4][:, 0:nt], in1=W[4][:, 0:nt], op=ALU.mult), ["W4"], ["V5"])
                T(lambda e, uu=uu: e.matmul(psb[3][:, 0:nt], lhsT=ones_b[:], rhs=Vb[5][:, 0:nt], start=(uu == 0), stop=(uu == UPN - 1)), ["V5", "ones_b"], pk(3))
            Ac(lambda e: e.activation(out=W[1][:, 0:nt], in_=psb[3][:, 0:nt], func=AF.Ln, scale=1.0 / (UPN * 128), bias=EPS), pk(3), ["W1"])
            Ac(lambda e: e.activation(out=W[1][:, 0:nt], in_=W[1][:, 0:nt], func=AF.Exp, scale=-0.5), ["W1"], ["W1"])
            for uu in range(UPN):
                u = g * UPN + uu
                V(lambda e, uu=uu, u=u: e.scalar_tensor_tensor(out=scr[:, OB + u, 0:nt], in0=scr[:, OB + u, 0:nt], scalar=pv[:, l, PV_NW + u:PV_NW + u + 1], in1=W[1][:, 0:nt],
                                                                op0=ALU.mult, op1=ALU.mult), [f"scr{OB + u}", "pv", "W1"], [f"scr{OB + u}"])
        if samp:
            P.dma("sp", lambda e: e.dma_start(out=so_cvc_d[l][:, :, 0:2, :], in_=shC[:, :, 1:3, :]), "shst", reads=["shC"], writes=["odram"])
            P.dma("sp", lambda e: e.dma_start(out=so_cvc_d[l][:, :, 2, :], in_=xsC[:]), "shst", reads=["xsC"], writes=["odram"])
        elif state.get("ptile", 0) == NPT - 1:
            P.dma("sp", lambda e: e.dma_start(out=o_cvc_d[l], in_=histc[:, l, :, :]), "ost", reads=[f"histc{l}"], writes=["odram"])
        merge_branch(l, nt, 2, wbc_d[l], NU_C, OB, first=False)

        for dc in range(KC):
            s = ring_load(wo_d[l][dc])
            b = dc % 2
            for kc in range(KC):
                T(lambda e, s=s, kc=kc, b=b: e.matmul(psb[b][:, 0:nt], lhsT=ring[:, s, kc, :], rhs=scr[:, kc, 0:nt], start=(kc == 0), stop=(kc == KC - 1)),
                  [f"ring{s}", f"scr{kc}"], pk(b))
            resid_add(nt, samp, dc, b, Gt, lname)

    def hh_off(hs):
        return hs.start

    def merge_branch(l, nt, br, w_d, nk, OB, first):
        for dc in range(KC):
            b = dc % 2
            proj(l, f"gate{br}_{dc}", 2 + b, nt)
            t = nexttmp()
            Ac(lambda e, b=b, t=t: e.activation(out=tmp[:, t, 0:nt], in_=psb[2 + b][:, 0:nt], func=AF.Sigmoid), pk(2 + b), [f"tmp{t}"])
            s = ring_load(w_d[dc], nk)
            for kc in range(nk):
                T(lambda e, s=s, kc=kc, b=b: e.matmul(psb[b][:, 0:nt], lhsT=ring[:, s, kc, :], rhs=scr[:, OB + kc, 0:nt], start=(kc == 0), stop=(kc == nk - 1)),
                  [f"ring{s}", f"scr{OB + kc}"], pk(b))
            if first:
                V(lambda e, b=b, t=t, dc=dc: e.tensor_tensor(out=scr[:, dc, 0:nt], in0=tmp[:, t, 0:nt], in1=psb[b][:, 0:nt], op=ALU.mult),
                  [f"tmp{t}"] + pk(b), [f"scr{dc}"])
            else:
                V(lambda e, b=b, t=t: e.tensor_tensor(out=tmp[:, t, 0:nt], in0=tmp[:, t, 0:nt], in1=psb[b][:, 0:nt], op=ALU.mult),
                  [f"tmp{t}"] + pk(b), [f"tmp{t}"])
                V(lambda e, t=t, dc=dc: e.tensor_tensor(out=scr[:, dc, 0:nt], in0=scr[:, dc, 0:nt], in1=tmp[:, t, 0:nt], op=ALU.add),
                  [f"tmp{t}", f"scr{dc}"], [f"scr{dc}"])


    tiles = [("p", i) for i in range(NPT)] + [("s", 0)]
    xkeys = [f"x{kc}" for kc in range(KC)]
    V(lambda e: e.memset(hista[:], 0.0), [], ["hista0", "hista1"])
    V(lambda e: e.memset(histc[:], 0.0), [], ["histc0", "histc1"])
    for (kind, ti) in tiles:
        samp = kind == "s"
        nt = NS if samp else NT
        state["ptile"] = ti
        src = xs_d if samp else xp_d[:, :, ti * NT:(ti + 1) * NT]
        P.dma("sp", lambda e, src=src, nt=nt: e.dma_start(out=x[:, :, 0:nt], in_=src), "xload", writes=xkeys)
        for l in range(2):
            ffn(l, 1, nt, samp)
            mixer(l, nt, samp)
            ffn(l, 2, nt, samp)
        dst = ys_d if samp else yp_d[:, :, ti * NT:(ti + 1) * NT]
        rms_stats(nt)
        fn_w = norms[:, 6, :]
        for kc in range(KC):
            V(lambda e, kc=kc, nt=nt: e.scalar_tensor_tensor(out=x[:, kc, 0:nt], in0=x[:, kc, 0:nt], scalar=fn_w[:, kc:kc + 1],
                                                            in1=rstd[:, 0:nt], op0=ALU.mult, op1=ALU.mult),
              [f"x{kc}", "rstd", "norms"], [f"x{kc}"])
        P.dma("sp", lambda e, dst=dst, nt=nt: e.dma_start(out=dst, in_=x[:, :, 0:nt]), "ystore", reads=xkeys, writes=["ydram"])

    P.emit(nc, es)
    es.close()
    return nc


def _units(w):
    K, N = w.shape
    return np.ascontiguousarray(w.reshape(K // 128, 128, N // 128, 128).transpose(2, 1, 0, 3))


def _fm(v):
    n, Fd = v.shape
    return np.ascontiguousarray(v.reshape(n, Fd // 128, 128).transpose(2, 1, 0))


def _consts():
    c = np.zeros((128, 640), np.float32)
    c[:, 0:128] = np.eye(128)
    c[:, 128:256] = np.triu(np.ones((128, 128)))
    c[:, 256:384] = np.tril(np.ones((128, 128)), -1)
    c[:, 384:512] = 1.0
    c[0:64, 512:576] = 1.0
    c[64:128, 576:640] = 1.0
    return c


def _padcols(w, n=128):
    K, c = w.shape
    out = np.zeros((K, n), np.float32)
    out[:, :c] = w
    return out


_NC = {}


def kernel(_cfg=None, **inp):
    cfg = dict(FULL if _cfg is None else _cfg)
    D, FF, SEQ, HA, HB, HC, G = (cfg[k] for k in ("D", "FF", "SEQ", "HA", "HB", "HC", "G"))
    KC = D // 128
    NU_C = HC // 2
    NCA, NCC = 3 * HA, NU_C + 2 * G
    f = lambda a: np.asarray(a, np.float32)
    shared = {"consts": _consts()}
    shared["wada"] = np.stack([_units(f(inp["w_ada"][l])) for l in range(2)])
    shared["bada"] = np.stack([np.ascontiguousarray(f(inp["b_ada"][l]).reshape(9 * KC, 128).T) for l in range(2)])
    nl = [f(inp[k][l]) for l in range(2) for k in ("norm1", "norm2", "norm3")] + [f(inp["final_norm"])]
    shared["norms"] = np.ascontiguousarray(np.stack(nl).reshape(7, KC, 128).transpose(2, 0, 1))
    for l in range(2):
        for w in (1, 2):
            shared[f"wg{l}{w}"] = _units(f(inp[f"ffn{w}_wg"][l]))
            shared[f"wu{l}{w}"] = _units(f(inp[f"ffn{w}_wu"][l]))
            shared[f"wd{l}{w}"] = _units(f(inp[f"ffn{w}_wd"][l]))
    QK_A, V_A = HA * 128, HA * 128
    splits = [2 * QK_A + V_A, V_A, HA, HA, HB * 128, HB * 128, HB * 256, 16, HB * 256, HC * 64, HC * 64 + 2 * G * 128, HC, 3 * D]
    offs = np.concatenate([[0], np.cumsum(splits)])
    UP = unit_plan(cfg)
    NPV = 4 + HB + 2 * NU_C + 32
    pvs, cwas, cwcs, nrms, wgates = [], [], [], [], []
    for l in range(2):
        w = f(inp["w_in"][l])
        grp = [w[:, offs[i]:offs[i + 1]] for i in range(13)]
        qkv_a, z_a, beta_a, dec_a, q_b, k_b, v_b, lr_b, r_b, z_c, xbc_c, dt_c, gates = grp
        cols = {}
        cols["beta"] = _padcols(beta_a)
        cols["dec"] = _padcols(dec_a)
        for h in range(HA):
            cols[f"aq{h}"] = qkv_a[:, h * 128:(h + 1) * 128]
            cols[f"ak{h}"] = qkv_a[:, QK_A + h * 128:QK_A + (h + 1) * 128]
            cols[f"av{h}"] = qkv_a[:, 2 * QK_A + h * 128:2 * QK_A + (h + 1) * 128]
            cols[f"az{h}"] = z_a[:, h * 128:(h + 1) * 128]
        cols["lr"] = _padcols(lr_b)
        for h in range(HB):
            cols[f"bq{h}"] = q_b[:, h * 128:(h + 1) * 128]
            cols[f"bk{h}"] = k_b[:, h * 128:(h + 1) * 128]
            for i in range(2):
                cols[f"bv{h}_{i}"] = v_b[:, h * 256 + i * 128:h * 256 + (i + 1) * 128]
                cols[f"br{h}_{i}"] = r_b[:, h * 256 + i * 128:h * 256 + (i + 1) * 128]
        cols["dt"] = _padcols(dt_c)
        inner = HC * 64
        for g in range(G):
            cols[f"cB{g}"] = xbc_c[:, inner + g * 128:inner + (g + 1) * 128]
            cols[f"cC{g}"] = xbc_c[:, inner + G * 128 + g * 128:inner + G * 128 + (g + 1) * 128]
        for u in range(NU_C):
            cols[f"cx{u}"] = xbc_c[:, u * 128:(u + 1) * 128]
            cols[f"cz{u}"] = z_c[:, u * 128:(u + 1) * 128]
        for br in range(3):
            for dc in range(KC):
                cols[f"gate{br}_{dc}"] = gates[:, br * D + dc * 128:br * D + (dc + 1) * 128]
        arr = np.zeros((len(UP), 128, KC, 128), np.float32)
        for n, i in UP.items():
            arr[i] = cols[n].reshape(KC, 128, 128).transpose(1, 0, 2)
        shared[f"win{l}"] = arr
        shared[f"wba{l}"] = _units(f(inp["w_branch_gdn"][l]))
        shared[f"wbb{l}"] = _units(f(inp["w_branch_gla"][l]))
        shared[f"wbc{l}"] = _units(f(inp["w_branch_ssd"][l]))
        shared[f"wo{l}"] = _units(f(inp["w_out"][l]))
        pvl = np.zeros((128, NPV), np.float32)
        pvl[:HA, 0] = f(inp["gdn_a_log"][l])
        pvl[:HA, 1] = f(inp["gdn_dt_bias"][l])
        pvl[:HC, 2] = f(inp["ssd_a_log"][l])
        pvl[:HC, 3] = f(inp["ssd_dt_bias"][l])
        pvl[:, 4:4 + HB] = f(inp["gla_b_gate"][l]).reshape(HB, 128).T
        pvl[:, 4 + HB:4 + HB + NU_C] = np.repeat(f(inp["ssd_d"][l]).reshape(NU_C, 2), 64, axis=1).T
        pvl[:, 4 + HB + NU_C:4 + HB + 2 * NU_C] = f(inp["ssd_norm_w"][l]).reshape(NU_C, 128).T
        pvl[:32, 4 + HB + 2 * NU_C:] = np.eye(32)
        pvs.append(pvl)
        cwas.append(np.ascontiguousarray(f(inp["gdn_conv_w"][l]).reshape(4, NCA, 128).transpose(2, 1, 0)))
        cw = f(inp["ssd_conv_w"][l])
        cb = f(inp["ssd_conv_b"][l])
        cwc = np.concatenate([cw.reshape(4, NCC, 128), cb.reshape(1, NCC, 128)], 0)
        cwcs.append(np.ascontiguousarray(cwc.transpose(2, 1, 0)))
        nrms.append(np.concatenate([np.tile(f(inp["gdn_norm_w"][l])[None], (128, 1)), np.tile(f(inp["gla_norm_w"][l])[None], (128, 1))], 1))
        wgates.append(f(inp["gla_w_gate"][l]))
    shared["pv"] = np.stack(pvs)
    shared["cwa"] = np.stack(cwas)
    shared["cwc"] = np.stack(cwcs)
    shared["nrm"] = np.ascontiguousarray(np.stack(nrms))
    shared["wgate"] = np.stack(wgates)

    xp = f(inp["x_prompt"])
    xs = f(inp["x_sample"])[:, 0, :]
    cp = f(inp["c_prompt"])
    cs = f(inp["c_sample"])
    sg, sl_, ss, sca, scc = (f(inp[k]) for k in ("state_gdn", "state_gla", "state_ssd", "state_gdn_conv", "state_ssd_conv"))
    in_maps = []
    for c in range(8):
        m = dict(shared)
        b0 = 16 * c
        m["xp"] = _fm(xp[c % 4])
        m["xs"] = _fm(xs[b0:b0 + 16])
        m["cT"] = _fm(np.concatenate([cp[c % 4][None], cs[b0:b0 + 16]], 0))
        m["s_gdn"] = np.ascontiguousarray(sg[:, b0:b0 + 16])
        m["s_gla"] = np.ascontiguousarray(sl_[:, b0:b0 + 16])
        m["s_ssd"] = np.ascontiguousarray(ss[:, b0:b0 + 16].reshape(2, 16, NU_C, 2, 64, 128).transpose(0, 1, 2, 5, 3, 4).reshape(2, 16, NU_C, 128, 128))
        m["s_cva"] = np.ascontiguousarray(sca[:, b0:b0 + 16].reshape(2, 16, 3, NCA, 128).transpose(0, 4, 3, 2, 1))
        m["s_cvc"] = np.ascontiguousarray(scc[:, b0:b0 + 16].reshape(2, 16, 3, NCC, 128).transpose(0, 4, 3, 2, 1))
        in_maps.append(m)
    key = tuple(sorted(cfg.items()))
    if key not in _NC:
        _NC[key] = build(cfg)
    res = run_bass_kernel_spmd(_NC[key], in_maps, core_ids=list(range(8)))
    R = res.results
    y_prompt = np.stack([R[c]["yp"].transpose(2, 1, 0).reshape(SEQ, D) for c in range(4)])
    y_sample = np.concatenate([R[c]["ys"].transpose(2, 1, 0).reshape(NS, D) for c in range(8)])[:, None, :]

    def conv_p(name, ncu):
        return np.stack([R[c][name].transpose(0, 3, 2, 1).reshape(2, 3, ncu * 128) for c in range(4)], 1)

    def conv_s(name, ncu):
        return np.concatenate([R[c][name].transpose(0, 4, 3, 2, 1).reshape(2, 16, 3, ncu * 128) for c in range(8)], 1)

    def ssd_back(a):
        sh = a.shape[:-3]
        return a.reshape(*sh, NU_C, 128, 2, 64).transpose(*range(len(sh)), len(sh), len(sh) + 2, len(sh) + 3, len(sh) + 1).reshape(*sh, HC, 64, 128)
    p_gdn_conv = conv_p("o_cva", NCA)
    p_gdn = np.stack([R[c]["o_gdn"] for c in range(4)], 1)
    p_gla = np.stack([R[c]["o_gla"] for c in range(4)], 1)
    p_ssd_conv = conv_p("o_cvc", NCC)
    p_ssd = np.stack([ssd_back(R[c]["o_ssd"]) for c in range(4)], 1)
    s_gdn_conv = conv_s("so_cva", NCA)
    s_gdn = np.concatenate([R[c]["so_gdn"] for c in range(8)], 1)
    s_gla = np.concatenate([R[c]["so_gla"] for c in range(8)], 1)
    s_ssd_conv = conv_s("so_cvc", NCC)
    s_ssd = np.concatenate([ssd_back(R[c]["so_ssd"]) for c in range(8)], 1)
    outs = (y_prompt, y_sample, p_gdn_conv, p_gdn, p_gla, p_ssd_conv, p_ssd, s_gdn_conv, s_gdn, s_gla, s_ssd_conv, s_ssd)
    return tuple(np.ascontiguousarray(o, dtype=np.float32) for o in outs)
```

```python
import numpy as np
from contextlib import ExitStack
import concourse.bass as bass
import concourse.mybir as mybir
from concourse.bass_utils import run_bass_kernel_spmd

F32, BF16 = mybir.dt.float32, mybir.dt.bfloat16
AF = mybir.ActivationFunctionType
ALU = mybir.AluOpType
AX = mybir.AxisListType

FULL = dict(D=2048, FF=5504, SEQ=2048, HA=8, HB=4, HC=32, G=4)
NT = 512
NS = 16
NV = 17
EPS = 1e-6
RS = 3


class _Rec:
    def __init__(self):
        self.call = None

    def __getattr__(self, name):
        def f(*a, **k):
            self.call = (name, a, k)
            return self
        return f


def _bind(fn):
    r = _Rec()
    fn(r)
    name, a, k = r.call
    return lambda eng: getattr(eng, name)(*a, **k)


class Prog:
    ENG = ("pe", "act", "dve", "pool", "sp")

    def __init__(self):
        self.ops = {e: [] for e in self.ENG}
        self.cnt = {e: 0 for e in ("pe", "act", "dve")}
        self.lastw = {}
        self.readers = {}
        self.dcnt = {}
        self.known = {e: {} for e in self.ENG}

    def _deps(self, eng, reads, writes):
        need = {}

        def add(t, raw):
            s, v, e = t
            if e == eng and eng == "pe":
                return
            if need.get(s, 0) < v:
                need[s] = v
        for k in reads:
            if k in self.lastw:
                add(self.lastw[k], True)
        for k in writes:
            if k in self.lastw:
                add(self.lastw[k], True)
            for r in self.readers.get(k, ()):
                add(r, False)
        kn = self.known[eng]
        out = []
        for s, v in need.items():
            if kn.get(s, 0) < v:
                kn[s] = v
                out.append((s, v))
        return out

    def _commit(self, tok, reads, writes):
        for k in reads:
            self.readers.setdefault(k, []).append(tok)
        for k in writes:
            self.lastw[k] = tok
            self.readers[k] = []

    def op(self, eng, fn, reads=(), writes=()):
        writes = list(writes) + [k for k in reads if k.startswith("ps")]
        reads = [k for k in reads if not k.startswith("ps")]
        waits = self._deps(eng, reads, writes)
        self.cnt[eng] += 1
        s = "c_" + eng
        self.ops[eng].append((waits, _bind(fn), s, 1))
        self._commit((s, self.cnt[eng], eng), reads, writes)

    def dma(self, q, fn, semkey, reads=(), writes=()):
        waits = self._deps(q, reads, writes)
        n = self.dcnt.get(semkey, 0)
        s = "d_" + semkey
        if n > 0 and self.known[q].get(s, 0) < 16 * n:
            self.known[q][s] = 16 * n
            waits.append((s, 16 * n))
        self.dcnt[semkey] = n + 1
        self.ops[q].append((waits, _bind(fn), s, 16))
        self._commit((s, 16 * (n + 1), "dma"), reads, writes)

    def emit(self, nc, es):
        names = ["c_pe", "c_act", "c_dve"] + ["d_" + k for k in self.dcnt]
        sems = {n: es.enter_context(nc.semaphore(n)) for n in names}
        finals = [(sems["d_" + k], 16 * n) for k, n in self.dcnt.items()]
        block = es.enter_context(nc.Block())
        decs = {"pe": block.tensor, "act": block.scalar, "dve": block.vector, "pool": block.gpsimd, "sp": block.sync}
        for e in self.ENG:
            ops = self.ops[e]

            def body(eng, ops=ops, last=(e == "sp")):
                for waits, fn, s, inc in ops:
                    for ws, wv in waits:
                        eng.wait_ge(sems[ws], wv)
                    fn(eng).then_inc(sems[s], inc)
                if last:
                    for sm, v in finals:
                        eng.wait_ge(sm, v)
            decs[e](body)


def unit_plan(cfg):
    HA, HB, HC, G, KC = cfg["HA"], cfg["HB"], cfg["HC"], cfg["G"], cfg["D"] // 128
    names = ["beta", "dec"]
    for h in range(HA):
        names += [f"aq{h}", f"ak{h}", f"av{h}", f"az{h}"]
    names += ["lr"]
    for h in range(HB):
        names += [f"bq{h}", f"bk{h}", f"bv{h}_0", f"bv{h}_1", f"br{h}_0", f"br{h}_1"]
    names += ["dt"]
    for g in range(G):
        names += [f"cB{g}", f"cC{g}"]
    for u in range(HC // 2):
        names += [f"cx{u}", f"cz{u}"]
    for br in range(3):
        for dc in range(KC):
            names += [f"gate{br}_{dc}"]
    return {n: i for i, n in enumerate(names)}


def build(cfg):
    D, FF, SEQ, HA, HB, HC, G = (cfg[k] for k in ("D", "FF", "SEQ", "HA", "HB", "HC", "G"))
    KC = D // 128
    FC = FF // 128
    FH = (FC + 2) // 3
    NU_C = HC // 2
    HPG = HC // G
    UPN = NU_C // G
    NCA = 3 * HA
    NCC = NU_C + 2 * G
    NPT = SEQ // NT
    UP = unit_plan(cfg)
    NUW = len(UP)
    assert KC <= 16 and NU_C <= 16 and HC <= 32

    nc = bass.Bass("TRN2", target_bir_lowering=False)
    P = Prog()
    es = ExitStack()

    def din(name, shape):
        return nc.dram_tensor(name, list(shape), F32, kind="ExternalInput").ap()

    def dout(name, shape):
        return nc.dram_tensor(name, list(shape), F32, kind="ExternalOutput").ap()

    def sb(name, shape, dt=F32):
        return es.enter_context(nc.sbuf_tensor(name, list(shape), dt))

    xp_d = din("xp", [128, KC, SEQ])
    xs_d = din("xs", [128, KC, NS])
    cT_d = din("cT", [128, KC, NV])
    consts_d = din("consts", [128, 640])
    wada_d = din("wada", [2, 9 * KC, 128, KC, 128])
    bada_d = din("bada", [2, 128, 9 * KC])
    norms_d = din("norms", [128, 7, KC])
    ffn_d = {}
    for l in range(2):
        for w in (1, 2):
            ffn_d[(l, w)] = (din(f"wg{l}{w}", [FC, 128, KC, 128]), din(f"wu{l}{w}", [FC, 128, KC, 128]),
                             din(f"wd{l}{w}", [KC, 128, FC, 128]))
    win_d = [din(f"win{l}", [NUW, 128, KC, 128]) for l in range(2)]
    wba_d = [din(f"wba{l}", [KC, 128, HA, 128]) for l in range(2)]
    wbb_d = [din(f"wbb{l}", [KC, 128, 2 * HB, 128]) for l in range(2)]
    wbc_d = [din(f"wbc{l}", [KC, 128, NU_C, 128]) for l in range(2)]
    wo_d = [din(f"wo{l}", [KC, 128, KC, 128]) for l in range(2)]
    NPV = 4 + HB + 2 * NU_C + 32
    pv_d = din("pv", [2, 128, NPV])
    cwa_d = din("cwa", [2, 128, NCA, 4])
    cwc_d = din("cwc", [2, 128, NCC, 5])
    nrm_d = din("nrm", [2, 128, 384])
    wgate_d = din("wgate", [2, 16, HB * 128])
    s_gdn_d = din("s_gdn", [2, NS, HA, 128, 128])
    s_gla_d = din("s_gla", [2, NS, HB, 128, 256])
    s_ssd_d = din("s_ssd", [2, NS, NU_C, 128, 128])
    s_cva_d = din("s_cva", [2, 128, NCA, 3, NS])
    s_cvc_d = din("s_cvc", [2, 128, NCC, 3, NS])
    yp_d = dout("yp", [128, KC, SEQ])
    ys_d = dout("ys", [128, KC, NS])
    o_gdn_d = dout("o_gdn", [2, HA, 128, 128])
    o_gla_d = dout("o_gla", [2, HB, 128, 256])
    o_ssd_d = dout("o_ssd", [2, NU_C, 128, 128])
    o_cva_d = dout("o_cva", [2, 128, NCA, 3])
    o_cvc_d = dout("o_cvc", [2, 128, NCC, 3])
    so_gdn_d = dout("so_gdn", [2, NS, HA, 128, 128])
    so_gla_d = dout("so_gla", [2, NS, HB, 128, 256])
    so_ssd_d = dout("so_ssd", [2, NS, NU_C, 128, 128])
    so_cva_d = dout("so_cva", [2, 128, NCA, 3, NS])
    so_cvc_d = dout("so_cvc", [2, 128, NCC, 3, NS])

    x = sb("x", [128, KC, NT])
    h = sb("h", [128, KC, NT], BF16)
    scr = sb("scr", [128, 32, NT], BF16)
    ring = sb("ring", [128, RS, 16, 128], BF16)
    wdr = sb("wdr", [128, 2, FH, 128], BF16)
    mod = [sb(f"mod{l}", [128, 9 * KC, NV]) for l in range(2)]
    cst = sb("cst", [128, 640])
    ones_b = sb("ones_b", [128, 128], BF16)
    norms = sb("norms_sb", [128, 7, KC])
    scT = sb("scT", [128, KC, NV], BF16)
    tmp = sb("tmp", [128, 2, NT])
    rstd = sb("rstd", [128, NT])
    pv = sb("pv_sb", [128, 2, NPV])
    cwa = sb("cwa_sb", [128, 2, NCA, 4])
    cwc = sb("cwc_sb", [128, 2, NCC, 5])
    nrm1 = sb("nrm_sb", [128, 384])
    wgate = sb("wgate_b", [16, 2, HB * 128], BF16)
    nSa, nSb, nSc = 2 * HA * 128, 2 * HB * 256, 2 * NU_C * 128
    nA, nC = NCA * 3 * NS, NCC * 3 * NS
    PSM = sb("PSM", [128, max(nSa + nSb + nSc, nA + nC + NCA * NS + NCC * NS + 512)])
    Sa = PSM[:, 0:nSa].rearrange("p (l h d) -> p l h d", l=2, h=HA)
    Sb_ = PSM[:, nSa:nSa + nSb].rearrange("p (l h d) -> p l h d", l=2, h=HB)
    Sc = PSM[:, nSa + nSb:nSa + nSb + nSc].rearrange("p (l h d) -> p l h d", l=2, h=NU_C)
    shA = PSM[:, 0:nA].rearrange("p (u j b) -> p u j b", u=NCA, j=3)
    shC = PSM[:, nA:nA + nC].rearrange("p (u j b) -> p u j b", u=NCC, j=3)
    o2 = nA + nC
    xsA = PSM[:, o2:o2 + NCA * NS].rearrange("p (u b) -> p u b", u=NCA)
    xsC = PSM[:, o2 + NCA * NS:o2 + NCA * NS + NCC * NS].rearrange("p (u b) -> p u b", u=NCC)
    o3 = o2 + NCA * NS + NCC * NS
    SS = PSM[:, o3:o3 + 512].rearrange("p (s d) -> p s d", s=2)
    PKEYS = [f"Sa{l}_{i}" for l in range(2) for i in range(HA)] + [f"Sb{l}_{i}" for l in range(2) for i in range(HB)] + \
            [f"Sc{l}_{i}" for l in range(2) for i in range(NU_C)]
    hista = sb("hista", [128, 2, NCA, 3])
    histc = sb("histc", [128, 2, NCC, 3])
    Wall = sb("Wall", [128, 9, NT + 4])
    W = [Wall[:, i, :] for i in range(9)]
    Wflat = Wall[:].rearrange("p a b -> p (a b)")
    XR = 4
    xring = [Wflat[:, 2 * k * (NT + 4):2 * k * (NT + 4) + 1024].bitcast(BF16).rearrange("p (k c) -> p k c", k=16) for k in range(XR)]
    Vb = [sb(f"V{i}", [128, NT], BF16) for i in range(6)]
    A_ = [sb(f"a{i}", [128, 256 if i == 4 else 128]) for i in range(13)]
    B_ = [sb(f"b{i}", [128, 256 if i in (0, 1, 10, 11, 12) else 128], BF16) for i in range(13)]
    tok = sb("tokv", [128, 16, 64])
    cs1 = sb("cs1", [128, 8])
    psb = [es.enter_context(nc.psum_tensor(f"ps{i}", [128, NT], F32)) for i in range(8)]
    ident = cst[:, 0:128]
    Umat = cst[:, 128:256]
    Lsmat = cst[:, 256:384]
    ones_f = cst[:, 384:512]
    BDmat = cst[:, 512:640]

    def pk(b, q0=0, q1=4):
        return [f"ps{b}"]

    state = {"ring": 0, "wd": 0, "tmp": 0}

    def ring_load(src_ap, nk=KC):
        s = state["ring"] % RS
        state["ring"] += 1
        P.dma("pool", lambda e, s=s: e.dma_start(out=ring[:, s, 0:nk, :], in_=src_ap), f"ring{s}", writes=[f"ring{s}"])
        return s

    def rslot(s):
        return ring[:, s] if s < RS else xring[s - RS]

    def rkeys(s):
        return [f"ring{s}"] if s < RS else [f"W{2 * (s - RS)}", f"W{2 * (s - RS) + 1}"]

    def ring_load_wide(src_ap):
        s = state.get("ringw", 0) % (RS + XR)
        state["ringw"] = state.get("ringw", 0) + 1
        P.dma("pool", lambda e, s=s: e.dma_start(out=rslot(s)[:, 0:KC, :], in_=src_ap), f"ring{s}", writes=rkeys(s))
        return s

    def nexttmp():
        i = state["tmp"] % 2
        state["tmp"] += 1
        return i

    def V(fn, r, w):
        P.op("dve", fn, r, w)

    def Ac(fn, r, w):
        P.op("act", fn, r, w)

    def T(fn, r, w):
        P.op("pe", fn, r, w)

    def ld(dst, src, key):
        P.dma("sp", lambda e: e.dma_start(out=dst, in_=src), key, writes=[key])
    ld(cst[:], consts_d, "cst")
    ld(norms[:], norms_d, "norms")
    bada = W[1][:, 0:2 * 9 * KC].rearrange("p (l u) -> p l u", l=2)
    cT = W[0][:, 0:KC * NV].rearrange("p (k v) -> p k v", k=KC)
    P.dma("sp", lambda e: e.dma_start(out=bada, in_=bada_d.rearrange("l p u -> p l u")), "bada", writes=["W1"])
    P.dma("sp", lambda e: e.dma_start(out=cT, in_=cT_d), "cT", writes=["W0"])
    ld(pv[:], pv_d.rearrange("l p u -> p l u"), "pv")
    ld(cwa[:], cwa_d.rearrange("l p u j -> p l u j"), "cwa")
    ld(cwc[:], cwc_d.rearrange("l p u j -> p l u j"), "cwc")
    P.dma("pool", lambda e: e.dma_start(out=wgate[:], in_=wgate_d.rearrange("l p u -> p l u")), "wgate", writes=["wgate"])
    V(lambda e: e.tensor_copy(out=ones_b[:], in_=cst[:, 384:512]), ["cst"], ["ones_b"])
    Ac(lambda e: e.activation(out=scT[:], in_=cT, func=AF.Silu), ["W0"], ["scT"])
    for l in range(2):
        for c in (0, 2):
            Ac(lambda e, l=l, c=c: e.activation(out=pv[:, l, c:c + 1], in_=pv[:, l, c:c + 1], func=AF.Exp), ["pv"], ["pv"])
            V(lambda e, l=l, c=c: e.tensor_scalar(out=pv[:, l, c:c + 1], in0=pv[:, l, c:c + 1], scalar1=-1.0, scalar2=None, op0=ALU.mult), ["pv"], ["pv"])
        V(lambda e, l=l: e.tensor_scalar(out=pv[:, l, 4:4 + HB], in0=pv[:, l, 4:4 + HB], scalar1=-1.0, scalar2=None, op0=ALU.mult), ["pv"], ["pv"])
    PV_BG, PV_D, PV_NW, PV_OH = 4, 4 + HB, 4 + HB + NU_C, 4 + HB + 2 * NU_C

    for l in range(2):
        for u in range(9 * KC):
            s = ring_load(wada_d[l, u])
            pb = u % 2
            for kc in range(KC):
                T(lambda e, s=s, kc=kc, pb=pb: e.matmul(psb[pb][:, 0:NV], lhsT=ring[:, s, kc, :], rhs=scT[:, kc, :],
                                                         start=(kc == 0), stop=(kc == KC - 1)),
                  [f"ring{s}", "scT"], pk(pb))
            V(lambda e, l=l, u=u, pb=pb: e.tensor_scalar(out=mod[l][:, u, :], in0=psb[pb][:, 0:NV], scalar1=bada[:, l, u:u + 1],
                                                          scalar2=None, op0=ALU.add),
              pk(pb) + ["W1"], [f"mod{l}"])
        for m in range(3):
            sc = mod[l][:, (3 * m + 1) * KC:(3 * m + 2) * KC, :]
            gt = mod[l][:, (3 * m + 2) * KC:(3 * m + 3) * KC, :]
            nb = norms[:, 3 * l + m, :].unsqueeze(2).to_broadcast([128, KC, NV])
            V(lambda e, sc=sc, nb=nb: e.scalar_tensor_tensor(out=sc, in0=sc, scalar=1.0, in1=nb, op0=ALU.add, op1=ALU.mult),
              [f"mod{l}", "norms"], [f"mod{l}"])
            if m != 1:
                V(lambda e, gt=gt: e.tensor_scalar(out=gt, in0=gt, scalar1=0.5, scalar2=None, op0=ALU.mult), [f"mod{l}"], [f"mod{l}"])

    def rms_stats(nt):
        for kc in range(KC):
            Ac(lambda e, kc=kc: e.activation(out=scr[:, kc, 0:nt], in_=x[:, kc, 0:nt], func=AF.Square), [f"x{kc}"], [f"scr{kc}"])
        for kc in range(KC):
            T(lambda e, kc=kc: e.matmul(psb[7][:, 0:nt], lhsT=ones_b[:], rhs=scr[:, kc, 0:nt], start=(kc == 0), stop=(kc == KC - 1)),
              [f"scr{kc}", "ones_b"], pk(7))
        Ac(lambda e: e.activation(out=rstd[:, 0:nt], in_=psb[7][:, 0:nt], func=AF.Ln, scale=1.0 / D, bias=EPS), pk(7), ["rstd"])
        Ac(lambda e: e.activation(out=rstd[:, 0:nt], in_=rstd[:, 0:nt], func=AF.Exp, scale=-0.5), ["rstd"], ["rstd"])

    def norm_mod(nt, samp, A, B, lname):
        rms_stats(nt)
        for kc in range(KC):
            t = nexttmp()
            o = h[:, kc, 0:nt]
            if not samp:
                V(lambda e, kc=kc, t=t: e.scalar_tensor_tensor(out=tmp[:, t, 0:nt], in0=x[:, kc, 0:nt], scalar=A[:, kc, 0:1],
                                                                in1=rstd[:, 0:nt], op0=ALU.mult, op1=ALU.mult),
                  [f"x{kc}", "rstd", lname], [f"tmp{t}"])
                Ac(lambda e, kc=kc, t=t, o=o: e.activation(out=o, in_=tmp[:, t, 0:nt], func=AF.Identity, bias=B[:, kc, 0:1]),
                   [f"tmp{t}", lname], [f"h{kc}"])
            else:
                V(lambda e, kc=kc, t=t: e.tensor_tensor(out=tmp[:, t, 0:nt], in0=x[:, kc, 0:nt], in1=rstd[:, 0:nt], op=ALU.mult),
                  [f"x{kc}", "rstd"], [f"tmp{t}"])
                V(lambda e, kc=kc, t=t: e.tensor_tensor(out=tmp[:, t, 0:nt], in0=tmp[:, t, 0:nt], in1=A[:, kc, 1:NV], op=ALU.mult),
                  [f"tmp{t}", lname], [f"tmp{t}"])
                V(lambda e, kc=kc, t=t, o=o: e.tensor_tensor(out=o, in0=tmp[:, t, 0:nt], in1=B[:, kc, 1:NV], op=ALU.add),
                  [f"tmp{t}", lname], [f"h{kc}"])

    def resid_add(nt, samp, dc, pbank, G, lname):
        if not samp:
            V(lambda e: e.scalar_tensor_tensor(out=x[:, dc, 0:nt], in0=psb[pbank][:, 0:nt], scalar=G[:, dc, 0:1],
                                               in1=x[:, dc, 0:nt], op0=ALU.mult, op1=ALU.add),
              pk(pbank) + [f"x{dc}", lname], [f"x{dc}"])
        else:
            t = nexttmp()
            V(lambda e: e.tensor_tensor(out=tmp[:, t, 0:nt], in0=psb[pbank][:, 0:nt], in1=G[:, dc, 1:NV], op=ALU.mult),
              pk(pbank) + [lname], [f"tmp{t}"])
            V(lambda e: e.tensor_tensor(out=x[:, dc, 0:nt], in0=x[:, dc, 0:nt], in1=tmp[:, t, 0:nt], op=ALU.add),
              [f"tmp{t}", f"x{dc}"], [f"x{dc}"])

    def ffn(l, w, nt, samp):
        m = 0 if w == 1 else 2
        lname = f"mod{l}"
        sh = mod[l][:, (3 * m) * KC:(3 * m + 1) * KC, :]
        A = mod[l][:, (3 * m + 1) * KC:(3 * m + 2) * KC, :]
        G = mod[l][:, (3 * m + 2) * KC:(3 * m + 3) * KC, :]
        norm_mod(nt, samp, A, sh, lname)
        wg_d, wu_d, wd_d = ffn_d[(l, w)]
        for (j0, j1) in ((0, FH), (FH, min(2 * FH, FC)), (min(2 * FH, FC), FC)):
            if j1 <= j0:
                continue
            for j in range(j0, j1):
                b = j % 2
                sg = ring_load_wide(wg_d[j])
                su = ring_load_wide(wu_d[j])
                for kc in range(KC):
                    T(lambda e, sg=sg, kc=kc, b=b: e.matmul(psb[b][:, 0:nt], lhsT=rslot(sg)[:, kc, :], rhs=h[:, kc, 0:nt],
                                                             start=(kc == 0), stop=(kc == KC - 1)),
                      rkeys(sg) + [f"h{kc}"], pk(b))
                for kc in range(KC):
                    T(lambda e, su=su, kc=kc, b=b: e.matmul(psb[2 + b][:, 0:nt], lhsT=rslot(su)[:, kc, :], rhs=h[:, kc, 0:nt],
                                                             start=(kc == 0), stop=(kc == KC - 1)),
                      rkeys(su) + [f"h{kc}"], pk(2 + b))
                t = nexttmp()
                Ac(lambda e, b=b, t=t: e.activation(out=tmp[:, t, 0:nt], in_=psb[b][:, 0:nt], func=AF.Silu), pk(b), [f"tmp{t}"])
                V(lambda e, b=b, t=t, jj=j - j0: e.tensor_tensor(out=scr[:, jj, 0:nt], in0=tmp[:, t, 0:nt], in1=psb[2 + b][:, 0:nt], op=ALU.mult),
                  [f"tmp{t}"] + pk(2 + b), [f"scr{j - j0}"])
            nj = j1 - j0
            for dc in range(KC):
                ws = state["wd"] % 2
                state["wd"] += 1
                b = 4 + dc % 2
                P.dma("pool", lambda e, ws=ws, dc=dc, j0=j0, j1=j1, nj=nj: e.dma_start(out=wdr[:, ws, 0:nj, :], in_=wd_d[dc, :, j0:j1, :]),
                      f"wdr{ws}", writes=[f"wdr{ws}"])
                for jj in range(nj):
                    T(lambda e, ws=ws, jj=jj, b=b, nj=nj: e.matmul(psb[b][:, 0:nt], lhsT=wdr[:, ws, jj, :], rhs=scr[:, jj, 0:nt],
                                                                    start=(jj == 0), stop=(jj == nj - 1)),
                      [f"wdr{ws}", f"scr{jj}"], pk(b))
                resid_add(nt, samp, dc, b, G, lname)

    hk = [f"h{kc}" for kc in range(KC)]

    def proj(l, uname, pbank, nt):
        s = ring_load(win_d[l][UP[uname]])
        for kc in range(KC):
            T(lambda e, s=s, kc=kc: e.matmul(psb[pbank][:, 0:nt], lhsT=ring[:, s, kc, :], rhs=h[:, kc, 0:nt], start=(kc == 0), stop=(kc == KC - 1)),
              [f"ring{s}", f"h{kc}"], pk(pbank))

    def conv_unit(l, pbank, nt, samp, cw_ap, cwkey, bias_ap, hist_ap, histkey, shist, xstash, shkey, xskey, rawW, outW):
        raw, out = W[rawW], W[outW]
        rk, ok = f"W{rawW}", f"W{outW}"
        if not samp:
            V(lambda e: e.tensor_copy(out=raw[:, 0:3], in_=hist_ap), [histkey], [rk])
            Ac(lambda e: e.activation(out=raw[:, 3:3 + nt], in_=psb[pbank][:, 0:nt], func=AF.Copy), pk(pbank), [rk])
            V(lambda e: e.tensor_scalar(out=out[:, 0:nt], in0=raw[:, 0:nt], scalar1=cw_ap[:, 0:1], scalar2=None, op0=ALU.mult), [rk, cwkey], [ok])
            for j in range(1, 4):
                V(lambda e, j=j: e.scalar_tensor_tensor(out=out[:, 0:nt], in0=raw[:, j:j + nt], scalar=cw_ap[:, j:j + 1], in1=out[:, 0:nt],
                                                        op0=ALU.mult, op1=ALU.add), [rk, ok, cwkey], [ok])
            V(lambda e: e.tensor_copy(out=hist_ap, in_=raw[:, nt:nt + 3]), [rk], [histkey])
        else:
            Ac(lambda e: e.activation(out=xstash, in_=psb[pbank][:, 0:nt], func=AF.Copy), pk(pbank), [xskey])
            V(lambda e: e.tensor_scalar(out=out[:, 0:nt], in0=xstash, scalar1=cw_ap[:, 3:4], scalar2=None, op0=ALU.mult), [xskey, cwkey], [ok])
            for j in range(3):
                V(lambda e, j=j: e.scalar_tensor_tensor(out=out[:, 0:nt], in0=shist[:, j, :], scalar=cw_ap[:, j:j + 1], in1=out[:, 0:nt],
                                                        op0=ALU.mult, op1=ALU.add), [shkey, ok, cwkey], [ok])
        if bias_ap is not None:
            Ac(lambda e: e.activation(out=out[:, 0:nt], in_=out[:, 0:nt], func=AF.Silu, bias=bias_ap), [ok, cwkey], [ok])
        else:
            Ac(lambda e: e.activation(out=out[:, 0:nt], in_=out[:, 0:nt], func=AF.Silu), [ok], [ok])

    def rinv_of(srcW, nt, dstW):
        V(lambda e: e.tensor_tensor(out=Vb[5][:, 0:nt], in0=W[srcW][:, 0:nt], in1=W[srcW][:, 0:nt], op=ALU.mult), [f"W{srcW}"], ["V5"])
        T(lambda e: e.matmul(psb[6][:, 0:nt], lhsT=ones_b[:], rhs=Vb[5][:, 0:nt], start=True, stop=True), ["V5", "ones_b"], pk(6))
        Ac(lambda e: e.activation(out=W[dstW][:, 0:nt], in_=psb[6][:, 0:nt], func=AF.Ln, bias=EPS), pk(6), [f"W{dstW}"])
        Ac(lambda e: e.activation(out=W[dstW][:, 0:nt], in_=W[dstW][:, 0:nt], func=AF.Exp, scale=-0.5), [f"W{dstW}"], [f"W{dstW}"])

    def decay_setup(l, nt, CL, nch, uname, col_alog, col_dtb, nheads, dtW, with_dt, kstride):
        proj(l, uname, 5, nt)
        Ac(lambda e: e.activation(out=W[6][:, 0:nt], in_=psb[5][:, 0:nt], func=AF.Exp, bias=pv[:, l, col_dtb:col_dtb + 1]), pk(5) + ["pv"], ["W6"])
        Ac(lambda e: e.activation(out=W[6][:, 0:nt], in_=W[6][:, 0:nt], func=AF.Ln, bias=1.0), ["W6"], ["W6"])
        if with_dt:
            V(lambda e: e.tensor_copy(out=W[dtW][:, 0:nt], in_=W[6][:, 0:nt]), ["W6"], [f"W{dtW}"])
        V(lambda e: e.tensor_scalar(out=W[6][:, 0:nt], in0=W[6][:, 0:nt], scalar1=pv[:, l, col_alog:col_alog + 1], scalar2=None, op0=ALU.mult),
          ["W6", "pv"], ["W6"])
        if CL == 1:
            V(lambda e: e.tensor_copy(out=W[7][:, 0:nt], in_=W[6][:, 0:nt]), ["W6"], ["W7"])
        else:
            for c in range(nch):
                sl = slice(c * CL, (c + 1) * CL)
                V(lambda e, sl=sl: e.tensor_tensor_scan(out=W[7][:, sl], data0=W[6][:, sl], data1=W[6][:, sl], initial=0.0, op0=ALU.add, op1=ALU.bypass),
                  ["W6"], ["W7"])
        for c in range(nch):
            sl = slice(c * CL, (c + 1) * CL)
            T(lambda e, sl=sl: e.transpose(out=psb[6][0:CL, 0:128], in_=W[7][:, sl], identity=ident), ["W7", "cst"], pk(6, 0, 1))
            T(lambda e, sl=sl: e.transpose(out=psb[6][0:CL, 128:256], in_=W[dtW][:, sl], identity=ident), [f"W{dtW}", "cst"], pk(6, 1, 2))
            V(lambda e, c=c: e.tensor_copy(out=tok[0:CL, c, 0:nheads], in_=psb[6][0:CL, 0:nheads]), pk(6, 0, 1), ["tok"])
            V(lambda e, c=c: e.tensor_copy(out=tok[0:CL, c, kstride:kstride + nheads], in_=psb[6][0:CL, 128:128 + nheads]), pk(6, 1, 2), ["tok"])

    def decay_mats(l, CL, c, hd, need_dl):
        sl = slice(c * CL, (c + 1) * CL)
        V(lambda e: e.tensor_scalar(out=A_[3][:, 0:CL], in0=W[7][:, sl], scalar1=pv[:, l, PV_OH + hd:PV_OH + hd + 1], scalar2=None, op0=ALU.mult),
          ["W7", "pv"], ["a3"])
        if CL == 1:
            return
        T(lambda e: e.matmul(psb[4][0:CL, 0:CL], lhsT=ones_f[:, 0:CL], rhs=A_[3][:, 0:CL], start=True, stop=True), ["a3", "cst"], pk(4, 0, 1))
        V(lambda e: e.tensor_scalar(out=A_[0][0:CL, 0:CL], in0=psb[4][0:CL, 0:CL], scalar1=tok[0:CL, c, hd:hd + 1], scalar2=None, op0=ALU.subtract),
          pk(4, 0, 1) + ["tok"], ["a0"])
        if CL == 128:
            Ac(lambda e: e.activation(out=A_[10][:, 0:CL], in_=psb[4][0:CL, 0:CL], func=AF.Exp), pk(4, 0, 1), ["a10"])
        V(lambda e: e.tensor_scalar(out=A_[1][0:CL, 0:CL], in0=A_[0][0:CL, 0:CL], scalar1=0.0, scalar2=None, op0=ALU.min), ["a0"], ["a1"])
        Ac(lambda e: e.activation(out=A_[1][0:CL, 0:CL], in_=A_[1][0:CL, 0:CL], func=AF.Exp), ["a1"], ["a1"])
        V(lambda e: e.tensor_tensor(out=A_[1][0:CL, 0:CL], in0=A_[1][0:CL, 0:CL], in1=Umat[0:CL, 0:CL], op=ALU.mult), ["a1", "cst"], ["a1"])
        if need_dl and CL > 1:
            V(lambda e: e.tensor_scalar(out=A_[2][0:CL, 0:CL], in0=A_[0][0:CL, 0:CL], scalar1=0.0, scalar2=None, op0=ALU.max), ["a0"], ["a2"])
            Ac(lambda e: e.activation(out=A_[2][0:CL, 0:CL], in_=A_[2][0:CL, 0:CL], func=AF.Exp, scale=-1.0), ["a2"], ["a2"])
            V(lambda e: e.tensor_tensor(out=A_[2][0:CL, 0:CL], in0=A_[2][0:CL, 0:CL], in1=Lsmat[0:CL, 0:CL], op=ALU.mult), ["a2", "cst"], ["a2"])
        Ac(lambda e: e.activation(out=cs1[0:CL, 0:1], in_=A_[0][0:CL, CL - 1:CL], func=AF.Exp), ["a0"], ["cs1"])

    def out_norm_T(l, CL, c, pbank, dvw, nw_ap, gateW, dst_units, dstkeys):
        sl = slice(c * CL, (c + 1) * CL)
        Ac(lambda e: e.activation(out=A_[4][0:CL, 0:dvw], in_=psb[pbank][0:CL, 0:dvw], func=AF.Square), pk(pbank, 0, 2), ["a4"])
        V(lambda e: e.tensor_reduce(out=cs1[0:CL, 4:5], in_=A_[4][0:CL, 0:dvw], axis=AX.X, op=ALU.add), ["a4"], ["cs1b"])
        Ac(lambda e: e.activation(out=cs1[0:CL, 4:5], in_=cs1[0:CL, 4:5], func=AF.Ln, scale=1.0 / dvw, bias=EPS), ["cs1b"], ["cs1b"])
        Ac(lambda e: e.activation(out=cs1[0:CL, 4:5], in_=cs1[0:CL, 4:5], func=AF.Exp, scale=-0.5), ["cs1b"], ["cs1b"])
        V(lambda e: e.scalar_tensor_tensor(out=A_[4][0:CL, 0:dvw], in0=psb[pbank][0:CL, 0:dvw], scalar=cs1[0:CL, 4:5], in1=nw_ap[0:CL, 0:dvw],
                                           op0=ALU.mult, op1=ALU.mult), pk(pbank, 0, 2) + ["cs1b", "nrm"], ["a4"])
        for i, (du, dk_) in enumerate(zip(dst_units, dstkeys)):
            T(lambda e, i=i: e.transpose(out=psb[6][:, 256 + i * 128:256 + i * 128 + CL], in_=A_[4][0:CL, i * 128:(i + 1) * 128], identity=ident[0:CL, 0:CL]),
              ["a4", "cst"], pk(6, 2 + i, 3 + i))
            V(lambda e, i=i, du=du: e.tensor_tensor(out=scr[:, du, sl], in0=psb[6][:, 256 + i * 128:256 + i * 128 + CL], in1=W[gateW[i]][:, sl], op=ALU.mult),
              pk(6, 2 + i, 3 + i) + [f"W{gateW[i]}"], [dk_])

    def mixer(l, nt, samp):
        CL = 1 if samp else 128
        nch = nt // CL
        lname = f"mod{l}"
        sh = mod[l][:, 3 * KC:4 * KC, :]
        A = mod[l][:, 4 * KC:5 * KC, :]
        Gt = mod[l][:, 5 * KC:6 * KC, :]
        norm_mod(nt, samp, A, sh, lname)
        P.dma("sp", lambda e: e.dma_start(out=nrm1[:], in_=nrm_d[l]), "nrm", writes=["nrm"])
        if samp:
            P.dma("sp", lambda e: e.dma_start(out=shA[:], in_=s_cva_d[l]), "shA", writes=["shA"] + PKEYS)
            P.dma("sp", lambda e: e.dma_start(out=shC[:], in_=s_cvc_d[l]), "shC", writes=["shC"] + PKEYS)
        OB = 16

        def load_state(src_ap, width, buf):
            P.dma("sp", lambda e: e.dma_start(out=SS[:, buf, 0:width], in_=src_ap), f"SS{buf}", writes=[f"SS{buf}"])

        def store_state(dst_ap, src_sb, key, skey):
            P.dma("sp", lambda e: e.dma_start(out=dst_ap, in_=src_sb), skey, reads=[key], writes=["odram"])

        CL_all, nch_all = CL, nch
        CL = 1 if samp else 128
        nch = nt // CL
        proj(l, "beta", 4, nt)
        Ac(lambda e: e.activation(out=W[8][:, 0:nt], in_=psb[4][:, 0:nt], func=AF.Sigmoid), pk(4), ["W8"])
        decay_setup(l, nt, CL, nch, "dec", 0, 1, HA, 8, False, 8)
        for c in range(nch):
            V(lambda e, c=c: e.tensor_scalar(out=tok[0:CL, c, 16:16 + HA], in0=tok[0:CL, c, 8:8 + HA], scalar1=-1.0, scalar2=None, op0=ALU.mult), ["tok"], ["tok"])
            Ac(lambda e, c=c: e.activation(out=tok[0:CL, c, 24:24 + HA], in_=tok[0:CL, c, 0:HA], func=AF.Exp), ["tok"], ["tok"])
            V(lambda e, c=c: e.tensor_tensor(out=tok[0:CL, c, 24:24 + HA], in0=tok[0:CL, c, 24:24 + HA], in1=tok[0:CL, c, 8:8 + HA], op=ALU.mult), ["tok"], ["tok"])

        for hd in range(HA):
            for i, (nm, ow) in enumerate((("aq", 1), ("ak", 2), ("av", 3))):
                proj(l, f"{nm}{hd}", i % 2, nt)
                u = i * HA + hd
                conv_unit(l, i % 2, nt, samp, cwa[:, l, u, :], "cwa", None, hista[:, l, u, :], f"hista{l}", shA[:, u, :, :], xsA[:, u, :], "shA", "xsA", 0, ow)
            proj(l, f"az{hd}", 3, nt)
            Ac(lambda e: e.activation(out=W[4][:, 0:nt], in_=psb[3][:, 0:nt], func=AF.Silu), pk(3), ["W4"])
            rinv_of(1, nt, 5)
            V(lambda e: e.scalar_tensor_tensor(out=Vb[0][:, 0:nt], in0=W[1][:, 0:nt], scalar=128 ** -0.5, in1=W[5][:, 0:nt], op0=ALU.mult, op1=ALU.mult),
              ["W1", "W5"], ["V0"])
            rinv_of(2, nt, 5)
            V(lambda e: e.tensor_tensor(out=W[2][:, 0:nt], in0=W[2][:, 0:nt], in1=W[5][:, 0:nt], op=ALU.mult), ["W2", "W5"], ["W2"])
            V(lambda e: e.tensor_copy(out=Vb[1][:, 0:nt], in_=W[2][:, 0:nt]), ["W2"], ["V1"])
            for c in range(nch):
                sl = slice(c * CL, (c + 1) * CL)
                tk = "tok"
                if samp:
                    buf = c % 2
                    load_state(s_gdn_d[l, c, hd], 128, buf)
                    S = SS[:, buf, 0:128]
                    Sk = f"SS{buf}"
                else:
                    S = Sa[:, l, hd, :]
                    Sk = f"Sa{l}_{hd}"
                    if c == 0 and state.get("ptile", 0) == 0:
                        V(lambda e, S=S: e.memset(S, 0.0), [], [Sk])
                V(lambda e, S=S: e.tensor_copy(out=B_[0][:, 0:128], in_=S), [Sk], ["b0"])
                decay_mats(l, CL, c, hd, True)
                if CL > 1:
                    q = slice(0, CL)
                    blocked = CL > 64
                    T(lambda e, sl=sl: e.matmul(psb[4][q, 128:128 + CL], lhsT=Vb[1][:, sl], rhs=Vb[1][:, sl], start=True, stop=True), ["V1"], pk(4, 1, 2))
                    V(lambda e, c=c: e.scalar_tensor_tensor(out=A_[5][q, q], in0=psb[4][q, 128:128 + CL], scalar=tok[q, c, 16 + hd:17 + hd], in1=A_[2][q, q],
                                                            op0=ALU.mult, op1=ALU.mult), pk(4, 1, 2) + [tk, "a2"], ["a5"])
                    T(lambda e: e.transpose(out=psb[4][q, 256:256 + CL], in_=A_[5][q, q], identity=ident[q, q]), ["a5", "cst"], pk(4, 2, 3))
                    Ac(lambda e: e.activation(out=A_[6][q, q], in_=psb[4][q, 256:256 + CL], func=AF.Copy), pk(4, 2, 3), ["a6"])
                    if blocked:
                        V(lambda e: e.tensor_tensor(out=A_[7][q, q], in0=A_[5][q, q], in1=BDmat[q, q], op=ALU.mult), ["a5", "cst"], ["a7"])
                        V(lambda e: e.tensor_tensor(out=A_[8][q, q], in0=A_[6][q, q], in1=BDmat[q, q], op=ALU.mult), ["a6", "cst"], ["a8"])
                        V(lambda e: e.tensor_tensor(out=A_[5][q, q], in0=A_[5][q, q], in1=A_[7][q, q], op=ALU.subtract), ["a5", "a7"], ["a5"])
                        V(lambda e: e.tensor_tensor(out=A_[6][q, q], in0=A_[6][q, q], in1=A_[8][q, q], op=ALU.subtract), ["a6", "a8"], ["a6"])
                        V(lambda e: e.tensor_tensor(out=A_[9][q, q], in0=A_[8][q, q], in1=ident[q, q], op=ALU.add), ["a8", "cst"], ["a9"])
                        Nc, Mc, Nn, Mn = 7, 8, 11, 12
                        nlev = 5
                    else:
                        V(lambda e: e.tensor_tensor(out=A_[9][q, q], in0=psb[4][q, 256:256 + CL], in1=ident[q, q], op=ALU.add), pk(4, 2, 3) + ["cst"], ["a9"])
                        Nc, Mc, Nn, Mn = 5, 6, 7, 8
                        nlev = CL.bit_length() - 2
                    for lev in range(1, nlev + 1):
                        T(lambda e, Mc=Mc, Nc=Nc: e.matmul(psb[4][q, 128:128 + CL], lhsT=A_[Mc][q, q], rhs=A_[Nc][q, q], start=True, stop=True),
                          [f"a{Mc}", f"a{Nc}"], pk(4, 1, 2))
                        if lev < nlev:
                            T(lambda e, Mc=Mc, Nc=Nc: e.matmul(psb[4][q, 256:256 + CL], lhsT=A_[Nc][q, q], rhs=A_[Mc][q, q], start=True, stop=True),
                              [f"a{Mc}", f"a{Nc}"], pk(4, 2, 3))
                        Ac(lambda e, Nn=Nn: e.activation(out=A_[Nn][q, q], in_=psb[4][q, 128:128 + CL], func=AF.Copy), pk(4, 1, 2), [f"a{Nn}"])
                        if lev < nlev:
                            V(lambda e, Mn=Mn: e.tensor_copy(out=A_[Mn][q, q], in_=psb[4][q, 256:256 + CL]), pk(4, 2, 3), [f"a{Mn}"])
                        T(lambda e, Nn=Nn: e.matmul(psb[4][q, 384:384 + CL], lhsT=A_[Nn][q, q], rhs=A_[9][q, q], start=True, stop=True),
                          [f"a{Nn}", "a9"], pk(4, 3, 4))
                        V(lambda e: e.tensor_tensor(out=A_[9][q, q], in0=A_[9][q, q], in1=psb[4][q, 384:384 + CL], op=ALU.add), pk(4, 3, 4) + ["a9"], ["a9"])
                        Nc, Mc, Nn, Mn = Nn, Mn, Nc, Mc
                    if blocked:
                        T(lambda e: e.matmul(psb[4][q, 128:128 + CL], lhsT=A_[5][q, q], rhs=A_[9][q, q], start=True, stop=True), ["a5", "a9"], pk(4, 1, 2))
                        T(lambda e: e.transpose(out=psb[4][q, 256:256 + CL], in_=A_[9][q, q], identity=ident[q, q]), ["a9", "cst"], pk(4, 2, 3))
                        Ac(lambda e: e.activation(out=A_[7][q, q], in_=psb[4][q, 128:128 + CL], func=AF.Copy), pk(4, 1, 2), ["a7"])
                        V(lambda e: e.tensor_copy(out=A_[8][q, q], in_=psb[4][q, 256:256 + CL]), pk(4, 2, 3), ["a8"])
                        T(lambda e: e.matmul(psb[4][q, 384:384 + CL], lhsT=A_[8][q, q], rhs=A_[7][q, q], start=True, stop=True), ["a8", "a7"], pk(4, 3, 4))
                        V(lambda e: e.tensor_tensor(out=A_[9][q, q], in0=A_[9][q, q], in1=psb[4][q, 384:384 + CL], op=ALU.add), pk(4, 3, 4) + ["a9"], ["a9"])
                    Rap = A_[9]
                    Rk = "a9"
                else:
                    Rap = cst
                    Rk = "cst"
                T(lambda e, sl=sl: e.transpose(out=psb[5][0:CL, 0:128], in_=W[3][:, sl], identity=ident), ["W3", "cst"], pk(5, 0, 1))
                Ac(lambda e: e.activation(out=A_[12][0:CL, 0:128], in_=psb[5][0:CL, 0:128], func=AF.Copy), pk(5, 0, 1), ["a12"])
                T(lambda e, sl=sl: e.transpose(out=psb[5][0:CL, 128:256], in_=W[2][:, sl], identity=ident), ["W2", "cst"], pk(5, 1, 2))
                Ac(lambda e: e.activation(out=A_[11][0:CL, 0:128], in_=psb[5][0:CL, 128:256], func=AF.Copy), pk(5, 1, 2), ["a11"])
                if CL == 1:
                    V(lambda e: e.tensor_copy(out=B_[3][0:CL, 0:128], in_=psb[5][0:CL, 128:256]), pk(5, 1, 2), ["b3"])
                else:
                    V(lambda e: e.tensor_scalar(out=B_[3][0:CL, 0:128], in0=psb[5][0:CL, 128:256], scalar1=cs1[0:CL, 0:1], scalar2=None, op0=ALU.mult),
                      pk(5, 1, 2) + ["cs1"], ["b3"])
                V(lambda e, c=c, Rap=Rap: e.tensor_scalar(out=A_[5][0:CL, 0:CL], in0=Rap[0:CL, 0:CL], scalar1=tok[0:CL, c, 8 + hd:9 + hd], scalar2=None, op0=ALU.mult),
                  [Rk, tk], ["a5"])
                V(lambda e, c=c, Rap=Rap: e.tensor_scalar(out=A_[6][0:CL, 0:CL], in0=Rap[0:CL, 0:CL], scalar1=tok[0:CL, c, 24 + hd:25 + hd], scalar2=None, op0=ALU.mult),
                  [Rk, tk], ["a6"])
                T(lambda e: e.matmul(psb[5][:, 256:256 + CL], lhsT=A_[11][0:CL, 0:128], rhs=A_[6][0:CL, 0:CL], start=True, stop=True), ["a11", "a6"], pk(5, 2, 3))
                V(lambda e: e.tensor_scalar(out=A_[7][:, 0:CL], in0=psb[5][:, 256:256 + CL], scalar1=-1.0, scalar2=None, op0=ALU.mult), pk(5, 2, 3), ["a7"])
                T(lambda e: e.matmul(psb[5][0:CL, 384:512], lhsT=A_[5][0:CL, 0:CL], rhs=A_[12][0:CL, 0:128], start=True, stop=False), ["a5", "a12"], pk(5, 3, 4))
                T(lambda e, S=S: e.matmul(psb[5][0:CL, 384:512], lhsT=A_[7][:, 0:CL], rhs=S, start=False, stop=True), ["a7", Sk], pk(5, 3, 4))
                V(lambda e: e.tensor_copy(out=B_[7][0:CL, 0:128], in_=psb[5][0:CL, 384:512]), pk(5, 3, 4), ["b7"])
                T(lambda e, sl=sl: e.matmul(psb[6][0:CL, 0:CL], lhsT=Vb[1][:, sl], rhs=Vb[0][:, sl], start=True, stop=True), ["V1", "V0"], pk(6, 0, 1))
                if CL == 1:
                    V(lambda e: e.tensor_copy(out=B_[8][0:CL, 0:CL], in_=psb[6][0:CL, 0:CL]), pk(6, 0, 1), ["b8"])
                else:
                    V(lambda e: e.tensor_tensor(out=B_[8][0:CL, 0:CL], in0=psb[6][0:CL, 0:CL], in1=A_[1][0:CL, 0:CL], op=ALU.mult), pk(6, 0, 1) + ["a1"], ["b8"])
                if CL != 128:
                    T(lambda e: e.matmul(psb[6][:, 128:128 + CL], lhsT=ones_f, rhs=A_[3][:, 0:CL], start=True, stop=True), ["a3", "cst"], pk(6, 1, 2))
                    Ac(lambda e: e.activation(out=A_[10][:, 0:CL], in_=psb[6][:, 128:128 + CL], func=AF.Exp), pk(6, 1, 2), ["a10"])
                V(lambda e, sl=sl: e.tensor_tensor(out=B_[9][:, 0:CL], in0=Vb[0][:, sl], in1=A_[10][:, 0:CL], op=ALU.mult), ["V0", "a10"], ["b9"])
                T(lambda e: e.matmul(psb[7][0:CL, 0:128], lhsT=B_[9][:, 0:CL], rhs=B_[0][:, 0:128], start=True, stop=False), ["b9", "b0"], pk(7, 0, 1))
                T(lambda e: e.matmul(psb[7][0:CL, 0:128], lhsT=B_[8][0:CL, 0:CL], rhs=B_[7][0:CL, 0:128], start=False, stop=True), ["b8", "b7"], pk(7, 0, 1))
                T(lambda e: e.matmul(psb[7][:, 256:384], lhsT=B_[3][0:CL, 0:128], rhs=B_[7][0:CL, 0:128], start=True, stop=True), ["b3", "b7"], pk(7, 2, 3))
                V(lambda e, S=S: e.scalar_tensor_tensor(out=S, in0=S, scalar=A_[10][:, CL - 1:CL], in1=psb[7][:, 256:384], op0=ALU.mult, op1=ALU.add),
                  [Sk, "a10"] + pk(7, 2, 3), [Sk])
                if samp:
                    store_state(so_gdn_d[l, c, hd], S, Sk, f"sst{buf}")
                out_norm_T(l, CL, c, 7, 128, nrm1[:, 0:128], [4], [OB + hd], [f"scr{OB + hd}"])
            if not samp and state.get("ptile", 0) == NPT - 1:
                store_state(o_gdn_d[l, hd], Sa[:, l, hd, :], f"Sa{l}_{hd}", "ost")
        if samp:
            P.dma("sp", lambda e: e.dma_start(out=so_cva_d[l][:, :, 0:2, :], in_=shA[:, :, 1:3, :]), "shst", reads=["shA"], writes=["odram"])
            P.dma("sp", lambda e: e.dma_start(out=so_cva_d[l][:, :, 2, :], in_=xsA[:]), "shst", reads=["xsA"], writes=["odram"])
        elif state.get("ptile", 0) == NPT - 1:
            P.dma("sp", lambda e: e.dma_start(out=o_cva_d[l], in_=hista[:, l, :, :]), "ost", reads=[f"hista{l}"], writes=["odram"])
        merge_branch(l, nt, 0, wba_d[l], HA, OB, first=True)
        CL, nch = CL_all, nch_all

        proj(l, "lr", 4, nt)
        Ac(lambda e: e.activation(out=Vb[2][0:16, 0:nt], in_=psb[4][0:16, 0:nt], func=AF.Copy), pk(4), ["V2"])
        for hd in range(HB):
            T(lambda e, hd=hd: e.matmul(psb[4][:, 0:nt], lhsT=wgate[:, l, hd * 128:(hd + 1) * 128], rhs=Vb[2][0:16, 0:nt], start=True, stop=True),
              ["wgate", "V2"], pk(4))
            Ac(lambda e, hd=hd: e.activation(out=W[5][:, 0:nt], in_=psb[4][:, 0:nt], func=AF.Exp, scale=-1.0, bias=pv[:, l, PV_BG + hd:PV_BG + hd + 1]),
               pk(4) + ["pv"], ["W5"])
            Ac(lambda e: e.activation(out=W[5][:, 0:nt], in_=W[5][:, 0:nt], func=AF.Ln, bias=1.0), ["W5"], ["W5"])
            if CL > 1:
                for c in range(nch):
                    sl = slice(c * CL, (c + 1) * CL)
                    V(lambda e, sl=sl: e.tensor_tensor_scan(out=W[6][:, sl], data0=W[5][:, sl], data1=W[5][:, sl], initial=0.0, op0=ALU.add, op1=ALU.bypass),
                      ["W5"], ["W6"])
            else:
                V(lambda e: e.tensor_copy(out=W[6][:, 0:nt], in_=W[5][:, 0:nt]), ["W5"], ["W6"])
            Ac(lambda e: e.activation(out=W[7][:, 0:nt], in_=W[6][:, 0:nt], func=AF.Exp, scale=-1.0 / 16), ["W6"], ["W7"])
            Ac(lambda e: e.activation(out=W[8][:, 0:nt], in_=W[6][:, 0:nt], func=AF.Exp, scale=1.0 / 16), ["W6"], ["W8"])
            proj(l, f"bq{hd}", 0, nt)
            V(lambda e: e.scalar_tensor_tensor(out=Vb[0][:, 0:nt], in0=psb[0][:, 0:nt], scalar=128 ** -0.5, in1=W[7][:, 0:nt], op0=ALU.mult, op1=ALU.mult),
              pk(0) + ["W7"], ["V0"])
            proj(l, f"bk{hd}", 1, nt)
            Ac(lambda e: e.activation(out=W[1][:, 0:nt], in_=psb[1][:, 0:nt], func=AF.Copy), pk(1), ["W1"])
            V(lambda e: e.tensor_tensor(out=Vb[1][:, 0:nt], in0=W[1][:, 0:nt], in1=W[8][:, 0:nt], op=ALU.mult), ["W1", "W8"], ["V1"])
            for c in range(nch):
                sl = slice(c * CL, (c + 1) * CL)
                V(lambda e, c=c: e.tensor_scalar(out=cs1[:, 2:3], in0=W[6][:, (c + 1) * CL - 1:(c + 1) * CL], scalar1=-1.0 / 16, scalar2=None, op0=ALU.mult),
                  ["W6"], ["cs1c"])
                Ac(lambda e, sl=sl: e.activation(out=W[2][:, sl], in_=W[6][:, sl], func=AF.Exp, scale=1.0 / 16, bias=cs1[:, 2:3]), ["W6", "cs1c"], ["W2"])
            V(lambda e: e.tensor_tensor(out=W[2][:, 0:nt], in0=W[2][:, 0:nt], in1=W[1][:, 0:nt], op=ALU.mult), ["W2", "W1"], ["W2"])
            proj(l, f"bv{hd}_0", 2, nt)
            Ac(lambda e: e.activation(out=W[3][:, 0:nt], in_=psb[2][:, 0:nt], func=AF.Copy), pk(2), ["W3"])
            proj(l, f"bv{hd}_1", 3, nt)
            Ac(lambda e: e.activation(out=W[4][:, 0:nt], in_=psb[3][:, 0:nt], func=AF.Copy), pk(3), ["W4"])
            proj(l, f"br{hd}_0", 0, nt)
            Ac(lambda e: e.activation(out=W[1][:, 0:nt], in_=psb[0][:, 0:nt], func=AF.Silu), pk(0), ["W1"])
            proj(l, f"br{hd}_1", 1, nt)
            Ac(lambda e: e.activation(out=W[5][:, 0:nt], in_=psb[1][:, 0:nt], func=AF.Silu), pk(1), ["W5"])
            for c in range(nch):
                sl = slice(c * CL, (c + 1) * CL)
                if samp:
                    buf = c % 2
                    load_state(s_gla_d[l, c, hd], 256, buf)
                    S = SS[:, buf, 0:256]
                    Sk = f"SS{buf}"
                else:
                    S = Sb_[:, l, hd, :]
                    Sk = f"Sb{l}_{hd}"
                    if c == 0 and state.get("ptile", 0) == 0:
                        V(lambda e, S=S: e.memset(S, 0.0), [], [Sk])
                V(lambda e, S=S: e.tensor_copy(out=B_[0][:, 0:256], in_=S), [Sk], ["b0"])
                T(lambda e, sl=sl: e.transpose(out=psb[5][0:CL, 0:128], in_=W[3][:, sl], identity=ident), ["W3", "cst"], pk(5, 0, 1))
                T(lambda e, sl=sl: e.transpose(out=psb[5][0:CL, 128:256], in_=W[4][:, sl], identity=ident), ["W4", "cst"], pk(5, 1, 2))
                Ac(lambda e: e.activation(out=B_[1][0:CL, 0:256], in_=psb[5][0:CL, 0:256], func=AF.Copy), pk(5, 0, 2), ["b1"])
                T(lambda e, sl=sl: e.transpose(out=psb[5][0:CL, 256:384], in_=W[2][:, sl], identity=ident), ["W2", "cst"], pk(5, 2, 3))
                Ac(lambda e: e.activation(out=B_[3][0:CL, 0:128], in_=psb[5][0:CL, 256:384], func=AF.Copy), pk(5, 2, 3), ["b3"])
                T(lambda e, sl=sl: e.matmul(psb[6][0:CL, 0:CL], lhsT=Vb[1][:, sl], rhs=Vb[0][:, sl], start=True, stop=True), ["V1", "V0"], pk(6, 0, 1))
                V(lambda e: e.tensor_tensor(out=B_[8][0:CL, 0:CL], in0=psb[6][0:CL, 0:CL], in1=Umat[0:CL, 0:CL], op=ALU.mult), pk(6, 0, 1) + ["cst"], ["b8"])
                T(lambda e, sl=sl: e.matmul(psb[7][0:CL, 0:256], lhsT=Vb[0][:, sl], rhs=B_[0][:, 0:256], start=True, stop=False), ["V0", "b0"], pk(7, 0, 2))
                T(lambda e: e.matmul(psb[7][0:CL, 0:256], lhsT=B_[8][0:CL, 0:CL], rhs=B_[1][0:CL, 0:256], start=False, stop=True), ["b8", "b1"], pk(7, 0, 2))
                T(lambda e: e.matmul(psb[7][:, 256:512], lhsT=B_[3][0:CL, 0:128], rhs=B_[1][0:CL, 0:256], start=True, stop=True), ["b3", "b1"], pk(7, 2, 4))
                V(lambda e, S=S, c=c: e.scalar_tensor_tensor(out=S, in0=S, scalar=W[7][:, (c + 1) * CL - 1:(c + 1) * CL], in1=psb[7][:, 256:512],
                                                             op0=ALU.mult, op1=ALU.add), [Sk, "W7"] + pk(7, 2, 4), [Sk])
                if samp:
                    store_state(so_gla_d[l, c, hd], S, Sk, f"sst{buf}")
                out_norm_T(l, CL, c, 7, 256, nrm1[:, 128:384], [1, 5], [OB + 2 * hd, OB + 2 * hd + 1], [f"scr{OB + 2 * hd}", f"scr{OB + 2 * hd + 1}"])
            if not samp and state.get("ptile", 0) == NPT - 1:
                store_state(o_gla_d[l, hd], Sb_[:, l, hd, :], f"Sb{l}_{hd}", "ost")
        merge_branch(l, nt, 1, wbb_d[l], 2 * HB, OB, first=False)

        decay_setup(l, nt, CL, nch, "dt", 2, 3, HC, 8, True, 32)

        V(lambda e: e.memset(B_[10][:, 0:256], 0.0), [], ["b10"])
        V(lambda e: e.memset(B_[11][:, 0:256], 0.0), [], ["b11"])
        V(lambda e: e.memset(B_[12][:, 0:256], 0.0), [], ["b12"])
        for g in range(G):
            for (nm, cu, dstV, keepW) in ((f"cB{g}", NU_C + g, 3, None), (f"cC{g}", NU_C + G + g, 4, 5)):
                proj(l, nm, 0, nt)
                conv_unit(l, 0, nt, samp, cwc[:, l, cu, 0:4], "cwc", cwc[:, l, cu, 4:5], histc[:, l, cu, :], f"histc{l}", shC[:, cu, :, :], xsC[:, cu, :],
                          "shC", "xsC", 0, 1)
                V(lambda e, dstV=dstV: e.tensor_copy(out=Vb[dstV][:, 0:nt], in_=W[1][:, 0:nt]), ["W1"], [f"V{dstV}"])
                if keepW is not None:
                    V(lambda e: e.tensor_copy(out=W[5][:, 0:nt], in_=W[1][:, 0:nt]), ["W1"], ["W5"])
                else:
                    V(lambda e: e.tensor_copy(out=W[6][:, 0:nt], in_=W[1][:, 0:nt]), ["W1"], ["W6"])
            for uu in range(UPN):
                u = g * UPN + uu
                proj(l, f"cx{u}", 1, nt)
                conv_unit(l, 1, nt, samp, cwc[:, l, u, 0:4], "cwc", cwc[:, l, u, 4:5], histc[:, l, u, :], f"histc{l}", shC[:, u, :, :], xsC[:, u, :],
                          "shC", "xsC", 0, 2)
                proj(l, f"cz{u}", 2, nt)
                Ac(lambda e: e.activation(out=W[3][:, 0:nt], in_=psb[2][:, 0:nt], func=AF.Silu), pk(2), ["W3"])
                for c in range(nch):
                    sl = slice(c * CL, (c + 1) * CL)
                    tk = "tok"
                    if samp:
                        buf = c % 2
                        load_state(s_ssd_d[l, c, u], 128, buf)
                        S = SS[:, buf, 0:128]
                        Sk = f"SS{buf}"
                    else:
                        S = Sc[:, l, u, :]
                        Sk = f"Sc{l}_{u}"
                        if c == 0 and state.get("ptile", 0) == 0:
                            V(lambda e, S=S: e.memset(S, 0.0), [], [Sk])
                    T(lambda e, sl=sl: e.matmul(psb[5][0:CL, 0:CL], lhsT=Vb[3][:, sl], rhs=Vb[4][:, sl], start=True, stop=True), ["V3", "V4"], pk(5, 0, 1))
                    V(lambda e: e.tensor_copy(out=A_[11][0:CL, 0:CL], in_=psb[5][0:CL, 0:CL]), pk(5, 0, 1), ["a11"])
                    T(lambda e, sl=sl: e.transpose(out=psb[5][0:CL, 128:256], in_=W[6][:, sl], identity=ident), ["W6", "cst"], pk(5, 1, 2))
                    Ac(lambda e: e.activation(out=B_[2][0:CL, 0:128], in_=psb[5][0:CL, 128:256], func=AF.Copy), pk(5, 1, 2), ["b2"])
                    T(lambda e, sl=sl: e.transpose(out=psb[5][0:CL, 256:384], in_=W[2][:, sl], identity=ident), ["W2", "cst"], pk(5, 2, 3))
                    V(lambda e: e.tensor_copy(out=A_[12][0:CL, 0:128], in_=psb[5][0:CL, 256:384]), pk(5, 2, 3), ["a12"])
                    for hh in range(2):
                        hd = 2 * u + hh
                        hs = slice(hh * 64, hh * 64 + 64)
                        po = hh * 128
                        decay_mats(l, CL, c, hd, False)
                        if CL == 1:
                            V(lambda e: e.tensor_copy(out=B_[8][0:CL, 0:CL], in_=A_[11][0:CL, 0:CL]), ["a11"], ["b8"])
                        else:
                            V(lambda e: e.tensor_tensor(out=B_[8][0:CL, 0:CL], in0=A_[11][0:CL, 0:CL], in1=A_[1][0:CL, 0:CL], op=ALU.mult), ["a11", "a1"], ["b8"])
                        if CL != 128:
                            T(lambda e: e.matmul(psb[6][:, 128:128 + CL], lhsT=ones_f, rhs=A_[3][:, 0:CL], start=True, stop=True), ["a3", "cst"], pk(6, 1, 2))
                            Ac(lambda e: e.activation(out=A_[10][:, 0:CL], in_=psb[6][:, 128:128 + CL], func=AF.Exp), pk(6, 1, 2), ["a10"])
                        V(lambda e, sl=sl: e.tensor_tensor(out=B_[9][:, 0:CL], in0=W[5][:, sl], in1=A_[10][:, 0:CL], op=ALU.mult), ["W5", "a10"], ["b9"])
                        V(lambda e, hs=hs, po=po, c=c, hd=hd: e.tensor_scalar(out=B_[10][0:CL, po + hh_off(hs):po + hh_off(hs) + 64], in0=A_[12][0:CL, hs],
                                                                              scalar1=tok[0:CL, c, 32 + hd:33 + hd], scalar2=None, op0=ALU.mult),
                          ["a12", tk], ["b10"])
                        if CL == 1:
                            V(lambda e, hs=hs, po=po: e.tensor_copy(out=B_[12][0:CL, po + hh_off(hs):po + hh_off(hs) + 64],
                                                                    in_=B_[10][0:CL, po + hh_off(hs):po + hh_off(hs) + 64]), ["b10"], ["b12"])
                        else:
                            V(lambda e, hs=hs, po=po: e.tensor_scalar(out=B_[12][0:CL, po + hh_off(hs):po + hh_off(hs) + 64],
                                                                      in0=B_[10][0:CL, po + hh_off(hs):po + hh_off(hs) + 64],
                                                                      scalar1=cs1[0:CL, 0:1], scalar2=None, op0=ALU.mult), ["b10", "cs1"], ["b12"])
                        V(lambda e, hs=hs, po=po, S=S: e.tensor_copy(out=B_[11][:, po + hh_off(hs):po + hh_off(hs) + 64], in_=S[:, hs]), [Sk], ["b11"])
                        T(lambda e, po=po, hh=hh: e.matmul(psb[7][:, 0:CL], lhsT=B_[10][0:CL, po:po + 128], rhs=B_[8][0:CL, 0:CL], start=(hh == 0), stop=False),
                          ["b10", "b8"], pk(7, 0, 1))
                        T(lambda e, po=po, hh=hh: e.matmul(psb[7][:, 0:CL], lhsT=B_[11][:, po:po + 128], rhs=B_[9][:, 0:CL], start=False, stop=(hh == 1)),
                          ["b11", "b9"], pk(7, 0, 1))
                        T(lambda e, po=po, hh=hh: e.matmul(psb[2][:, 0:128], lhsT=B_[2][0:CL, 0:128], rhs=B_[12][0:CL, po:po + 128], start=(hh == 0), stop=(hh == 1)),
                          ["b2", "b12"], pk(2))
                        V(lambda e, hh=hh: e.tensor_copy(out=cs1[:, 5 + hh:6 + hh], in_=A_[10][:, CL - 1:CL]), ["a10"], ["cs1d"])
                    for hh in range(2):
                        hs = slice(hh * 64, hh * 64 + 64)
                        V(lambda e, hs=hs, hh=hh, S=S: e.scalar_tensor_tensor(out=S[:, hs], in0=S[:, hs], scalar=cs1[:, 5 + hh:6 + hh], in1=psb[2][:, hh * 64:hh * 64 + 64],
                                                                              op0=ALU.mult, op1=ALU.add), [Sk, "cs1d"] + pk(2), [Sk])
                    if samp:
                        store_state(so_ssd_d[l, c, u], S, Sk, f"sst{buf}")
                    V(lambda e, sl=sl, u=u: e.scalar_tensor_tensor(out=W[4][:, sl], in0=W[2][:, sl], scalar=pv[:, l, PV_D + u:PV_D + u + 1], in1=psb[7][:, 0:CL],
                                                                    op0=ALU.mult, op1=ALU.add), ["W2", "pv"] + pk(7, 0, 1), ["W4"])
                    V(lambda e, sl=sl: e.tensor_tensor(out=W[4][:, sl], in0=W[4][:, sl], in1=W[3][:, sl], op=ALU.mult), ["W4", "W3"], ["W4"])
                if not samp and state.get("ptile", 0) == NPT - 1:
                    store_state(o_ssd_d[l, u], Sc[:, l, u, :], f"Sc{l}_{u}", "ost")
                V(lambda e, u=u: e.tensor_copy(out=scr[:, OB + u, 0:nt], in_=W[4][:, 0:nt]), ["W4"], [f"scr{OB + u}"])
                V(lambda e, uu=uu: e.tensor_tensor(out=Vb[5][:, 0:nt], in0=W[4][:, 0:nt], in1=W[4][:, 0:nt], op=ALU.mult), ["W4"], ["V5"])
                T(lambda e, uu=uu: e.matmul(psb[3][:, 0:nt], lhsT=ones_b[:], rhs=Vb[5][:, 0:nt], start=(uu == 0), stop=(uu == UPN - 1)), ["V5", "ones_b"], pk(3))
            Ac(lambda e: e.activation(out=W[1][:, 0:nt], in_=psb[3][:, 0:nt], func=AF.Ln, scale=1.0 / (UPN * 128), bias=EPS), pk(3), ["W1"])
            Ac(lambda e: e.activation(out=W[1][:, 0:nt], in_=W[1][:, 0:nt], func=AF.Exp, scale=-0.5), ["W1"], ["W1"])
            for uu in range(UPN):
                u = g * UPN + uu
                V(lambda e, uu=uu, u=u: e.scalar_tensor_tensor(out=scr[:, OB + u, 0:nt], in0=scr[:, OB + u, 0:nt], scalar=pv[:, l, PV_NW + u:PV_NW + u + 1], in1=W[1][:, 0:nt],
                                                                op0=ALU.mult, op1=ALU.mult), [f"scr{OB + u}", "pv", "W1"], [f"scr{OB + u}"])
        if samp:
            P.dma("sp", lambda e: e.dma_start(out=so_cvc_d[l][:, :, 0:2, :], in_=shC[:, :, 1:3, :]), "shst", reads=["shC"], writes=["odram"])
            P.dma("sp", lambda e: e.dma_start(out=so_cvc_d[l][:, :, 2, :], in_=xsC[:]), "shst", reads=["xsC"], writes=["odram"])
        elif state.get("ptile", 0) == NPT - 1:
            P.dma("sp", lambda e: e.dma_start(out=o_cvc_d[l], in_=histc[:, l, :, :]), "ost", reads=[f"histc{l}"], writes=["odram"])
        merge_branch(l, nt, 2, wbc_d[l], NU_C, OB, first=False)

        for dc in range(KC):
            s = ring_load(wo_d[l][dc])
            b = dc % 2
            for kc in range(KC):
                T(lambda e, s=s, kc=kc, b=b: e.matmul(psb[b][:, 0:nt], lhsT=ring[:, s, kc, :], rhs=scr[:, kc, 0:nt], start=(kc == 0), stop=(kc == KC - 1)),
                  [f"ring{s}", f"scr{kc}"], pk(b))
            resid_add(nt, samp, dc, b, Gt, lname)

    def hh_off(hs):
        return hs.start

    def merge_branch(l, nt, br, w_d, nk, OB, first):
        for dc in range(KC):
            b = dc % 2
            proj(l, f"gate{br}_{dc}", 2 + b, nt)
            t = nexttmp()
            Ac(lambda e, b=b, t=t: e.activation(out=tmp[:, t, 0:nt], in_=psb[2 + b][:, 0:nt], func=AF.Sigmoid), pk(2 + b), [f"tmp{t}"])
            s = ring_load(w_d[dc], nk)
            for kc in range(nk):
                T(lambda e, s=s, kc=kc, b=b: e.matmul(psb[b][:, 0:nt], lhsT=ring[:, s, kc, :], rhs=scr[:, OB + kc, 0:nt], start=(kc == 0), stop=(kc == nk - 1)),
                  [f"ring{s}", f"scr{OB + kc}"], pk(b))
            if first:
                V(lambda e, b=b, t=t, dc=dc: e.tensor_tensor(out=scr[:, dc, 0:nt], in0=tmp[:, t, 0:nt], in1=psb[b][:, 0:nt], op=ALU.mult),
                  [f"tmp{t}"] + pk(b), [f"scr{dc}"])
            else:
                V(lambda e, b=b, t=t: e.tensor_tensor(out=tmp[:, t, 0:nt], in0=tmp[:, t, 0:nt], in1=psb[b][:, 0:nt], op=ALU.mult),
                  [f"tmp{t}"] + pk(b), [f"tmp{t}"])
                V(lambda e, t=t, dc=dc: e.tensor_tensor(out=scr[:, dc, 0:nt], in0=scr[:, dc, 0:nt], in1=tmp[:, t, 0:nt], op=ALU.add),
                  [f"tmp{t}", f"scr{dc}"], [f"scr{dc}"])


    tiles = [("p", i) for i in range(NPT)] + [("s", 0)]
    xkeys = [f"x{kc}" for kc in range(KC)]
    V(lambda e: e.memset(hista[:], 0.0), [], ["hista0", "hista1"])
    V(lambda e: e.memset(histc[:], 0.0), [], ["histc0", "histc1"])
    for (kind, ti) in tiles:
        samp = kind == "s"
        nt = NS if samp else NT
        state["ptile"] = ti
        src = xs_d if samp else xp_d[:, :, ti * NT:(ti + 1) * NT]
        P.dma("sp", lambda e, src=src, nt=nt: e.dma_start(out=x[:, :, 0:nt], in_=src), "xload", writes=xkeys)
        for l in range(2):
            ffn(l, 1, nt, samp)
            mixer(l, nt, samp)
            ffn(l, 2, nt, samp)
        dst = ys_d if samp else yp_d[:, :, ti * NT:(ti + 1) * NT]
        rms_stats(nt)
        fn_w = norms[:, 6, :]
        for kc in range(KC):
            V(lambda e, kc=kc, nt=nt: e.scalar_tensor_tensor(out=x[:, kc, 0:nt], in0=x[:, kc, 0:nt], scalar=fn_w[:, kc:kc + 1],
                                                            in1=rstd[:, 0:nt], op0=ALU.mult, op1=ALU.mult),
              [f"x{kc}", "rstd", "norms"], [f"x{kc}"])
        P.dma("sp", lambda e, dst=dst, nt=nt: e.dma_start(out=dst, in_=x[:, :, 0:nt]), "ystore", reads=xkeys, writes=["ydram"])

    P.emit(nc, es)
    es.close()
    return nc


def _units(w):
    K, N = w.shape
    return np.ascontiguousarray(w.reshape(K // 128, 128, N // 128, 128).transpose(2, 1, 0, 3))


def _fm(v):
    n, Fd = v.shape
    return np.ascontiguousarray(v.reshape(n, Fd // 128, 128).transpose(2, 1, 0))


def _consts():
    c = np.zeros((128, 640), np.float32)
    c[:, 0:128] = np.eye(128)
    c[:, 128:256] = np.triu(np.ones((128, 128)))
    c[:, 256:384] = np.tril(np.ones((128, 128)), -1)
    c[:, 384:512] = 1.0
    c[0:64, 512:576] = 1.0
    c[64:128, 576:640] = 1.0
    return c


def _padcols(w, n=128):
    K, c = w.shape
    out = np.zeros((K, n), np.float32)
    out[:, :c] = w
    return out


_NC = {}


def kernel(_cfg=None, **inp):
    cfg = dict(FULL if _cfg is None else _cfg)
    D, FF, SEQ, HA, HB, HC, G = (cfg[k] for k in ("D", "FF", "SEQ", "HA", "HB", "HC", "G"))
    KC = D // 128
    NU_C = HC // 2
    NCA, NCC = 3 * HA, NU_C + 2 * G
    f = lambda a: np.asarray(a, np.float32)
    shared = {"consts": _consts()}
    shared["wada"] = np.stack([_units(f(inp["w_ada"][l])) for l in range(2)])
    shared["bada"] = np.stack([np.ascontiguousarray(f(inp["b_ada"][l]).reshape(9 * KC, 128).T) for l in range(2)])
    nl = [f(inp[k][l]) for l in range(2) for k in ("norm1", "norm2", "norm3")] + [f(inp["final_norm"])]
    shared["norms"] = np.ascontiguousarray(np.stack(nl).reshape(7, KC, 128).transpose(2, 0, 1))
    for l in range(2):
        for w in (1, 2):
            shared[f"wg{l}{w}"] = _units(f(inp[f"ffn{w}_wg"][l]))
            shared[f"wu{l}{w}"] = _units(f(inp[f"ffn{w}_wu"][l]))
            shared[f"wd{l}{w}"] = _units(f(inp[f"ffn{w}_wd"][l]))
    QK_A, V_A = HA * 128, HA * 128
    splits = [2 * QK_A + V_A, V_A, HA, HA, HB * 128, HB * 128, HB * 256, 16, HB * 256, HC * 64, HC * 64 + 2 * G * 128, HC, 3 * D]
    offs = np.concatenate([[0], np.cumsum(splits)])
    UP = unit_plan(cfg)
    NPV = 4 + HB + 2 * NU_C + 32
    pvs, cwas, cwcs, nrms, wgates = [], [], [], [], []
    for l in range(2):
        w = f(inp["w_in"][l])
        grp = [w[:, offs[i]:offs[i + 1]] for i in range(13)]
        qkv_a, z_a, beta_a, dec_a, q_b, k_b, v_b, lr_b, r_b, z_c, xbc_c, dt_c, gates = grp
        cols = {}
        cols["beta"] = _padcols(beta_a)
        cols["dec"] = _padcols(dec_a)
        for h in range(HA):
            cols[f"aq{h}"] = qkv_a[:, h * 128:(h + 1) * 128]
            cols[f"ak{h}"] = qkv_a[:, QK_A + h * 128:QK_A + (h + 1) * 128]
            cols[f"av{h}"] = qkv_a[:, 2 * QK_A + h * 128:2 * QK_A + (h + 1) * 128]
            cols[f"az{h}"] = z_a[:, h * 128:(h + 1) * 128]
        cols["lr"] = _padcols(lr_b)
        for h in range(HB):
            cols[f"bq{h}"] = q_b[:, h * 128:(h + 1) * 128]
            cols[f"bk{h}"] = k_b[:, h * 128:(h + 1) * 128]
            for i in range(2):
                cols[f"bv{h}_{i}"] = v_b[:, h * 256 + i * 128:h * 256 + (i + 1) * 128]
                cols[f"br{h}_{i}"] = r_b[:, h * 256 + i * 128:h * 256 + (i + 1) * 128]
        cols["dt"] = _padcols(dt_c)
        inner = HC * 64
        for g in range(G):
            cols[f"cB{g}"] = xbc_c[:, inner + g * 128:inner + (g + 1) * 128]
            cols[f"cC{g}"] = xbc_c[:, inner + G * 128 + g * 128:inner + G * 128 + (g + 1) * 128]
        for u in range(NU_C):
            cols[f"cx{u}"] = xbc_c[:, u * 128:(u + 1) * 128]
            cols[f"cz{u}"] = z_c[:, u * 128:(u + 1) * 128]
        for br in range(3):
            for dc in range(KC):
                cols[f"gate{br}_{dc}"] = gates[:, br * D + dc * 128:br * D + (dc + 1) * 128]
        arr = np.zeros((len(UP), 128, KC, 128), np.float32)
        for n, i in UP.items():
            arr[i] = cols[n].reshape(KC, 128, 128).transpose(1, 0, 2)
        shared[f"win{l}"] = arr
        shared[f"wba{l}"] = _units(f(inp["w_branch_gdn"][l]))
        shared[f"wbb{l}"] = _units(f(inp["w_branch_gla"][l]))
        shared[f"wbc{l}"] = _units(f(inp["w_branch_ssd"][l]))
        shared[f"wo{l}"] = _units(f(inp["w_out"][l]))
        pvl = np.zeros((128, NPV), np.float32)
        pvl[:HA, 0] = f(inp["gdn_a_log"][l])
        pvl[:HA, 1] = f(inp["gdn_dt_bias"][l])
        pvl[:HC, 2] = f(inp["ssd_a_log"][l])
        pvl[:HC, 3] = f(inp["ssd_dt_bias"][l])
        pvl[:, 4:4 + HB] = f(inp["gla_b_gate"][l]).reshape(HB, 128).T
        pvl[:, 4 + HB:4 + HB + NU_C] = np.repeat(f(inp["ssd_d"][l]).reshape(NU_C, 2), 64, axis=1).T
        pvl[:, 4 + HB + NU_C:4 + HB + 2 * NU_C] = f(inp["ssd_norm_w"][l]).reshape(NU_C, 128).T
        pvl[:32, 4 + HB + 2 * NU_C:] = np.eye(32)
        pvs.append(pvl)
        cwas.append(np.ascontiguousarray(f(inp["gdn_conv_w"][l]).reshape(4, NCA, 128).transpose(2, 1, 0)))
        cw = f(inp["ssd_conv_w"][l])
        cb = f(inp["ssd_conv_b"][l])
        cwc = np.concatenate([cw.reshape(4, NCC, 128), cb.reshape(1, NCC, 128)], 0)
        cwcs.append(np.ascontiguousarray(cwc.transpose(2, 1, 0)))
        nrms.append(np.concatenate([np.tile(f(inp["gdn_norm_w"][l])[None], (128, 1)), np.tile(f(inp["gla_norm_w"][l])[None], (128, 1))], 1))
        wgates.append(f(inp["gla_w_gate"][l]))
    shared["pv"] = np.stack(pvs)
    shared["cwa"] = np.stack(cwas)
    shared["cwc"] = np.stack(cwcs)
    shared["nrm"] = np.ascontiguousarray(np.stack(nrms))
    shared["wgate"] = np.stack(wgates)

    xp = f(inp["x_prompt"])
    xs = f(inp["x_sample"])[:, 0, :]
    cp = f(inp["c_prompt"])
    cs = f(inp["c_sample"])
    sg, sl_, ss, sca, scc = (f(inp[k]) for k in ("state_gdn", "state_gla", "state_ssd", "state_gdn_conv", "state_ssd_conv"))
    in_maps = []
    for c in range(8):
        m = dict(shared)
        b0 = 16 * c
        m["xp"] = _fm(xp[c % 4])
        m["xs"] = _fm(xs[b0:b0 + 16])
        m["cT"] = _fm(np.concatenate([cp[c % 4][None], cs[b0:b0 + 16]], 0))
        m["s_gdn"] = np.ascontiguousarray(sg[:, b0:b0 + 16])
        m["s_gla"] = np.ascontiguousarray(sl_[:, b0:b0 + 16])
        m["s_ssd"] = np.ascontiguousarray(ss[:, b0:b0 + 16].reshape(2, 16, NU_C, 2, 64, 128).transpose(0, 1, 2, 5, 3, 4).reshape(2, 16, NU_C, 128, 128))
        m["s_cva"] = np.ascontiguousarray(sca[:, b0:b0 + 16].reshape(2, 16, 3, NCA, 128).transpose(0, 4, 3, 2, 1))
        m["s_cvc"] = np.ascontiguousarray(scc[:, b0:b0 + 16].reshape(2, 16, 3, NCC, 128).transpose(0, 4, 3, 2, 1))
        in_maps.append(m)
    key = tuple(sorted(cfg.items()))
    if key not in _NC:
        _NC[key] = build(cfg)
    res = run_bass_kernel_spmd(_NC[key], in_maps, core_ids=list(range(8)))
    R = res.results
    y_prompt = np.stack([R[c]["yp"].transpose(2, 1, 0).reshape(SEQ, D) for c in range(4)])
    y_sample = np.concatenate([R[c]["ys"].transpose(2, 1, 0).reshape(NS, D) for c in range(8)])[:, None, :]

    def conv_p(name, ncu):
        return np.stack([R[c][name].transpose(0, 3, 2, 1).reshape(2, 3, ncu * 128) for c in range(4)], 1)

    def conv_s(name, ncu):
        return np.concatenate([R[c][name].transpose(0, 4, 3, 2, 1).reshape(2, 16, 3, ncu * 128) for c in range(8)], 1)

    def ssd_back(a):
        sh = a.shape[:-3]
        return a.reshape(*sh, NU_C, 128, 2, 64).transpose(*range(len(sh)), len(sh), len(sh) + 2, len(sh) + 3, len(sh) + 1).reshape(*sh, HC, 64, 128)
    p_gdn_conv = conv_p("o_cva", NCA)
    p_gdn = np.stack([R[c]["o_gdn"] for c in range(4)], 1)
    p_gla = np.stack([R[c]["o_gla"] for c in range(4)], 1)
    p_ssd_conv = conv_p("o_cvc", NCC)
    p_ssd = np.stack([ssd_back(R[c]["o_ssd"]) for c in range(4)], 1)
    s_gdn_conv = conv_s("so_cva", NCA)
    s_gdn = np.concatenate([R[c]["so_gdn"] for c in range(8)], 1)
    s_gla = np.concatenate([R[c]["so_gla"] for c in range(8)], 1)
    s_ssd_conv = conv_s("so_cvc", NCC)
    s_ssd = np.concatenate([ssd_back(R[c]["so_ssd"]) for c in range(8)], 1)
    outs = (y_prompt, y_sample, p_gdn_conv, p_gdn, p_gla, p_ssd_conv, p_ssd, s_gdn_conv, s_gdn, s_gla, s_ssd_conv, s_ssd)
    return tuple(np.ascontiguousarray(o, dtype=np.float32) for o in outs)
```

```python
import numpy as np
from contextlib import ExitStack
import concourse.bass as bass
import concourse.mybir as mybir
from concourse.bass_utils import run_bass_kernel_spmd

F32, BF16 = mybir.dt.float32, mybir.dt.bfloat16
AF = mybir.ActivationFunctionType
ALU = mybir.AluOpType
AX = mybir.AxisListType

FULL = dict(D=2048, FF=5504, SEQ=2048, HA=8, HB=4, HC=32, G=4)
NT = 512
NS = 16
NV = 17
EPS = 1e-6
RS = 3


class _Rec:
    def __init__(self):
        self.call = None

    def __getattr__(self, name):
        def f(*a, **k):
            self.call = (name, a, k)
            return self
        return f


def _bind(fn):
    r = _Rec()
    fn(r)
    name, a, k = r.call
    return lambda eng: getattr(eng, name)(*a, **k)


class Prog:
    ENG = ("pe", "act", "dve", "pool", "sp")

    def __init__(self):
        self.ops = {e: [] for e in self.ENG}
        self.cnt = {e: 0 for e in ("pe", "act", "dve")}
        self.lastw = {}
        self.readers = {}
        self.dcnt = {}
        self.known = {e: {} for e in self.ENG}

    def _deps(self, eng, reads, writes):
        need = {}

        def add(t, raw):
            s, v, e = t
            if e == eng and eng == "pe":
                return
            if need.get(s, 0) < v:
                need[s] = v
        for k in reads:
            if k in self.lastw:
                add(self.lastw[k], True)
        for k in writes:
            if k in self.lastw:
                add(self.lastw[k], True)
            for r in self.readers.get(k, ()):
                add(r, False)
        kn = self.known[eng]
        out = []
        for s, v in need.items():
            if kn.get(s, 0) < v:
                kn[s] = v
                out.append((s, v))
        return out

    def _commit(self, tok, reads, writes):
        for k in reads:
            self.readers.setdefault(k, []).append(tok)
        for k in writes:
            self.lastw[k] = tok
            self.readers[k] = []

    def op(self, eng, fn, reads=(), writes=()):
        writes = list(writes) + [k for k in reads if k.startswith("ps")]
        reads = [k for k in reads if not k.startswith("ps")]
        waits = self._deps(eng, reads, writes)
        self.cnt[eng] += 1
        s = "c_" + eng
        self.ops[eng].append((waits, _bind(fn), s, 1))
        self._commit((s, self.cnt[eng], eng), reads, writes)

    def dma(self, q, fn, semkey, reads=(), writes=()):
        waits = self._deps(q, reads, writes)
        n = self.dcnt.get(semkey, 0)
        s = "d_" + semkey
        if n > 0 and self.known[q].get(s, 0) < 16 * n:
            self.known[q][s] = 16 * n
            waits.append((s, 16 * n))
        self.dcnt[semkey] = n + 1
        self.ops[q].append((waits, _bind(fn), s, 16))
        self._commit((s, 16 * (n + 1), "dma"), reads, writes)

    def emit(self, nc, es):
        names = ["c_pe", "c_act", "c_dve"] + ["d_" + k for k in self.dcnt]
        sems = {n: es.enter_context(nc.semaphore(n)) for n in names}
        finals = [(sems["d_" + k], 16 * n) for k, n in self.dcnt.items()]
        block = es.enter_context(nc.Block())
        decs = {"pe": block.tensor, "act": block.scalar, "dve": block.vector, "pool": block.gpsimd, "sp": block.sync}
        for e in self.ENG:
            ops = self.ops[e]

            def body(eng, ops=ops, last=(e == "sp")):
                for waits, fn, s, inc in ops:
                    for ws, wv in waits:
                        eng.wait_ge(sems[ws], wv)
                    fn(eng).then_inc(sems[s], inc)
                if last:
                    for sm, v in finals:
                        eng.wait_ge(sm, v)
            decs[e](body)


def unit_plan(cfg):
    HA, HB, HC, G, KC = cfg["HA"], cfg["HB"], cfg["HC"], cfg["G"], cfg["D"] // 128
    names = ["beta", "dec"]
    for h in range(HA):
        names += [f"aq{h}", f"ak{h}", f"av{h}", f"az{h}"]
    names += ["lr"]
    for h in range(HB):
        names += [f"bq{h}", f"bk{h}", f"bv{h}_0", f"bv{h}_1", f"br{h}_0", f"br{h}_1"]
    names += ["dt"]
    for g in range(G):
        names += [f"cB{g}", f"cC{g}"]
    for u in range(HC // 2):
        names += [f"cx{u}", f"cz{u}"]
    for br in range(3):
        for dc in range(KC):
            names += [f"gate{br}_{dc}"]
    return {n: i for i, n in enumerate(names)}


def build(cfg):
    D, FF, SEQ, HA, HB, HC, G = (cfg[k] for k in ("D", "FF", "SEQ", "HA", "HB", "HC", "G"))
    KC = D // 128
    FC = FF // 128
    FH = (FC + 2) // 3
    NU_C = HC // 2
    HPG = HC // G
    UPN = NU_C // G
    NCA = 3 * HA
    NCC = NU_C + 2 * G
    NPT = SEQ // NT
    UP = unit_plan(cfg)
    NUW = len(UP)
    assert KC <= 16 and NU_C <= 16 and HC <= 32

    nc = bass.Bass("TRN2", target_bir_lowering=False)
    P = Prog()
    es = ExitStack()

    def din(name, shape):
        return nc.dram_tensor(name, list(shape), F32, kind="ExternalInput").ap()

    def dout(name, shape):
        return nc.dram_tensor(name, list(shape), F32, kind="ExternalOutput").ap()

    def sb(name, shape, dt=F32):
        return es.enter_context(nc.sbuf_tensor(name, list(shape), dt))

    xp_d = din("xp", [128, KC, SEQ])
    xs_d = din("xs", [128, KC, NS])
    cT_d = din("cT", [128, KC, NV])
    consts_d = din("consts", [128, 640])
    wada_d = din("wada", [2, 9 * KC, 128, KC, 128])
    bada_d = din("bada", [2, 128, 9 * KC])
    norms_d = din("norms", [128, 7, KC])
    ffn_d = {}
    for l in range(2):
        for w in (1, 2):
            ffn_d[(l, w)] = (din(f"wg{l}{w}", [FC, 128, KC, 128]), din(f"wu{l}{w}", [FC, 128, KC, 128]),
                             din(f"wd{l}{w}", [KC, 128, FC, 128]))
    win_d = [din(f"win{l}", [NUW, 128, KC, 128]) for l in range(2)]
    wba_d = [din(f"wba{l}", [KC, 128, HA, 128]) for l in range(2)]
    wbb_d = [din(f"wbb{l}", [KC, 128, 2 * HB, 128]) for l in range(2)]
    wbc_d = [din(f"wbc{l}", [KC, 128, NU_C, 128]) for l in range(2)]
    wo_d = [din(f"wo{l}", [KC, 128, KC, 128]) for l in range(2)]
    NPV = 4 + HB + 2 * NU_C + 32
    pv_d = din("pv", [2, 128, NPV])
    cwa_d = din("cwa", [2, 128, NCA, 4])
    cwc_d = din("cwc", [2, 128, NCC, 5])
    nrm_d = din("nrm", [2, 128, 384])
    wgate_d = din("wgate", [2, 16, HB * 128])
    s_gdn_d = din("s_gdn", [2, NS, HA, 128, 128])
    s_gla_d = din("s_gla", [2, NS, HB, 128, 256])
    s_ssd_d = din("s_ssd", [2, NS, NU_C, 128, 128])
    s_cva_d = din("s_cva", [2, 128, NCA, 3, NS])
    s_cvc_d = din("s_cvc", [2, 128, NCC, 3, NS])
    yp_d = dout("yp", [128, KC, SEQ])
    ys_d = dout("ys", [128, KC, NS])
    o_gdn_d = dout("o_gdn", [2, HA, 128, 128])
    o_gla_d = dout("o_gla", [2, HB, 128, 256])
    o_ssd_d = dout("o_ssd", [2, NU_C, 128, 128])
    o_cva_d = dout("o_cva", [2, 128, NCA, 3])
    o_cvc_d = dout("o_cvc", [2, 128, NCC, 3])
    so_gdn_d = dout("so_gdn", [2, NS, HA, 128, 128])
    so_gla_d = dout("so_gla", [2, NS, HB, 128, 256])
    so_ssd_d = dout("so_ssd", [2, NS, NU_C, 128, 128])
    so_cva_d = dout("so_cva", [2, 128, NCA, 3, NS])
    so_cvc_d = dout("so_cvc", [2, 128, NCC, 3, NS])

    x = sb("x", [128, KC, NT])
    h = sb("h", [128, KC, NT], BF16)
    scr = sb("scr", [128, 32, NT], BF16)
    ring = sb("ring", [128, RS, 16, 128], BF16)
    wdr = sb("wdr", [128, 2, FH, 128], BF16)
    mod = [sb(f"mod{l}", [128, 9 * KC, NV]) for l in range(2)]
    cst = sb("cst", [128, 640])
    ones_b = sb("ones_b", [128, 128], BF16)
    norms = sb("norms_sb", [128, 7, KC])
    scT = sb("scT", [128, KC, NV], BF16)
    tmp = sb("tmp", [128, 2, NT])
    rstd = sb("rstd", [128, NT])
    pv = sb("pv_sb", [128, 2, NPV])
    cwa = sb("cwa_sb", [128, 2, NCA, 4])
    cwc = sb("cwc_sb", [128, 2, NCC, 5])
    nrm1 = sb("nrm_sb", [128, 384])
    wgate = sb("wgate_b", [16, 2, HB * 128], BF16)
    nSa, nSb, nSc = 2 * HA * 128, 2 * HB * 256, 2 * NU_C * 128
    nA, nC = NCA * 3 * NS, NCC * 3 * NS
    PSM = sb("PSM", [128, max(nSa + nSb + nSc, nA + nC + NCA * NS + NCC * NS + 512)])
    Sa = PSM[:, 0:nSa].rearrange("p (l h d) -> p l h d", l=2, h=HA)
    Sb_ = PSM[:, nSa:nSa + nSb].rearrange("p (l h d) -> p l h d", l=2, h=HB)
    Sc = PSM[:, nSa + nSb:nSa + nSb + nSc].rearrange("p (l h d) -> p l h d", l=2, h=NU_C)
    shA = PSM[:, 0:nA].rearrange("p (u j b) -> p u j b", u=NCA, j=3)
    shC = PSM[:, nA:nA + nC].rearrange("p (u j b) -> p u j b", u=NCC, j=3)
    o2 = nA + nC
    xsA = PSM[:, o2:o2 + NCA * NS].rearrange("p (u b) -> p u b", u=NCA)
    xsC = PSM[:, o2 + NCA * NS:o2 + NCA * NS + NCC * NS].rearrange("p (u b) -> p u b", u=NCC)
    o3 = o2 + NCA * NS + NCC * NS
    SS = PSM[:, o3:o3 + 512].rearrange("p (s d) -> p s d", s=2)
    PKEYS = [f"Sa{l}_{i}" for l in range(2) for i in range(HA)] + [f"Sb{l}_{i}" for l in range(2) for i in range(HB)] + \
            [f"Sc{l}_{i}" for l in range(2) for i in range(NU_C)]
    hista = sb("hista", [128, 2, NCA, 3])
    histc = sb("histc", [128, 2, NCC, 3])
    Wall = sb("Wall", [128, 9, NT + 4])
    W = [Wall[:, i, :] for i in range(9)]
    Wflat = Wall[:].rearrange("p a b -> p (a b)")
    XR = 4
    xring = [Wflat[:, 2 * k * (NT + 4):2 * k * (NT + 4) + 1024].bitcast(BF16).rearrange("p (k c) -> p k c", k=16) for k in range(XR)]
    Vb = [sb(f"V{i}", [128, NT], BF16) for i in range(6)]
    A_ = [sb(f"a{i}", [128, 256 if i == 4 else 128]) for i in range(13)]
    B_ = [sb(f"b{i}", [128, 256 if i in (0, 1, 10, 11, 12) else 128], BF16) for i in range(13)]
    tok = sb("tokv", [128, 16, 64])
    cs1 = sb("cs1", [128, 8])
    psb = [es.enter_context(nc.psum_tensor(f"ps{i}", [128, NT], F32)) for i in range(8)]
    ident = cst[:, 0:128]
    Umat = cst[:, 128:256]
    Lsmat = cst[:, 256:384]
    ones_f = cst[:, 384:512]
    BDmat = cst[:, 512:640]

    def pk(b, q0=0, q1=4):
        return [f"ps{b}"]

    state = {"ring": 0, "wd": 0, "tmp": 0}

    def ring_load(src_ap, nk=KC):
        s = state["ring"] % RS
        state["ring"] += 1
        P.dma("pool", lambda e, s=s: e.dma_start(out=ring[:, s, 0:nk, :], in_=src_ap), f"ring{s}", writes=[f"ring{s}"])
        return s

    def rslot(s):
        return ring[:, s] if s < RS else xring[s - RS]

    def rkeys(s):
        return [f"ring{s}"] if s < RS else [f"W{2 * (s - RS)}", f"W{2 * (s - RS) + 1}"]

    def ring_load_wide(src_ap):
        s = state.get("ringw", 0) % (RS + XR)
        state["ringw"] = state.get("ringw", 0) + 1
        P.dma("pool", lambda e, s=s: e.dma_start(out=rslot(s)[:, 0:KC, :], in_=src_ap), f"ring{s}", writes=rkeys(s))
        return s

    def nexttmp():
        i = state["tmp"] % 2
        state["tmp"] += 1
        return i

    def V(fn, r, w):
        P.op("dve", fn, r, w)

    def Ac(fn, r, w):
        P.op("act", fn, r, w)

    def T(fn, r, w):
        P.op("pe", fn, r, w)

    def ld(dst, src, key):
        P.dma("sp", lambda e: e.dma_start(out=dst, in_=src), key, writes=[key])
    ld(cst[:], consts_d, "cst")
    ld(norms[:], norms_d, "norms")
    bada = W[1][:, 0:2 * 9 * KC].rearrange("p (l u) -> p l u", l=2)
    cT = W[0][:, 0:KC * NV].rearrange("p (k v) -> p k v", k=KC)
    P.dma("sp", lambda e: e.dma_start(out=bada, in_=bada_d.rearrange("l p u -> p l u")), "bada", writes=["W1"])
    P.dma("sp", lambda e: e.dma_start(out=cT, in_=cT_d), "cT", writes=["W0"])
    ld(pv[:], pv_d.rearrange("l p u -> p l u"), "pv")
    ld(cwa[:], cwa_d.rearrange("l p u j -> p l u j"), "cwa")
    ld(cwc[:], cwc_d.rearrange("l p u j -> p l u j"), "cwc")
    P.dma("pool", lambda e: e.dma_start(out=wgate[:], in_=wgate_d.rearrange("l p u -> p l u")), "wgate", writes=["wgate"])
    V(lambda e: e.tensor_copy(out=ones_b[:], in_=cst[:, 384:512]), ["cst"], ["ones_b"])
    Ac(lambda e: e.activation(out=scT[:], in_=cT, func=AF.Silu), ["W0"], ["scT"])
    for l in range(2):
        for c in (0, 2):
            Ac(lambda e, l=l, c=c: e.activation(out=pv[:, l, c:c + 1], in_=pv[:, l, c:c + 1], func=AF.Exp), ["pv"], ["pv"])
            V(lambda e, l=l, c=c: e.tensor_scalar(out=pv[:, l, c:c + 1], in0=pv[:, l, c:c + 1], scalar1=-1.0, scalar2=None, op0=ALU.mult), ["pv"], ["pv"])
        V(lambda e, l=l: e.tensor_scalar(out=pv[:, l, 4:4 + HB], in0=pv[:, l, 4:4 + HB], scalar1=-1.0, scalar2=None, op0=ALU.mult), ["pv"], ["pv"])
    PV_BG, PV_D, PV_NW, PV_OH = 4, 4 + HB, 4 + HB + NU_C, 4 + HB + 2 * NU_C

    for l in range(2):
        for u in range(9 * KC):
            s = ring_load(wada_d[l, u])
            pb = u % 2
            for kc in range(KC):
                T(lambda e, s=s, kc=kc, pb=pb: e.matmul(psb[pb][:, 0:NV], lhsT=ring[:, s, kc, :], rhs=scT[:, kc, :],
                                                         start=(kc == 0), stop=(kc == KC - 1)),
                  [f"ring{s}", "scT"], pk(pb))
            V(lambda e, l=l, u=u, pb=pb: e.tensor_scalar(out=mod[l][:, u, :], in0=psb[pb][:, 0:NV], scalar1=bada[:, l, u:u + 1],
                                                          scalar2=None, op0=ALU.add),
              pk(pb) + ["W1"], [f"mod{l}"])
        for m in range(3):
            sc = mod[l][:, (3 * m + 1) * KC:(3 * m + 2) * KC, :]
            gt = mod[l][:, (3 * m + 2) * KC:(3 * m + 3) * KC, :]
            nb = norms[:, 3 * l + m, :].unsqueeze(2).to_broadcast([128, KC, NV])
            V(lambda e, sc=sc, nb=nb: e.scalar_tensor_tensor(out=sc, in0=sc, scalar=1.0, in1=nb, op0=ALU.add, op1=ALU.mult),
              [f"mod{l}", "norms"], [f"mod{l}"])
            if m != 1:
                V(lambda e, gt=gt: e.tensor_scalar(out=gt, in0=gt, scalar1=0.5, scalar2=None, op0=ALU.mult), [f"mod{l}"], [f"mod{l}"])

    def rms_stats(nt):
        for kc in range(KC):
            Ac(lambda e, kc=kc: e.activation(out=scr[:, kc, 0:nt], in_=x[:, kc, 0:nt], func=AF.Square), [f"x{kc}"], [f"scr{kc}"])
        for kc in range(KC):
            T(lambda e, kc=kc: e.matmul(psb[7][:, 0:nt], lhsT=ones_b[:], rhs=scr[:, kc, 0:nt], start=(kc == 0), stop=(kc == KC - 1)),
              [f"scr{kc}", "ones_b"], pk(7))
        Ac(lambda e: e.activation(out=rstd[:, 0:nt], in_=psb[7][:, 0:nt], func=AF.Ln, scale=1.0 / D, bias=EPS), pk(7), ["rstd"])
        Ac(lambda e: e.activation(out=rstd[:, 0:nt], in_=rstd[:, 0:nt], func=AF.Exp, scale=-0.5), ["rstd"], ["rstd"])

    def norm_mod(nt, samp, A, B, lname):
        rms_stats(nt)
        for kc in range(KC):
            t = nexttmp()
            o = h[:, kc, 0:nt]
            if not samp:
                V(lambda e, kc=kc, t=t: e.scalar_tensor_tensor(out=tmp[:, t, 0:nt], in0=x[:, kc, 0:nt], scalar=A[:, kc, 0:1],
                                                                in1=rstd[:, 0:nt], op0=ALU.mult, op1=ALU.mult),
                  [f"x{kc}", "rstd", lname], [f"tmp{t}"])
                Ac(lambda e, kc=kc, t=t, o=o: e.activation(out=o, in_=tmp[:, t, 0:nt], func=AF.Identity, bias=B[:, kc, 0:1]),
                   [f"tmp{t}", lname], [f"h{kc}"])
            else:
                V(lambda e, kc=kc, t=t: e.tensor_tensor(out=tmp[:, t, 0:nt], in0=x[:, kc, 0:nt], in1=rstd[:, 0:nt], op=ALU.mult),
                  [f"x{kc}", "rstd"], [f"tmp{t}"])
                V(lambda e, kc=kc, t=t: e.tensor_tensor(out=tmp[:, t, 0:nt], in0=tmp[:, t, 0:nt], in1=A[:, kc, 1:NV], op=ALU.mult),
                  [f"tmp{t}", lname], [f"tmp{t}"])
                V(lambda e, kc=kc, t=t, o=o: e.tensor_tensor(out=o, in0=tmp[:, t, 0:nt], in1=B[:, kc, 1:NV], op=ALU.add),
                  [f"tmp{t}", lname], [f"h{kc}"])

    def resid_add(nt, samp, dc, pbank, G, lname):
        if not samp:
            V(lambda e: e.scalar_tensor_tensor(out=x[:, dc, 0:nt], in0=psb[pbank][:, 0:nt], scalar=G[:, dc, 0:1],
                                               in1=x[:, dc, 0:nt], op0=ALU.mult, op1=ALU.add),
              pk(pbank) + [f"x{dc}", lname], [f"x{dc}"])
        else:
            t = nexttmp()
            V(lambda e: e.tensor_tensor(out=tmp[:, t, 0:nt], in0=psb[pbank][:, 0:nt], in1=G[:, dc, 1:NV], op=ALU.mult),
              pk(pbank) + [lname], [f"tmp{t}"])
            V(lambda e: e.tensor_tensor(out=x[:, dc, 0:nt], in0=x[:, dc, 0:nt], in1=tmp[:, t, 0:nt], op=ALU.add),
              [f"tmp{t}", f"x{dc}"], [f"x{dc}"])

    def ffn(l, w, nt, samp):
        m = 0 if w == 1 else 2
        lname = f"mod{l}"
        sh = mod[l][:, (3 * m) * KC:(3 * m + 1) * KC, :]
        A = mod[l][:, (3 * m + 1) * KC:(3 * m + 2) * KC, :]
        G = mod[l][:, (3 * m + 2) * KC:(3 * m + 3) * KC, :]
        norm_mod(nt, samp, A, sh, lname)
        wg_d, wu_d, wd_d = ffn_d[(l, w)]
        for (j0, j1) in ((0, FH), (FH, min(2 * FH, FC)), (min(2 * FH, FC), FC)):
            if j1 <= j0:
                continue
            for j in range(j0, j1):
                b = j % 2
                sg = ring_load_wide(wg_d[j])
                su = ring_load_wide(wu_d[j])
                for kc in range(KC):
                    T(lambda e, sg=sg, kc=kc, b=b: e.matmul(psb[b][:, 0:nt], lhsT=rslot(sg)[:, kc, :], rhs=h[:, kc, 0:nt],
                                                             start=(kc == 0), stop=(kc == KC - 1)),
                      rkeys(sg) + [f"h{kc}"], pk(b))
                for kc in range(KC):
                    T(lambda e, su=su, kc=kc, b=b: e.matmul(psb[2 + b][:, 0:nt], lhsT=rslot(su)[:, kc, :], rhs=h[:, kc, 0:nt],
                                                             start=(kc == 0), stop=(kc == KC - 1)),
                      rkeys(su) + [f"h{kc}"], pk(2 + b))
                t = nexttmp()
                Ac(lambda e, b=b, t=t: e.activation(out=tmp[:, t, 0:nt], in_=psb[b][:, 0:nt], func=AF.Silu), pk(b), [f"tmp{t}"])
                V(lambda e, b=b, t=t, jj=j - j0: e.tensor_tensor(out=scr[:, jj, 0:nt], in0=tmp[:, t, 0:nt], in1=psb[2 + b][:, 0:nt], op=ALU.mult),
                  [f"tmp{t}"] + pk(2 + b), [f"scr{j - j0}"])
            nj = j1 - j0
            for dc in range(KC):
                ws = state.get("ringw", 0) % (RS + XR)
                state["ringw"] = state.get("ringw", 0) + 1
                b = 4 + dc % 2
                P.dma("pool", lambda e, ws=ws, dc=dc, j0=j0, j1=j1, nj=nj: e.dma_start(out=rslot(ws)[:, 0:nj, :], in_=wd_d[dc, :, j0:j1, :]),
                      f"ring{ws}", writes=rkeys(ws))
                for jj in range(nj):
                    T(lambda e, ws=ws, jj=jj, b=b, nj=nj: e.matmul(psb[b][:, 0:nt], lhsT=rslot(ws)[:, jj, :], rhs=scr[:, jj, 0:nt],
                                                                    start=(jj == 0), stop=(jj == nj - 1)),
                      rkeys(ws) + [f"scr{jj}"], pk(b))
                resid_add(nt, samp, dc, b, G, lname)

    hk = [f"h{kc}" for kc in range(KC)]

    def proj(l, uname, pbank, nt):
        s = ring_load(win_d[l][UP[uname]])
        for kc in range(KC):
            T(lambda e, s=s, kc=kc: e.matmul(psb[pbank][:, 0:nt], lhsT=ring[:, s, kc, :], rhs=h[:, kc, 0:nt], start=(kc == 0), stop=(kc == KC - 1)),
              [f"ring{s}", f"h{kc}"], pk(pbank))

    def conv_unit(l, pbank, nt, samp, cw_ap, cwkey, bias_ap, hist_ap, histkey, shist, xstash, shkey, xskey, rawW, outW):
        raw, out = W[rawW], W[outW]
        rk, ok = f"W{rawW}", f"W{outW}"
        if not samp:
            V(lambda e: e.tensor_copy(out=raw[:, 0:3], in_=hist_ap), [histkey], [rk])
            Ac(lambda e: e.activation(out=raw[:, 3:3 + nt], in_=psb[pbank][:, 0:nt], func=AF.Copy), pk(pbank), [rk])
            V(lambda e: e.tensor_scalar(out=out[:, 0:nt], in0=raw[:, 0:nt], scalar1=cw_ap[:, 0:1], scalar2=None, op0=ALU.mult), [rk, cwkey], [ok])
            for j in range(1, 4):
                V(lambda e, j=j: e.scalar_tensor_tensor(out=out[:, 0:nt], in0=raw[:, j:j + nt], scalar=cw_ap[:, j:j + 1], in1=out[:, 0:nt],
                                                        op0=ALU.mult, op1=ALU.add), [rk, ok, cwkey], [ok])
            V(lambda e: e.tensor_copy(out=hist_ap, in_=raw[:, nt:nt + 3]), [rk], [histkey])
        else:
            Ac(lambda e: e.activation(out=xstash, in_=psb[pbank][:, 0:nt], func=AF.Copy), pk(pbank), [xskey])
            V(lambda e: e.tensor_scalar(out=out[:, 0:nt], in0=xstash, scalar1=cw_ap[:, 3:4], scalar2=None, op0=ALU.mult), [xskey, cwkey], [ok])
            for j in range(3):
                V(lambda e, j=j: e.scalar_tensor_tensor(out=out[:, 0:nt], in0=shist[:, j, :], scalar=cw_ap[:, j:j + 1], in1=out[:, 0:nt],
                                                        op0=ALU.mult, op1=ALU.add), [shkey, ok, cwkey], [ok])
        if bias_ap is not None:
            Ac(lambda e: e.activation(out=out[:, 0:nt], in_=out[:, 0:nt], func=AF.Silu, bias=bias_ap), [ok, cwkey], [ok])
        else:
            Ac(lambda e: e.activation(out=out[:, 0:nt], in_=out[:, 0:nt], func=AF.Silu), [ok], [ok])

    def rinv_of(srcW, nt, dstW):
        V(lambda e: e.tensor_tensor(out=Vb[5][:, 0:nt], in0=W[srcW][:, 0:nt], in1=W[srcW][:, 0:nt], op=ALU.mult), [f"W{srcW}"], ["V5"])
        T(lambda e: e.matmul(psb[6][:, 0:nt], lhsT=ones_b[:], rhs=Vb[5][:, 0:nt], start=True, stop=True), ["V5", "ones_b"], pk(6))
        Ac(lambda e: e.activation(out=W[dstW][:, 0:nt], in_=psb[6][:, 0:nt], func=AF.Ln, bias=EPS), pk(6), [f"W{dstW}"])
        Ac(lambda e: e.activation(out=W[dstW][:, 0:nt], in_=W[dstW][:, 0:nt], func=AF.Exp, scale=-0.5), [f"W{dstW}"], [f"W{dstW}"])

    def decay_setup(l, nt, CL, nch, uname, col_alog, col_dtb, nheads, dtW, with_dt, kstride):
        proj(l, uname, 5, nt)
        Ac(lambda e: e.activation(out=W[6][:, 0:nt], in_=psb[5][:, 0:nt], func=AF.Exp, bias=pv[:, l, col_dtb:col_dtb + 1]), pk(5) + ["pv"], ["W6"])
        Ac(lambda e: e.activation(out=W[6][:, 0:nt], in_=W[6][:, 0:nt], func=AF.Ln, bias=1.0), ["W6"], ["W6"])
        if with_dt:
            V(lambda e: e.tensor_copy(out=W[dtW][:, 0:nt], in_=W[6][:, 0:nt]), ["W6"], [f"W{dtW}"])
        V(lambda e: e.tensor_scalar(out=W[6][:, 0:nt], in0=W[6][:, 0:nt], scalar1=pv[:, l, col_alog:col_alog + 1], scalar2=None, op0=ALU.mult),
          ["W6", "pv"], ["W6"])
        if CL == 1:
            V(lambda e: e.tensor_copy(out=W[7][:, 0:nt], in_=W[6][:, 0:nt]), ["W6"], ["W7"])
        else:
            for c in range(nch):
                sl = slice(c * CL, (c + 1) * CL)
                V(lambda e, sl=sl: e.tensor_tensor_scan(out=W[7][:, sl], data0=W[6][:, sl], data1=W[6][:, sl], initial=0.0, op0=ALU.add, op1=ALU.bypass),
                  ["W6"], ["W7"])
        for c in range(nch):
            sl = slice(c * CL, (c + 1) * CL)
            T(lambda e, sl=sl: e.transpose(out=psb[6][0:CL, 0:128], in_=W[7][:, sl], identity=ident), ["W7", "cst"], pk(6, 0, 1))
            T(lambda e, sl=sl: e.transpose(out=psb[6][0:CL, 128:256], in_=W[dtW][:, sl], identity=ident), [f"W{dtW}", "cst"], pk(6, 1, 2))
            V(lambda e, c=c: e.tensor_copy(out=tok[0:CL, c, 0:nheads], in_=psb[6][0:CL, 0:nheads]), pk(6, 0, 1), ["tok"])
            V(lambda e, c=c: e.tensor_copy(out=tok[0:CL, c, kstride:kstride + nheads], in_=psb[6][0:CL, 128:128 + nheads]), pk(6, 1, 2), ["tok"])

    def decay_mats(l, CL, c, hd, need_dl):
        sl = slice(c * CL, (c + 1) * CL)
        V(lambda e: e.tensor_scalar(out=A_[3][:, 0:CL], in0=W[7][:, sl], scalar1=pv[:, l, PV_OH + hd:PV_OH + hd + 1], scalar2=None, op0=ALU.mult),
          ["W7", "pv"], ["a3"])
        if CL == 1:
            return
        T(lambda e: e.matmul(psb[4][0:CL, 0:CL], lhsT=ones_f[:, 0:CL], rhs=A_[3][:, 0:CL], start=True, stop=True), ["a3", "cst"], pk(4, 0, 1))
        V(lambda e: e.tensor_scalar(out=A_[0][0:CL, 0:CL], in0=psb[4][0:CL, 0:CL], scalar1=tok[0:CL, c, hd:hd + 1], scalar2=None, op0=ALU.subtract),
          pk(4, 0, 1) + ["tok"], ["a0"])
        if CL == 128:
            Ac(lambda e: e.activation(out=A_[10][:, 0:CL], in_=psb[4][0:CL, 0:CL], func=AF.Exp), pk(4, 0, 1), ["a10"])
        V(lambda e: e.tensor_scalar(out=A_[1][0:CL, 0:CL], in0=A_[0][0:CL, 0:CL], scalar1=0.0, scalar2=None, op0=ALU.min), ["a0"], ["a1"])
        Ac(lambda e: e.activation(out=A_[1][0:CL, 0:CL], in_=A_[1][0:CL, 0:CL], func=AF.Exp), ["a1"], ["a1"])
        V(lambda e: e.tensor_tensor(out=A_[1][0:CL, 0:CL], in0=A_[1][0:CL, 0:CL], in1=Umat[0:CL, 0:CL], op=ALU.mult), ["a1", "cst"], ["a1"])
        if need_dl and CL > 1:
            V(lambda e: e.tensor_scalar(out=A_[2][0:CL, 0:CL], in0=A_[0][0:CL, 0:CL], scalar1=0.0, scalar2=None, op0=ALU.max), ["a0"], ["a2"])
            Ac(lambda e: e.activation(out=A_[2][0:CL, 0:CL], in_=A_[2][0:CL, 0:CL], func=AF.Exp, scale=-1.0), ["a2"], ["a2"])
            V(lambda e: e.tensor_tensor(out=A_[2][0:CL, 0:CL], in0=A_[2][0:CL, 0:CL], in1=Lsmat[0:CL, 0:CL], op=ALU.mult), ["a2", "cst"], ["a2"])
        Ac(lambda e: e.activation(out=cs1[0:CL, 0:1], in_=A_[0][0:CL, CL - 1:CL], func=AF.Exp), ["a0"], ["cs1"])

    def out_norm_T(l, CL, c, pbank, dvw, nw_ap, gateW, dst_units, dstkeys):
        sl = slice(c * CL, (c + 1) * CL)
        Ac(lambda e: e.activation(out=A_[4][0:CL, 0:dvw], in_=psb[pbank][0:CL, 0:dvw], func=AF.Square), pk(pbank, 0, 2), ["a4"])
        V(lambda e: e.tensor_reduce(out=cs1[0:CL, 4:5], in_=A_[4][0:CL, 0:dvw], axis=AX.X, op=ALU.add), ["a4"], ["cs1b"])
        Ac(lambda e: e.activation(out=cs1[0:CL, 4:5], in_=cs1[0:CL, 4:5], func=AF.Ln, scale=1.0 / dvw, bias=EPS), ["cs1b"], ["cs1b"])
        Ac(lambda e: e.activation(out=cs1[0:CL, 4:5], in_=cs1[0:CL, 4:5], func=AF.Exp, scale=-0.5), ["cs1b"], ["cs1b"])
        V(lambda e: e.scalar_tensor_tensor(out=A_[4][0:CL, 0:dvw], in0=psb[pbank][0:CL, 0:dvw], scalar=cs1[0:CL, 4:5], in1=nw_ap[0:CL, 0:dvw],
                                           op0=ALU.mult, op1=ALU.mult), pk(pbank, 0, 2) + ["cs1b", "nrm"], ["a4"])
        for i, (du, dk_) in enumerate(zip(dst_units, dstkeys)):
            T(lambda e, i=i: e.transpose(out=psb[6][:, 256 + i * 128:256 + i * 128 + CL], in_=A_[4][0:CL, i * 128:(i + 1) * 128], identity=ident[0:CL, 0:CL]),
              ["a4", "cst"], pk(6, 2 + i, 3 + i))
            V(lambda e, i=i, du=du: e.tensor_tensor(out=scr[:, du, sl], in0=psb[6][:, 256 + i * 128:256 + i * 128 + CL], in1=W[gateW[i]][:, sl], op=ALU.mult),
              pk(6, 2 + i, 3 + i) + [f"W{gateW[i]}"], [dk_])

    def mixer(l, nt, samp):
        CL = 1 if samp else 128
        nch = nt // CL
        lname = f"mod{l}"
        sh = mod[l][:, 3 * KC:4 * KC, :]
        A = mod[l][:, 4 * KC:5 * KC, :]
        Gt = mod[l][:, 5 * KC:6 * KC, :]
        norm_mod(nt, samp, A, sh, lname)
        P.dma("sp", lambda e: e.dma_start(out=nrm1[:], in_=nrm_d[l]), "nrm", writes=["nrm"])
        if samp:
            P.dma("sp", lambda e: e.dma_start(out=shA[:], in_=s_cva_d[l]), "shA", writes=["shA"] + PKEYS)
            P.dma("sp", lambda e: e.dma_start(out=shC[:], in_=s_cvc_d[l]), "shC", writes=["shC"] + PKEYS)
        OB = 16

        def load_state(src_ap, width, buf):
            P.dma("sp", lambda e: e.dma_start(out=SS[:, buf, 0:width], in_=src_ap), f"SS{buf}", writes=[f"SS{buf}"])

        def store_state(dst_ap, src_sb, key, skey):
            P.dma("sp", lambda e: e.dma_start(out=dst_ap, in_=src_sb), skey, reads=[key], writes=["odram"])

        CL_all, nch_all = CL, nch
        CL = 1 if samp else 128
        nch = nt // CL
        proj(l, "beta", 4, nt)
        Ac(lambda e: e.activation(out=W[8][:, 0:nt], in_=psb[4][:, 0:nt], func=AF.Sigmoid), pk(4), ["W8"])
        decay_setup(l, nt, CL, nch, "dec", 0, 1, HA, 8, False, 8)
        for c in range(nch):
            V(lambda e, c=c: e.tensor_scalar(out=tok[0:CL, c, 16:16 + HA], in0=tok[0:CL, c, 8:8 + HA], scalar1=-1.0, scalar2=None, op0=ALU.mult), ["tok"], ["tok"])
            Ac(lambda e, c=c: e.activation(out=tok[0:CL, c, 24:24 + HA], in_=tok[0:CL, c, 0:HA], func=AF.Exp), ["tok"], ["tok"])
            V(lambda e, c=c: e.tensor_tensor(out=tok[0:CL, c, 24:24 + HA], in0=tok[0:CL, c, 24:24 + HA], in1=tok[0:CL, c, 8:8 + HA], op=ALU.mult), ["tok"], ["tok"])

        for hd in range(HA):
            for i, (nm, ow) in enumerate((("aq", 1), ("ak", 2), ("av", 3))):
                proj(l, f"{nm}{hd}", i % 2, nt)
                u = i * HA + hd
                conv_unit(l, i % 2, nt, samp, cwa[:, l, u, :], "cwa", None, hista[:, l, u, :], f"hista{l}", shA[:, u, :, :], xsA[:, u, :], "shA", "xsA", 0, ow)
            proj(l, f"az{hd}", 3, nt)
            Ac(lambda e: e.activation(out=W[4][:, 0:nt], in_=psb[3][:, 0:nt], func=AF.Silu), pk(3), ["W4"])
            rinv_of(1, nt, 5)
            V(lambda e: e.scalar_tensor_tensor(out=Vb[0][:, 0:nt], in0=W[1][:, 0:nt], scalar=128 ** -0.5, in1=W[5][:, 0:nt], op0=ALU.mult, op1=ALU.mult),
              ["W1", "W5"], ["V0"])
            rinv_of(2, nt, 5)
            V(lambda e: e.tensor_tensor(out=W[2][:, 0:nt], in0=W[2][:, 0:nt], in1=W[5][:, 0:nt], op=ALU.mult), ["W2", "W5"], ["W2"])
            V(lambda e: e.tensor_copy(out=Vb[1][:, 0:nt], in_=W[2][:, 0:nt]), ["W2"], ["V1"])
            for c in range(nch):
                sl = slice(c * CL, (c + 1) * CL)
                tk = "tok"
                if samp:
                    buf = c % 2
                    load_state(s_gdn_d[l, c, hd], 128, buf)
                    S = SS[:, buf, 0:128]
                    Sk = f"SS{buf}"
                else:
                    S = Sa[:, l, hd, :]
                    Sk = f"Sa{l}_{hd}"
                    if c == 0 and state.get("ptile", 0) == 0:
                        V(lambda e, S=S: e.memset(S, 0.0), [], [Sk])
                V(lambda e, S=S: e.tensor_copy(out=B_[0][:, 0:128], in_=S), [Sk], ["b0"])
                decay_mats(l, CL, c, hd, True)
                if CL > 1:
                    q = slice(0, CL)
                    blocked = CL > 64
                    T(lambda e, sl=sl: e.matmul(psb[4][q, 128:128 + CL], lhsT=Vb[1][:, sl], rhs=Vb[1][:, sl], start=True, stop=True), ["V1"], pk(4, 1, 2))
                    V(lambda e, c=c: e.scalar_tensor_tensor(out=A_[5][q, q], in0=psb[4][q, 128:128 + CL], scalar=tok[q, c, 16 + hd:17 + hd], in1=A_[2][q, q],
                                                            op0=ALU.mult, op1=ALU.mult), pk(4, 1, 2) + [tk, "a2"], ["a5"])
                    T(lambda e: e.transpose(out=psb[4][q, 256:256 + CL], in_=A_[5][q, q], identity=ident[q, q]), ["a5", "cst"], pk(4, 2, 3))
                    Ac(lambda e: e.activation(out=A_[6][q, q], in_=psb[4][q, 256:256 + CL], func=AF.Copy), pk(4, 2, 3), ["a6"])
                    if blocked:
                        V(lambda e: e.tensor_tensor(out=A_[7][q, q], in0=A_[5][q, q], in1=BDmat[q, q], op=ALU.mult), ["a5", "cst"], ["a7"])
                        V(lambda e: e.tensor_tensor(out=A_[8][q, q], in0=A_[6][q, q], in1=BDmat[q, q], op=ALU.mult), ["a6", "cst"], ["a8"])
                        V(lambda e: e.tensor_tensor(out=A_[5][q, q], in0=A_[5][q, q], in1=A_[7][q, q], op=ALU.subtract), ["a5", "a7"], ["a5"])
                        V(lambda e: e.tensor_tensor(out=A_[6][q, q], in0=A_[6][q, q], in1=A_[8][q, q], op=ALU.subtract), ["a6", "a8"], ["a6"])
                        V(lambda e: e.tensor_tensor(out=A_[9][q, q], in0=A_[8][q, q], in1=ident[q, q], op=ALU.add), ["a8", "cst"], ["a9"])
                        Nc, Mc, Nn, Mn = 7, 8, 11, 12
                        nlev = 5
                    else:
                        V(lambda e: e.tensor_tensor(out=A_[9][q, q], in0=psb[4][q, 256:256 + CL], in1=ident[q, q], op=ALU.add), pk(4, 2, 3) + ["cst"], ["a9"])
                        Nc, Mc, Nn, Mn = 5, 6, 7, 8
                        nlev = CL.bit_length() - 2
                    for lev in range(1, nlev + 1):
                        T(lambda e, Mc=Mc, Nc=Nc: e.matmul(psb[4][q, 128:128 + CL], lhsT=A_[Mc][q, q], rhs=A_[Nc][q, q], start=True, stop=True),
                          [f"a{Mc}", f"a{Nc}"], pk(4, 1, 2))
                        if lev < nlev:
                            T(lambda e, Mc=Mc, Nc=Nc: e.matmul(psb[4][q, 256:256 + CL], lhsT=A_[Nc][q, q], rhs=A_[Mc][q, q], start=True, stop=True),
                              [f"a{Mc}", f"a{Nc}"], pk(4, 2, 3))
                        Ac(lambda e, Nn=Nn: e.activation(out=A_[Nn][q, q], in_=psb[4][q, 128:128 + CL], func=AF.Copy), pk(4, 1, 2), [f"a{Nn}"])
                        if lev < nlev:
                            V(lambda e, Mn=Mn: e.tensor_copy(out=A_[Mn][q, q], in_=psb[4][q, 256:256 + CL]), pk(4, 2, 3), [f"a{Mn}"])
                        T(lambda e, Nn=Nn: e.matmul(psb[4][q, 384:384 + CL], lhsT=A_[Nn][q, q], rhs=A_[9][q, q], start=True, stop=True),
                          [f"a{Nn}", "a9"], pk(4, 3, 4))
                        V(lambda e: e.tensor_tensor(out=A_[9][q, q], in0=A_[9][q, q], in1=psb[4][q, 384:384 + CL], op=ALU.add), pk(4, 3, 4) + ["a9"], ["a9"])
                        Nc, Mc, Nn, Mn = Nn, Mn, Nc, Mc
                    if blocked:
                        T(lambda e: e.matmul(psb[4][q, 128:128 + CL], lhsT=A_[5][q, q], rhs=A_[9][q, q], start=True, stop=True), ["a5", "a9"], pk(4, 1, 2))
                        T(lambda e: e.transpose(out=psb[4][q, 256:256 + CL], in_=A_[9][q, q], identity=ident[q, q]), ["a9", "cst"], pk(4, 2, 3))
                        Ac(lambda e: e.activation(out=A_[7][q, q], in_=psb[4][q, 128:128 + CL], func=AF.Copy), pk(4, 1, 2), ["a7"])
                        V(lambda e: e.tensor_copy(out=A_[8][q, q], in_=psb[4][q, 256:256 + CL]), pk(4, 2, 3), ["a8"])
                        T(lambda e: e.matmul(psb[4][q, 384:384 + CL], lhsT=A_[8][q, q], rhs=A_[7][q, q], start=True, stop=True), ["a8", "a7"], pk(4, 3, 4))
                        V(lambda e: e.tensor_tensor(out=A_[9][q, q], in0=A_[9][q, q], in1=psb[4][q, 384:384 + CL], op=ALU.add), pk(4, 3, 4) + ["a9"], ["a9"])
                    Rap = A_[9]
                    Rk = "a9"
                else:
                    Rap = cst
                    Rk = "cst"
                T(lambda e, sl=sl: e.transpose(out=psb[5][0:CL, 0:128], in_=W[3][:, sl], identity=ident), ["W3", "cst"], pk(5, 0, 1))
                Ac(lambda e: e.activation(out=A_[12][0:CL, 0:128], in_=psb[5][0:CL, 0:128], func=AF.Copy), pk(5, 0, 1), ["a12"])
                T(lambda e, sl=sl: e.transpose(out=psb[5][0:CL, 128:256], in_=W[2][:, sl], identity=ident), ["W2", "cst"], pk(5, 1, 2))
                Ac(lambda e: e.activation(out=A_[11][0:CL, 0:128], in_=psb[5][0:CL, 128:256], func=AF.Copy), pk(5, 1, 2), ["a11"])
                if CL == 1:
                    V(lambda e: e.tensor_copy(out=B_[3][0:CL, 0:128], in_=psb[5][0:CL, 128:256]), pk(5, 1, 2), ["b3"])
                else:
                    V(lambda e: e.tensor_scalar(out=B_[3][0:CL, 0:128], in0=psb[5][0:CL, 128:256], scalar1=cs1[0:CL, 0:1], scalar2=None, op0=ALU.mult),
                      pk(5, 1, 2) + ["cs1"], ["b3"])
                V(lambda e, c=c, Rap=Rap: e.tensor_scalar(out=A_[5][0:CL, 0:CL], in0=Rap[0:CL, 0:CL], scalar1=tok[0:CL, c, 8 + hd:9 + hd], scalar2=None, op0=ALU.mult),
                  [Rk, tk], ["a5"])
                V(lambda e, c=c, Rap=Rap: e.tensor_scalar(out=A_[6][0:CL, 0:CL], in0=Rap[0:CL, 0:CL], scalar1=tok[0:CL, c, 24 + hd:25 + hd], scalar2=None, op0=ALU.mult),
                  [Rk, tk], ["a6"])
                T(lambda e: e.matmul(psb[5][:, 256:256 + CL], lhsT=A_[11][0:CL, 0:128], rhs=A_[6][0:CL, 0:CL], start=True, stop=True), ["a11", "a6"], pk(5, 2, 3))
                V(lambda e: e.tensor_scalar(out=A_[7][:, 0:CL], in0=psb[5][:, 256:256 + CL], scalar1=-1.0, scalar2=None, op0=ALU.mult), pk(5, 2, 3), ["a7"])
                T(lambda e: e.matmul(psb[5][0:CL, 384:512], lhsT=A_[5][0:CL, 0:CL], rhs=A_[12][0:CL, 0:128], start=True, stop=False), ["a5", "a12"], pk(5, 3, 4))
                T(lambda e, S=S: e.matmul(psb[5][0:CL, 384:512], lhsT=A_[7][:, 0:CL], rhs=S, start=False, stop=True), ["a7", Sk], pk(5, 3, 4))
                V(lambda e: e.tensor_copy(out=B_[7][0:CL, 0:128], in_=psb[5][0:CL, 384:512]), pk(5, 3, 4), ["b7"])
                T(lambda e, sl=sl: e.matmul(psb[6][0:CL, 0:CL], lhsT=Vb[1][:, sl], rhs=Vb[0][:, sl], start=True, stop=True), ["V1", "V0"], pk(6, 0, 1))
                if CL == 1:
                    V(lambda e: e.tensor_copy(out=B_[8][0:CL, 0:CL], in_=psb[6][0:CL, 0:CL]), pk(6, 0, 1), ["b8"])
                else:
                    V(lambda e: e.tensor_tensor(out=B_[8][0:CL, 0:CL], in0=psb[6][0:CL, 0:CL], in1=A_[1][0:CL, 0:CL], op=ALU.mult), pk(6, 0, 1) + ["a1"], ["b8"])
                if CL != 128:
                    T(lambda e: e.matmul(psb[6][:, 128:128 + CL], lhsT=ones_f, rhs=A_[3][:, 0:CL], start=True, stop=True), ["a3", "cst"], pk(6, 1, 2))
                    Ac(lambda e: e.activation(out=A_[10][:, 0:CL], in_=psb[6][:, 128:128 + CL], func=AF.Exp), pk(6, 1, 2), ["a10"])
                V(lambda e, sl=sl: e.tensor_tensor(out=B_[9][:, 0:CL], in0=Vb[0][:, sl], in1=A_[10][:, 0:CL], op=ALU.mult), ["V0", "a10"], ["b9"])
                T(lambda e: e.matmul(psb[7][0:CL, 0:128], lhsT=B_[9][:, 0:CL], rhs=B_[0][:, 0:128], start=True, stop=False), ["b9", "b0"], pk(7, 0, 1))
                T(lambda e: e.matmul(psb[7][0:CL, 0:128], lhsT=B_[8][0:CL, 0:CL], rhs=B_[7][0:CL, 0:128], start=False, stop=True), ["b8", "b7"], pk(7, 0, 1))
                T(lambda e: e.matmul(psb[7][:, 256:384], lhsT=B_[3][0:CL, 0:128], rhs=B_[7][0:CL, 0:128], start=True, stop=True), ["b3", "b7"], pk(7, 2, 3))
                V(lambda e, S=S: e.scalar_tensor_tensor(out=S, in0=S, scalar=A_[10][:, CL - 1:CL], in1=psb[7][:, 256:384], op0=ALU.mult, op1=ALU.add),
                  [Sk, "a10"] + pk(7, 2, 3), [Sk])
                if samp:
                    store_state(so_gdn_d[l, c, hd], S, Sk, f"sst{buf}")
                out_norm_T(l, CL, c, 7, 128, nrm1[:, 0:128], [4], [OB + hd], [f"scr{OB + hd}"])
            if not samp and state.get("ptile", 0) == NPT - 1:
                store_state(o_gdn_d[l, hd], Sa[:, l, hd, :], f"Sa{l}_{hd}", "ost")
        if samp:
            P.dma("sp", lambda e: e.dma_start(out=so_cva_d[l][:, :, 0:2, :], in_=shA[:, :, 1:3, :]), "shst", reads=["shA"], writes=["odram"])
            P.dma("sp", lambda e: e.dma_start(out=so_cva_d[l][:, :, 2, :], in_=xsA[:]), "shst", reads=["xsA"], writes=["odram"])
        elif state.get("ptile", 0) == NPT - 1:
            P.dma("sp", lambda e: e.dma_start(out=o_cva_d[l], in_=hista[:, l, :, :]), "ost", reads=[f"hista{l}"], writes=["odram"])
        merge_branch(l, nt, 0, wba_d[l], HA, OB, first=True)
        CL, nch = CL_all, nch_all

        proj(l, "lr", 4, nt)
        Ac(lambda e: e.activation(out=Vb[2][0:16, 0:nt], in_=psb[4][0:16, 0:nt], func=AF.Copy), pk(4), ["V2"])
        for hd in range(HB):
            T(lambda e, hd=hd: e.matmul(psb[4][:, 0:nt], lhsT=wgate[:, l, hd * 128:(hd + 1) * 128], rhs=Vb[2][0:16, 0:nt], start=True, stop=True),
              ["wgate", "V2"], pk(4))
            Ac(lambda e, hd=hd: e.activation(out=W[5][:, 0:nt], in_=psb[4][:, 0:nt], func=AF.Exp, scale=-1.0, bias=pv[:, l, PV_BG + hd:PV_BG + hd + 1]),
               pk(4) + ["pv"], ["W5"])
            Ac(lambda e: e.activation(out=W[5][:, 0:nt], in_=W[5][:, 0:nt], func=AF.Ln, bias=1.0), ["W5"], ["W5"])
            if CL > 1:
                for c in range(nch):
                    sl = slice(c * CL, (c + 1) * CL)
                    V(lambda e, sl=sl: e.tensor_tensor_scan(out=W[6][:, sl], data0=W[5][:, sl], data1=W[5][:, sl], initial=0.0, op0=ALU.add, op1=ALU.bypass),
                      ["W5"], ["W6"])
            else:
                V(lambda e: e.tensor_copy(out=W[6][:, 0:nt], in_=W[5][:, 0:nt]), ["W5"], ["W6"])
            Ac(lambda e: e.activation(out=W[7][:, 0:nt], in_=W[6][:, 0:nt], func=AF.Exp, scale=-1.0 / 16), ["W6"], ["W7"])
            Ac(lambda e: e.activation(out=W[8][:, 0:nt], in_=W[6][:, 0:nt], func=AF.Exp, scale=1.0 / 16), ["W6"], ["W8"])
            proj(l, f"bq{hd}", 0, nt)
            V(lambda e: e.scalar_tensor_tensor(out=Vb[0][:, 0:nt], in0=psb[0][:, 0:nt], scalar=128 ** -0.5, in1=W[7][:, 0:nt], op0=ALU.mult, op1=ALU.mult),
              pk(0) + ["W7"], ["V0"])
            proj(l, f"bk{hd}", 1, nt)
            Ac(lambda e: e.activation(out=W[1][:, 0:nt], in_=psb[1][:, 0:nt], func=AF.Copy), pk(1), ["W1"])
            V(lambda e: e.tensor_tensor(out=Vb[1][:, 0:nt], in0=W[1][:, 0:nt], in1=W[8][:, 0:nt], op=ALU.mult), ["W1", "W8"], ["V1"])
            for c in range(nch):
                sl = slice(c * CL, (c + 1) * CL)
                V(lambda e, c=c: e.tensor_scalar(out=cs1[:, 2:3], in0=W[6][:, (c + 1) * CL - 1:(c + 1) * CL], scalar1=-1.0 / 16, scalar2=None, op0=ALU.mult),
                  ["W6"], ["cs1c"])
                Ac(lambda e, sl=sl: e.activation(out=W[2][:, sl], in_=W[6][:, sl], func=AF.Exp, scale=1.0 / 16, bias=cs1[:, 2:3]), ["W6", "cs1c"], ["W2"])
            V(lambda e: e.tensor_tensor(out=W[2][:, 0:nt], in0=W[2][:, 0:nt], in1=W[1][:, 0:nt], op=ALU.mult), ["W2", "W1"], ["W2"])
            proj(l, f"bv{hd}_0", 2, nt)
            Ac(lambda e: e.activation(out=W[3][:, 0:nt], in_=psb[2][:, 0:nt], func=AF.Copy), pk(2), ["W3"])
            proj(l, f"bv{hd}_1", 3, nt)
            Ac(lambda e: e.activation(out=W[4][:, 0:nt], in_=psb[3][:, 0:nt], func=AF.Copy), pk(3), ["W4"])
            proj(l, f"br{hd}_0", 0, nt)
            Ac(lambda e: e.activation(out=W[1][:, 0:nt], in_=psb[0][:, 0:nt], func=AF.Silu), pk(0), ["W1"])
            proj(l, f"br{hd}_1", 1, nt)
            Ac(lambda e: e.activation(out=W[5][:, 0:nt], in_=psb[1][:, 0:nt], func=AF.Silu), pk(1), ["W5"])
            for c in range(nch):
                sl = slice(c * CL, (c + 1) * CL)
                if samp:
                    buf = c % 2
                    load_state(s_gla_d[l, c, hd], 256, buf)
                    S = SS[:, buf, 0:256]
                    Sk = f"SS{buf}"
                else:
                    S = Sb_[:, l, hd, :]
                    Sk = f"Sb{l}_{hd}"
                    if c == 0 and state.get("ptile", 0) == 0:
                        V(lambda e, S=S: e.memset(S, 0.0), [], [Sk])
                V(lambda e, S=S: e.tensor_copy(out=B_[0][:, 0:256], in_=S), [Sk], ["b0"])
                T(lambda e, sl=sl: e.transpose(out=psb[5][0:CL, 0:128], in_=W[3][:, sl], identity=ident), ["W3", "cst"], pk(5, 0, 1))
                T(lambda e, sl=sl: e.transpose(out=psb[5][0:CL, 128:256], in_=W[4][:, sl], identity=ident), ["W4", "cst"], pk(5, 1, 2))
                Ac(lambda e: e.activation(out=B_[1][0:CL, 0:256], in_=psb[5][0:CL, 0:256], func=AF.Copy), pk(5, 0, 2), ["b1"])
                T(lambda e, sl=sl: e.transpose(out=psb[5][0:CL, 256:384], in_=W[2][:, sl], identity=ident), ["W2", "cst"], pk(5, 2, 3))
                Ac(lambda e: e.activation(out=B_[3][0:CL, 0:128], in_=psb[5][0:CL, 256:384], func=AF.Copy), pk(5, 2, 3), ["b3"])
                T(lambda e, sl=sl: e.matmul(psb[6][0:CL, 0:CL], lhsT=Vb[1][:, sl], rhs=Vb[0][:, sl], start=True, stop=True), ["V1", "V0"], pk(6, 0, 1))
                V(lambda e: e.tensor_tensor(out=B_[8][0:CL, 0:CL], in0=psb[6][0:CL, 0:CL], in1=Umat[0:CL, 0:CL], op=ALU.mult), pk(6, 0, 1) + ["cst"], ["b8"])
                T(lambda e, sl=sl: e.matmul(psb[7][0:CL, 0:256], lhsT=Vb[0][:, sl], rhs=B_[0][:, 0:256], start=True, stop=False), ["V0", "b0"], pk(7, 0, 2))
                T(lambda e: e.matmul(psb[7][0:CL, 0:256], lhsT=B_[8][0:CL, 0:CL], rhs=B_[1][0:CL, 0:256], start=False, stop=True), ["b8", "b1"], pk(7, 0, 2))
                T(lambda e: e.matmul(psb[7][:, 256:512], lhsT=B_[3][0:CL, 0:128], rhs=B_[1][0:CL, 0:256], start=True, stop=True), ["b3", "b1"], pk(7, 2, 4))
                V(lambda e, S=S, c=c: e.scalar_tensor_tensor(out=S, in0=S, scalar=W[7][:, (c + 1) * CL - 1:(c + 1) * CL], in1=psb[7][:, 256:512],
                                                             op0=ALU.mult, op1=ALU.add), [Sk, "W7"] + pk(7, 2, 4), [Sk])
                if samp:
                    store_state(so_gla_d[l, c, hd], S, Sk, f"sst{buf}")
                out_norm_T(l, CL, c, 7, 256, nrm1[:, 128:384], [1, 5], [OB + 2 * hd, OB + 2 * hd + 1], [f"scr{OB + 2 * hd}", f"scr{OB + 2 * hd + 1}"])
            if not samp and state.get("ptile", 0) == NPT - 1:
                store_state(o_gla_d[l, hd], Sb_[:, l, hd, :], f"Sb{l}_{hd}", "ost")
        merge_branch(l, nt, 1, wbb_d[l], 2 * HB, OB, first=False)

        decay_setup(l, nt, CL, nch, "dt", 2, 3, HC, 8, True, 32)

        V(lambda e: e.memset(B_[10][:, 0:256], 0.0), [], ["b10"])
        V(lambda e: e.memset(B_[11][:, 0:256], 0.0), [], ["b11"])
        V(lambda e: e.memset(B_[12][:, 0:256], 0.0), [], ["b12"])
        for g in range(G):
            for (nm, cu, dstV, keepW) in ((f"cB{g}", NU_C + g, 3, None), (f"cC{g}", NU_C + G + g, 4, 5)):
                proj(l, nm, 0, nt)
                conv_unit(l, 0, nt, samp, cwc[:, l, cu, 0:4], "cwc", cwc[:, l, cu, 4:5], histc[:, l, cu, :], f"histc{l}", shC[:, cu, :, :], xsC[:, cu, :],
                          "shC", "xsC", 0, 1)
                V(lambda e, dstV=dstV: e.tensor_copy(out=Vb[dstV][:, 0:nt], in_=W[1][:, 0:nt]), ["W1"], [f"V{dstV}"])
                if keepW is not None:
                    V(lambda e: e.tensor_copy(out=W[5][:, 0:nt], in_=W[1][:, 0:nt]), ["W1"], ["W5"])
                else:
                    V(lambda e: e.tensor_copy(out=W[6][:, 0:nt], in_=W[1][:, 0:nt]), ["W1"], ["W6"])
            for uu in range(UPN):
                u = g * UPN + uu
                proj(l, f"cx{u}", 1, nt)
                conv_unit(l, 1, nt, samp, cwc[:, l, u, 0:4], "cwc", cwc[:, l, u, 4:5], histc[:, l, u, :], f"histc{l}", shC[:, u, :, :], xsC[:, u, :],
                          "shC", "xsC", 0, 2)
                proj(l, f"cz{u}", 2, nt)
                Ac(lambda e: e.activation(out=W[3][:, 0:nt], in_=psb[2][:, 0:nt], func=AF.Silu), pk(2), ["W3"])
                for c in range(nch):
                    sl = slice(c * CL, (c + 1) * CL)
                    tk = "tok"
                    if samp:
                        buf = c % 2
                        load_state(s_ssd_d[l, c, u], 128, buf)
                        S = SS[:, buf, 0:128]
                        Sk = f"SS{buf}"
                    else:
                        S = Sc[:, l, u, :]
                        Sk = f"Sc{l}_{u}"
                        if c == 0 and state.get("ptile", 0) == 0:
                            V(lambda e, S=S: e.memset(S, 0.0), [], [Sk])
                    T(lambda e, sl=sl: e.matmul(psb[5][0:CL, 0:CL], lhsT=Vb[3][:, sl], rhs=Vb[4][:, sl], start=True, stop=True), ["V3", "V4"], pk(5, 0, 1))
                    V(lambda e: e.tensor_copy(out=A_[11][0:CL, 0:CL], in_=psb[5][0:CL, 0:CL]), pk(5, 0, 1), ["a11"])
                    T(lambda e, sl=sl: e.transpose(out=psb[5][0:CL, 128:256], in_=W[6][:, sl], identity=ident), ["W6", "cst"], pk(5, 1, 2))
                    Ac(lambda e: e.activation(out=B_[2][0:CL, 0:128], in_=psb[5][0:CL, 128:256], func=AF.Copy), pk(5, 1, 2), ["b2"])
                    T(lambda e, sl=sl: e.transpose(out=psb[5][0:CL, 256:384], in_=W[2][:, sl], identity=ident), ["W2", "cst"], pk(5, 2, 3))
                    V(lambda e: e.tensor_copy(out=A_[12][0:CL, 0:128], in_=psb[5][0:CL, 256:384]), pk(5, 2, 3), ["a12"])
                    for hh in range(2):
                        hd = 2 * u + hh
                        hs = slice(hh * 64, hh * 64 + 64)
                        po = hh * 128
                        decay_mats(l, CL, c, hd, False)
                        if CL == 1:
                            V(lambda e: e.tensor_copy(out=B_[8][0:CL, 0:CL], in_=A_[11][0:CL, 0:CL]), ["a11"], ["b8"])
                        else:
                            V(lambda e: e.tensor_tensor(out=B_[8][0:CL, 0:CL], in0=A_[11][0:CL, 0:CL], in1=A_[1][0:CL, 0:CL], op=ALU.mult), ["a11", "a1"], ["b8"])
                        if CL != 128:
                            T(lambda e: e.matmul(psb[6][:, 128:128 + CL], lhsT=ones_f, rhs=A_[3][:, 0:CL], start=True, stop=True), ["a3", "cst"], pk(6, 1, 2))
                            Ac(lambda e: e.activation(out=A_[10][:, 0:CL], in_=psb[6][:, 128:128 + CL], func=AF.Exp), pk(6, 1, 2), ["a10"])
                        V(lambda e, sl=sl: e.tensor_tensor(out=B_[9][:, 0:CL], in0=W[5][:, sl], in1=A_[10][:, 0:CL], op=ALU.mult), ["W5", "a10"], ["b9"])
                        V(lambda e, hs=hs, po=po, c=c, hd=hd: e.tensor_scalar(out=B_[10][0:CL, po + hh_off(hs):po + hh_off(hs) + 64], in0=A_[12][0:CL, hs],
                                                                              scalar1=tok[0:CL, c, 32 + hd:33 + hd], scalar2=None, op0=ALU.mult),
                          ["a12", tk], ["b10"])
                        if CL == 1:
                            V(lambda e, hs=hs, po=po: e.tensor_copy(out=B_[12][0:CL, po + hh_off(hs):po + hh_off(hs) + 64],
                                                                    in_=B_[10][0:CL, po + hh_off(hs):po + hh_off(hs) + 64]), ["b10"], ["b12"])
                        else:
                            V(lambda e, hs=hs, po=po: e.tensor_scalar(out=B_[12][0:CL, po + hh_off(hs):po + hh_off(hs) + 64],
                                                                      in0=B_[10][0:CL, po + hh_off(hs):po + hh_off(hs) + 64],
                                                                      scalar1=cs1[0:CL, 0:1], scalar2=None, op0=ALU.mult), ["b10", "cs1"], ["b12"])
                        V(lambda e, hs=hs, po=po, S=S: e.tensor_copy(out=B_[11][:, po + hh_off(hs):po + hh_off(hs) + 64], in_=S[:, hs]), [Sk], ["b11"])
                        T(lambda e, po=po, hh=hh: e.matmul(psb[7][:, 0:CL], lhsT=B_[10][0:CL, po:po + 128], rhs=B_[8][0:CL, 0:CL], start=(hh == 0), stop=False),
                          ["b10", "b8"], pk(7, 0, 1))
                        T(lambda e, po=po, hh=hh: e.matmul(psb[7][:, 0:CL], lhsT=B_[11][:, po:po + 128], rhs=B_[9][:, 0:CL], start=False, stop=(hh == 1)),
                          ["b11", "b9"], pk(7, 0, 1))
                        T(lambda e, po=po, hh=hh: e.matmul(psb[2][:, 0:128], lhsT=B_[2][0:CL, 0:128], rhs=B_[12][0:CL, po:po + 128], start=(hh == 0), stop=(hh == 1)),
                          ["b2", "b12"], pk(2))
                        V(lambda e, hh=hh: e.tensor_copy(out=cs1[:, 5 + hh:6 + hh], in_=A_[10][:, CL - 1:CL]), ["a10"], ["cs1d"])
                    for hh in range(2):
                        hs = slice(hh * 64, hh * 64 + 64)
                        V(lambda e, hs=hs, hh=hh, S=S: e.scalar_tensor_tensor(out=S[:, hs], in0=S[:, hs], scalar=cs1[:, 5 + hh:6 + hh], in1=psb[2][:, hh * 64:hh * 64 + 64],
                                                                              op0=ALU.mult, op1=ALU.add), [Sk, "cs1d"] + pk(2), [Sk])
                    if samp:
                        store_state(so_ssd_d[l, c, u], S, Sk, f"sst{buf}")
                    V(lambda e, sl=sl, u=u: e.scalar_tensor_tensor(out=W[4][:, sl], in0=W[2][:, sl], scalar=pv[:, l, PV_D + u:PV_D + u + 1], in1=psb[7][:, 0:CL],
                                                                    op0=ALU.mult, op1=ALU.add), ["W2", "pv"] + pk(7, 0, 1), ["W4"])
                    V(lambda e, sl=sl: e.tensor_tensor(out=W[4][:, sl], in0=W[4][:, sl], in1=W[3][:, sl], op=ALU.mult), ["W4", "W3"], ["W4"])
                if not samp and state.get("ptile", 0) == NPT - 1:
                    store_state(o_ssd_d[l, u], Sc[:, l, u, :], f"Sc{l}_{u}", "ost")
                V(lambda e, u=u: e.tensor_copy(out=scr[:, OB + u, 0:nt], in_=W[4][:, 0:nt]), ["W4"], [f"scr{OB + u}"])
                V(lambda e, uu=uu: e.tensor_tensor(out=Vb[5][:, 0:nt], in0=W[4][:, 0:nt], in1=W[4][:, 0:nt], op=ALU.mult), ["W4"], ["V5"])
                T(lambda e, uu=uu: e.matmul(psb[3][:, 0:nt], lhsT=ones_b[:], rhs=Vb[5][:, 0:nt], start=(uu == 0), stop=(uu == UPN - 1)), ["V5", "ones_b"], pk(3))
            Ac(lambda e: e.activation(out=W[1][:, 0:nt], in_=psb[3][:, 0:nt], func=AF.Ln, scale=1.0 / (UPN * 128), bias=EPS), pk(3), ["W1"])
            Ac(lambda e: e.activation(out=W[1][:, 0:nt], in_=W[1][:, 0:nt], func=AF.Exp, scale=-0.5), ["W1"], ["W1"])
            for uu in range(UPN):
                u = g * UPN + uu
                V(lambda e, uu=uu, u=u: e.scalar_tensor_tensor(out=scr[:, OB + u, 0:nt], in0=scr[:, OB + u, 0:nt], scalar=pv[:, l, PV_NW + u:PV_NW + u + 1], in1=W[1][:, 0:nt],
                                                                op0=ALU.mult, op1=ALU.mult), [f"scr{OB + u}", "pv", "W1"], [f"scr{OB + u}"])
        if samp:
            P.dma("sp", lambda e: e.dma_start(out=so_cvc_d[l][:, :, 0:2, :], in_=shC[:, :, 1:3, :]), "shst", reads=["shC"], writes=["odram"])
            P.dma("sp", lambda e: e.dma_start(out=so_cvc_d[l][:, :, 2, :], in_=xsC[:]), "shst", reads=["xsC"], writes=["odram"])
        elif state.get("ptile", 0) == NPT - 1:
            P.dma("sp", lambda e: e.dma_start(out=o_cvc_d[l], in_=histc[:, l, :, :]), "ost", reads=[f"histc{l}"], writes=["odram"])
        merge_branch(l, nt, 2, wbc_d[l], NU_C, OB, first=False)

        for dc in range(KC):
            s = ring_load(wo_d[l][dc])
            b = dc % 2
            for kc in range(KC):
                T(lambda e, s=s, kc=kc, b=b: e.matmul(psb[b][:, 0:nt], lhsT=ring[:, s, kc, :], rhs=scr[:, kc, 0:nt], start=(kc == 0), stop=(kc == KC - 1)),
                  [f"ring{s}", f"scr{kc}"], pk(b))
            resid_add(nt, samp, dc, b, Gt, lname)

    def hh_off(hs):
        return hs.start

    def merge_branch(l, nt, br, w_d, nk, OB, first):
        for dc in range(KC):
            b = dc % 2
            proj(l, f"gate{br}_{dc}", 2 + b, nt)
            t = nexttmp()
            Ac(lambda e, b=b, t=t: e.activation(out=tmp[:, t, 0:nt], in_=psb[2 + b][:, 0:nt], func=AF.Sigmoid), pk(2 + b), [f"tmp{t}"])
            s = ring_load(w_d[dc], nk)
            for kc in range(nk):
                T(lambda e, s=s, kc=kc, b=b: e.matmul(psb[b][:, 0:nt], lhsT=ring[:, s, kc, :], rhs=scr[:, OB + kc, 0:nt], start=(kc == 0), stop=(kc == nk - 1)),
                  [f"ring{s}", f"scr{OB + kc}"], pk(b))
            if first:
                V(lambda e, b=b, t=t, dc=dc: e.tensor_tensor(out=scr[:, dc, 0:nt], in0=tmp[:, t, 0:nt], in1=psb[b][:, 0:nt], op=ALU.mult),
                  [f"tmp{t}"] + pk(b), [f"scr{dc}"])
            else:
                V(lambda e, b=b, t=t: e.tensor_tensor(out=tmp[:, t, 0:nt], in0=tmp[:, t, 0:nt], in1=psb[b][:, 0:nt], op=ALU.mult),
                  [f"tmp{t}"] + pk(b), [f"tmp{t}"])
                V(lambda e, t=t, dc=dc: e.tensor_tensor(out=scr[:, dc, 0:nt], in0=scr[:, dc, 0:nt], in1=tmp[:, t, 0:nt], op=ALU.add),
                  [f"tmp{t}", f"scr{dc}"], [f"scr{dc}"])


    tiles = [("p", i) for i in range(NPT)] + [("s", 0)]
    xkeys = [f"x{kc}" for kc in range(KC)]
    V(lambda e: e.memset(hista[:], 0.0), [], ["hista0", "hista1"])
    V(lambda e: e.memset(histc[:], 0.0), [], ["histc0", "histc1"])
    for (kind, ti) in tiles:
        samp = kind == "s"
        nt = NS if samp else NT
        state["ptile"] = ti
        src = xs_d if samp else xp_d[:, :, ti * NT:(ti + 1) * NT]
        P.dma("sp", lambda e, src=src, nt=nt: e.dma_start(out=x[:, :, 0:nt], in_=src), "xload", writes=xkeys)
        for l in range(2):
            ffn(l, 1, nt, samp)
            mixer(l, nt, samp)
            ffn(l, 2, nt, samp)
        dst = ys_d if samp else yp_d[:, :, ti * NT:(ti + 1) * NT]
        rms_stats(nt)
        fn_w = norms[:, 6, :]
        for kc in range(KC):
            V(lambda e, kc=kc, nt=nt: e.scalar_tensor_tensor(out=x[:, kc, 0:nt], in0=x[:, kc, 0:nt], scalar=fn_w[:, kc:kc + 1],
                                                            in1=rstd[:, 0:nt], op0=ALU.mult, op1=ALU.mult),
              [f"x{kc}", "rstd", "norms"], [f"x{kc}"])
        P.dma("sp", lambda e, dst=dst, nt=nt: e.dma_start(out=dst, in_=x[:, :, 0:nt]), "ystore", reads=xkeys, writes=["ydram"])

    P.emit(nc, es)
    es.close()
    return nc


def _units(w):
    K, N = w.shape
    return np.ascontiguousarray(w.reshape(K // 128, 128, N // 128, 128).transpose(2, 1, 0, 3))


def _fm(v):
    n, Fd = v.shape
    return np.ascontiguousarray(v.reshape(n, Fd // 128, 128).transpose(2, 1, 0))


def _consts():
    c = np.zeros((128, 640), np.float32)
    c[:, 0:128] = np.eye(128)
    c[:, 128:256] = np.triu(np.ones((128, 128)))
    c[:, 256:384] = np.tril(np.ones((128, 128)), -1)
    c[:, 384:512] = 1.0
    c[0:64, 512:576] = 1.0
    c[64:128, 576:640] = 1.0
    return c


def _padcols(w, n=128):
    K, c = w.shape
    out = np.zeros((K, n), np.float32)
    out[:, :c] = w
    return out


_NC = {}


def kernel(_cfg=None, **inp):
    cfg = dict(FULL if _cfg is None else _cfg)
    D, FF, SEQ, HA, HB, HC, G = (cfg[k] for k in ("D", "FF", "SEQ", "HA", "HB", "HC", "G"))
    KC = D // 128
    NU_C = HC // 2
    NCA, NCC = 3 * HA, NU_C + 2 * G
    f = lambda a: np.asarray(a, np.float32)
    shared = {"consts": _consts()}
    shared["wada"] = np.stack([_units(f(inp["w_ada"][l])) for l in range(2)])
    shared["bada"] = np.stack([np.ascontiguousarray(f(inp["b_ada"][l]).reshape(9 * KC, 128).T) for l in range(2)])
    nl = [f(inp[k][l]) for l in range(2) for k in ("norm1", "norm2", "norm3")] + [f(inp["final_norm"])]
    shared["norms"] = np.ascontiguousarray(np.stack(nl).reshape(7, KC, 128).transpose(2, 0, 1))
    for l in range(2):
        for w in (1, 2):
            shared[f"wg{l}{w}"] = _units(f(inp[f"ffn{w}_wg"][l]))
            shared[f"wu{l}{w}"] = _units(f(inp[f"ffn{w}_wu"][l]))
            shared[f"wd{l}{w}"] = _units(f(inp[f"ffn{w}_wd"][l]))
    QK_A, V_A = HA * 128, HA * 128
    splits = [2 * QK_A + V_A, V_A, HA, HA, HB * 128, HB * 128, HB * 256, 16, HB * 256, HC * 64, HC * 64 + 2 * G * 128, HC, 3 * D]
    offs = np.concatenate([[0], np.cumsum(splits)])
    UP = unit_plan(cfg)
    NPV = 4 + HB + 2 * NU_C + 32
    pvs, cwas, cwcs, nrms, wgates = [], [], [], [], []
    for l in range(2):
        w = f(inp["w_in"][l])
        grp = [w[:, offs[i]:offs[i + 1]] for i in range(13)]
        qkv_a, z_a, beta_a, dec_a, q_b, k_b, v_b, lr_b, r_b, z_c, xbc_c, dt_c, gates = grp
        cols = {}
        cols["beta"] = _padcols(beta_a)
        cols["dec"] = _padcols(dec_a)
        for h in range(HA):
            cols[f"aq{h}"] = qkv_a[:, h * 128:(h + 1) * 128]
            cols[f"ak{h}"] = qkv_a[:, QK_A + h * 128:QK_A + (h + 1) * 128]
            cols[f"av{h}"] = qkv_a[:, 2 * QK_A + h * 128:2 * QK_A + (h + 1) * 128]
            cols[f"az{h}"] = z_a[:, h * 128:(h + 1) * 128]
        cols["lr"] = _padcols(lr_b)
        for h in range(HB):
            cols[f"bq{h}"] = q_b[:, h * 128:(h + 1) * 128]
            cols[f"bk{h}"] = k_b[:, h * 128:(h + 1) * 128]
            for i in range(2):
                cols[f"bv{h}_{i}"] = v_b[:, h * 256 + i * 128:h * 256 + (i + 1) * 128]
                cols[f"br{h}_{i}"] = r_b[:, h * 256 + i * 128:h * 256 + (i + 1) * 128]
        cols["dt"] = _padcols(dt_c)
        inner = HC * 64
        for g in range(G):
            cols[f"cB{g}"] = xbc_c[:, inner + g * 128:inner + (g + 1) * 128]
            cols[f"cC{g}"] = xbc_c[:, inner + G * 128 + g * 128:inner + G * 128 + (g + 1) * 128]
        for u in range(NU_C):
            cols[f"cx{u}"] = xbc_c[:, u * 128:(u + 1) * 128]
            cols[f"cz{u}"] = z_c[:, u * 128:(u + 1) * 128]
        for br in range(3):
            for dc in range(KC):
                cols[f"gate{br}_{dc}"] = gates[:, br * D + dc * 128:br * D + (dc + 1) * 128]
        arr = np.zeros((len(UP), 128, KC, 128), np.float32)
        for n, i in UP.items():
            arr[i] = cols[n].reshape(KC, 128, 128).transpose(1, 0, 2)
        shared[f"win{l}"] = arr
        shared[f"wba{l}"] = _units(f(inp["w_branch_gdn"][l]))
        shared[f"wbb{l}"] = _units(f(inp["w_branch_gla"][l]))
        shared[f"wbc{l}"] = _units(f(inp["w_branch_ssd"][l]))
        shared[f"wo{l}"] = _units(f(inp["w_out"][l]))
        pvl = np.zeros((128, NPV), np.float32)
        pvl[:HA, 0] = f(inp["gdn_a_log"][l])
        pvl[:HA, 1] = f(inp["gdn_dt_bias"][l])
        pvl[:HC, 2] = f(inp["ssd_a_log"][l])
        pvl[:HC, 3] = f(inp["ssd_dt_bias"][l])
        pvl[:, 4:4 + HB] = f(inp["gla_b_gate"][l]).reshape(HB, 128).T
        pvl[:, 4 + HB:4 + HB + NU_C] = np.repeat(f(inp["ssd_d"][l]).reshape(NU_C, 2), 64, axis=1).T
        pvl[:, 4 + HB + NU_C:4 + HB + 2 * NU_C] = f(inp["ssd_norm_w"][l]).reshape(NU_C, 128).T
        pvl[:32, 4 + HB + 2 * NU_C:] = np.eye(32)
        pvs.append(pvl)
        cwas.append(np.ascontiguousarray(f(inp["gdn_conv_w"][l]).reshape(4, NCA, 128).transpose(2, 1, 0)))
        cw = f(inp["ssd_conv_w"][l])
        cb = f(inp["ssd_conv_b"][l])
        cwc = np.concatenate([cw.reshape(4, NCC, 128), cb.reshape(1, NCC, 128)], 0)
        cwcs.append(np.ascontiguousarray(cwc.transpose(2, 1, 0)))
        nrms.append(np.concatenate([np.tile(f(inp["gdn_norm_w"][l])[None], (128, 1)), np.tile(f(inp["gla_norm_w"][l])[None], (128, 1))], 1))
        wgates.append(f(inp["gla_w_gate"][l]))
    shared["pv"] = np.stack(pvs)
    shared["cwa"] = np.stack(cwas)
    shared["cwc"] = np.stack(cwcs)
    shared["nrm"] = np.ascontiguousarray(np.stack(nrms))
    shared["wgate"] = np.stack(wgates)

    xp = f(inp["x_prompt"])
    xs = f(inp["x_sample"])[:, 0, :]
    cp = f(inp["c_prompt"])
    cs = f(inp["c_sample"])
    sg, sl_, ss, sca, scc = (f(inp[k]) for k in ("state_gdn", "state_gla", "state_ssd", "state_gdn_conv", "state_ssd_conv"))
    in_maps = []
    for c in range(8):
        m = dict(shared)
        b0 = 16 * c
        m["xp"] = _fm(xp[c % 4])
        m["xs"] = _fm(xs[b0:b0 + 16])
        m["cT"] = _fm(np.concatenate([cp[c % 4][None], cs[b0:b0 + 16]], 0))
        m["s_gdn"] = np.ascontiguousarray(sg[:, b0:b0 + 16])
        m["s_gla"] = np.ascontiguousarray(sl_[:, b0:b0 + 16])
        m["s_ssd"] = np.ascontiguousarray(ss[:, b0:b0 + 16].reshape(2, 16, NU_C, 2, 64, 128).transpose(0, 1, 2, 5, 3, 4).reshape(2, 16, NU_C, 128, 128))
        m["s_cva"] = np.ascontiguousarray(sca[:, b0:b0 + 16].reshape(2, 16, 3, NCA, 128).transpose(0, 4, 3, 2, 1))
        m["s_cvc"] = np.ascontiguousarray(scc[:, b0:b0 + 16].reshape(2, 16, 3, NCC, 128).transpose(0, 4, 3, 2, 1))
        in_maps.append(m)
    key = tuple(sorted(cfg.items()))
    if key not in _NC:
        _NC[key] = build(cfg)
    res = run_bass_kernel_spmd(_NC[key], in_maps, core_ids=list(range(8)))
    R = res.results
    y_prompt = np.stack([R[c]["yp"].transpose(2, 1, 0).reshape(SEQ, D) for c in range(4)])
    y_sample = np.concatenate([R[c]["ys"].transpose(2, 1, 0).reshape(NS, D) for c in range(8)])[:, None, :]

    def conv_p(name, ncu):
        return np.stack([R[c][name].transpose(0, 3, 2, 1).reshape(2, 3, ncu * 128) for c in range(4)], 1)

    def conv_s(name, ncu):
        return np.concatenate([R[c][name].transpose(0, 4, 3, 2, 1).reshape(2, 16, 3, ncu * 128) for c in range(8)], 1)

    def ssd_back(a):
        sh = a.shape[:-3]
        return a.reshape(*sh, NU_C, 128, 2, 64).transpose(*range(len(sh)), len(sh), len(sh) + 2, len(sh) + 3, len(sh) + 1).reshape(*sh, HC, 64, 128)
    p_gdn_conv = conv_p("o_cva", NCA)
    p_gdn = np.stack([R[c]["o_gdn"] for c in range(4)], 1)
    p_gla = np.stack([R[c]["o_gla"] for c in range(4)], 1)
    p_ssd_conv = conv_p("o_cvc", NCC)
    p_ssd = np.stack([ssd_back(R[c]["o_ssd"]) for c in range(4)], 1)
    s_gdn_conv = conv_s("so_cva", NCA)
    s_gdn = np.concatenate([R[c]["so_gdn"] for c in range(8)], 1)
    s_gla = np.concatenate([R[c]["so_gla"] for c in range(8)], 1)
    s_ssd_conv = conv_s("so_cvc", NCC)
    s_ssd = np.concatenate([ssd_back(R[c]["so_ssd"]) for c in range(8)], 1)
    outs = (y_prompt, y_sample, p_gdn_conv, p_gdn, p_gla, p_ssd_conv, p_ssd, s_gdn_conv, s_gdn, s_gla, s_ssd_conv, s_ssd)
    return tuple(np.ascontiguousarray(o, dtype=np.float32) for o in outs)
```
